# Optimizing a Trainium2 kernel written in Bass

```python
import jax, jax.numpy as jnp
from jax import lax
import numpy as np

D_MODEL = 1024
BATCH = 8
SEQ = 4096
DEPTH = 1
DEC_BATCH = 32
DEC_SEQ = 16
PAST_LEN = 1024

CHUNK = 64
HEAD_DIM = 64
N_FOX = 6
N_RWKV = 6
N_MEM = 4
W_FOX = N_FOX * HEAD_DIM
W_RWKV = N_RWKV * HEAD_DIM
W_MEM = N_MEM * HEAD_DIM
D_MIX = W_FOX + W_RWKV + W_MEM
N_MEM_TOK = 256
LORA_W = 32
LORA_A = 32
Q_BLOCK = 128
NORM_EPS = 1e-6
GN_EPS = 64e-5

FOX_COLS = 4 * W_FOX + N_FOX
RWKV_COLS = 4 * W_RWKV + LORA_W + LORA_A
MEM_COLS = 2 * W_MEM
N_IN = FOX_COLS + RWKV_COLS + MEM_COLS

kernel_name = "fox_rwkv7_memxattn_streaming_step"


def rms_norm(x, g, eps=NORM_EPS):
    xf = x.astype(jnp.float32)
    y = xf * lax.rsqrt(jnp.mean(xf * xf, axis=-1, keepdims=True) + eps)
    return (y * g.astype(jnp.float32)).astype(x.dtype)


def fox_prep(cols, q_g, k_g, b_f):
    B, T, _ = cols.shape
    q, k, v, f, g = jnp.split(cols, [W_FOX, 2 * W_FOX, 3 * W_FOX, 3 * W_FOX + N_FOX], axis=-1)
    q = rms_norm(q.reshape(B, T, N_FOX, HEAD_DIM), q_g)
    k = rms_norm(k.reshape(B, T, N_FOX, HEAD_DIM), k_g)
    v = v.reshape(B, T, N_FOX, HEAD_DIM)
    logf = jax.nn.log_sigmoid((f + b_f).astype(jnp.float32))
    return q, k, v, logf, g


def fox_attend_block(q, cq, qpos, k, v, ck, kpos):
    s = jnp.einsum('bqhd,bkhd->bhqk', q, k, preferred_element_type=jnp.float32) * (HEAD_DIM ** -0.5)
    s = s + jnp.transpose(cq, (0, 2, 1))[..., :, None] - jnp.transpose(ck, (0, 2, 1))[..., None, :]
    mask = kpos[None, :] <= qpos[:, None]
    s = jnp.where(mask, s, jnp.finfo(jnp.float32).min)
    p = jax.nn.softmax(s, axis=-1)
    return jnp.einsum('bhqk,bkhd->bqhd', p.astype(v.dtype), v)


def fox_prompt(q, k, v, logf):
    B, T = q.shape[:2]
    c = jnp.cumsum(logf, axis=1)
    kpos = jnp.arange(T)

    def block(i):
        s0 = i * Q_BLOCK
        qb = lax.dynamic_slice_in_dim(q, s0, Q_BLOCK, axis=1)
        cb = lax.dynamic_slice_in_dim(c, s0, Q_BLOCK, axis=1)
        return fox_attend_block(qb, cb, s0 + jnp.arange(Q_BLOCK), k, v, c, kpos)

    o = lax.map(block, jnp.arange(T // Q_BLOCK))
    return jnp.transpose(o, (1, 0, 2, 3, 4)).reshape(B, T, W_FOX)


def fox_sample(q, k, v, logf, k_past, v_past, logf_past):
    B, S = q.shape[:2]
    P = k_past.shape[1]
    keys = jnp.concatenate([k_past.astype(k.dtype), k], axis=1)
    vals = jnp.concatenate([v_past.astype(v.dtype), v], axis=1)
    c = jnp.cumsum(jnp.concatenate([logf_past.astype(jnp.float32), logf], axis=1), axis=1)
    o = fox_attend_block(q, c[:, P:], P + jnp.arange(S), keys, vals, c, jnp.arange(P + S))
    return o.reshape(B, S, W_FOX)


def token_shift(cols, prev, mu):
    shifted = jnp.concatenate([prev.astype(cols.dtype), cols[:, :-1]], axis=1)
    return cols + (shifted - cols) * mu


def rwkv_scan(s0, r, w, k, v, kk, a):
    def step(S, inp):
        r_t, w_t, k_t, v_t, kk_t, a_t = inp
        sa = jnp.einsum('bhvk,bhk->bhv', S, -kk_t)
        S = S * w_t[:, :, None, :] + sa[..., :, None] * (kk_t * a_t)[:, :, None, :] \
            + v_t[..., :, None] * k_t[:, :, None, :]
        y = jnp.einsum('bhvk,bhk->bhv', S, r_t)
        return S, y

    xs = tuple(jnp.moveaxis(t, 1, 0) for t in (r, w, k, v, kk, a))
    S, ys = lax.scan(step, s0.astype(jnp.float32), xs)
    return S, jnp.moveaxis(ys, 0, 1)


def rwkv_branch(xs, s0, w0, w_up, a0, a_up, k_k, k_a, r_k, gn_w, gn_b):
    B, T, _ = xs.shape
    f32 = jnp.float32
    r, k, v, wd, ad, g = jnp.split(
        xs, [W_RWKV, 2 * W_RWKV, 3 * W_RWKV, 3 * W_RWKV + LORA_W, 3 * W_RWKV + LORA_W + LORA_A], axis=-1)
    w_raw = -jax.nn.softplus(-(w0 + jnp.tanh(wd) @ w_up).astype(f32)) - 0.5
    decay = jnp.exp(-jnp.exp(w_raw))
    a = jax.nn.sigmoid((a0 + ad @ a_up).astype(f32))

    def hd(t):
        return t.reshape(B, T, N_RWKV, HEAD_DIM).astype(f32)

    def ph(t):
        return t.reshape(N_RWKV, HEAD_DIM).astype(f32)

    r, k, v, decay, a = hd(r), hd(k), hd(v), hd(decay), hd(a)
    kk = k * ph(k_k)
    kk = kk * lax.rsqrt(jnp.maximum(jnp.sum(kk * kk, axis=-1, keepdims=True), 1e-24))
    k = k * (1.0 + (a - 1.0) * ph(k_a))
    S, y = rwkv_scan(s0, r, decay, k, v, kk, a)
    mu = jnp.mean(y, axis=-1, keepdims=True)
    var = jnp.mean(jnp.square(y - mu), axis=-1, keepdims=True)
    yn = (y - mu) * lax.rsqrt(var + GN_EPS) * ph(gn_w) + ph(gn_b)
    bonus = jnp.sum(r * k * ph(r_k), axis=-1, keepdims=True) * v
    out = (yn + bonus).reshape(B, T, W_RWKV).astype(xs.dtype)
    return out, S, g


def mem_kv(mem, mem_norm_g, w_mem_kv, mem_k_g):
    B, M, _ = mem.shape
    kv = rms_norm(mem, mem_norm_g) @ w_mem_kv
    k, v = jnp.split(kv, [W_MEM], axis=-1)
    k = rms_norm(k.reshape(B, M, N_MEM, HEAD_DIM), mem_k_g)
    return k, v.reshape(B, M, N_MEM, HEAD_DIM)


def mem_attend(q_cols, mem_q_g, mk, mv):
    B, T, _ = q_cols.shape
    q = rms_norm(q_cols.reshape(B, T, N_MEM, HEAD_DIM), mem_q_g)
    s = jnp.einsum('bthd,bmhd->bhtm', q, mk.astype(q.dtype), preferred_element_type=jnp.float32) * (HEAD_DIM ** -0.5)
    p = jax.nn.softmax(s, axis=-1)
    o = jnp.einsum('bhtm,bmhd->bthd', p.astype(q.dtype), mv.astype(q.dtype))
    return o.reshape(B, T, W_MEM)


def mixer_layer(x, mk, mv, shift_prev, rwkv_s0, fox_past, p):
    (norm_g, w_in, fox_q_g, fox_k_g, fox_b_f, rwkv_mu, rwkv_w0, rwkv_w_up, rwkv_a0, rwkv_a_up,
     rwkv_k_k, rwkv_k_a, rwkv_r_k, rwkv_gn_w, rwkv_gn_b, mem_q_g, w_out) = p
    h = rms_norm(x, norm_g) @ w_in
    fox_cols, rwkv_cols, mem_cols = jnp.split(h, [FOX_COLS, FOX_COLS + RWKV_COLS], axis=-1)
    q, k, v, logf, g_f = fox_prep(fox_cols, fox_q_g, fox_k_g, fox_b_f)
    if fox_past is None:
        o_f = fox_prompt(q, k, v, logf)
    else:
        o_f = fox_sample(q, k, v, logf, *fox_past)
    xs = token_shift(rwkv_cols, shift_prev, rwkv_mu)
    o_r, s_new, g_r = rwkv_branch(xs, rwkv_s0, rwkv_w0, rwkv_w_up, rwkv_a0, rwkv_a_up,
                                  rwkv_k_k, rwkv_k_a, rwkv_r_k, rwkv_gn_w, rwkv_gn_b)
    mq, g_m = jnp.split(mem_cols, [W_MEM], axis=-1)
    o_m = mem_attend(mq, mem_q_g, mk, mv)
    o = jnp.concatenate([o_f.astype(x.dtype) * jax.nn.silu(g_f),
                         o_r * jax.nn.silu(g_r),
                         o_m * jax.nn.silu(g_m)], axis=-1)
    y = x + o @ w_out
    return y, k, v, logf, s_new, rwkv_cols[:, -1:]


def setup_inputs(seed: int = 0) -> dict:
    key = jax.random.key(seed)
    ks = iter(jax.random.split(key, 48))

    def nrm(shape, s=1.0):
        return s * jax.random.normal(next(ks), shape, jnp.float32)

    def gain(shape):
        return 1.0 + 0.05 * nrm(shape)

    return {
        "x_prompt": nrm((BATCH, SEQ, D_MODEL)),
        "x_sample": nrm((DEC_BATCH, DEC_SEQ, D_MODEL)),
        "mem_prompt": nrm((BATCH, N_MEM_TOK, D_MODEL)),
        "cache_fox_k": nrm((DEPTH, DEC_BATCH, PAST_LEN, N_FOX, HEAD_DIM)),
        "cache_fox_v": nrm((DEPTH, DEC_BATCH, PAST_LEN, N_FOX, HEAD_DIM)),
        "cache_fox_logf": jax.nn.log_sigmoid(2.0 + nrm((DEPTH, DEC_BATCH, PAST_LEN, N_FOX))),
        "cache_mem_k": nrm((DEPTH, DEC_BATCH, N_MEM_TOK, N_MEM, HEAD_DIM)),
        "cache_mem_v": nrm((DEPTH, DEC_BATCH, N_MEM_TOK, N_MEM, HEAD_DIM)),
        "state_rwkv": nrm((DEPTH, DEC_BATCH, N_RWKV, HEAD_DIM, HEAD_DIM), 0.3),
        "state_rwkv_shift": nrm((DEPTH, DEC_BATCH, 1, RWKV_COLS)),
        "norm_g": gain((DEPTH, D_MODEL)),
        "w_in": nrm((DEPTH, D_MODEL, N_IN), D_MODEL ** -0.5),
        "fox_q_g": gain((DEPTH, HEAD_DIM)),
        "fox_k_g": gain((DEPTH, HEAD_DIM)),
        "fox_b_f": 2.0 + 0.1 * nrm((DEPTH, N_FOX)),
        "rwkv_mu": jax.random.uniform(next(ks), (DEPTH, RWKV_COLS), jnp.float32, 0.1, 0.9),
        "rwkv_w0": -0.5 + 0.5 * nrm((DEPTH, W_RWKV)),
        "rwkv_w_up": nrm((DEPTH, LORA_W, W_RWKV), 0.3 * LORA_W ** -0.5),
        "rwkv_a0": 0.1 * nrm((DEPTH, W_RWKV)),
        "rwkv_a_up": nrm((DEPTH, LORA_A, W_RWKV), 0.5 * LORA_A ** -0.5),
        "rwkv_k_k": 0.85 + 0.05 * nrm((DEPTH, W_RWKV)),
        "rwkv_k_a": gain((DEPTH, W_RWKV)),
        "rwkv_r_k": 0.1 * nrm((DEPTH, W_RWKV)),
        "rwkv_gn_w": gain((DEPTH, W_RWKV)),
        "rwkv_gn_b": 0.01 * nrm((DEPTH, W_RWKV)),
        "mem_norm_g": gain((DEPTH, D_MODEL)),
        "w_mem_kv": nrm((DEPTH, D_MODEL, 2 * W_MEM), D_MODEL ** -0.5),
        "mem_q_g": gain((DEPTH, HEAD_DIM)),
        "mem_k_g": gain((DEPTH, HEAD_DIM)),
        "w_out": nrm((DEPTH, D_MIX, D_MODEL), D_MIX ** -0.5),
    }


def reference(x_prompt, x_sample, mem_prompt, cache_fox_k, cache_fox_v, cache_fox_logf,
              cache_mem_k, cache_mem_v, state_rwkv, state_rwkv_shift,
              norm_g, w_in, fox_q_g, fox_k_g, fox_b_f, rwkv_mu, rwkv_w0, rwkv_w_up, rwkv_a0,
              rwkv_a_up, rwkv_k_k, rwkv_k_a, rwkv_r_k, rwkv_gn_w, rwkv_gn_b,
              mem_norm_g, w_mem_kv, mem_q_g, mem_k_g, w_out):
    B = x_prompt.shape[0]
    yp, ys = x_prompt, x_sample
    fkp, fvp, flp, mkp, mvp, rsp, rhp = [], [], [], [], [], [], []
    fks, fvs, fls, rss, rhs = [], [], [], [], []
    for l in range(DEPTH):
        p = (norm_g[l], w_in[l], fox_q_g[l], fox_k_g[l], fox_b_f[l], rwkv_mu[l], rwkv_w0[l],
             rwkv_w_up[l], rwkv_a0[l], rwkv_a_up[l], rwkv_k_k[l], rwkv_k_a[l], rwkv_r_k[l],
             rwkv_gn_w[l], rwkv_gn_b[l], mem_q_g[l], w_out[l])
        mk, mv = mem_kv(mem_prompt, mem_norm_g[l], w_mem_kv[l], mem_k_g[l])
        shift0 = jnp.zeros((B, 1, RWKV_COLS), x_prompt.dtype)
        s0 = jnp.zeros((B, N_RWKV, HEAD_DIM, HEAD_DIM), jnp.float32)
        yp, k, v, lf, s_new, sh_new = mixer_layer(yp, mk, mv, shift0, s0, None, p)
        fkp.append(k); fvp.append(v); flp.append(lf); mkp.append(mk); mvp.append(mv)
        rsp.append(s_new); rhp.append(sh_new)
        ys, k, v, lf, s_new, sh_new = mixer_layer(
            ys, cache_mem_k[l], cache_mem_v[l], state_rwkv_shift[l], state_rwkv[l],
            (cache_fox_k[l], cache_fox_v[l], cache_fox_logf[l]), p)
        fks.append(k); fvs.append(v); fls.append(lf); rss.append(s_new); rhs.append(sh_new)
    return (yp, ys,
            jnp.stack(fkp), jnp.stack(fvp), jnp.stack(flp), jnp.stack(mkp), jnp.stack(mvp),
            jnp.stack(rsp), jnp.stack(rhp),
            jnp.stack(fks), jnp.stack(fvs), jnp.stack(fls), jnp.stack(rss), jnp.stack(rhs))
```

```python
import numpy as np
from contextlib import ExitStack
import concourse.bass as bass
import concourse.mybir as mybir
from concourse.bass_utils import run_bass_kernel_spmd

F32 = mybir.dt.float32
BF16 = mybir.dt.bfloat16
AF = mybir.ActivationFunctionType
ALU = mybir.AluOpType
AX = mybir.AxisListType

D = 1024
T = 4096
NT = T // 128
SB_ = 4
SS = 16
PAST = 1024
NIN = 3654
EPS = 1e-6
GN_EPS = 64e-5
C_Q, C_K, C_V, C_F, C_GF = 0, 384, 768, 1152, 1158
C_RW = 1542
C_RR, C_RK, C_RV, C_WD, C_AD, C_GR = C_RW, C_RW + 384, C_RW + 768, C_RW + 1152, C_RW + 1184, C_RW + 1216
C_MQ, C_GM = 3142, 3398


class Buf:
    __slots__ = ("w", "r", "dsem", "dcnt", "name", "excl")

    def __init__(self, name="", excl=False):
        self.w = None
        self.r = []
        self.dsem = None
        self.dcnt = 0
        self.name = name
        self.excl = excl


class Emit:
    ENG = ("pe", "act", "dve", "pool", "sp")

    def __init__(self, nc, stack):
        self.nc = nc
        self.stack = stack
        self.ops = {e: [] for e in self.ENG}
        self.cnt = {e: 0 for e in self.ENG}
        self.sems = {}
        for e in self.ENG:
            self.sems[e] = stack.enter_context(nc.semaphore("sem_" + e))
        self.known = {e: {} for e in self.ENG}
        self.nd = 0
        self.dbufs = []

    def _waits(self, eng, reads, writes):
        need = {}
        for b in reads:
            if b.w is not None:
                k, v = b.w
                if need.get(k, 0) < v:
                    need[k] = v
            if b.excl:
                for k, v in b.r:
                    if k != eng and need.get(k, 0) < v:
                        need[k] = v
        for b in writes:
            if b.w is not None:
                k, v = b.w
                if need.get(k, 0) < v:
                    need[k] = v
            for k, v in b.r:
                if need.get(k, 0) < v:
                    need[k] = v
        out = []
        kn = self.known[eng]
        for k, v in need.items():
            if kn.get(k, 0) < v:
                kn[k] = v
                out.append((self.sems[k], v))
        return out

    def _mark(self, ev, reads, writes):
        for b in reads:
            b.r = [x for x in b.r if x[0] != ev[0]]
            b.r.append(ev)
        for b in writes:
            b.w = ev
            b.r = []

    def op(self, eng, fn, reads=(), writes=(), chain=False):
        prev_known_pe = self.known["pe"].get("pe", 0) if eng == "pe" else None
        wl = self._waits(eng, reads, writes)
        if chain and eng == "pe":
            sem_pe = self.sems["pe"]
            keep = []
            for s_, v_ in wl:
                if s_ is sem_pe and v_ == self.cnt["pe"]:
                    self.known["pe"]["pe"] = prev_known_pe
                    continue
                keep.append((s_, v_))
            wl = keep
        self.cnt[eng] += 1
        ev = (eng, self.cnt[eng])
        sem = self.sems[eng]

        def run(e, fn=fn, wl=wl, sem=sem):
            for s, v in wl:
                e.wait_ge(s, v)
            fn(e).then_inc(sem, 1)
        self.ops[eng].append(run)
        self._mark(ev, reads, writes)
        return ev

    def dma(self, eng, out, in_, reads=(), writes=(), dbuf=None, **kw):
        if dbuf.dsem is None:
            dbuf.dsem = {}
            dbuf.dcnt = {}
            self.dbufs.append(dbuf)
        if eng not in dbuf.dsem:
            self.nd += 1
            key = "d%d" % self.nd
            self.sems[key] = self.stack.enter_context(self.nc.semaphore("sem_" + key))
            dbuf.dsem[eng] = key
            dbuf.dcnt[eng] = 0
        wl = self._waits(eng, reads, writes)
        dbuf.dcnt[eng] += 16
        key = dbuf.dsem[eng]
        ev = (key, dbuf.dcnt[eng])
        sem = self.sems[key]

        def run(e, wl=wl, sem=sem, out=out, in_=in_, kw=kw):
            for s, v in wl:
                e.wait_ge(s, v)
            e.dma_start(out=out, in_=in_, **kw).then_inc(sem, 16)
        self.ops[eng].append(run)
        self._mark(ev, reads, writes)
        return ev

    def final_wait(self, eng):
        wl = []
        kn = self.known[eng]
        for b in self.dbufs:
            for q, key in b.dsem.items():
                v = b.dcnt[q]
                if kn.get(key, 0) < v:
                    kn[key] = v
                    wl.append((self.sems[key], v))

        def run(e, wl=wl):
            for s, v in wl:
                e.wait_ge(s, v)
        self.ops[eng].append(run)

    def replay(self):
        nc = self.nc
        ops = self.ops
        with nc.Block() as block:
            @block.tensor
            def _(e):
                for f in ops["pe"]:
                    f(e)

            @block.scalar
            def _(e):
                for f in ops["act"]:
                    f(e)

            @block.vector
            def _(e):
                for f in ops["dve"]:
                    f(e)

            @block.gpsimd
            def _(e):
                for f in ops["pool"]:
                    f(e)

            @block.sync
            def _(e):
                for f in ops["sp"]:
                    f(e)


class TT:
    def __init__(self, ap, name=""):
        self.t = ap
        self.b = Buf(name)

    def __getitem__(self, k):
        return self.t[k]


class Prog:
    def __init__(self):
        self.nc = bass.Bass("TRN2", target_bir_lowering=False)
        self.st = ExitStack()
        self.em = Emit(self.nc, self.st)
        self.rr = 0

    def dram(self, name, shape, kind):
        return self.nc.dram_tensor(name, list(shape), F32, kind=kind).ap()

    def sb(self, name, shape, dt=F32):
        return TT(self.st.enter_context(self.nc.sbuf_tensor(name, list(shape), dt)), name)

    def ps(self, name, shape, dt=F32):
        t = TT(self.st.enter_context(self.nc.psum_tensor(name, list(shape), dt)), name)
        t.b.excl = True
        return t

    def act(self, out, in_, func, r, w, **kw):
        return self.em.op("act", lambda e: e.activation(out=out, in_=in_, func=func, **kw), r, w)

    def tt(self, eng, out, in0, in1, op, r, w):
        return self.em.op(eng, lambda e: e.tensor_tensor(out=out, in0=in0, in1=in1, op=op), r, w)

    def ts(self, eng, out, in0, s1, s2, op0, op1, r, w):
        if s2 is None:
            return self.em.op(eng, lambda e: e.tensor_scalar(out=out, in0=in0, scalar1=s1, scalar2=None, op0=op0), r, w)
        return self.em.op(eng, lambda e: e.tensor_scalar(out=out, in0=in0, scalar1=s1, scalar2=s2, op0=op0, op1=op1), r, w)

    def stt(self, out, in0, scalar, in1, op0, op1, r, w):
        return self.em.op("dve", lambda e: e.scalar_tensor_tensor(out=out, in0=in0, scalar=scalar, in1=in1, op0=op0, op1=op1), r, w)

    def cp(self, eng, out, in_, r, w):
        if eng == "act":
            return self.em.op("act", lambda e: e.activation(out=out, in_=in_, func=AF.Copy), r, w)
        return self.em.op(eng, lambda e: e.tensor_copy(out=out, in_=in_), r, w)

    def mm(self, out, lhsT, rhs, start, stop, r, w, chain=False):
        return self.em.op("pe", lambda e: e.matmul(out, lhsT=lhsT, rhs=rhs, start=start, stop=stop), r, w, chain=chain)

    def tr(self, out, in_, ident, r, w):
        return self.em.op("pe", lambda e: e.transpose(out, in_, ident), r, w)

    def memset(self, eng, ap, val, w):
        return self.em.op(eng, lambda e: e.memset(ap, val), (), w)

    def asel(self, out, in_, pattern, op, fill, base, cm, r, w):
        return self.em.op("pool", lambda e: e.affine_select(out=out, in_=in_, pattern=pattern, compare_op=op,
                                                            fill=fill, base=base, channel_multiplier=cm), r, w)

    def load(self, out_tt, out_ap, in_ap, **kw):
        return self.em.dma("sp", out_ap, in_ap, reads=(), writes=[out_tt.b], dbuf=out_tt.b, **kw)

    def store(self, out_ap, in_tt, in_ap, **kw):
        return self.em.dma("pool", out_ap, in_ap, reads=[in_tt.b], writes=(), dbuf=in_tt.b, **kw)

    def rot(self):
        self.rr += 1
        return ("act", "dve", "pool")[self.rr % 3]


def build():
    P = Prog()
    nc = P.nc
    IN, OUT = "ExternalInput", "ExternalOutput"
    NQ = 256
    xp = P.dram("xp", [T, D], IN)
    xsm = P.dram("xsm", [SB_ * SS, D], IN)
    memp = P.dram("memp", [256, D], IN)
    cfk = P.dram("cfk", [SB_, PAST, 384], IN)
    cfv = P.dram("cfv", [SB_, PAST, 384], IN)
    cfl = P.dram("cfl", [SB_, PAST, 6], IN)
    cmk = P.dram("cmk", [SB_, 256, 256], IN)
    cmv = P.dram("cmv", [SB_, 256, 256], IN)
    srw = P.dram("srw", [SB_, 6, 64, 64], IN)
    ssh = P.dram("ssh", [SB_, 1600], IN)
    norm_g = P.dram("norm_g", [D], IN)
    w_in = P.dram("w_in", [D, NIN], IN)
    fox_q_g = P.dram("fox_q_g", [64], IN)
    fox_k_g = P.dram("fox_k_g", [64], IN)
    fox_b_f = P.dram("fox_b_f", [6], IN)
    rwkv_mu = P.dram("rwkv_mu", [1600], IN)
    rwkv_w0 = P.dram("rwkv_w0", [384], IN)
    rwkv_w_up = P.dram("rwkv_w_up", [32, 384], IN)
    rwkv_a0 = P.dram("rwkv_a0", [384], IN)
    rwkv_a_up = P.dram("rwkv_a_up", [32, 384], IN)
    rwkv_k_k = P.dram("rwkv_k_k", [384], IN)
    rwkv_k_a = P.dram("rwkv_k_a", [384], IN)
    rwkv_r_k = P.dram("rwkv_r_k", [384], IN)
    rwkv_gn_w = P.dram("rwkv_gn_w", [384], IN)
    rwkv_gn_b = P.dram("rwkv_gn_b", [384], IN)
    mem_norm_g = P.dram("mem_norm_g", [D], IN)
    w_mem_kv = P.dram("w_mem_kv", [D, 512], IN)
    mem_q_g = P.dram("mem_q_g", [64], IN)
    mem_k_g = P.dram("mem_k_g", [64], IN)
    w_out = P.dram("w_out", [D, D], IN)

    o_yp = P.dram("o_yp", [T, D], OUT)
    o_ys = P.dram("o_ys", [SB_ * SS, D], OUT)
    o_fkp = P.dram("o_fkp", [T, 384], OUT)
    o_fvp = P.dram("o_fvp", [T, 384], OUT)
    o_flp = P.dram("o_flp", [T, 6], OUT)
    o_mkp = P.dram("o_mkp", [256, 256], OUT)
    o_mvp = P.dram("o_mvp", [256, 256], OUT)
    o_rsp = P.dram("o_rsp", [6, 64, 64], OUT)
    o_rhp = P.dram("o_rhp", [1600], OUT)
    o_fks = P.dram("o_fks", [SB_ * SS, 384], OUT)
    o_fvs = P.dram("o_fvs", [SB_ * SS, 384], OUT)
    o_fls = P.dram("o_fls", [SB_ * SS, 6], OUT)
    o_rss = P.dram("o_rss", [SB_, 6, 64, 64], OUT)
    o_rhs = P.dram("o_rhs", [SB_, 1600], OUT)

    Wb = P.sb("Wb", [128, 8, NIN], BF16)
    wst = [P.sb("wst%d" % i, [128, D], BF16) for i in range(2)]
    wo_bf = nc.dram_tensor("wo_bf", [128, 8, D], BF16, kind="Internal").ap()
    wo_b = Buf("wo_bf")
    kT = P.sb("kT", [67, 6, T], BF16)
    Vaug = P.sb("Vaug", [128, NT, 6, 65], BF16)
    negc = P.sb("negc", [128, NT, 6])
    kvb = [Buf("kv%d" % i) for i in range(NT)]
    xt = [P.sb("xt%d" % i, [128, D]) for i in range(2)]
    xb = P.sb("xb", [128, D], BF16)
    xnT = P.sb("xnT", [128, 8, 128], BF16)
    identb = P.sb("identb", [128, 128], BF16)
    identf = P.sb("identf", [128, 128])
    trif = P.sb("trif", [128, 128])
    lastf = P.sb("lastf", [128, 128])
    bones = P.sb("bones", [128, 128])
    bavg = P.sb("bavg", [128, 128])
    ones = P.sb("ones", [128, 128])
    mask4 = P.sb("mask4", [128, 4, 128], BF16)
    msl = P.sb("msl", [128, 128], BF16)
    ng = P.sb("ng", [128, 8])
    mng = P.sb("mng", [128, 8])
    gq = P.sb("gq", [128, 64])
    gk = P.sb("gk", [128, 64])
    gmq = P.sb("gmq", [128, 64])
    gmk = P.sb("gmk", [128, 64])
    bfb = P.sb("bfb", [128, 6])
    small = P.sb("small", [128, 64])
    tmpA = P.sb("tmpA", [128, 384])
    tmpB = P.sb("tmpB", [128, 384])
    tmpC = P.sb("tmpC", [128, 384])
    qaug = P.sb("qaug", [128, 6, 67], BF16)
    kaug = P.sb("kaug", [128, 6, 67], BF16)
    mqa = P.sb("mqa", [128, 4, 64], BF16)
    cc = [P.sb("cc%d" % i, [128, 6]) for i in range(2)]
    cr = P.sb("cr", [128, 6])
    lf = P.sb("lf", [128, 6])
    raw = P.sb("raw", [128, 13, 129])
    xs = P.sb("xs", [128, 4, 128])
    xw = P.sb("xw", [64, 128])
    mu = P.sb("mu", [128, 13])
    rp = P.sb("rp", [128, 3, 8])
    lora = P.sb("lora", [64, 384], BF16)
    qT = P.sb("qT", [67, 6, NQ], BF16)
    mqT = P.sb("mqT", [64, 4, NQ], BF16)
    mkT = P.sb("mkT", [64, 4, 256], BF16)
    mvaug = P.sb("mvaug", [128, 2, 4, 65], BF16)
    gt = P.sb("gt", [128, 8, NQ], BF16)
    og = P.sb("og", [128, 8, NQ], BF16)
    gtb = [Buf("gt%d" % i) for i in range(8)]
    ogb = [Buf("og%d" % i) for i in range(8)]
    pts = [P.sb("pt%d" % i, [128, NQ], BF16) for i in range(3)]
    oun = P.sb("oun", [65, NQ])
    ounB = P.sb("ounB", [65, NQ])
    opair = P.sb("opair", [128, NQ])
    W3 = lambda name, dt=F32: P.sb(name, [128, 1, 128], dt)
    r_lw, r_a, r_g, r_eg, r_egm, r_eng = W3("r_lw"), W3("r_a"), W3("r_g"), W3("r_eg"), W3("r_egm"), W3("r_eng")
    r_kk, r_t1, r_t2 = W3("r_kk"), W3("r_t1"), W3("r_t2")
    r_yT2 = [W3("r_yT0"), W3("r_yT1")]
    r_bon2 = [W3("r_bon0"), W3("r_bon1")]
    rpc = [0]
    pend_gn = [None]

    def flush_gn():
        if pend_gn[0] is not None:
            t_ = pend_gn[0]
            pend_gn[0] = None
            t_()
    ART = P.sb("ART", [128, 1, 2, 128], BF16)
    BTt = W3("BTt", BF16)
    KTt = W3("KTt", BF16)
    vbt = W3("vbt", BF16)
    twd = P.sb("twd", [64, 128], BF16)
    tokA = P.sb("tokA", [128, 128], BF16)
    tokB = P.sb("tokB", [128, 128], BF16)
    tokK = P.sb("tokK", [128, 128], BF16)
    tokV = P.sb("tokV", [128, 128], BF16)
    gm = [P.sb("gm%d" % h, [128, 4, 128], BF16) for h in range(2)]
    PP = [[P.sb("PP%d_%d" % (h, i), [128, 2, 128], BF16) for i in range(2)] for h in range(2)]
    XX = [[P.sb("XX%d_%d" % (h, i), [128, 128], BF16) for i in range(2)] for h in range(2)]
    WT = P.sb("WT", [128, 1, 128], BF16)
    LVs = P.sb("LVs", [128, 2, 64], BF16)
    U0 = P.sb("U0", [128, 2, 64])
    Ub = P.sb("Ub", [128, 2, 64], BF16)
    Hf = P.sb("Hf", [128, 3, 64])
    Hb = P.sb("Hb", [128, 3, 64], BF16)
    Dp = P.sb("Dp", [128, 1, 64])
    stS = P.sb("stS", [64, 6, 64])

    ps_tr = P.ps("ps_tr", [128, 1024], BF16)
    ps_sA = P.ps("ps_sA", [128, 512])
    ps_sB = P.ps("ps_sB", [128, 512])
    ps_o = P.ps("ps_o", [128, 512])
    pgs = [P.ps("pg%d" % i, [128, 512]) for i in range(4)]
    pgi = [0]

    def pg():
        pgi[0] += 1
        return pgs[pgi[0] % len(pgs)]

    class View:
        def __init__(self, ap):
            self.t = ap
            self.b = Buf(excl=True)

        def __getitem__(self, k):
            return self.t[k]
    ps_s2 = [ps_sA, ps_sB]
    ps_o2 = [View(ps_o[:, 0:256]), View(ps_o[:, 256:512])]
    ps_o2[1].b = ps_o2[0].b

    P.memset("pool", identb[:], 0.0, [identb.b])
    P.asel(identb[:], identb[:], [[-1, 128]], ALU.not_equal, 1.0, 0, 1, [identb.b], [identb.b])
    P.memset("pool", identf[:], 0.0, [identf.b])
    P.asel(identf[:], identf[:], [[-1, 128]], ALU.not_equal, 1.0, 0, 1, [identf.b], [identf.b])
    P.memset("pool", trif[:], 1.0, [trif.b])
    P.asel(trif[:], trif[:], [[1, 128]], ALU.is_ge, 0.0, 0, -1, [trif.b], [trif.b])
    P.memset("pool", lastf[:], 1.0, [lastf.b])
    P.asel(lastf[:], lastf[:], [[0, 128]], ALU.is_ge, 0.0, -127, 1, [lastf.b], [lastf.b])
    P.memset("dve", bones[:], 0.0, [bones.b])
    P.memset("dve", bones[0:64, 0:64], 1.0, [bones.b])
    P.memset("dve", bones[64:128, 64:128], 1.0, [bones.b])
    P.ts("dve", bavg[:], bones[:], 1.0 / 64, None, ALU.mult, None, [bones.b], [bavg.b])
    P.memset("dve", ones[:], 1.0, [ones.b])
    P.memset("pool", mask4[:], 1.0, [mask4.b])
    for i in range(4):
        P.asel(mask4[:, i, :], mask4[:, i, :], [[1, 128]], ALU.is_ge, 0.0, (-1 if i % 2 == 0 else 0), -1, [mask4.b], [mask4.b])
    P.memset("pool", msl[:], 1.0, [msl.b])
    P.asel(msl[:], msl[:], [[-1, 128]], ALU.is_ge, 0.0, -1, 1, [msl.b], [msl.b])

    def cload(out_ap, in_ap, tt_, **kw):
        P.em.dma("sp", out_ap, in_ap, reads=(), writes=[tt_.b], dbuf=tt_.b, **kw)

    cload(ng[:], norm_g.rearrange("(c p) -> p c", p=128), ng, allow_slow_non_contiguous=True)
    cload(mng[:], mem_norm_g.rearrange("(c p) -> p c", p=128), mng, allow_slow_non_contiguous=True)
    for tl, src in ((gq, fox_q_g), (gk, fox_k_g), (gmq, mem_q_g), (gmk, mem_k_g)):
        cload(tl[:], src.partition_broadcast(128), tl)
    cload(bfb[:], fox_b_f.partition_broadcast(128), bfb)
    P.ts("dve", gq[:], gq[:], 0.125, None, ALU.mult, None, [gq.b], [gq.b])
    P.ts("dve", gmq[:], gmq[:], 0.125, None, ALU.mult, None, [gmq.b], [gmq.b])
    cload(mu[:, 0:9], rwkv_mu[0:1152].rearrange("(b p) -> p b", p=128), mu, allow_slow_non_contiguous=True)
    cload(mu[0:64, 9:10], rwkv_mu[1152:1216].rearrange("(b p) -> p b", p=64), mu, allow_slow_non_contiguous=True)
    cload(mu[:, 10:13], rwkv_mu[1216:1600].rearrange("(b p) -> p b", p=128), mu, allow_slow_non_contiguous=True)
    for i, src in enumerate((rwkv_w0, rwkv_a0, rwkv_k_k, rwkv_k_a, rwkv_k_a, rwkv_r_k, rwkv_gn_w, rwkv_gn_b)):
        cload(rp[:, :, i], src.rearrange("(b p) -> p b", p=128), rp, allow_slow_non_contiguous=True)
    P.ts("dve", rp[:, :, 4], rp[:, :, 4], -1.0, 1.0, ALU.mult, ALU.add, [rp.b], [rp.b])
    cload(tmpA[0:32, :], rwkv_w_up, tmpA)
    cload(tmpA[32:64, :], rwkv_a_up, tmpA)
    P.cp("dve", lora[:], tmpA[0:64, :], [tmpA.b], [lora.b])
    P.memset("dve", kaug[:, :, 64:67], 1.0, [kaug.b])
    P.memset("dve", raw[:], 0.0, [raw.b])
    P.memset("pool", mvaug[:, :, :, :].rearrange("p a h e -> p (a h) e")[:, :, 64:65], 1.0, [mvaug.b])

    def norm_T(src_dram, n, gtile, xtile):
        P.load(xtile, xtile[0:n, :], src_dram)
        P.em.op("act", lambda e: e.activation(out=xb[0:n, :], in_=xtile[0:n, :], func=AF.Square,
                                              accum_out=small[0:n, 0:1]), [xtile.b], [xb.b, small.b])
        P.act(small[0:n, 1:2], small[0:n, 0:1], AF.Ln, [small.b], [small.b], scale=1.0 / D, bias=EPS)
        P.act(small[0:n, 2:3], small[0:n, 1:2], AF.Exp, [small.b], [small.b], scale=-0.5)
        P.act(xb[0:n, :], xtile[0:n, :], AF.Copy, [xtile.b, small.b], [xb.b], scale=small[0:n, 2:3])
        for c in range(8):
            P.tr(ps_tr[:, c * 128:c * 128 + n], xb[0:n, c * 128:(c + 1) * 128], identb[0:n, 0:n], [xb.b, identb.b], [ps_tr.b])
        P.tt("dve", xnT[:, :, 0:n], ps_tr[:, :].rearrange("p (c t) -> p c t", t=128)[:, :, 0:n],
             gtile[:, :].unsqueeze(2).to_broadcast([128, 8, n]), ALU.mult, [ps_tr.b, gtile.b], [xnT.b])

    def proj_tm(n, c0, c1, pst, W=None):
        W = W or Wb
        for c in range(8):
            P.mm(pst[0:n, 0:c1 - c0], xnT[:, c, 0:n], W[:, c, c0:c1], c == 0, c == 7, [xnT.b, W.b], [pst.b], chain=(c > 0))

    def headnorm(n, pst, nh, gain, dst, out_bf=None, out_bf_b=None):
        w = nh * 64
        v3 = lambda ap: ap.rearrange("p (h d) -> p h d", d=64)
        P.act(tmpA[0:n, 0:w], pst[0:n, 0:w], AF.Square, [pst.b], [tmpA.b])
        P.em.op("dve", lambda e: e.tensor_reduce(out=small[0:n, 8:8 + nh], in_=v3(tmpA[0:n, 0:w]),
                                                 axis=AX.X, op=ALU.add), [tmpA.b], [small.b])
        P.act(small[0:n, 16:16 + nh], small[0:n, 8:8 + nh], AF.Ln, [small.b], [small.b], scale=1.0 / 64, bias=EPS)
        P.act(small[0:n, 24:24 + nh], small[0:n, 16:16 + nh], AF.Exp, [small.b], [small.b], scale=-0.5)
        P.tt("dve", v3(tmpA[0:n, 0:w]), v3(pst[0:n, 0:w]),
             small[0:n, 24:24 + nh].unsqueeze(2).to_broadcast([n, nh, 64]), ALU.mult, [pst.b, small.b], [tmpA.b])
        P.tt("dve", v3(dst[0:n, 0:w]), v3(tmpA[0:n, 0:w]),
             gain[0:n, :].unsqueeze(1).to_broadcast([n, nh, 64]), ALU.mult, [tmpA.b, gain.b], [dst.b])
        if out_bf is not None:
            P.cp("act", out_bf, v3(dst[0:n, 0:w]), [dst.b], [out_bf_b])

    def c_update(n, j, cprev, ccur):
        pst = pg()
        P.mm(pst[0:n, 0:6], trif[0:n, 0:n], lf[0:n, :], True, cprev is None, [trif.b, lf.b], [pst.b])
        if cprev is not None:
            P.mm(pst[0:n, 0:6], lastf[:, 0:n], cprev[:, :], False, True, [lastf.b, cprev.b], [pst.b], chain=True)
        P.cp("act", ccur[0:n, :], pst[0:n, 0:6], [pst.b], [ccur.b])
        P.ts("dve", negc[0:n, j, :], ccur[0:n, :], -1.0, None, ALU.mult, None, [ccur.b], [kvb[j]])

    def q_cpieces(n, ccur):
        P.cp("dve", qaug[0:n, :, 64], ccur[0:n, :], [ccur.b], [qaug.b])
        P.tt("dve", cr[0:n, :], ccur[0:n, :], qaug[0:n, :, 64], ALU.subtract, [ccur.b, qaug.b], [cr.b])
        P.cp("dve", qaug[0:n, :, 65], cr[0:n, :], [cr.b], [qaug.b])
        P.tt("dve", cr[0:n, :], cr[0:n, :], qaug[0:n, :, 65], ALU.subtract, [cr.b, qaug.b], [cr.b])
        P.cp("dve", qaug[0:n, :, 66], cr[0:n, :], [cr.b], [qaug.b])

    def k_to_T(n, j):
        for h in range(6):
            P.tr(ps_tr[0:67, h * 128:h * 128 + n], kaug[0:n, h, :], identb[0:n, 0:n], [kaug.b, identb.b], [ps_tr.b])
        P.cp("act", kT[:, :, j * 128:j * 128 + n], ps_tr[0:67, 0:768].rearrange("p (h t) -> p h t", t=128)[:, :, 0:n],
             [ps_tr.b], [kvb[j]])

    def q_to_T(n, qoff):
        for h in range(6):
            P.tr(ps_tr[0:67, h * 128:h * 128 + n], qaug[0:n, h, :], identb[0:n, 0:n], [qaug.b, identb.b], [ps_tr.b])
        P.cp("act", qT[:, :, qoff:qoff + n], ps_tr[0:67, 0:768].rearrange("p (h t) -> p h t", t=128)[:, :, 0:n],
             [ps_tr.b], [qT.b])

    def token_tile(src, n, j, qoff, o_k, o_v, o_l, cprev, ccur):
        xtile = xt[0]
        norm_T(src, n, ng, xtile)
        p0 = pg()
        proj_tm(n, C_Q, C_Q + 384, p0)
        headnorm(n, p0, 6, gq, tmpB, out_bf=qaug[0:n, :, 0:64], out_bf_b=qaug.b)
        p1 = pg()
        proj_tm(n, C_K, C_K + 384, p1)
        headnorm(n, p1, 6, gk, tmpB, out_bf=kaug[0:n, :, 0:64], out_bf_b=kaug.b)
        P.store(o_k, tmpB, tmpB[0:n, 0:384])
        p2 = pg()
        proj_tm(n, C_V, C_V + 390, p2)
        P.cp("act", tmpC[0:n, 0:384], p2[0:n, 0:384], [p2.b], [tmpC.b])
        P.store(o_v, tmpC, tmpC[0:n, 0:384])
        P.cp("dve", Vaug[0:n, j, :, 0:64], tmpC[0:n, 0:384].rearrange("p (h d) -> p h d", d=64), [tmpC.b], [kvb[j]])
        P.tt("dve", lf[0:n, :], p2[0:n, 384:390], bfb[0:n, :], ALU.add, [p2.b, bfb.b], [lf.b])
        P.act(lf[0:n, :], lf[0:n, :], AF.Exp, [lf.b], [lf.b], scale=-1.0)
        P.act(lf[0:n, :], lf[0:n, :], AF.Ln, [lf.b], [lf.b], bias=1.0)
        P.ts("dve", lf[0:n, :], lf[0:n, :], -1.0, None, ALU.mult, None, [lf.b], [lf.b])
        P.store(o_l, lf, lf[0:n, :])
        c_update(n, j, cprev, ccur)
        q_cpieces(n, ccur)
        k_to_T(n, j)
        q_to_T(n, qoff)
        p3 = pg()
        proj_tm(n, C_MQ, C_MQ + 256, p3)
        headnorm(n, p3, 4, gmq, tmpB, out_bf=mqa[0:n, :, :], out_bf_b=mqa.b)
        for h in range(4):
            P.tr(ps_tr[0:64, h * 128:h * 128 + n], mqa[0:n, h, :], identb[0:n, 0:n], [mqa.b, identb.b], [ps_tr.b])
        P.cp("act", mqT[:, :, qoff:qoff + n], ps_tr[0:64, 0:512].rearrange("p (h t) -> p h t", t=128)[:, :, 0:n],
             [ps_tr.b], [mqT.b])

    def fm_proj(n, qoff):
        gblocks = [(C_GF + 128 * i, i) for i in range(3)] + [(C_GM + 128 * i, 6 + i) for i in range(2)]
        for g0 in (0, 4):
            pst = pg()
            lst_ = gblocks[g0:g0 + 4]
            for jj, (c0, ch) in enumerate(lst_):
                for c in range(8):
                    P.mm(pst[:, jj * 128:jj * 128 + n], Wb[:, c, c0:c0 + 128], xnT[:, c, 0:n], c == 0, c == 7, [xnT.b, Wb.b], [pst.b], chain=(c > 0))
            for jj, (c0, ch) in enumerate(lst_):
                P.act(gt[:, ch, qoff:qoff + n], pst[:, jj * 128:jj * 128 + n], AF.Silu, [pst.b], [gtb[ch]])
        blocks = [(C_RW + 128 * i, 128) for i in range(9)] + [(C_WD, 64)] + [(C_GR + 128 * i, 128) for i in range(3)]
        for g0 in range(0, 13, 4):
            pst = pg()
            nb = min(4, 13 - g0)
            for jj in range(nb):
                c0, m = blocks[g0 + jj]
                for c in range(8):
                    P.mm(pst[0:m, jj * 128:jj * 128 + n], Wb[:, c, c0:c0 + m], xnT[:, c, 0:n], c == 0, c == 7, [xnT.b, Wb.b], [pst.b], chain=(c > 0))
            P.cp("act" if (g0 // 4) % 2 == 0 else "dve", raw[:, g0:g0 + nb, 1:1 + n], pst[:, 0:nb * 128].rearrange("p (b t) -> p b t", t=128)[:, :, 0:n], [pst.b], [raw.b])

    def store_shift(dst, n):
        P.store(dst[0:1152].rearrange("(b p) -> p b", p=128), raw, raw[:, 0:9, n], allow_slow_non_contiguous=True)
        P.store(dst[1152:1216].rearrange("(b p) -> p b", p=64), raw, raw[0:64, 9:10, n], allow_slow_non_contiguous=True)
        P.store(dst[1216:1600].rearrange("(b p) -> p b", p=128), raw, raw[:, 10:13, n], allow_slow_non_contiguous=True)

    def rwkv_pre(n, qoff):
        cur = lambda blk, p0=0, p1=128: raw[p0:p1, blk, 1:1 + n]
        prv = lambda blk, p0=0, p1=128: raw[p0:p1, blk, 0:n]
        P.tt("dve", xw[:, 0:n], prv(9, 0, 64), cur(9, 0, 64), ALU.subtract, [raw.b], [xw.b])
        P.stt(xw[:, 0:n], xw[:, 0:n], mu[0:64, 9:10], cur(9, 0, 64), ALU.mult, ALU.add, [xw.b, mu.b, raw.b], [xw.b])
        P.act(twd[0:32, 0:n], xw[0:32, 0:n], AF.Tanh, [xw.b], [twd.b])
        P.cp("dve", twd[32:64, 0:n], xw[32:64, 0:n], [xw.b], [twd.b])
        for p in range(3):
            blk = 10 + p
            P.tt("dve", xs[:, 3, 0:n], prv(blk), cur(blk), ALU.subtract, [raw.b], [xs.b])
            P.stt(xs[:, 3, 0:n], xs[:, 3, 0:n], mu[:, blk:blk + 1], cur(blk), ALU.mult, ALU.add, [xs.b, mu.b, raw.b], [xs.b])
            P.act(gt[:, 3 + p, qoff:qoff + n], xs[:, 3, 0:n], AF.Silu, [xs.b], [gtb[3 + p]])
        nlev = 0
        while (1 << nlev) < n:
            nlev += 1
        nlev -= 1
        return nlev

    def rwkv_pair(n, qoff, p, nlev):
        par = rpc[0] % 2
        rpc[0] += 1
        yT_, bon_ = r_yT2[par], r_bon2[par]
        cur = lambda blk, p0=0, p1=128: raw[p0:p1, blk, 1:1 + n]
        prv = lambda blk, p0=0, p1=128: raw[p0:p1, blk, 0:n]
        if True:
            S3 = lambda tl: tl[:, 0, 0:n]
            bc = lambda i: rp[:, p, i:i + 1]
            for i, blk in enumerate((p, 3 + p, 6 + p)):
                P.tt("dve", xs[:, i, 0:n], prv(blk), cur(blk), ALU.subtract, [raw.b], [xs.b])
                P.stt(xs[:, i, 0:n], xs[:, i, 0:n], mu[:, blk:blk + 1], cur(blk), ALU.mult, ALU.add, [xs.b, mu.b, raw.b], [xs.b])
            xr, xk, xv = xs[:, 0, 0:n], xs[:, 1, 0:n], xs[:, 2, 0:n]
            pw = pg()
            P.mm(pw[:, 0:n], lora[0:32, p * 128:(p + 1) * 128], twd[0:32, 0:n], True, True, [lora.b, twd.b], [pw.b])
            P.mm(pw[:, 128:128 + n], lora[32:64, p * 128:(p + 1) * 128], twd[32:64, 0:n], True, True, [lora.b, twd.b], [pw.b])
            P.act(S3(r_lw), pw[:, 0:n], AF.Sigmoid, [pw.b, rp.b], [r_lw.b], bias=bc(0))
            P.ts("dve", S3(r_lw), S3(r_lw), -0.6065306597126334, None, ALU.mult, None, [r_lw.b], [r_lw.b])
            P.act(S3(r_a), pw[:, 128:128 + n], AF.Sigmoid, [pw.b, rp.b], [r_a.b], bias=bc(1))
            P.em.op("dve", lambda e: e.tensor_tensor_scan(out=r_g[:, 0, 0:n], data0=ones[:, 0:n], data1=r_lw[:, 0, 0:n],
                                                          initial=0.0, op0=ALU.mult, op1=ALU.add),
                    [ones.b, r_lw.b], [r_g.b])
            P.act(S3(r_eg), S3(r_g), AF.Exp, [r_g.b], [r_eg.b])
            P.act(S3(r_eng), S3(r_g), AF.Exp, [r_g.b], [r_eng.b], scale=-1.0)
            P.tt("dve", S3(r_egm), S3(r_g), S3(r_lw), ALU.subtract, [r_g.b, r_lw.b], [r_egm.b])
            P.act(S3(r_egm), S3(r_egm), AF.Exp, [r_egm.b], [r_egm.b])
            P.ts("dve", S3(r_kk), xk, bc(2), None, ALU.mult, None, [xs.b, rp.b], [r_kk.b])
            P.tt("dve", S3(r_t1), S3(r_kk), S3(r_kk), ALU.mult, [r_kk.b], [r_t1.b])
            pss = pg()
            P.mm(pss[:, 0:n], bones[:, :], r_t1[:, 0, 0:n], True, True, [bones.b, r_t1.b], [pss.b])
            P.ts("dve", S3(r_t1), pss[:, 0:n], 1e-24, None, ALU.max, None, [pss.b], [r_t1.b])
            P.act(S3(r_t1), S3(r_t1), AF.Ln, [r_t1.b], [r_t1.b], scale=float(2 ** 40))
            P.act(S3(r_t1), S3(r_t1), AF.Exp, [r_t1.b], [r_t1.b], scale=-0.5, bias=13.862943611198906)
            P.tt("dve", S3(r_kk), S3(r_kk), S3(r_t1), ALU.mult, [r_kk.b, r_t1.b], [r_kk.b])
            P.ts("pool", S3(r_t2), S3(r_a), bc(3), bc(4), ALU.mult, ALU.add, [r_a.b, rp.b], [r_t2.b])
            P.tt("pool", S3(r_t2), S3(r_t2), xk, ALU.mult, [r_t2.b, xs.b], [r_t2.b])
            P.stt(ART[:, 0, 0, 0:n], S3(r_kk), -1.0, S3(r_egm), ALU.mult, ALU.mult, [r_kk.b, r_egm.b], [ART.b])
            P.tt("pool", ART[:, 0, 1, 0:n], xr, S3(r_eg), ALU.mult, [xs.b, r_eg.b], [ART.b])
            P.tt("dve", S3(r_t1), S3(r_a), S3(r_kk), ALU.mult, [r_a.b, r_kk.b], [r_t1.b])
            P.tt("dve", S3(BTt), S3(r_t1), S3(r_eng), ALU.mult, [r_t1.b, r_eng.b], [BTt.b])
            P.tt("pool", S3(KTt), S3(r_t2), S3(r_eng), ALU.mult, [r_t2.b, r_eng.b], [KTt.b])
            P.cp("act", S3(vbt), xv, [xs.b], [vbt.b])
            P.tt("dve", S3(r_t1), xr, S3(r_t2), ALU.mult, [xs.b, r_t2.b], [r_t1.b])
            P.ts("dve", S3(r_t1), S3(r_t1), bc(5), None, ALU.mult, None, [r_t1.b, rp.b], [r_t1.b])
            psb = pg()
            P.mm(psb[:, 0:n], bones[:, :], r_t1[:, 0, 0:n], True, True, [bones.b, r_t1.b], [psb.b])
            P.tt("dve", S3(bon_), psb[:, 0:n], xv, ALU.mult, [psb.b, xs.b], [bon_.b])
            for i, (src_ap, sb_) in enumerate(((ART[:, 0, 0, 0:n], ART.b), (BTt[:, 0, 0:n], BTt.b), (KTt[:, 0, 0:n], KTt.b), (vbt[:, 0, 0:n], vbt.b))):
                P.tr(ps_tr[0:n, i * 128:(i + 1) * 128], src_ap, identb[:, :], [sb_, identb.b], [ps_tr.b])
            for i, dstt in enumerate((tokA, tokB, tokK, tokV)):
                P.cp("act" if i % 2 else "dve", dstt[0:n, :], ps_tr[0:n, i * 128:(i + 1) * 128], [ps_tr.b], [dstt.b])
            for hh in range(2):
                hb = hh * 64
                g12 = pg()
                for a_ in range(2):
                    P.mm(g12[0:n, a_ * 128:a_ * 128 + n], BTt[hb:hb + 64, 0, 0:n], ART[hb:hb + 64, 0, a_, 0:n], True, True, [BTt.b, ART.b], [g12.b])
                    P.mm(g12[0:n, 256 + a_ * 128:256 + a_ * 128 + n], KTt[hb:hb + 64, 0, 0:n], ART[hb:hb + 64, 0, a_, 0:n], True, True, [KTt.b, ART.b], [g12.b])
                P.tt("dve", gm[hh][0:n, :, 0:n], g12[0:n, :].rearrange("s (a t) -> s a t", t=128)[:, :, 0:n], mask4[0:n, :, 0:n], ALU.mult,
                     [g12.b, mask4.b], [gm[hh].b])
                g3 = pg()
                P.mm(g3[0:n, 0:n], ART[hb:hb + 64, 0, 0, 0:n], BTt[hb:hb + 64, 0, 0:n], True, True, [ART.b, BTt.b], [g3.b])
                P.tt("dve", PP[hh][0][0:n, 0, 0:n], g3[0:n, 0:n], msl[0:n, 0:n], ALU.mult, [g3.b, msl.b], [PP[hh][0].b])
                P.cp("act", PP[hh][0][0:n, 1, 0:n], gm[hh][0:n, 0, 0:n], [gm[hh].b], [PP[hh][0].b])
                P.tt("pool", XX[hh][0][0:n, 0:n], gm[hh][0:n, 0, 0:n], identb[0:n, 0:n], ALU.add, [gm[hh].b, identb.b], [XX[hh][0].b])
            for j in range(1, nlev + 1):
                ci, ni = (j - 1) % 2, j % 2
                for hh in range(2):
                    psq = pg()
                    Pc = PP[hh][ci]
                    P.mm(psq[0:n, 0:n], Pc[0:n, 1, 0:n], Pc[0:n, 0, 0:n], True, True, [Pc.b], [psq.b])
                    if j < nlev:
                        P.mm(psq[0:n, 128:128 + n], Pc[0:n, 0, 0:n], Pc[0:n, 1, 0:n], True, True, [Pc.b], [psq.b])
                        P.cp("dve" if hh else "act", PP[hh][ni][0:n, :, 0:n], psq[0:n, 0:256].rearrange("s (a t) -> s a t", t=128)[:, :, 0:n], [psq.b], [PP[hh][ni].b])
                    else:
                        P.cp("dve" if hh else "act", PP[hh][ni][0:n, 0, 0:n], psq[0:n, 0:n], [psq.b], [PP[hh][ni].b])
                for hh in range(2):
                    px = pg()
                    P.mm(px[0:n, 0:n], PP[hh][ni][0:n, 0, 0:n], XX[hh][ci][0:n, 0:n], True, True, [PP[hh][ni].b, XX[hh][ci].b], [px.b])
                    P.tt("dve", XX[hh][ni][0:n, 0:n], px[0:n, 0:n], XX[hh][ci][0:n, 0:n], ALU.add, [px.b, XX[hh][ci].b], [XX[hh][ni].b])
            fi = nlev % 2
            for hh in range(2):
                hb = hh * 64
                TTm = XX[hh][fi]
                pw_ = pg()
                P.mm(pw_[0:64, 0:n], tokA[0:n, hb:hb + 64], TTm[0:n, 0:n], True, True, [tokA.b, TTm.b], [pw_.b])
                P.cp("act", WT[hb:hb + 64, 0, 0:n], pw_[0:64, 0:n], [pw_.b], [WT.b])
                P.mm(pw_[0:n, 128:192], gm[hh][0:n, 2, 0:n], tokV[0:n, hb:hb + 64], True, True, [gm[hh].b, tokV.b], [pw_.b])
                P.cp("dve", LVs[0:n, hh, :], pw_[0:n, 128:192], [pw_.b], [LVs.b])
                P.mm(pw_[0:n, 256:320], TTm[0:n, 0:n], LVs[0:n, hh, :], True, True, [TTm.b, LVs.b], [pw_.b])
                P.cp("act", U0[0:n, hh, :], pw_[0:n, 256:320], [pw_.b], [U0.b])
            flush_gn()
            pU = pg()
            for hh in range(2):
                hb = hh * 64
                P.mm(pU[0:n, hb:hb + 64], WT[hb:hb + 64, 0, 0:n], Hb[hb:hb + 64, p, :], True, True, [WT.b, Hb.b], [pU.b])
            P.tt("dve", Ub[0:n, :, :], pU[0:n, 0:128].rearrange("t (h v) -> t h v", v=64), U0[0:n, :, :], ALU.add, [pU.b, U0.b], [Ub.b])
            pY = pg()
            for hh in range(2):
                hb = hh * 64
                dst = pY[0:64, hh * 128:hh * 128 + n]
                P.mm(dst, Hb[hb:hb + 64, p, :], ART[hb:hb + 64, 0, 1, 0:n], True, False, [Hb.b, ART.b], [pY.b])
                P.mm(dst, Ub[0:n, hh, :], gm[hh][0:n, 1, 0:n], False, False, [Ub.b, gm[hh].b], [pY.b], chain=True)
                P.mm(dst, tokV[0:n, hb:hb + 64], gm[hh][0:n, 3, 0:n], False, True, [tokV.b, gm[hh].b], [pY.b], chain=True)
            for hh in range(2):
                hb = hh * 64
                P.cp("act" if hh else "dve", yT_[hb:hb + 64, 0, 0:n], pY[0:64, hh * 128:hh * 128 + n], [pY.b], [yT_.b])
            pD = pg()
            for hh in range(2):
                hb = hh * 64
                P.mm(pD[0:64, hb:hb + 64], tokB[0:n, hb:hb + 64], Ub[0:n, hh, :], True, False, [tokB.b, Ub.b], [pD.b])
                P.mm(pD[0:64, hb:hb + 64], tokK[0:n, hb:hb + 64], tokV[0:n, hb:hb + 64], False, True, [tokK.b, tokV.b], [pD.b], chain=True)
            for hh in range(2):
                hb = hh * 64
                P.cp("act" if hh else "dve", Dp[hb:hb + 64, 0, :], pD[0:64, hb:hb + 64], [pD.b], [Dp.b])
            P.tt("dve", Hf[:, p, :], Hf[:, p, :], Dp[:, 0, :], ALU.add, [Hf.b, Dp.b], [Hf.b])
            P.ts("dve", Hf[:, p, :], Hf[:, p, :], r_eg[:, 0, n - 1:n], None, ALU.mult, None, [Hf.b, r_eg.b], [Hf.b])
            P.cp("act", Hb[:, p, :], Hf[:, p, :], [Hf.b], [Hb.b])
            def gn_tail(p=p, n=n, qoff=qoff, yT_=yT_, bon_=bon_):
                y_ = yT_[:, 0, 0:n]
                t_ = tmpA[:, 0:n]
                pm = pg()
                P.mm(pm[:, 0:n], bavg[:, :], y_, True, True, [bavg.b, yT_.b], [pm.b])
                P.tt("dve", y_, y_, pm[:, 0:n], ALU.subtract, [yT_.b, pm.b], [yT_.b])
                P.tt("dve", t_, y_, y_, ALU.mult, [yT_.b], [tmpA.b])
                pvv = pg()
                P.mm(pvv[:, 0:n], bavg[:, :], t_, True, True, [bavg.b, tmpA.b], [pvv.b])
                P.act(t_, pvv[:, 0:n], AF.Ln, [pvv.b], [tmpA.b], bias=GN_EPS)
                P.act(t_, t_, AF.Exp, [tmpA.b], [tmpA.b], scale=-0.5)
                P.tt("dve", y_, y_, t_, ALU.mult, [yT_.b, tmpA.b], [yT_.b])
                P.ts("dve", y_, y_, rp[:, p, 6:7], rp[:, p, 7:8], ALU.mult, ALU.add, [yT_.b, rp.b], [yT_.b])
                P.tt("dve", y_, y_, bon_[:, 0, 0:n], ALU.add, [yT_.b, bon_.b], [yT_.b])
                P.tt("dve", og[:, 3 + p, qoff:qoff + n], y_, gt[:, 3 + p, qoff:qoff + n], ALU.mult, [yT_.b, gtb[3 + p]], [ogb[3 + p]])
            pend_gn[0] = gn_tail

    def rwkv_chunk(n, qoff):
        nlev = rwkv_pre(n, qoff)
        for p in range(3):
            rwkv_pair(n, qoff, p, nlev)

    def store_state(dst):
        pst = pg()
        for p in range(3):
            P.tr(pst[0:64, p * 128:(p + 1) * 128], Hf[:, p, :], identf[:, :], [Hf.b, identf.b], [pst.b])
        P.cp("act", stS[:, :, :], pst[0:64, 0:384].rearrange("v (h k) -> v h k", k=64), [pst.b], [stS.b])
        P.store(dst.rearrange("h v k -> v h k"), stS, stS[:, :, :])

    pti = [0]

    oun2 = [oun, ounB]
    hcnt = [0]
    pending = [None]

    def flush_tail():
        if pending[0] is not None:
            t_ = pending[0]
            pending[0] = None
            t_()

    def attention(nq, heads, kfn, vfn, bfn, entries, krows, out_fn):
        for h in heads:
            po = ps_o2[h % 2]
            nent = len(entries)

            def pv(i, ent, ptt):
                j, nk, q0, diag = ent
                vap, vb_ = vfn(h, j, nk)
                P.mm(po[0:65, q0:nq], vap, ptt[0:nk, 0:nq - q0], i == 0, i == nent - 1, [vb_, ptt.b], [po.b])
            prev = None
            for i, ent in enumerate(entries):
                j, nk, q0, diag = ent
                pss_ = ps_s2[pti[0] % 2]
                ptt = pts[pti[0] % 3]
                pti[0] += 1
                kap, kb = kfn(h, j, nk)
                qap, qb = qfn_cur[0](h, q0, nq)
                P.mm(pss_[0:nk, 0:nq - q0], kap, qap, True, True, [kb, qb], [pss_.b])
                bias = bfn(h, j, nk)
                if bias is not None:
                    P.act(ptt[0:nk, 0:nq - q0], pss_[0:nk, 0:nq - q0], AF.Exp, [pss_.b, bias[1]], [ptt.b], bias=bias[0])
                else:
                    P.act(ptt[0:nk, 0:nq - q0], pss_[0:nk, 0:nq - q0], AF.Exp, [pss_.b], [ptt.b])
                if diag:
                    P.asel(ptt[0:nk, 0:nk], ptt[0:nk, 0:nk], [[1, nk]], ALU.is_ge, 0.0, 0, -1, [ptt.b], [ptt.b])
                if prev is not None:
                    pv(*prev)
                prev = (i, ent, ptt)
            pv(*prev)
            ou = oun2[hcnt[0] % 2]
            hcnt[0] += 1
            P.cp("act", ou[0:65, 0:nq], po[0:65, 0:nq], [po.b], [ou.b])

            def tail(h=h, ou=ou, nq=nq, out_fn=out_fn):
                P.act(ou[64:65, 0:nq], ou[64:65, 0:nq], AF.Ln, [ou.b], [ou.b])
                P.act(ou[64:65, 0:nq], ou[64:65, 0:nq], AF.Exp, [ou.b], [ou.b], scale=-1.0)
                pb = pg()
                P.mm(pb[0:64, 0:nq], ones[64:65, 0:64], ou[64:65, 0:nq], True, True, [ones.b, ou.b], [pb.b])
                out_fn(h, pb, ou)
            flush_tail()
            pending[0] = tail

    qfn_cur = [None]

    def fox_heads(nq, heads, fox_entries):
        qfn_cur[0] = lambda h, q0, nq_: (qT[0:67, h, q0:nq_], qT.b)

        def fox_out(h, pb, ou):
            hb = (h % 2) * 64
            P.tt("dve", opair[hb:hb + 64, 0:nq], ou[0:64, 0:nq], pb[0:64, 0:nq], ALU.mult, [ou.b, pb.b], [opair.b])
            if h % 2 == 1:
                c = h // 2
                P.tt("dve", og[:, c, 0:nq], opair[:, 0:nq], gt[:, c, 0:nq], ALU.mult, [opair.b, gtb[c]], [ogb[c]])
        attention(nq, heads,
                  lambda h, j, nk: (kT[0:67, h, j * 128:j * 128 + nk], kvb[j]),
                  lambda h, j, nk: (Vaug[0:nk, j, h, :], kvb[j]),
                  lambda h, j, nk: (negc[0:nk, j, h:h + 1], kvb[j]),
                  fox_entries, 67, fox_out)

    def mem_heads(nq, heads, mem_k, mem_v, mem_kb, mem_vb):
        qfn_cur[0] = lambda h, q0, nq_: (mqT[0:64, h, q0:nq_], mqT.b)

        def mem_out(h, pb, ou):
            hb = (h % 2) * 64
            P.tt("dve", opair[hb:hb + 64, 0:nq], ou[0:64, 0:nq], pb[0:64, 0:nq], ALU.mult, [ou.b, pb.b], [opair.b])
            if h % 2 == 1:
                c = 6 + h // 2
                P.tt("dve", og[:, c, 0:nq], opair[:, 0:nq], gt[:, c, 0:nq], ALU.mult, [opair.b, gtb[c]], [ogb[c]])
        attention(nq, heads,
                  lambda h, j, nk: (mem_k[0:64, h, j * 128:j * 128 + nk], mem_kb),
                  lambda h, j, nk: (mem_v[0:nk, j, h, :], mem_vb),
                  lambda h, j, nk: None,
                  [(0, 128, 0, False), (1, 128, 0, False)], 64, mem_out)

    def run_attention(nq, fox_entries, mem_k, mem_v, mem_kb, mem_vb):
        fox_heads(nq, range(6), fox_entries)
        mem_heads(nq, range(4), mem_k, mem_v, mem_kb, mem_vb)

    wsti = [0]

    def out_proj(tiles):
        accs = [[pg(), pg()] for _ in tiles]
        for c in range(8):
            w = wst[wsti[0] % 2]
            wsti[0] += 1
            P.em.dma("sp", w[:, :], wo_bf[:, c, :], reads=[wo_b], writes=[w.b], dbuf=w.b)
            for ti, (n, qoff, src, dst) in enumerate(tiles):
                for cb in range(2):
                    pst = accs[ti][cb]
                    P.mm(pst[0:n, 0:512], og[:, c, qoff:qoff + n], w[:, cb * 512:(cb + 1) * 512], c == 0, c == 7, [ogb[c], w.b], [pst.b])
        for ti, (n, qoff, src, dst) in enumerate(tiles):
            xtile = xt[1]
            P.load(xtile, xtile[0:n, :], src)
            for cb in range(2):
                pst = accs[ti][cb]
                P.tt("dve", xtile[0:n, cb * 512:(cb + 1) * 512], xtile[0:n, cb * 512:(cb + 1) * 512], pst[0:n, 0:512], ALU.add,
                     [xtile.b, pst.b], [xtile.b])
            P.store(dst, xtile, xtile[0:n, :])

    w_in_v = w_in.rearrange("(c p) n -> p c n", p=128)
    w_out_v = w_out.rearrange("(c p) n -> p c n", p=128)
    w_mem_v = w_mem_kv.rearrange("(c p) n -> p c n", p=128)
    kq = 0
    Wm = Vaug[:, :, :, :].rearrange("p a h e -> p (a h e)")[:, 0:4096].rearrange("p (c n) -> p c n", n=512)
    for c in range(8):
        s_ = xt[kq % 2]
        P.load(s_, s_[:, 0:512], w_mem_v[:, c, :])
        P.cp(P.rot(), Wm[:, c, :], s_[:, 0:512], [s_.b], kvb)
        kq += 1
    for blk in range(2):
        xtile = xt[blk % 2]
        norm_T(memp[blk * 128:(blk + 1) * 128, :], 128, mng, xtile)
        pst = pg()
        for c in range(8):
            P.mm(pst[:, 0:512], xnT[:, c, :], Wm[:, c, :], c == 0, c == 7, [xnT.b] + kvb, [pst.b], chain=(c > 0))
        headnorm(128, pst, 4, gmk, tmpB, out_bf=mqa[:, :, :], out_bf_b=mqa.b)
        P.store(o_mkp[blk * 128:(blk + 1) * 128, :], tmpB, tmpB[:, 0:256])
        P.cp("act", tmpC[:, 0:256], pst[:, 256:512], [pst.b], [tmpC.b])
        P.store(o_mvp[blk * 128:(blk + 1) * 128, :], tmpC, tmpC[:, 0:256])
        P.cp("dve", mvaug[:, blk, :, 0:64], pst[:, 256:512].rearrange("p (h d) -> p h d", d=64), [pst.b], [mvaug.b])
        for h in range(4):
            P.tr(ps_tr[0:64, h * 128:(h + 1) * 128], mqa[:, h, :], identb[:, :], [mqa.b, identb.b], [ps_tr.b])
        P.cp("act", mkT[:, :, blk * 128:(blk + 1) * 128], ps_tr[0:64, 0:512].rearrange("p (h t) -> p h t", t=128), [ps_tr.b], [mkT.b])
    P.memset("pool", Vaug[:, :, :, :].rearrange("p a h e -> p (a h) e")[:, :, 64:65], 1.0, kvb)
    for c in range(8):
        for (c0, c1) in ((0, 1024), (1024, 2048), (2048, 3072), (3072, NIN)):
            s_ = xt[kq % 2]
            P.load(s_, s_[:, 0:c1 - c0], w_in_v[:, c, c0:c1])
            P.cp(P.rot(), Wb[:, c, c0:c1], s_[:, 0:c1 - c0], [s_.b], [Wb.b])
            kq += 1
        s_ = xt[kq % 2]
        P.load(s_, s_[:, :], w_out_v[:, c, :])
        w = wst[c % 2]
        P.cp(P.rot(), w[:, :], s_[:, :], [s_.b], [w.b])
        P.em.dma("pool", wo_bf[:, c, :], w[:, :], reads=[w.b], writes=[wo_b], dbuf=w.b)
        kq += 1

    P.memset("dve", Hf[:], 0.0, [Hf.b])
    P.memset("dve", Hb[:], 0.0, [Hb.b])
    NG = NT // 2
    for g in range(NG):
        entries = [(j, 128, 0, False) for j in range(2 * g)] + [(2 * g, 128, 0, True), (2 * g + 1, 128, 128, True)]
        for tt_ in range(2):
            t = 2 * g + tt_
            sl = slice(t * 128, (t + 1) * 128)
            token_tile(xp[sl, :], 128, t, tt_ * 128, o_fkp[sl, :], o_fvp[sl, :], o_flp[sl, :],
                       None if t == 0 else cc[(t - 1) % 2], cc[t % 2])
            fm_proj(128, tt_ * 128)
            if tt_ == 0:
                rwkv_chunk(128, 0)
            else:
                nlev = rwkv_pre(128, 128)
                for p in range(3):
                    rwkv_pair(128, 128, p, nlev)
                    fox_heads(NQ, (2 * p, 2 * p + 1), entries)
            if t == NT - 1:
                store_shift(o_rhp, 128)
            else:
                P.cp("dve", raw[:, :, 0:1], raw[:, :, 128:129], [raw.b], [raw.b])
        mem_heads(NQ, range(4), mkT, mvaug, mkT.b, mvaug.b)
        flush_tail()
        flush_gn()
        out_proj([(128, tt_ * 128, xp[(2 * g + tt_) * 128:(2 * g + tt_ + 1) * 128, :], o_yp[(2 * g + tt_) * 128:(2 * g + tt_ + 1) * 128, :])
                  for tt_ in range(2)])
    store_state(o_rsp)

    for b in range(SB_):
        sl = slice(b * SS, (b + 1) * SS)
        for j in range(8):
            ks = slice(j * 128, (j + 1) * 128)
            P.load(tmpB, tmpB[:, 0:384], cfk[b, ks, :])
            P.cp("act", kaug[:, :, 0:64], tmpB[:, 0:384].rearrange("p (h d) -> p h d", d=64), [tmpB.b], [kaug.b])
            k_to_T(128, j)
            P.load(tmpC, tmpC[:, 0:384], cfv[b, ks, :])
            P.cp("dve", Vaug[:, j, :, 0:64], tmpC[:, 0:384].rearrange("p (h d) -> p h d", d=64), [tmpC.b], [kvb[j]])
            P.load(lf, lf[:, :], cfl[b, ks, :])
            c_update(128, j, None if j == 0 else cc[(j - 1) % 2], cc[j % 2])
        P.load(stS, stS[:, :, :], srw[b].rearrange("h v k -> v h k"))
        pst = pg()
        for h in range(6):
            P.tr(pst[0:64, h * 64:(h + 1) * 64], stS[:, h, :], identf[0:64, 0:64], [stS.b, identf.b], [pst.b])
        for h in range(6):
            p, hb = h // 2, (h % 2) * 64
            P.cp("act" if h % 2 else "dve", Hf[hb:hb + 64, p, :], pst[0:64, h * 64:(h + 1) * 64], [pst.b], [Hf.b])
        P.cp("act", Hb[:, :, :], Hf[:, :, :], [Hf.b], [Hb.b])
        P.em.dma("sp", raw[:, 0:9, 0], ssh[b, 0:1152].rearrange("(b p) -> p b", p=128), reads=(), writes=[raw.b], dbuf=raw.b,
                 allow_slow_non_contiguous=True)
        P.em.dma("sp", raw[0:64, 9:10, 0], ssh[b, 1152:1216].rearrange("(b p) -> p b", p=64), reads=(), writes=[raw.b], dbuf=raw.b,
                 allow_slow_non_contiguous=True)
        P.em.dma("sp", raw[:, 10:13, 0], ssh[b, 1216:1600].rearrange("(b p) -> p b", p=128), reads=(), writes=[raw.b], dbuf=raw.b,
                 allow_slow_non_contiguous=True)
        token_tile(xsm[sl, :], SS, 8, 0, o_fks[sl, :], o_fvs[sl, :], o_fls[sl, :], cc[7 % 2], cc[8 % 2])
        fm_proj(SS, 0)
        rwkv_chunk(SS, 0)
        store_shift(o_rhs[b], SS)
        store_state(o_rss[b])
        for blk in range(2):
            ks = slice(blk * 128, (blk + 1) * 128)
            P.load(tmpB, tmpB[:, 0:256], cmk[b, ks, :])
            P.cp("act", mqa[:, :, :], tmpB[:, 0:256].rearrange("p (h d) -> p h d", d=64), [tmpB.b], [mqa.b])
            for h in range(4):
                P.tr(ps_tr[0:64, h * 128:(h + 1) * 128], mqa[:, h, :], identb[:, :], [mqa.b, identb.b], [ps_tr.b])
            P.cp("act", mkT[:, :, blk * 128:(blk + 1) * 128], ps_tr[0:64, 0:512].rearrange("p (h t) -> p h t", t=128), [ps_tr.b], [mkT.b])
            P.load(tmpC, tmpC[:, 0:256], cmv[b, ks, :])
            P.cp("dve", mvaug[:, blk, :, 0:64], tmpC[:, 0:256].rearrange("p (h d) -> p h d", d=64), [tmpC.b], [mvaug.b])
        entries = [(j, 128, 0, False) for j in range(8)] + [(8, SS, 0, True)]
        run_attention(SS, entries, mkT, mvaug, mkT.b, mvaug.b)
        flush_tail()
        flush_gn()
        out_proj([(SS, 0, xsm[sl, :], o_ys[sl, :])])

    P.em.final_wait("pool")
    P.em.replay()
    return nc


_NC = None


def kernel(x_prompt, x_sample, mem_prompt, cache_fox_k, cache_fox_v, cache_fox_logf,
           cache_mem_k, cache_mem_v, state_rwkv, state_rwkv_shift,
           norm_g, w_in, fox_q_g, fox_k_g, fox_b_f, rwkv_mu, rwkv_w0, rwkv_w_up, rwkv_a0,
           rwkv_a_up, rwkv_k_k, rwkv_k_a, rwkv_r_k, rwkv_gn_w, rwkv_gn_b,
           mem_norm_g, w_mem_kv, mem_q_g, mem_k_g, w_out):
    global _NC
    f = lambda a: np.ascontiguousarray(np.asarray(a, dtype=np.float32))
    if _NC is None:
        _NC = build()
    nc = _NC
    shared = dict(norm_g=f(norm_g[0]), w_in=f(w_in[0]), fox_q_g=f(fox_q_g[0]), fox_k_g=f(fox_k_g[0]),
                  fox_b_f=f(fox_b_f[0]), rwkv_mu=f(rwkv_mu[0]), rwkv_w0=f(rwkv_w0[0]), rwkv_w_up=f(rwkv_w_up[0]),
                  rwkv_a0=f(rwkv_a0[0]), rwkv_a_up=f(rwkv_a_up[0]), rwkv_k_k=f(rwkv_k_k[0]), rwkv_k_a=f(rwkv_k_a[0]),
                  rwkv_r_k=f(rwkv_r_k[0]), rwkv_gn_w=f(rwkv_gn_w[0]), rwkv_gn_b=f(rwkv_gn_b[0]),
                  mem_norm_g=f(mem_norm_g[0]), w_mem_kv=f(w_mem_kv[0]), mem_q_g=f(mem_q_g[0]), mem_k_g=f(mem_k_g[0]),
                  w_out=f(w_out[0]))
    in_maps = []
    for c in range(8):
        bs = slice(4 * c, 4 * c + 4)
        m = dict(shared)
        m.update(xp=f(x_prompt[c]), xsm=f(x_sample[bs]).reshape(64, D), memp=f(mem_prompt[c]),
                 cfk=f(cache_fox_k[0, bs]).reshape(4, PAST, 384), cfv=f(cache_fox_v[0, bs]).reshape(4, PAST, 384),
                 cfl=f(cache_fox_logf[0, bs]), cmk=f(cache_mem_k[0, bs]).reshape(4, 256, 256),
                 cmv=f(cache_mem_v[0, bs]).reshape(4, 256, 256), srw=f(state_rwkv[0, bs]),
                 ssh=f(state_rwkv_shift[0, bs]).reshape(4, 1600))
        in_maps.append(m)
    res = run_bass_kernel_spmd(nc, in_maps, core_ids=list(range(8)))
    R = res.results
    cat = lambda k: np.stack([np.asarray(R[c][k]) for c in range(8)])
    yp = cat("o_yp")
    ys = cat("o_ys").reshape(32, 16, D)
    fkp = cat("o_fkp").reshape(1, 8, T, 6, 64)
    fvp = cat("o_fvp").reshape(1, 8, T, 6, 64)
    flp = cat("o_flp").reshape(1, 8, T, 6)
    mkp = cat("o_mkp").reshape(1, 8, 256, 4, 64)
    mvp = cat("o_mvp").reshape(1, 8, 256, 4, 64)
    rsp = cat("o_rsp").reshape(1, 8, 6, 64, 64)
    rhp = cat("o_rhp").reshape(1, 8, 1, 1600)
    fks = cat("o_fks").reshape(1, 32, 16, 6, 64)
    fvs = cat("o_fvs").reshape(1, 32, 16, 6, 64)
    fls = cat("o_fls").reshape(1, 32, 16, 6)
    rss = cat("o_rss").reshape(1, 32, 6, 64, 64)
    rhs = cat("o_rhs").reshape(1, 32, 1, 1600)
    return (yp, ys, fkp, fvp, flp, mkp, mvp, rsp, rhp, fks, fvs, fls, rss, rhs)
```

```python
import numpy as np
from contextlib import ExitStack
import concourse.bass as bass
import concourse.mybir as mybir
from concourse.bass_utils import run_bass_kernel_spmd

F32 = mybir.dt.float32
BF16 = mybir.dt.bfloat16
AF = mybir.ActivationFunctionType
ALU = mybir.AluOpType
AX = mybir.AxisListType

D = 1024
T = 4096
NT = T // 128
SB_ = 4
SS = 16
PAST = 1024
NIN = 3654
EPS = 1e-6
GN_EPS = 64e-5
C_Q, C_K, C_V, C_F, C_GF = 0, 384, 768, 1152, 1158
C_RW = 1542
C_RR, C_RK, C_RV, C_WD, C_AD, C_GR = C_RW, C_RW + 384, C_RW + 768, C_RW + 1152, C_RW + 1184, C_RW + 1216
C_MQ, C_GM = 3142, 3398


class Buf:
    __slots__ = ("w", "r", "dsem", "dcnt", "name", "excl")

    def __init__(self, name="", excl=False):
        self.w = None
        self.r = []
        self.dsem = None
        self.dcnt = 0
        self.name = name
        self.excl = excl


class Emit:
    ENG = ("pe", "act", "dve", "pool", "sp")

    def __init__(self, nc, stack):
        self.nc = nc
        self.stack = stack
        self.ops = {e: [] for e in self.ENG}
        self.cnt = {e: 0 for e in self.ENG}
        self.sems = {}
        for e in self.ENG:
            self.sems[e] = stack.enter_context(nc.semaphore("sem_" + e))
        self.known = {e: {} for e in self.ENG}
        self.nd = 0
        self.dbufs = []

    def _waits(self, eng, reads, writes):
        need = {}
        for b in reads:
            if b.w is not None:
                k, v = b.w
                if need.get(k, 0) < v:
                    need[k] = v
            if b.excl:
                for k, v in b.r:
                    if k != eng and need.get(k, 0) < v:
                        need[k] = v
        for b in writes:
            if b.w is not None:
                k, v = b.w
                if need.get(k, 0) < v:
                    need[k] = v
            for k, v in b.r:
                if need.get(k, 0) < v:
                    need[k] = v
        out = []
        kn = self.known[eng]
        for k, v in need.items():
            if kn.get(k, 0) < v:
                kn[k] = v
                out.append((self.sems[k], v))
        return out

    def _mark(self, ev, reads, writes):
        for b in reads:
            b.r = [x for x in b.r if x[0] != ev[0]]
            b.r.append(ev)
        for b in writes:
            b.w = ev
            b.r = []

    def op(self, eng, fn, reads=(), writes=(), chain=False):
        prev_known_pe = self.known["pe"].get("pe", 0) if eng == "pe" else None
        wl = self._waits(eng, reads, writes)
        if chain and eng == "pe":
            sem_pe = self.sems["pe"]
            keep = []
            for s_, v_ in wl:
                if s_ is sem_pe and v_ == self.cnt["pe"]:
                    self.known["pe"]["pe"] = prev_known_pe
                    continue
                keep.append((s_, v_))
            wl = keep
        self.cnt[eng] += 1
        ev = (eng, self.cnt[eng])
        sem = self.sems[eng]

        def run(e, fn=fn, wl=wl, sem=sem):
            for s, v in wl:
                e.wait_ge(s, v)
            fn(e).then_inc(sem, 1)
        self.ops[eng].append(run)
        self._mark(ev, reads, writes)
        return ev

    def dma(self, eng, out, in_, reads=(), writes=(), dbuf=None, **kw):
        if dbuf.dsem is None:
            dbuf.dsem = {}
            dbuf.dcnt = {}
            self.dbufs.append(dbuf)
        if eng not in dbuf.dsem:
            self.nd += 1
            key = "d%d" % self.nd
            self.sems[key] = self.stack.enter_context(self.nc.semaphore("sem_" + key))
            dbuf.dsem[eng] = key
            dbuf.dcnt[eng] = 0
        wl = self._waits(eng, reads, writes)
        dbuf.dcnt[eng] += 16
        key = dbuf.dsem[eng]
        ev = (key, dbuf.dcnt[eng])
        sem = self.sems[key]

        def run(e, wl=wl, sem=sem, out=out, in_=in_, kw=kw):
            for s, v in wl:
                e.wait_ge(s, v)
            e.dma_start(out=out, in_=in_, **kw).then_inc(sem, 16)
        self.ops[eng].append(run)
        self._mark(ev, reads, writes)
        return ev

    def final_wait(self, eng):
        wl = []
        kn = self.known[eng]
        for b in self.dbufs:
            for q, key in b.dsem.items():
                v = b.dcnt[q]
                if kn.get(key, 0) < v:
                    kn[key] = v
                    wl.append((self.sems[key], v))

        def run(e, wl=wl):
            for s, v in wl:
                e.wait_ge(s, v)
        self.ops[eng].append(run)

    def replay(self):
        nc = self.nc
        ops = self.ops
        with nc.Block() as block:
            @block.tensor
            def _(e):
                for f in ops["pe"]:
                    f(e)

            @block.scalar
            def _(e):
                for f in ops["act"]:
                    f(e)

            @block.vector
            def _(e):
                for f in ops["dve"]:
                    f(e)

            @block.gpsimd
            def _(e):
                for f in ops["pool"]:
                    f(e)

            @block.sync
            def _(e):
                for f in ops["sp"]:
                    f(e)


class TT:
    def __init__(self, ap, name=""):
        self.t = ap
        self.b = Buf(name)

    def __getitem__(self, k):
        return self.t[k]


class Prog:
    def __init__(self):
        self.nc = bass.Bass("TRN2", target_bir_lowering=False)
        self.st = ExitStack()
        self.em = Emit(self.nc, self.st)
        self.rr = 0

    def dram(self, name, shape, kind):
        return self.nc.dram_tensor(name, list(shape), F32, kind=kind).ap()

    def sb(self, name, shape, dt=F32):
        return TT(self.st.enter_context(self.nc.sbuf_tensor(name, list(shape), dt)), name)

    def ps(self, name, shape, dt=F32):
        t = TT(self.st.enter_context(self.nc.psum_tensor(name, list(shape), dt)), name)
        t.b.excl = True
        return t

    def act(self, out, in_, func, r, w, **kw):
        return self.em.op("act", lambda e: e.activation(out=out, in_=in_, func=func, **kw), r, w)

    def tt(self, eng, out, in0, in1, op, r, w):
        return self.em.op(eng, lambda e: e.tensor_tensor(out=out, in0=in0, in1=in1, op=op), r, w)

    def ts(self, eng, out, in0, s1, s2, op0, op1, r, w):
        if s2 is None:
            return self.em.op(eng, lambda e: e.tensor_scalar(out=out, in0=in0, scalar1=s1, scalar2=None, op0=op0), r, w)
        return self.em.op(eng, lambda e: e.tensor_scalar(out=out, in0=in0, scalar1=s1, scalar2=s2, op0=op0, op1=op1), r, w)

    def stt(self, out, in0, scalar, in1, op0, op1, r, w):
        return self.em.op("dve", lambda e: e.scalar_tensor_tensor(out=out, in0=in0, scalar=scalar, in1=in1, op0=op0, op1=op1), r, w)

    def cp(self, eng, out, in_, r, w):
        if eng == "act":
            return self.em.op("act", lambda e: e.activation(out=out, in_=in_, func=AF.Copy), r, w)
        return self.em.op(eng, lambda e: e.tensor_copy(out=out, in_=in_), r, w)

    def mm(self, out, lhsT, rhs, start, stop, r, w, chain=False):
        return self.em.op("pe", lambda e: e.matmul(out, lhsT=lhsT, rhs=rhs, start=start, stop=stop), r, w, chain=chain)

    def tr(self, out, in_, ident, r, w):
        return self.em.op("pe", lambda e: e.transpose(out, in_, ident), r, w)

    def memset(self, eng, ap, val, w):
        return self.em.op(eng, lambda e: e.memset(ap, val), (), w)

    def asel(self, out, in_, pattern, op, fill, base, cm, r, w):
        return self.em.op("pool", lambda e: e.affine_select(out=out, in_=in_, pattern=pattern, compare_op=op,
                                                            fill=fill, base=base, channel_multiplier=cm), r, w)

    def load(self, out_tt, out_ap, in_ap, **kw):
        return self.em.dma("sp", out_ap, in_ap, reads=(), writes=[out_tt.b], dbuf=out_tt.b, **kw)

    def store(self, out_ap, in_tt, in_ap, **kw):
        return self.em.dma("pool", out_ap, in_ap, reads=[in_tt.b], writes=(), dbuf=in_tt.b, **kw)

    def rot(self):
        self.rr += 1
        return ("act", "dve", "pool")[self.rr % 3]


def build():
    P = Prog()
    nc = P.nc
    IN, OUT = "ExternalInput", "ExternalOutput"
    NQ = 256
    xp = P.dram("xp", [T, D], IN)
    xsm = P.dram("xsm", [SB_ * SS, D], IN)
    memp = P.dram("memp", [256, D], IN)
    cfk = P.dram("cfk", [SB_, PAST, 384], IN)
    cfv = P.dram("cfv", [SB_, PAST, 384], IN)
    cfl = P.dram("cfl", [SB_, PAST, 6], IN)
    cmk = P.dram("cmk", [SB_, 256, 256], IN)
    cmv = P.dram("cmv", [SB_, 256, 256], IN)
    srw = P.dram("srw", [SB_, 6, 64, 64], IN)
    ssh = P.dram("ssh", [SB_, 1600], IN)
    norm_g = P.dram("norm_g", [D], IN)
    w_in = P.dram("w_in", [D, NIN], IN)
    fox_q_g = P.dram("fox_q_g", [64], IN)
    fox_k_g = P.dram("fox_k_g", [64], IN)
    fox_b_f = P.dram("fox_b_f", [6], IN)
    rwkv_mu = P.dram("rwkv_mu", [1600], IN)
    rwkv_w0 = P.dram("rwkv_w0", [384], IN)
    rwkv_w_up = P.dram("rwkv_w_up", [32, 384], IN)
    rwkv_a0 = P.dram("rwkv_a0", [384], IN)
    rwkv_a_up = P.dram("rwkv_a_up", [32, 384], IN)
    rwkv_k_k = P.dram("rwkv_k_k", [384], IN)
    rwkv_k_a = P.dram("rwkv_k_a", [384], IN)
    rwkv_r_k = P.dram("rwkv_r_k", [384], IN)
    rwkv_gn_w = P.dram("rwkv_gn_w", [384], IN)
    rwkv_gn_b = P.dram("rwkv_gn_b", [384], IN)
    mem_norm_g = P.dram("mem_norm_g", [D], IN)
    w_mem_kv = P.dram("w_mem_kv", [D, 512], IN)
    mem_q_g = P.dram("mem_q_g", [64], IN)
    mem_k_g = P.dram("mem_k_g", [64], IN)
    w_out = P.dram("w_out", [D, D], IN)

    o_yp = P.dram("o_yp", [T, D], OUT)
    o_ys = P.dram("o_ys", [SB_ * SS, D], OUT)
    o_fkp = P.dram("o_fkp", [T, 384], OUT)
    o_fvp = P.dram("o_fvp", [T, 384], OUT)
    o_flp = P.dram("o_flp", [T, 6], OUT)
    o_mkp = P.dram("o_mkp", [256, 256], OUT)
    o_mvp = P.dram("o_mvp", [256, 256], OUT)
    o_rsp = P.dram("o_rsp", [6, 64, 64], OUT)
    o_rhp = P.dram("o_rhp", [1600], OUT)
    o_fks = P.dram("o_fks", [SB_ * SS, 384], OUT)
    o_fvs = P.dram("o_fvs", [SB_ * SS, 384], OUT)
    o_fls = P.dram("o_fls", [SB_ * SS, 6], OUT)
    o_rss = P.dram("o_rss", [SB_, 6, 64, 64], OUT)
    o_rhs = P.dram("o_rhs", [SB_, 1600], OUT)

    Wb = P.sb("Wb", [128, 8, NIN], BF16)
    wst = [P.sb("wst%d" % i, [128, D], BF16) for i in range(2)]
    wo_bf = nc.dram_tensor("wo_bf", [128, 8, D], BF16, kind="Internal").ap()
    wo_b = Buf("wo_bf")
    kT = P.sb("kT", [67, 6, T], BF16)
    Vaug = P.sb("Vaug", [128, NT, 6, 65], BF16)
    negc = P.sb("negc", [128, NT, 6])
    kvb = [Buf("kv%d" % i) for i in range(NT)]
    xt = [P.sb("xt%d" % i, [128, D]) for i in range(2)]
    xb = P.sb("xb", [128, D], BF16)
    xnT = P.sb("xnT", [128, 8, 128], BF16)
    identb = P.sb("identb", [128, 128], BF16)
    identf = P.sb("identf", [128, 128])
    trif = P.sb("trif", [128, 128])
    lastf = P.sb("lastf", [128, 128])
    bones = P.sb("bones", [128, 128])
    bavg = P.sb("bavg", [128, 128])
    ones = P.sb("ones", [128, 128])
    mask4 = P.sb("mask4", [128, 4, 128], BF16)
    msl = P.sb("msl", [128, 128], BF16)
    ng = P.sb("ng", [128, 8])
    mng = P.sb("mng", [128, 8])
    gq = P.sb("gq", [128, 64])
    gk = P.sb("gk", [128, 64])
    gmq = P.sb("gmq", [128, 64])
    gmk = P.sb("gmk", [128, 64])
    bfb = P.sb("bfb", [128, 6])
    small = P.sb("small", [128, 64])
    tmpA = P.sb("tmpA", [128, 384])
    tmpB = P.sb("tmpB", [128, 384])
    tmpC = P.sb("tmpC", [128, 384])
    qaug = P.sb("qaug", [128, 6, 67], BF16)
    kaug = P.sb("kaug", [128, 6, 67], BF16)
    mqa = P.sb("mqa", [128, 4, 64], BF16)
    cc = [P.sb("cc%d" % i, [128, 6]) for i in range(2)]
    cr = P.sb("cr", [128, 6])
    lf = P.sb("lf", [128, 6])
    raw = P.sb("raw", [128, 13, 129])
    xs = P.sb("xs", [128, 4, 128])
    xw = P.sb("xw", [64, 128])
    mu = P.sb("mu", [128, 13])
    rp = P.sb("rp", [128, 3, 8])
    lora = P.sb("lora", [64, 384], BF16)
    qT = P.sb("qT", [67, 6, NQ], BF16)
    mqT = P.sb("mqT", [64, 4, NQ], BF16)
    mkT = P.sb("mkT", [64, 4, 256], BF16)
    mvaug = P.sb("mvaug", [128, 2, 4, 65], BF16)
    gt = P.sb("gt", [128, 8, NQ], BF16)
    og = P.sb("og", [128, 8, NQ], BF16)
    gtb = [Buf("gt%d" % i) for i in range(8)]
    ogb = [Buf("og%d" % i) for i in range(8)]
    pts = [P.sb("pt%d" % i, [128, NQ], BF16) for i in range(3)]
    oun = P.sb("oun", [65, NQ])
    ounB = P.sb("ounB", [65, NQ])
    opair = P.sb("opair", [128, NQ])
    W3 = lambda name, dt=F32: P.sb(name, [128, 1, 128], dt)
    r_lw, r_a, r_g, r_eg, r_egm, r_eng = W3("r_lw"), W3("r_a"), W3("r_g"), W3("r_eg"), W3("r_egm"), W3("r_eng")
    r_kk, r_t1, r_t2 = W3("r_kk"), W3("r_t1"), W3("r_t2")
    r_yT2 = [W3("r_yT0"), W3("r_yT1")]
    r_bon2 = [W3("r_bon0"), W3("r_bon1")]
    rpc = [0]
    pend_gn = [None]

    def gn_step(k=1):
        for _ in range(k):
            if pend_gn[0]:
                pend_gn[0].pop(0)()

    def flush_gn():
        while pend_gn[0]:
            pend_gn[0].pop(0)()
    ART = P.sb("ART", [128, 1, 2, 128], BF16)
    BTt = W3("BTt", BF16)
    KTt = W3("KTt", BF16)
    vbt = W3("vbt", BF16)
    twd = P.sb("twd", [64, 128], BF16)
    tokA = P.sb("tokA", [128, 128], BF16)
    tokB = P.sb("tokB", [128, 128], BF16)
    tokK = P.sb("tokK", [128, 128], BF16)
    tokV = P.sb("tokV", [128, 128], BF16)
    gm = [P.sb("gm%d" % h, [128, 4, 128], BF16) for h in range(2)]
    PP = [[P.sb("PP%d_%d" % (h, i), [128, 2, 128], BF16) for i in range(2)] for h in range(2)]
    XX = [[P.sb("XX%d_%d" % (h, i), [128, 128], BF16) for i in range(2)] for h in range(2)]
    WT = P.sb("WT", [128, 1, 128], BF16)
    LVs = P.sb("LVs", [128, 2, 64], BF16)
    U0 = P.sb("U0", [128, 2, 64])
    Ub = P.sb("Ub", [128, 2, 64], BF16)
    Hf = P.sb("Hf", [128, 3, 64])
    Hb = P.sb("Hb", [128, 3, 64], BF16)
    Dp = P.sb("Dp", [128, 1, 64])
    stS = P.sb("stS", [64, 6, 64])

    ps_tr = P.ps("ps_tr", [128, 1024], BF16)
    ps_sA = P.ps("ps_sA", [128, 512])
    ps_sB = P.ps("ps_sB", [128, 512])
    ps_o = P.ps("ps_o", [128, 512])
    pgs = [P.ps("pg%d" % i, [128, 512]) for i in range(4)]
    pgi = [0]

    def pg():
        pgi[0] += 1
        return pgs[pgi[0] % len(pgs)]

    class View:
        def __init__(self, ap):
            self.t = ap
            self.b = Buf(excl=True)

        def __getitem__(self, k):
            return self.t[k]
    ps_s2 = [ps_sA, ps_sB]
    ps_o2 = [View(ps_o[:, 0:256]), View(ps_o[:, 256:512])]
    ps_o2[1].b = ps_o2[0].b

    P.memset("pool", identb[:], 0.0, [identb.b])
    P.asel(identb[:], identb[:], [[-1, 128]], ALU.not_equal, 1.0, 0, 1, [identb.b], [identb.b])
    P.memset("pool", identf[:], 0.0, [identf.b])
    P.asel(identf[:], identf[:], [[-1, 128]], ALU.not_equal, 1.0, 0, 1, [identf.b], [identf.b])
    P.memset("pool", trif[:], 1.0, [trif.b])
    P.asel(trif[:], trif[:], [[1, 128]], ALU.is_ge, 0.0, 0, -1, [trif.b], [trif.b])
    P.memset("pool", lastf[:], 1.0, [lastf.b])
    P.asel(lastf[:], lastf[:], [[0, 128]], ALU.is_ge, 0.0, -127, 1, [lastf.b], [lastf.b])
    P.memset("dve", bones[:], 0.0, [bones.b])
    P.memset("dve", bones[0:64, 0:64], 1.0, [bones.b])
    P.memset("dve", bones[64:128, 64:128], 1.0, [bones.b])
    P.ts("dve", bavg[:], bones[:], 1.0 / 64, None, ALU.mult, None, [bones.b], [bavg.b])
    P.memset("dve", ones[:], 1.0, [ones.b])
    P.memset("pool", mask4[:], 1.0, [mask4.b])
    for i in range(4):
        P.asel(mask4[:, i, :], mask4[:, i, :], [[1, 128]], ALU.is_ge, 0.0, (-1 if i % 2 == 0 else 0), -1, [mask4.b], [mask4.b])
    P.memset("pool", msl[:], 1.0, [msl.b])
    P.asel(msl[:], msl[:], [[-1, 128]], ALU.is_ge, 0.0, -1, 1, [msl.b], [msl.b])

    def cload(out_ap, in_ap, tt_, **kw):
        P.em.dma("sp", out_ap, in_ap, reads=(), writes=[tt_.b], dbuf=tt_.b, **kw)

    cload(ng[:], norm_g.rearrange("(c p) -> p c", p=128), ng, allow_slow_non_contiguous=True)
    cload(mng[:], mem_norm_g.rearrange("(c p) -> p c", p=128), mng, allow_slow_non_contiguous=True)
    for tl, src in ((gq, fox_q_g), (gk, fox_k_g), (gmq, mem_q_g), (gmk, mem_k_g)):
        cload(tl[:], src.partition_broadcast(128), tl)
    cload(bfb[:], fox_b_f.partition_broadcast(128), bfb)
    P.ts("dve", gq[:], gq[:], 0.125, None, ALU.mult, None, [gq.b], [gq.b])
    P.ts("dve", gmq[:], gmq[:], 0.125, None, ALU.mult, None, [gmq.b], [gmq.b])
    cload(mu[:, 0:9], rwkv_mu[0:1152].rearrange("(b p) -> p b", p=128), mu, allow_slow_non_contiguous=True)
    cload(mu[0:64, 9:10], rwkv_mu[1152:1216].rearrange("(b p) -> p b", p=64), mu, allow_slow_non_contiguous=True)
    cload(mu[:, 10:13], rwkv_mu[1216:1600].rearrange("(b p) -> p b", p=128), mu, allow_slow_non_contiguous=True)
    for i, src in enumerate((rwkv_w0, rwkv_a0, rwkv_k_k, rwkv_k_a, rwkv_k_a, rwkv_r_k, rwkv_gn_w, rwkv_gn_b)):
        cload(rp[:, :, i], src.rearrange("(b p) -> p b", p=128), rp, allow_slow_non_contiguous=True)
    P.ts("dve", rp[:, :, 4], rp[:, :, 4], -1.0, 1.0, ALU.mult, ALU.add, [rp.b], [rp.b])
    cload(tmpA[0:32, :], rwkv_w_up, tmpA)
    cload(tmpA[32:64, :], rwkv_a_up, tmpA)
    P.cp("dve", lora[:], tmpA[0:64, :], [tmpA.b], [lora.b])
    P.memset("dve", kaug[:, :, 64:67], 1.0, [kaug.b])
    P.memset("dve", raw[:], 0.0, [raw.b])
    P.memset("pool", mvaug[:, :, :, :].rearrange("p a h e -> p (a h) e")[:, :, 64:65], 1.0, [mvaug.b])

    def norm_T(src_dram, n, gtile, xtile):
        P.load(xtile, xtile[0:n, :], src_dram)
        P.em.op("act", lambda e: e.activation(out=xb[0:n, :], in_=xtile[0:n, :], func=AF.Square,
                                              accum_out=small[0:n, 0:1]), [xtile.b], [xb.b, small.b])
        P.act(small[0:n, 1:2], small[0:n, 0:1], AF.Ln, [small.b], [small.b], scale=1.0 / D, bias=EPS)
        P.act(small[0:n, 2:3], small[0:n, 1:2], AF.Exp, [small.b], [small.b], scale=-0.5)
        P.act(xb[0:n, :], xtile[0:n, :], AF.Copy, [xtile.b, small.b], [xb.b], scale=small[0:n, 2:3])
        for c in range(8):
            P.tr(ps_tr[:, c * 128:c * 128 + n], xb[0:n, c * 128:(c + 1) * 128], identb[0:n, 0:n], [xb.b, identb.b], [ps_tr.b])
        P.tt("dve", xnT[:, :, 0:n], ps_tr[:, :].rearrange("p (c t) -> p c t", t=128)[:, :, 0:n],
             gtile[:, :].unsqueeze(2).to_broadcast([128, 8, n]), ALU.mult, [ps_tr.b, gtile.b], [xnT.b])

    def proj_tm(n, c0, c1, pst, W=None):
        W = W or Wb
        for c in range(8):
            P.mm(pst[0:n, 0:c1 - c0], xnT[:, c, 0:n], W[:, c, c0:c1], c == 0, c == 7, [xnT.b, W.b], [pst.b], chain=(c > 0))

    def headnorm(n, pst, nh, gain, dst, out_bf=None, out_bf_b=None):
        w = nh * 64
        v3 = lambda ap: ap.rearrange("p (h d) -> p h d", d=64)
        P.act(tmpA[0:n, 0:w], pst[0:n, 0:w], AF.Square, [pst.b], [tmpA.b])
        P.em.op("dve", lambda e: e.tensor_reduce(out=small[0:n, 8:8 + nh], in_=v3(tmpA[0:n, 0:w]),
                                                 axis=AX.X, op=ALU.add), [tmpA.b], [small.b])
        P.act(small[0:n, 16:16 + nh], small[0:n, 8:8 + nh], AF.Ln, [small.b], [small.b], scale=1.0 / 64, bias=EPS)
        P.act(small[0:n, 24:24 + nh], small[0:n, 16:16 + nh], AF.Exp, [small.b], [small.b], scale=-0.5)
        P.tt("dve", v3(tmpA[0:n, 0:w]), v3(pst[0:n, 0:w]),
             small[0:n, 24:24 + nh].unsqueeze(2).to_broadcast([n, nh, 64]), ALU.mult, [pst.b, small.b], [tmpA.b])
        P.tt("dve", v3(dst[0:n, 0:w]), v3(tmpA[0:n, 0:w]),
             gain[0:n, :].unsqueeze(1).to_broadcast([n, nh, 64]), ALU.mult, [tmpA.b, gain.b], [dst.b])
        if out_bf is not None:
            P.cp("act", out_bf, v3(dst[0:n, 0:w]), [dst.b], [out_bf_b])

    def c_update(n, j, cprev, ccur):
        pst = pg()
        P.mm(pst[0:n, 0:6], trif[0:n, 0:n], lf[0:n, :], True, cprev is None, [trif.b, lf.b], [pst.b])
        if cprev is not None:
            P.mm(pst[0:n, 0:6], lastf[:, 0:n], cprev[:, :], False, True, [lastf.b, cprev.b], [pst.b], chain=True)
        P.cp("act", ccur[0:n, :], pst[0:n, 0:6], [pst.b], [ccur.b])
        P.ts("dve", negc[0:n, j, :], ccur[0:n, :], -1.0, None, ALU.mult, None, [ccur.b], [kvb[j]])

    def q_cpieces(n, ccur):
        P.cp("dve", qaug[0:n, :, 64], ccur[0:n, :], [ccur.b], [qaug.b])
        P.tt("dve", cr[0:n, :], ccur[0:n, :], qaug[0:n, :, 64], ALU.subtract, [ccur.b, qaug.b], [cr.b])
        P.cp("dve", qaug[0:n, :, 65], cr[0:n, :], [cr.b], [qaug.b])
        P.tt("dve", cr[0:n, :], cr[0:n, :], qaug[0:n, :, 65], ALU.subtract, [cr.b, qaug.b], [cr.b])
        P.cp("dve", qaug[0:n, :, 66], cr[0:n, :], [cr.b], [qaug.b])

    def k_to_T(n, j):
        for h in range(6):
            P.tr(ps_tr[0:67, h * 128:h * 128 + n], kaug[0:n, h, :], identb[0:n, 0:n], [kaug.b, identb.b], [ps_tr.b])
        P.cp("act", kT[:, :, j * 128:j * 128 + n], ps_tr[0:67, 0:768].rearrange("p (h t) -> p h t", t=128)[:, :, 0:n],
             [ps_tr.b], [kvb[j]])

    def q_to_T(n, qoff):
        for h in range(6):
            P.tr(ps_tr[0:67, h * 128:h * 128 + n], qaug[0:n, h, :], identb[0:n, 0:n], [qaug.b, identb.b], [ps_tr.b])
        P.cp("act", qT[:, :, qoff:qoff + n], ps_tr[0:67, 0:768].rearrange("p (h t) -> p h t", t=128)[:, :, 0:n],
             [ps_tr.b], [qT.b])

    def token_tile(src, n, j, qoff, o_k, o_v, o_l, cprev, ccur):
        xtile = xt[0]
        norm_T(src, n, ng, xtile)
        p0 = pg()
        proj_tm(n, C_Q, C_Q + 384, p0)
        headnorm(n, p0, 6, gq, tmpB, out_bf=qaug[0:n, :, 0:64], out_bf_b=qaug.b)
        p1 = pg()
        proj_tm(n, C_K, C_K + 384, p1)
        headnorm(n, p1, 6, gk, tmpB, out_bf=kaug[0:n, :, 0:64], out_bf_b=kaug.b)
        P.store(o_k, tmpB, tmpB[0:n, 0:384])
        p2 = pg()
        proj_tm(n, C_V, C_V + 390, p2)
        P.cp("act", tmpC[0:n, 0:384], p2[0:n, 0:384], [p2.b], [tmpC.b])
        P.store(o_v, tmpC, tmpC[0:n, 0:384])
        P.cp("dve", Vaug[0:n, j, :, 0:64], tmpC[0:n, 0:384].rearrange("p (h d) -> p h d", d=64), [tmpC.b], [kvb[j]])
        P.tt("dve", lf[0:n, :], p2[0:n, 384:390], bfb[0:n, :], ALU.add, [p2.b, bfb.b], [lf.b])
        P.act(lf[0:n, :], lf[0:n, :], AF.Exp, [lf.b], [lf.b], scale=-1.0)
        P.act(lf[0:n, :], lf[0:n, :], AF.Ln, [lf.b], [lf.b], bias=1.0)
        P.ts("dve", lf[0:n, :], lf[0:n, :], -1.0, None, ALU.mult, None, [lf.b], [lf.b])
        P.store(o_l, lf, lf[0:n, :])
        c_update(n, j, cprev, ccur)
        q_cpieces(n, ccur)
        k_to_T(n, j)
        q_to_T(n, qoff)
        p3 = pg()
        proj_tm(n, C_MQ, C_MQ + 256, p3)
        headnorm(n, p3, 4, gmq, tmpB, out_bf=mqa[0:n, :, :], out_bf_b=mqa.b)
        for h in range(4):
            P.tr(ps_tr[0:64, h * 128:h * 128 + n], mqa[0:n, h, :], identb[0:n, 0:n], [mqa.b, identb.b], [ps_tr.b])
        P.cp("act", mqT[:, :, qoff:qoff + n], ps_tr[0:64, 0:512].rearrange("p (h t) -> p h t", t=128)[:, :, 0:n],
             [ps_tr.b], [mqT.b])

    def fm_proj(n, qoff):
        gblocks = [(C_GF + 128 * i, i) for i in range(3)] + [(C_GM + 128 * i, 6 + i) for i in range(2)]
        for g0 in (0, 4):
            pst = pg()
            lst_ = gblocks[g0:g0 + 4]
            for jj, (c0, ch) in enumerate(lst_):
                for c in range(8):
                    P.mm(pst[:, jj * 128:jj * 128 + n], Wb[:, c, c0:c0 + 128], xnT[:, c, 0:n], c == 0, c == 7, [xnT.b, Wb.b], [pst.b], chain=(c > 0))
            for jj, (c0, ch) in enumerate(lst_):
                P.act(gt[:, ch, qoff:qoff + n], pst[:, jj * 128:jj * 128 + n], AF.Silu, [pst.b], [gtb[ch]])
        blocks = [(C_RW + 128 * i, 128) for i in range(9)] + [(C_WD, 64)] + [(C_GR + 128 * i, 128) for i in range(3)]
        for g0 in range(0, 13, 4):
            pst = pg()
            nb = min(4, 13 - g0)
            for jj in range(nb):
                c0, m = blocks[g0 + jj]
                for c in range(8):
                    P.mm(pst[0:m, jj * 128:jj * 128 + n], Wb[:, c, c0:c0 + m], xnT[:, c, 0:n], c == 0, c == 7, [xnT.b, Wb.b], [pst.b], chain=(c > 0))
            P.cp("act" if (g0 // 4) % 2 == 0 else "dve", raw[:, g0:g0 + nb, 1:1 + n], pst[:, 0:nb * 128].rearrange("p (b t) -> p b t", t=128)[:, :, 0:n], [pst.b], [raw.b])

    def store_shift(dst, n):
        P.store(dst[0:1152].rearrange("(b p) -> p b", p=128), raw, raw[:, 0:9, n], allow_slow_non_contiguous=True)
        P.store(dst[1152:1216].rearrange("(b p) -> p b", p=64), raw, raw[0:64, 9:10, n], allow_slow_non_contiguous=True)
        P.store(dst[1216:1600].rearrange("(b p) -> p b", p=128), raw, raw[:, 10:13, n], allow_slow_non_contiguous=True)

    def rwkv_pre(n, qoff):
        cur = lambda blk, p0=0, p1=128: raw[p0:p1, blk, 1:1 + n]
        prv = lambda blk, p0=0, p1=128: raw[p0:p1, blk, 0:n]
        P.tt("dve", xw[:, 0:n], prv(9, 0, 64), cur(9, 0, 64), ALU.subtract, [raw.b], [xw.b])
        P.stt(xw[:, 0:n], xw[:, 0:n], mu[0:64, 9:10], cur(9, 0, 64), ALU.mult, ALU.add, [xw.b, mu.b, raw.b], [xw.b])
        P.act(twd[0:32, 0:n], xw[0:32, 0:n], AF.Tanh, [xw.b], [twd.b])
        P.cp("dve", twd[32:64, 0:n], xw[32:64, 0:n], [xw.b], [twd.b])
        for p in range(3):
            blk = 10 + p
            P.tt("dve", xs[:, 3, 0:n], prv(blk), cur(blk), ALU.subtract, [raw.b], [xs.b])
            P.stt(xs[:, 3, 0:n], xs[:, 3, 0:n], mu[:, blk:blk + 1], cur(blk), ALU.mult, ALU.add, [xs.b, mu.b, raw.b], [xs.b])
            P.act(gt[:, 3 + p, qoff:qoff + n], xs[:, 3, 0:n], AF.Silu, [xs.b], [gtb[3 + p]])
        nlev = 0
        while (1 << nlev) < n:
            nlev += 1
        nlev -= 1
        return nlev

    def rwkv_pair(n, qoff, p, nlev):
        par = rpc[0] % 2
        rpc[0] += 1
        yT_, bon_ = r_yT2[par], r_bon2[par]
        cur = lambda blk, p0=0, p1=128: raw[p0:p1, blk, 1:1 + n]
        prv = lambda blk, p0=0, p1=128: raw[p0:p1, blk, 0:n]
        if True:
            S3 = lambda tl: tl[:, 0, 0:n]
            bc = lambda i: rp[:, p, i:i + 1]
            for i, blk in enumerate((p, 3 + p, 6 + p)):
                P.tt("dve", xs[:, i, 0:n], prv(blk), cur(blk), ALU.subtract, [raw.b], [xs.b])
                P.stt(xs[:, i, 0:n], xs[:, i, 0:n], mu[:, blk:blk + 1], cur(blk), ALU.mult, ALU.add, [xs.b, mu.b, raw.b], [xs.b])
            xr, xk, xv = xs[:, 0, 0:n], xs[:, 1, 0:n], xs[:, 2, 0:n]
            pw = pg()
            P.mm(pw[:, 0:n], lora[0:32, p * 128:(p + 1) * 128], twd[0:32, 0:n], True, True, [lora.b, twd.b], [pw.b])
            P.mm(pw[:, 128:128 + n], lora[32:64, p * 128:(p + 1) * 128], twd[32:64, 0:n], True, True, [lora.b, twd.b], [pw.b])
            P.act(S3(r_lw), pw[:, 0:n], AF.Sigmoid, [pw.b, rp.b], [r_lw.b], bias=bc(0))
            P.ts("dve", S3(r_lw), S3(r_lw), -0.6065306597126334, None, ALU.mult, None, [r_lw.b], [r_lw.b])
            P.act(S3(r_a), pw[:, 128:128 + n], AF.Sigmoid, [pw.b, rp.b], [r_a.b], bias=bc(1))
            P.em.op("dve", lambda e: e.tensor_tensor_scan(out=r_g[:, 0, 0:n], data0=ones[:, 0:n], data1=r_lw[:, 0, 0:n],
                                                          initial=0.0, op0=ALU.mult, op1=ALU.add),
                    [ones.b, r_lw.b], [r_g.b])
            P.act(S3(r_eg), S3(r_g), AF.Exp, [r_g.b], [r_eg.b])
            P.act(S3(r_eng), S3(r_g), AF.Exp, [r_g.b], [r_eng.b], scale=-1.0)
            P.tt("dve", S3(r_egm), S3(r_g), S3(r_lw), ALU.subtract, [r_g.b, r_lw.b], [r_egm.b])
            P.act(S3(r_egm), S3(r_egm), AF.Exp, [r_egm.b], [r_egm.b])
            P.ts("dve", S3(r_kk), xk, bc(2), None, ALU.mult, None, [xs.b, rp.b], [r_kk.b])
            P.tt("dve", S3(r_t1), S3(r_kk), S3(r_kk), ALU.mult, [r_kk.b], [r_t1.b])
            pss = pg()
            P.mm(pss[:, 0:n], bones[:, :], r_t1[:, 0, 0:n], True, True, [bones.b, r_t1.b], [pss.b])
            P.ts("dve", S3(r_t1), pss[:, 0:n], 1e-24, None, ALU.max, None, [pss.b], [r_t1.b])
            P.act(S3(r_t1), S3(r_t1), AF.Ln, [r_t1.b], [r_t1.b], scale=float(2 ** 40))
            P.act(S3(r_t1), S3(r_t1), AF.Exp, [r_t1.b], [r_t1.b], scale=-0.5, bias=13.862943611198906)
            P.tt("dve", S3(r_kk), S3(r_kk), S3(r_t1), ALU.mult, [r_kk.b, r_t1.b], [r_kk.b])
            P.ts("pool", S3(r_t2), S3(r_a), bc(3), bc(4), ALU.mult, ALU.add, [r_a.b, rp.b], [r_t2.b])
            P.tt("pool", S3(r_t2), S3(r_t2), xk, ALU.mult, [r_t2.b, xs.b], [r_t2.b])
            P.stt(ART[:, 0, 0, 0:n], S3(r_kk), -1.0, S3(r_egm), ALU.mult, ALU.mult, [r_kk.b, r_egm.b], [ART.b])
            P.tt("pool", ART[:, 0, 1, 0:n], xr, S3(r_eg), ALU.mult, [xs.b, r_eg.b], [ART.b])
            P.tt("dve", S3(r_t1), S3(r_a), S3(r_kk), ALU.mult, [r_a.b, r_kk.b], [r_t1.b])
            P.tt("dve", S3(BTt), S3(r_t1), S3(r_eng), ALU.mult, [r_t1.b, r_eng.b], [BTt.b])
            P.tt("pool", S3(KTt), S3(r_t2), S3(r_eng), ALU.mult, [r_t2.b, r_eng.b], [KTt.b])
            P.cp("act", S3(vbt), xv, [xs.b], [vbt.b])
            P.tt("dve", S3(r_t1), xr, S3(r_t2), ALU.mult, [xs.b, r_t2.b], [r_t1.b])
            P.ts("dve", S3(r_t1), S3(r_t1), bc(5), None, ALU.mult, None, [r_t1.b, rp.b], [r_t1.b])
            psb = pg()
            P.mm(psb[:, 0:n], bones[:, :], r_t1[:, 0, 0:n], True, True, [bones.b, r_t1.b], [psb.b])
            P.tt("dve", S3(bon_), psb[:, 0:n], xv, ALU.mult, [psb.b, xs.b], [bon_.b])
            for i, (src_ap, sb_) in enumerate(((ART[:, 0, 0, 0:n], ART.b), (BTt[:, 0, 0:n], BTt.b), (KTt[:, 0, 0:n], KTt.b), (vbt[:, 0, 0:n], vbt.b))):
                P.tr(ps_tr[0:n, i * 128:(i + 1) * 128], src_ap, identb[:, :], [sb_, identb.b], [ps_tr.b])
            for i, dstt in enumerate((tokA, tokB, tokK, tokV)):
                P.cp("act" if i % 2 else "dve", dstt[0:n, :], ps_tr[0:n, i * 128:(i + 1) * 128], [ps_tr.b], [dstt.b])
            for hh in range(2):
                hb = hh * 64
                g12 = pg()
                for a_ in range(2):
                    P.mm(g12[0:n, a_ * 128:a_ * 128 + n], BTt[hb:hb + 64, 0, 0:n], ART[hb:hb + 64, 0, a_, 0:n], True, True, [BTt.b, ART.b], [g12.b])
                    P.mm(g12[0:n, 256 + a_ * 128:256 + a_ * 128 + n], KTt[hb:hb + 64, 0, 0:n], ART[hb:hb + 64, 0, a_, 0:n], True, True, [KTt.b, ART.b], [g12.b])
                P.tt("dve", gm[hh][0:n, :, 0:n], g12[0:n, :].rearrange("s (a t) -> s a t", t=128)[:, :, 0:n], mask4[0:n, :, 0:n], ALU.mult,
                     [g12.b, mask4.b], [gm[hh].b])
                g3 = pg()
                P.mm(g3[0:n, 0:n], ART[hb:hb + 64, 0, 0, 0:n], BTt[hb:hb + 64, 0, 0:n], True, True, [ART.b, BTt.b], [g3.b])
                P.tt("dve", PP[hh][0][0:n, 0, 0:n], g3[0:n, 0:n], msl[0:n, 0:n], ALU.mult, [g3.b, msl.b], [PP[hh][0].b])
                P.cp("act", PP[hh][0][0:n, 1, 0:n], gm[hh][0:n, 0, 0:n], [gm[hh].b], [PP[hh][0].b])
                P.tt("pool", XX[hh][0][0:n, 0:n], gm[hh][0:n, 0, 0:n], identb[0:n, 0:n], ALU.add, [gm[hh].b, identb.b], [XX[hh][0].b])
            for j in range(1, nlev + 1):
                ci, ni = (j - 1) % 2, j % 2
                for hh in range(2):
                    psq = pg()
                    Pc = PP[hh][ci]
                    P.mm(psq[0:n, 0:n], Pc[0:n, 1, 0:n], Pc[0:n, 0, 0:n], True, True, [Pc.b], [psq.b])
                    if j < nlev:
                        P.mm(psq[0:n, 128:128 + n], Pc[0:n, 0, 0:n], Pc[0:n, 1, 0:n], True, True, [Pc.b], [psq.b])
                        P.cp("dve" if hh else "act", PP[hh][ni][0:n, :, 0:n], psq[0:n, 0:256].rearrange("s (a t) -> s a t", t=128)[:, :, 0:n], [psq.b], [PP[hh][ni].b])
                    else:
                        P.cp("dve" if hh else "act", PP[hh][ni][0:n, 0, 0:n], psq[0:n, 0:n], [psq.b], [PP[hh][ni].b])
                for hh in range(2):
                    px = pg()
                    P.mm(px[0:n, 0:n], PP[hh][ni][0:n, 0, 0:n], XX[hh][ci][0:n, 0:n], True, True, [PP[hh][ni].b, XX[hh][ci].b], [px.b])
                    P.tt("dve", XX[hh][ni][0:n, 0:n], px[0:n, 0:n], XX[hh][ci][0:n, 0:n], ALU.add, [px.b, XX[hh][ci].b], [XX[hh][ni].b])
                gn_step(1 if nlev >= 4 else 2)
            fi = nlev % 2
            for hh in range(2):
                hb = hh * 64
                TTm = XX[hh][fi]
                pw_ = pg()
                P.mm(pw_[0:64, 0:n], tokA[0:n, hb:hb + 64], TTm[0:n, 0:n], True, True, [tokA.b, TTm.b], [pw_.b])
                P.cp("act", WT[hb:hb + 64, 0, 0:n], pw_[0:64, 0:n], [pw_.b], [WT.b])
                P.mm(pw_[0:n, 128:192], gm[hh][0:n, 2, 0:n], tokV[0:n, hb:hb + 64], True, True, [gm[hh].b, tokV.b], [pw_.b])
                P.cp("dve", LVs[0:n, hh, :], pw_[0:n, 128:192], [pw_.b], [LVs.b])
                P.mm(pw_[0:n, 256:320], TTm[0:n, 0:n], LVs[0:n, hh, :], True, True, [TTm.b, LVs.b], [pw_.b])
                P.cp("act", U0[0:n, hh, :], pw_[0:n, 256:320], [pw_.b], [U0.b])
            flush_gn()
            pU = pg()
            for hh in range(2):
                hb = hh * 64
                P.mm(pU[0:n, hb:hb + 64], WT[hb:hb + 64, 0, 0:n], Hb[hb:hb + 64, p, :], True, True, [WT.b, Hb.b], [pU.b])
            P.tt("dve", Ub[0:n, :, :], pU[0:n, 0:128].rearrange("t (h v) -> t h v", v=64), U0[0:n, :, :], ALU.add, [pU.b, U0.b], [Ub.b])
            pY = pg()
            for hh in range(2):
                hb = hh * 64
                dst = pY[0:64, hh * 128:hh * 128 + n]
                P.mm(dst, Hb[hb:hb + 64, p, :], ART[hb:hb + 64, 0, 1, 0:n], True, False, [Hb.b, ART.b], [pY.b])
                P.mm(dst, Ub[0:n, hh, :], gm[hh][0:n, 1, 0:n], False, False, [Ub.b, gm[hh].b], [pY.b], chain=True)
                P.mm(dst, tokV[0:n, hb:hb + 64], gm[hh][0:n, 3, 0:n], False, True, [tokV.b, gm[hh].b], [pY.b], chain=True)
            for hh in range(2):
                hb = hh * 64
                P.cp("act" if hh else "dve", yT_[hb:hb + 64, 0, 0:n], pY[0:64, hh * 128:hh * 128 + n], [pY.b], [yT_.b])
            pD = pg()
            for hh in range(2):
                hb = hh * 64
                P.mm(pD[0:64, hb:hb + 64], tokB[0:n, hb:hb + 64], Ub[0:n, hh, :], True, False, [tokB.b, Ub.b], [pD.b])
                P.mm(pD[0:64, hb:hb + 64], tokK[0:n, hb:hb + 64], tokV[0:n, hb:hb + 64], False, True, [tokK.b, tokV.b], [pD.b], chain=True)
            for hh in range(2):
                hb = hh * 64
                P.cp("act" if hh else "dve", Dp[hb:hb + 64, 0, :], pD[0:64, hb:hb + 64], [pD.b], [Dp.b])
            P.tt("dve", Hf[:, p, :], Hf[:, p, :], Dp[:, 0, :], ALU.add, [Hf.b, Dp.b], [Hf.b])
            P.ts("dve", Hf[:, p, :], Hf[:, p, :], r_eg[:, 0, n - 1:n], None, ALU.mult, None, [Hf.b, r_eg.b], [Hf.b])
            P.cp("act", Hb[:, p, :], Hf[:, p, :], [Hf.b], [Hb.b])
            def gn_steps(p=p, n=n, qoff=qoff, yT_=yT_, bon_=bon_):
                y_ = yT_[:, 0, 0:n]
                t_ = tmpA[:, 0:n]

                def sA():
                    pm = pg()
                    P.mm(pm[:, 0:n], bavg[:, :], y_, True, True, [bavg.b, yT_.b], [pm.b])
                    P.tt("dve", y_, y_, pm[:, 0:n], ALU.subtract, [yT_.b, pm.b], [yT_.b])
                    P.tt("dve", t_, y_, y_, ALU.mult, [yT_.b], [tmpA.b])

                def sB():
                    pvv = pg()
                    P.mm(pvv[:, 0:n], bavg[:, :], t_, True, True, [bavg.b, tmpA.b], [pvv.b])
                    P.act(t_, pvv[:, 0:n], AF.Ln, [pvv.b], [tmpA.b], bias=GN_EPS)
                    P.act(t_, t_, AF.Exp, [tmpA.b], [tmpA.b], scale=-0.5)

                def sC():
                    P.tt("dve", y_, y_, t_, ALU.mult, [yT_.b, tmpA.b], [yT_.b])
                    P.ts("dve", y_, y_, rp[:, p, 6:7], rp[:, p, 7:8], ALU.mult, ALU.add, [yT_.b, rp.b], [yT_.b])

                def sD():
                    P.tt("dve", y_, y_, bon_[:, 0, 0:n], ALU.add, [yT_.b, bon_.b], [yT_.b])
                    P.tt("dve", og[:, 3 + p, qoff:qoff + n], y_, gt[:, 3 + p, qoff:qoff + n], ALU.mult, [yT_.b, gtb[3 + p]], [ogb[3 + p]])
                return [sA, sB, sC, sD]
            flush_gn()
            pend_gn[0] = gn_steps()

    def rwkv_chunk(n, qoff):
        nlev = rwkv_pre(n, qoff)
        for p in range(3):
            rwkv_pair(n, qoff, p, nlev)

    def store_state(dst):
        pst = pg()
        for p in range(3):
            P.tr(pst[0:64, p * 128:(p + 1) * 128], Hf[:, p, :], identf[:, :], [Hf.b, identf.b], [pst.b])
        P.cp("act", stS[:, :, :], pst[0:64, 0:384].rearrange("v (h k) -> v h k", k=64), [pst.b], [stS.b])
        P.store(dst.rearrange("h v k -> v h k"), stS, stS[:, :, :])

    pti = [0]

    oun2 = [oun, ounB]
    hcnt = [0]
    pending = [None]

    def flush_tail():
        if pending[0] is not None:
            t_ = pending[0]
            pending[0] = None
            t_()

    def attention(nq, heads, kfn, vfn, bfn, entries, krows, out_fn):
        for h in heads:
            po = ps_o2[h % 2]
            nent = len(entries)

            def pv(i, ent, ptt):
                j, nk, q0, diag = ent
                vap, vb_ = vfn(h, j, nk)
                P.mm(po[0:65, q0:nq], vap, ptt[0:nk, 0:nq - q0], i == 0, i == nent - 1, [vb_, ptt.b], [po.b])
            prev = None
            for i, ent in enumerate(entries):
                j, nk, q0, diag = ent
                pss_ = ps_s2[pti[0] % 2]
                ptt = pts[pti[0] % 3]
                pti[0] += 1
                kap, kb = kfn(h, j, nk)
                qap, qb = qfn_cur[0](h, q0, nq)
                P.mm(pss_[0:nk, 0:nq - q0], kap, qap, True, True, [kb, qb], [pss_.b])
                bias = bfn(h, j, nk)
                if bias is not None:
                    P.act(ptt[0:nk, 0:nq - q0], pss_[0:nk, 0:nq - q0], AF.Exp, [pss_.b, bias[1]], [ptt.b], bias=bias[0])
                else:
                    P.act(ptt[0:nk, 0:nq - q0], pss_[0:nk, 0:nq - q0], AF.Exp, [pss_.b], [ptt.b])
                if diag:
                    P.asel(ptt[0:nk, 0:nk], ptt[0:nk, 0:nk], [[1, nk]], ALU.is_ge, 0.0, 0, -1, [ptt.b], [ptt.b])
                if prev is not None:
                    pv(*prev)
                prev = (i, ent, ptt)
            pv(*prev)
            ou = oun2[hcnt[0] % 2]
            hcnt[0] += 1
            P.cp("act", ou[0:65, 0:nq], po[0:65, 0:nq], [po.b], [ou.b])

            def tail(h=h, ou=ou, nq=nq, out_fn=out_fn):
                P.act(ou[64:65, 0:nq], ou[64:65, 0:nq], AF.Ln, [ou.b], [ou.b])
                P.act(ou[64:65, 0:nq], ou[64:65, 0:nq], AF.Exp, [ou.b], [ou.b], scale=-1.0)
                pb = pg()
                P.mm(pb[0:64, 0:nq], ones[64:65, 0:64], ou[64:65, 0:nq], True, True, [ones.b, ou.b], [pb.b])
                out_fn(h, pb, ou)
            flush_tail()
            pending[0] = tail

    qfn_cur = [None]

    def fox_heads(nq, heads, fox_entries):
        qfn_cur[0] = lambda h, q0, nq_: (qT[0:67, h, q0:nq_], qT.b)

        def fox_out(h, pb, ou):
            hb = (h % 2) * 64
            P.tt("dve", opair[hb:hb + 64, 0:nq], ou[0:64, 0:nq], pb[0:64, 0:nq], ALU.mult, [ou.b, pb.b], [opair.b])
            if h % 2 == 1:
                c = h // 2
                P.tt("dve", og[:, c, 0:nq], opair[:, 0:nq], gt[:, c, 0:nq], ALU.mult, [opair.b, gtb[c]], [ogb[c]])
        attention(nq, heads,
                  lambda h, j, nk: (kT[0:67, h, j * 128:j * 128 + nk], kvb[j]),
                  lambda h, j, nk: (Vaug[0:nk, j, h, :], kvb[j]),
                  lambda h, j, nk: (negc[0:nk, j, h:h + 1], kvb[j]),
                  fox_entries, 67, fox_out)

    def mem_heads(nq, heads, mem_k, mem_v, mem_kb, mem_vb):
        qfn_cur[0] = lambda h, q0, nq_: (mqT[0:64, h, q0:nq_], mqT.b)

        def mem_out(h, pb, ou):
            hb = (h % 2) * 64
            P.tt("dve", opair[hb:hb + 64, 0:nq], ou[0:64, 0:nq], pb[0:64, 0:nq], ALU.mult, [ou.b, pb.b], [opair.b])
            if h % 2 == 1:
                c = 6 + h // 2
                P.tt("dve", og[:, c, 0:nq], opair[:, 0:nq], gt[:, c, 0:nq], ALU.mult, [opair.b, gtb[c]], [ogb[c]])
        attention(nq, heads,
                  lambda h, j, nk: (mem_k[0:64, h, j * 128:j * 128 + nk], mem_kb),
                  lambda h, j, nk: (mem_v[0:nk, j, h, :], mem_vb),
                  lambda h, j, nk: None,
                  [(0, 128, 0, False), (1, 128, 0, False)], 64, mem_out)

    def run_attention(nq, fox_entries, mem_k, mem_v, mem_kb, mem_vb):
        fox_heads(nq, range(6), fox_entries)
        mem_heads(nq, range(4), mem_k, mem_v, mem_kb, mem_vb)

    wsti = [0]

    def out_proj(tiles):
        accs = [[pg(), pg()] for _ in tiles]
        for c in range(8):
            w = wst[wsti[0] % 2]
            wsti[0] += 1
            P.em.dma("sp", w[:, :], wo_bf[:, c, :], reads=[wo_b], writes=[w.b], dbuf=w.b)
            for ti, (n, qoff, src, dst) in enumerate(tiles):
                for cb in range(2):
                    pst = accs[ti][cb]
                    P.mm(pst[0:n, 0:512], og[:, c, qoff:qoff + n], w[:, cb * 512:(cb + 1) * 512], c == 0, c == 7, [ogb[c], w.b], [pst.b])
        for ti, (n, qoff, src, dst) in enumerate(tiles):
            xtile = xt[1]
            P.load(xtile, xtile[0:n, :], src)
            for cb in range(2):
                pst = accs[ti][cb]
                P.tt("dve", xtile[0:n, cb * 512:(cb + 1) * 512], xtile[0:n, cb * 512:(cb + 1) * 512], pst[0:n, 0:512], ALU.add,
                     [xtile.b, pst.b], [xtile.b])
            P.store(dst, xtile, xtile[0:n, :])

    w_in_v = w_in.rearrange("(c p) n -> p c n", p=128)
    w_out_v = w_out.rearrange("(c p) n -> p c n", p=128)
    w_mem_v = w_mem_kv.rearrange("(c p) n -> p c n", p=128)
    kq = 0
    Wm = Vaug[:, :, :, :].rearrange("p a h e -> p (a h e)")[:, 0:4096].rearrange("p (c n) -> p c n", n=512)
    for c in range(8):
        s_ = xt[kq % 2]
        P.load(s_, s_[:, 0:512], w_mem_v[:, c, :])
        P.cp(P.rot(), Wm[:, c, :], s_[:, 0:512], [s_.b], kvb)
        kq += 1
    for blk in range(2):
        xtile = xt[blk % 2]
        norm_T(memp[blk * 128:(blk + 1) * 128, :], 128, mng, xtile)
        pst = pg()
        for c in range(8):
            P.mm(pst[:, 0:512], xnT[:, c, :], Wm[:, c, :], c == 0, c == 7, [xnT.b] + kvb, [pst.b], chain=(c > 0))
        headnorm(128, pst, 4, gmk, tmpB, out_bf=mqa[:, :, :], out_bf_b=mqa.b)
        P.store(o_mkp[blk * 128:(blk + 1) * 128, :], tmpB, tmpB[:, 0:256])
        P.cp("act", tmpC[:, 0:256], pst[:, 256:512], [pst.b], [tmpC.b])
        P.store(o_mvp[blk * 128:(blk + 1) * 128, :], tmpC, tmpC[:, 0:256])
        P.cp("dve", mvaug[:, blk, :, 0:64], pst[:, 256:512].rearrange("p (h d) -> p h d", d=64), [pst.b], [mvaug.b])
        for h in range(4):
            P.tr(ps_tr[0:64, h * 128:(h + 1) * 128], mqa[:, h, :], identb[:, :], [mqa.b, identb.b], [ps_tr.b])
        P.cp("act", mkT[:, :, blk * 128:(blk + 1) * 128], ps_tr[0:64, 0:512].rearrange("p (h t) -> p h t", t=128), [ps_tr.b], [mkT.b])
    P.memset("pool", Vaug[:, :, :, :].rearrange("p a h e -> p (a h) e")[:, :, 64:65], 1.0, kvb)
    for c in range(8):
        for (c0, c1) in ((0, 1024), (1024, 2048), (2048, 3072), (3072, NIN)):
            s_ = xt[kq % 2]
            P.load(s_, s_[:, 0:c1 - c0], w_in_v[:, c, c0:c1])
            P.cp(P.rot(), Wb[:, c, c0:c1], s_[:, 0:c1 - c0], [s_.b], [Wb.b])
            kq += 1
        s_ = xt[kq % 2]
        P.load(s_, s_[:, :], w_out_v[:, c, :])
        w = wst[c % 2]
        P.cp(P.rot(), w[:, :], s_[:, :], [s_.b], [w.b])
        P.em.dma("pool", wo_bf[:, c, :], w[:, :], reads=[w.b], writes=[wo_b], dbuf=w.b)
        kq += 1

    P.memset("dve", Hf[:], 0.0, [Hf.b])
    P.memset("dve", Hb[:], 0.0, [Hb.b])
    NG = NT // 2
    for g in range(NG):
        entries = [(j, 128, 0, False) for j in range(2 * g)] + [(2 * g, 128, 0, True), (2 * g + 1, 128, 128, True)]
        for tt_ in range(2):
            t = 2 * g + tt_
            sl = slice(t * 128, (t + 1) * 128)
            token_tile(xp[sl, :], 128, t, tt_ * 128, o_fkp[sl, :], o_fvp[sl, :], o_flp[sl, :],
                       None if t == 0 else cc[(t - 1) % 2], cc[t % 2])
            fm_proj(128, tt_ * 128)
            if tt_ == 0:
                rwkv_chunk(128, 0)
            else:
                nlev = rwkv_pre(128, 128)
                for p in range(3):
                    rwkv_pair(128, 128, p, nlev)
                    fox_heads(NQ, (2 * p, 2 * p + 1), entries)
            if t == NT - 1:
                store_shift(o_rhp, 128)
            else:
                P.cp("dve", raw[:, :, 0:1], raw[:, :, 128:129], [raw.b], [raw.b])
        mem_heads(NQ, range(4), mkT, mvaug, mkT.b, mvaug.b)
        flush_tail()
        flush_gn()
        out_proj([(128, tt_ * 128, xp[(2 * g + tt_) * 128:(2 * g + tt_ + 1) * 128, :], o_yp[(2 * g + tt_) * 128:(2 * g + tt_ + 1) * 128, :])
                  for tt_ in range(2)])
    store_state(o_rsp)

    for b in range(SB_):
        sl = slice(b * SS, (b + 1) * SS)
        for j in range(8):
            ks = slice(j * 128, (j + 1) * 128)
            P.load(tmpB, tmpB[:, 0:384], cfk[b, ks, :])
            P.cp("act", kaug[:, :, 0:64], tmpB[:, 0:384].rearrange("p (h d) -> p h d", d=64), [tmpB.b], [kaug.b])
            k_to_T(128, j)
            P.load(tmpC, tmpC[:, 0:384], cfv[b, ks, :])
            P.cp("dve", Vaug[:, j, :, 0:64], tmpC[:, 0:384].rearrange("p (h d) -> p h d", d=64), [tmpC.b], [kvb[j]])
            P.load(lf, lf[:, :], cfl[b, ks, :])
            c_update(128, j, None if j == 0 else cc[(j - 1) % 2], cc[j % 2])
        P.load(stS, stS[:, :, :], srw[b].rearrange("h v k -> v h k"))
        pst = pg()
        for h in range(6):
            P.tr(pst[0:64, h * 64:(h + 1) * 64], stS[:, h, :], identf[0:64, 0:64], [stS.b, identf.b], [pst.b])
        for h in range(6):
            p, hb = h // 2, (h % 2) * 64
            P.cp("act" if h % 2 else "dve", Hf[hb:hb + 64, p, :], pst[0:64, h * 64:(h + 1) * 64], [pst.b], [Hf.b])
        P.cp("act", Hb[:, :, :], Hf[:, :, :], [Hf.b], [Hb.b])
        P.em.dma("sp", raw[:, 0:9, 0], ssh[b, 0:1152].rearrange("(b p) -> p b", p=128), reads=(), writes=[raw.b], dbuf=raw.b,
                 allow_slow_non_contiguous=True)
        P.em.dma("sp", raw[0:64, 9:10, 0], ssh[b, 1152:1216].rearrange("(b p) -> p b", p=64), reads=(), writes=[raw.b], dbuf=raw.b,
                 allow_slow_non_contiguous=True)
        P.em.dma("sp", raw[:, 10:13, 0], ssh[b, 1216:1600].rearrange("(b p) -> p b", p=128), reads=(), writes=[raw.b], dbuf=raw.b,
                 allow_slow_non_contiguous=True)
        token_tile(xsm[sl, :], SS, 8, 0, o_fks[sl, :], o_fvs[sl, :], o_fls[sl, :], cc[7 % 2], cc[8 % 2])
        fm_proj(SS, 0)
        rwkv_chunk(SS, 0)
        store_shift(o_rhs[b], SS)
        store_state(o_rss[b])
        for blk in range(2):
            ks = slice(blk * 128, (blk + 1) * 128)
            P.load(tmpB, tmpB[:, 0:256], cmk[b, ks, :])
            P.cp("act", mqa[:, :, :], tmpB[:, 0:256].rearrange("p (h d) -> p h d", d=64), [tmpB.b], [mqa.b])
            for h in range(4):
                P.tr(ps_tr[0:64, h * 128:(h + 1) * 128], mqa[:, h, :], identb[:, :], [mqa.b, identb.b], [ps_tr.b])
            P.cp("act", mkT[:, :, blk * 128:(blk + 1) * 128], ps_tr[0:64, 0:512].rearrange("p (h t) -> p h t", t=128), [ps_tr.b], [mkT.b])
            P.load(tmpC, tmpC[:, 0:256], cmv[b, ks, :])
            P.cp("dve", mvaug[:, blk, :, 0:64], tmpC[:, 0:256].rearrange("p (h d) -> p h d", d=64), [tmpC.b], [mvaug.b])
        entries = [(j, 128, 0, False) for j in range(8)] + [(8, SS, 0, True)]
        run_attention(SS, entries, mkT, mvaug, mkT.b, mvaug.b)
        flush_tail()
        flush_gn()
        out_proj([(SS, 0, xsm[sl, :], o_ys[sl, :])])

    P.em.final_wait("pool")
    P.em.replay()
    return nc


_NC = None


def kernel(x_prompt, x_sample, mem_prompt, cache_fox_k, cache_fox_v, cache_fox_logf,
           cache_mem_k, cache_mem_v, state_rwkv, state_rwkv_shift,
           norm_g, w_in, fox_q_g, fox_k_g, fox_b_f, rwkv_mu, rwkv_w0, rwkv_w_up, rwkv_a0,
           rwkv_a_up, rwkv_k_k, rwkv_k_a, rwkv_r_k, rwkv_gn_w, rwkv_gn_b,
           mem_norm_g, w_mem_kv, mem_q_g, mem_k_g, w_out):
    global _NC
    f = lambda a: np.ascontiguousarray(np.asarray(a, dtype=np.float32))
    if _NC is None:
        _NC = build()
    nc = _NC
    shared = dict(norm_g=f(norm_g[0]), w_in=f(w_in[0]), fox_q_g=f(fox_q_g[0]), fox_k_g=f(fox_k_g[0]),
                  fox_b_f=f(fox_b_f[0]), rwkv_mu=f(rwkv_mu[0]), rwkv_w0=f(rwkv_w0[0]), rwkv_w_up=f(rwkv_w_up[0]),
                  rwkv_a0=f(rwkv_a0[0]), rwkv_a_up=f(rwkv_a_up[0]), rwkv_k_k=f(rwkv_k_k[0]), rwkv_k_a=f(rwkv_k_a[0]),
                  rwkv_r_k=f(rwkv_r_k[0]), rwkv_gn_w=f(rwkv_gn_w[0]), rwkv_gn_b=f(rwkv_gn_b[0]),
                  mem_norm_g=f(mem_norm_g[0]), w_mem_kv=f(w_mem_kv[0]), mem_q_g=f(mem_q_g[0]), mem_k_g=f(mem_k_g[0]),
                  w_out=f(w_out[0]))
    in_maps = []
    for c in range(8):
        bs = slice(4 * c, 4 * c + 4)
        m = dict(shared)
        m.update(xp=f(x_prompt[c]), xsm=f(x_sample[bs]).reshape(64, D), memp=f(mem_prompt[c]),
                 cfk=f(cache_fox_k[0, bs]).reshape(4, PAST, 384), cfv=f(cache_fox_v[0, bs]).reshape(4, PAST, 384),
                 cfl=f(cache_fox_logf[0, bs]), cmk=f(cache_mem_k[0, bs]).reshape(4, 256, 256),
                 cmv=f(cache_mem_v[0, bs]).reshape(4, 256, 256), srw=f(state_rwkv[0, bs]),
                 ssh=f(state_rwkv_shift[0, bs]).reshape(4, 1600))
        in_maps.append(m)
    res = run_bass_kernel_spmd(nc, in_maps, core_ids=list(range(8)))
    R = res.results
    cat = lambda k: np.stack([np.asarray(R[c][k]) for c in range(8)])
    yp = cat("o_yp")
    ys = cat("o_ys").reshape(32, 16, D)
    fkp = cat("o_fkp").reshape(1, 8, T, 6, 64)
    fvp = cat("o_fvp").reshape(1, 8, T, 6, 64)
    flp = cat("o_flp").reshape(1, 8, T, 6)
    mkp = cat("o_mkp").reshape(1, 8, 256, 4, 64)
    mvp = cat("o_mvp").reshape(1, 8, 256, 4, 64)
    rsp = cat("o_rsp").reshape(1, 8, 6, 64, 64)
    rhp = cat("o_rhp").reshape(1, 8, 1, 1600)
    fks = cat("o_fks").reshape(1, 32, 16, 6, 64)
    fvs = cat("o_fvs").reshape(1, 32, 16, 6, 64)
    fls = cat("o_fls").reshape(1, 32, 16, 6)
    rss = cat("o_rss").reshape(1, 32, 6, 64, 64)
    rhs = cat("o_rhs").reshape(1, 32, 1, 1600)
    return (yp, ys, fkp, fvp, flp, mkp, mvp, rsp, rhp, fks, fvs, fls, rss, rhs)
```

```python
import numpy as np
from contextlib import ExitStack
import concourse.bass as bass
import concourse.mybir as mybir
from concourse.bass_utils import run_bass_kernel_spmd

F32 = mybir.dt.float32
BF16 = mybir.dt.bfloat16
AF = mybir.ActivationFunctionType
ALU = mybir.AluOpType
AX = mybir.AxisListType

D = 1024
T = 4096
NT = T // 128
SB_ = 4
SS = 16
PAST = 1024
NIN = 3654
EPS = 1e-6
GN_EPS = 64e-5
C_Q, C_K, C_V, C_F, C_GF = 0, 384, 768, 1152, 1158
C_RW = 1542
C_RR, C_RK, C_RV, C_WD, C_AD, C_GR = C_RW, C_RW + 384, C_RW + 768, C_RW + 1152, C_RW + 1184, C_RW + 1216
C_MQ, C_GM = 3142, 3398


class Buf:
    __slots__ = ("w", "r", "dsem", "dcnt", "name", "excl")

    def __init__(self, name="", excl=False):
        self.w = None
        self.r = []
        self.dsem = None
        self.dcnt = 0
        self.name = name
        self.excl = excl


class Emit:
    ENG = ("pe", "act", "dve", "pool", "sp")

    def __init__(self, nc, stack):
        self.nc = nc
        self.stack = stack
        self.ops = {e: [] for e in self.ENG}
        self.cnt = {e: 0 for e in self.ENG}
        self.sems = {}
        for e in self.ENG:
            self.sems[e] = stack.enter_context(nc.semaphore("sem_" + e))
        self.known = {e: {} for e in self.ENG}
        self.nd = 0
        self.dbufs = []

    def _waits(self, eng, reads, writes):
        need = {}
        for b in reads:
            if b.w is not None:
                k, v = b.w
                if need.get(k, 0) < v:
                    need[k] = v
            if b.excl:
                for k, v in b.r:
                    if k != eng and need.get(k, 0) < v:
                        need[k] = v
        for b in writes:
            if b.w is not None:
                k, v = b.w
                if need.get(k, 0) < v:
                    need[k] = v
            for k, v in b.r:
                if need.get(k, 0) < v:
                    need[k] = v
        out = []
        kn = self.known[eng]
        for k, v in need.items():
            if kn.get(k, 0) < v:
                kn[k] = v
                out.append((self.sems[k], v))
        return out

    def _mark(self, ev, reads, writes):
        for b in reads:
            b.r = [x for x in b.r if x[0] != ev[0]]
            b.r.append(ev)
        for b in writes:
            b.w = ev
            b.r = []

    def op(self, eng, fn, reads=(), writes=(), chain=False):
        prev_known_pe = self.known["pe"].get("pe", 0) if eng == "pe" else None
        wl = self._waits(eng, reads, writes)
        if chain and eng == "pe":
            sem_pe = self.sems["pe"]
            keep = []
            for s_, v_ in wl:
                if s_ is sem_pe and v_ == self.cnt["pe"]:
                    self.known["pe"]["pe"] = prev_known_pe
                    continue
                keep.append((s_, v_))
            wl = keep
        self.cnt[eng] += 1
        ev = (eng, self.cnt[eng])
        sem = self.sems[eng]

        def run(e, fn=fn, wl=wl, sem=sem):
            for s, v in wl:
                e.wait_ge(s, v)
            fn(e).then_inc(sem, 1)
        self.ops[eng].append(run)
        self._mark(ev, reads, writes)
        return ev

    def dma(self, eng, out, in_, reads=(), writes=(), dbuf=None, **kw):
        if dbuf.dsem is None:
            dbuf.dsem = {}
            dbuf.dcnt = {}
            self.dbufs.append(dbuf)
        if eng not in dbuf.dsem:
            self.nd += 1
            key = "d%d" % self.nd
            self.sems[key] = self.stack.enter_context(self.nc.semaphore("sem_" + key))
            dbuf.dsem[eng] = key
            dbuf.dcnt[eng] = 0
        wl = self._waits(eng, reads, writes)
        dbuf.dcnt[eng] += 16
        key = dbuf.dsem[eng]
        ev = (key, dbuf.dcnt[eng])
        sem = self.sems[key]

        def run(e, wl=wl, sem=sem, out=out, in_=in_, kw=kw):
            for s, v in wl:
                e.wait_ge(s, v)
            e.dma_start(out=out, in_=in_, **kw).then_inc(sem, 16)
        self.ops[eng].append(run)
        self._mark(ev, reads, writes)
        return ev

    def final_wait(self, eng):
        wl = []
        kn = self.known[eng]
        for b in self.dbufs:
            for q, key in b.dsem.items():
                v = b.dcnt[q]
                if kn.get(key, 0) < v:
                    kn[key] = v
                    wl.append((self.sems[key], v))

        def run(e, wl=wl):
            for s, v in wl:
                e.wait_ge(s, v)
        self.ops[eng].append(run)

    def replay(self):
        nc = self.nc
        ops = self.ops
        with nc.Block() as block:
            @block.tensor
            def _(e):
                for f in ops["pe"]:
                    f(e)

            @block.scalar
            def _(e):
                for f in ops["act"]:
                    f(e)

            @block.vector
            def _(e):
                for f in ops["dve"]:
                    f(e)

            @block.gpsimd
            def _(e):
                for f in ops["pool"]:
                    f(e)

            @block.sync
            def _(e):
                for f in ops["sp"]:
                    f(e)


class TT:
    def __init__(self, ap, name=""):
        self.t = ap
        self.b = Buf(name)

    def __getitem__(self, k):
        return self.t[k]


class Prog:
    def __init__(self):
        self.nc = bass.Bass("TRN2", target_bir_lowering=False)
        self.st = ExitStack()
        self.em = Emit(self.nc, self.st)
        self.rr = 0

    def dram(self, name, shape, kind):
        return self.nc.dram_tensor(name, list(shape), F32, kind=kind).ap()

    def sb(self, name, shape, dt=F32):
        return TT(self.st.enter_context(self.nc.sbuf_tensor(name, list(shape), dt)), name)

    def ps(self, name, shape, dt=F32):
        t = TT(self.st.enter_context(self.nc.psum_tensor(name, list(shape), dt)), name)
        t.b.excl = True
        return t

    def act(self, out, in_, func, r, w, **kw):
        return self.em.op("act", lambda e: e.activation(out=out, in_=in_, func=func, **kw), r, w)

    def tt(self, eng, out, in0, in1, op, r, w):
        return self.em.op(eng, lambda e: e.tensor_tensor(out=out, in0=in0, in1=in1, op=op), r, w)

    def ts(self, eng, out, in0, s1, s2, op0, op1, r, w):
        if s2 is None:
            return self.em.op(eng, lambda e: e.tensor_scalar(out=out, in0=in0, scalar1=s1, scalar2=None, op0=op0), r, w)
        return self.em.op(eng, lambda e: e.tensor_scalar(out=out, in0=in0, scalar1=s1, scalar2=s2, op0=op0, op1=op1), r, w)

    def stt(self, out, in0, scalar, in1, op0, op1, r, w):
        return self.em.op("dve", lambda e: e.scalar_tensor_tensor(out=out, in0=in0, scalar=scalar, in1=in1, op0=op0, op1=op1), r, w)

    def cp(self, eng, out, in_, r, w):
        if eng == "act":
            return self.em.op("act", lambda e: e.activation(out=out, in_=in_, func=AF.Copy), r, w)
        return self.em.op(eng, lambda e: e.tensor_copy(out=out, in_=in_), r, w)

    def mm(self, out, lhsT, rhs, start, stop, r, w, chain=False):
        return self.em.op("pe", lambda e: e.matmul(out, lhsT=lhsT, rhs=rhs, start=start, stop=stop), r, w, chain=chain)

    def tr(self, out, in_, ident, r, w):
        return self.em.op("pe", lambda e: e.transpose(out, in_, ident), r, w)

    def memset(self, eng, ap, val, w):
        return self.em.op(eng, lambda e: e.memset(ap, val), (), w)

    def asel(self, out, in_, pattern, op, fill, base, cm, r, w):
        return self.em.op("pool", lambda e: e.affine_select(out=out, in_=in_, pattern=pattern, compare_op=op,
                                                            fill=fill, base=base, channel_multiplier=cm), r, w)

    def load(self, out_tt, out_ap, in_ap, **kw):
        return self.em.dma("sp", out_ap, in_ap, reads=(), writes=[out_tt.b], dbuf=out_tt.b, **kw)

    def store(self, out_ap, in_tt, in_ap, **kw):
        return self.em.dma("pool", out_ap, in_ap, reads=[in_tt.b], writes=(), dbuf=in_tt.b, **kw)

    def rot(self):
        self.rr += 1
        return ("act", "dve", "pool")[self.rr % 3]


def build():
    P = Prog()
    nc = P.nc
    IN, OUT = "ExternalInput", "ExternalOutput"
    NQ = 256
    xp = P.dram("xp", [T, D], IN)
    xsm = P.dram("xsm", [SB_ * SS, D], IN)
    memp = P.dram("memp", [256, D], IN)
    cfk = P.dram("cfk", [SB_, PAST, 384], IN)
    cfv = P.dram("cfv", [SB_, PAST, 384], IN)
    cfl = P.dram("cfl", [SB_, PAST, 6], IN)
    cmk = P.dram("cmk", [SB_, 256, 256], IN)
    cmv = P.dram("cmv", [SB_, 256, 256], IN)
    srw = P.dram("srw", [SB_, 6, 64, 64], IN)
    ssh = P.dram("ssh", [SB_, 1600], IN)
    norm_g = P.dram("norm_g", [D], IN)
    w_in = P.dram("w_in", [D, NIN], IN)
    fox_q_g = P.dram("fox_q_g", [64], IN)
    fox_k_g = P.dram("fox_k_g", [64], IN)
    fox_b_f = P.dram("fox_b_f", [6], IN)
    rwkv_mu = P.dram("rwkv_mu", [1600], IN)
    rwkv_w0 = P.dram("rwkv_w0", [384], IN)
    rwkv_w_up = P.dram("rwkv_w_up", [32, 384], IN)
    rwkv_a0 = P.dram("rwkv_a0", [384], IN)
    rwkv_a_up = P.dram("rwkv_a_up", [32, 384], IN)
    rwkv_k_k = P.dram("rwkv_k_k", [384], IN)
    rwkv_k_a = P.dram("rwkv_k_a", [384], IN)
    rwkv_r_k = P.dram("rwkv_r_k", [384], IN)
    rwkv_gn_w = P.dram("rwkv_gn_w", [384], IN)
    rwkv_gn_b = P.dram("rwkv_gn_b", [384], IN)
    mem_norm_g = P.dram("mem_norm_g", [D], IN)
    w_mem_kv = P.dram("w_mem_kv", [D, 512], IN)
    mem_q_g = P.dram("mem_q_g", [64], IN)
    mem_k_g = P.dram("mem_k_g", [64], IN)
    w_out = P.dram("w_out", [D, D], IN)

    o_yp = P.dram("o_yp", [T, D], OUT)
    o_ys = P.dram("o_ys", [SB_ * SS, D], OUT)
    o_fkp = P.dram("o_fkp", [T, 384], OUT)
    o_fvp = P.dram("o_fvp", [T, 384], OUT)
    o_flp = P.dram("o_flp", [T, 6], OUT)
    o_mkp = P.dram("o_mkp", [256, 256], OUT)
    o_mvp = P.dram("o_mvp", [256, 256], OUT)
    o_rsp = P.dram("o_rsp", [6, 64, 64], OUT)
    o_rhp = P.dram("o_rhp", [1600], OUT)
    o_fks = P.dram("o_fks", [SB_ * SS, 384], OUT)
    o_fvs = P.dram("o_fvs", [SB_ * SS, 384], OUT)
    o_fls = P.dram("o_fls", [SB_ * SS, 6], OUT)
    o_rss = P.dram("o_rss", [SB_, 6, 64, 64], OUT)
    o_rhs = P.dram("o_rhs", [SB_, 1600], OUT)

    Wb = P.sb("Wb", [128, 8, NIN], BF16)
    wst = [P.sb("wst%d" % i, [128, D], BF16) for i in range(2)]
    wo_bf = nc.dram_tensor("wo_bf", [128, 8, D], BF16, kind="Internal").ap()
    wo_b = Buf("wo_bf")
    kT = P.sb("kT", [67, 6, T], BF16)
    Vaug = P.sb("Vaug", [128, NT, 6, 65], BF16)
    negc = P.sb("negc", [128, NT, 6])
    kvb = [Buf("kv%d" % i) for i in range(NT)]
    xt = [P.sb("xt%d" % i, [128, D]) for i in range(2)]
    xb = P.sb("xb", [128, D], BF16)
    xnT = P.sb("xnT", [128, 8, 128], BF16)
    identb = P.sb("identb", [128, 128], BF16)
    identf = P.sb("identf", [128, 128])
    trif = P.sb("trif", [128, 128])
    lastf = P.sb("lastf", [128, 128])
    bones = P.sb("bones", [128, 128])
    bavg = P.sb("bavg", [128, 128])
    ones = P.sb("ones", [128, 128])
    mask4 = P.sb("mask4", [128, 4, 128], BF16)
    msl = P.sb("msl", [128, 128], BF16)
    ng = P.sb("ng", [128, 8])
    mng = P.sb("mng", [128, 8])
    gq = P.sb("gq", [128, 64])
    gk = P.sb("gk", [128, 64])
    gmq = P.sb("gmq", [128, 64])
    gmk = P.sb("gmk", [128, 64])
    bfb = P.sb("bfb", [128, 6])
    small = P.sb("small", [128, 64])
    tmpA = P.sb("tmpA", [128, 384])
    tmpB = P.sb("tmpB", [128, 384])
    tmpC = P.sb("tmpC", [128, 384])
    qaug = P.sb("qaug", [128, 6, 67], BF16)
    kaug = P.sb("kaug", [128, 6, 67], BF16)
    mqa = P.sb("mqa", [128, 4, 64], BF16)
    cc = [P.sb("cc%d" % i, [128, 6]) for i in range(2)]
    cr = P.sb("cr", [128, 6])
    lf = P.sb("lf", [128, 6])
    raw = P.sb("raw", [128, 13, 129])
    xs = P.sb("xs", [128, 4, 128])
    xw = P.sb("xw", [64, 128])
    mu = P.sb("mu", [128, 13])
    rp = P.sb("rp", [128, 3, 8])
    lora = P.sb("lora", [64, 384], BF16)
    qT = P.sb("qT", [67, 6, NQ], BF16)
    mqT = P.sb("mqT", [64, 4, NQ], BF16)
    mkT = P.sb("mkT", [64, 4, 256], BF16)
    mvaug = P.sb("mvaug", [128, 2, 4, 65], BF16)
    gt = P.sb("gt", [128, 8, NQ], BF16)
    og = P.sb("og", [128, 8, NQ], BF16)
    gtb = [Buf("gt%d" % i) for i in range(8)]
    ogb = [Buf("og%d" % i) for i in range(8)]
    pts = [P.sb("pt%d" % i, [128, NQ], BF16) for i in range(3)]
    oun = P.sb("oun", [65, NQ])
    ounB = P.sb("ounB", [65, NQ])
    opair = P.sb("opair", [128, NQ])
    W3 = lambda name, dt=F32: P.sb(name, [128, 1, 128], dt)
    r_lw, r_a, r_g, r_eg, r_egm, r_eng = W3("r_lw"), W3("r_a"), W3("r_g"), W3("r_eg"), W3("r_egm"), W3("r_eng")
    r_kk, r_t1, r_t2 = W3("r_kk"), W3("r_t1"), W3("r_t2")
    r_yT2 = [W3("r_yT0"), W3("r_yT1")]
    r_bon2 = [W3("r_bon0"), W3("r_bon1")]
    rpc = [0]
    pend_gn = [None]

    def gn_step(k=1):
        for _ in range(k):
            if pend_gn[0]:
                pend_gn[0].pop(0)()

    def flush_gn():
        while pend_gn[0]:
            pend_gn[0].pop(0)()
    ART = P.sb("ART", [128, 1, 2, 128], BF16)
    BTt = W3("BTt", BF16)
    KTt = W3("KTt", BF16)
    vbt = W3("vbt", BF16)
    twd = P.sb("twd", [64, 128], BF16)
    tokA = P.sb("tokA", [128, 128], BF16)
    tokB = P.sb("tokB", [128, 128], BF16)
    tokK = P.sb("tokK", [128, 128], BF16)
    tokV = P.sb("tokV", [128, 128], BF16)
    gm = [P.sb("gm%d" % h, [128, 4, 128], BF16) for h in range(2)]
    PP = [[P.sb("PP%d_%d" % (h, i), [128, 2, 128], BF16) for i in range(2)] for h in range(2)]
    XX = [[P.sb("XX%d_%d" % (h, i), [128, 128], BF16) for i in range(2)] for h in range(2)]
    WT = P.sb("WT", [128, 1, 128], BF16)
    LVs = P.sb("LVs", [128, 2, 64], BF16)
    U0 = P.sb("U0", [128, 2, 64])
    Ub = P.sb("Ub", [128, 2, 64], BF16)
    Hf = P.sb("Hf", [128, 3, 64])
    Hb = P.sb("Hb", [128, 3, 64], BF16)
    Dp = P.sb("Dp", [128, 1, 64])
    stS = P.sb("stS", [64, 6, 64])

    ps_tr = P.ps("ps_tr", [128, 1024], BF16)
    ps_sA = P.ps("ps_sA", [128, 512])
    ps_sB = P.ps("ps_sB", [128, 512])
    ps_o = P.ps("ps_o", [128, 512])
    pgs = [P.ps("pg%d" % i, [128, 512]) for i in range(4)]
    pgi = [0]

    def pg():
        pgi[0] += 1
        return pgs[pgi[0] % len(pgs)]

    class View:
        def __init__(self, ap):
            self.t = ap
            self.b = Buf(excl=True)

        def __getitem__(self, k):
            return self.t[k]
    ps_s2 = [ps_sA, ps_sB]
    ps_o2 = [View(ps_o[:, 0:256]), View(ps_o[:, 256:512])]
    ps_o2[1].b = ps_o2[0].b

    P.memset("pool", identb[:], 0.0, [identb.b])
    P.asel(identb[:], identb[:], [[-1, 128]], ALU.not_equal, 1.0, 0, 1, [identb.b], [identb.b])
    P.memset("pool", identf[:], 0.0, [identf.b])
    P.asel(identf[:], identf[:], [[-1, 128]], ALU.not_equal, 1.0, 0, 1, [identf.b], [identf.b])
    P.memset("pool", trif[:], 1.0, [trif.b])
    P.asel(trif[:], trif[:], [[1, 128]], ALU.is_ge, 0.0, 0, -1, [trif.b], [trif.b])
    P.memset("pool", lastf[:], 1.0, [lastf.b])
    P.asel(lastf[:], lastf[:], [[0, 128]], ALU.is_ge, 0.0, -127, 1, [lastf.b], [lastf.b])
    P.memset("dve", bones[:], 0.0, [bones.b])
    P.memset("dve", bones[0:64, 0:64], 1.0, [bones.b])
    P.memset("dve", bones[64:128, 64:128], 1.0, [bones.b])
    P.ts("dve", bavg[:], bones[:], 1.0 / 64, None, ALU.mult, None, [bones.b], [bavg.b])
    P.memset("dve", ones[:], 1.0, [ones.b])
    P.memset("pool", mask4[:], 1.0, [mask4.b])
    for i in range(4):
        P.asel(mask4[:, i, :], mask4[:, i, :], [[1, 128]], ALU.is_ge, 0.0, (-1 if i % 2 == 0 else 0), -1, [mask4.b], [mask4.b])
    P.memset("pool", msl[:], 1.0, [msl.b])
    P.asel(msl[:], msl[:], [[-1, 128]], ALU.is_ge, 0.0, -1, 1, [msl.b], [msl.b])

    def cload(out_ap, in_ap, tt_, **kw):
        P.em.dma("sp", out_ap, in_ap, reads=(), writes=[tt_.b], dbuf=tt_.b, **kw)

    cload(ng[:], norm_g.rearrange("(c p) -> p c", p=128), ng, allow_slow_non_contiguous=True)
    cload(mng[:], mem_norm_g.rearrange("(c p) -> p c", p=128), mng, allow_slow_non_contiguous=True)
    for tl, src in ((gq, fox_q_g), (gk, fox_k_g), (gmq, mem_q_g), (gmk, mem_k_g)):
        cload(tl[:], src.partition_broadcast(128), tl)
    cload(bfb[:], fox_b_f.partition_broadcast(128), bfb)
    P.ts("dve", gq[:], gq[:], 0.125, None, ALU.mult, None, [gq.b], [gq.b])
    P.ts("dve", gmq[:], gmq[:], 0.125, None, ALU.mult, None, [gmq.b], [gmq.b])
    cload(mu[:, 0:9], rwkv_mu[0:1152].rearrange("(b p) -> p b", p=128), mu, allow_slow_non_contiguous=True)
    cload(mu[0:64, 9:10], rwkv_mu[1152:1216].rearrange("(b p) -> p b", p=64), mu, allow_slow_non_contiguous=True)
    cload(mu[:, 10:13], rwkv_mu[1216:1600].rearrange("(b p) -> p b", p=128), mu, allow_slow_non_contiguous=True)
    for i, src in enumerate((rwkv_w0, rwkv_a0, rwkv_k_k, rwkv_k_a, rwkv_k_a, rwkv_r_k, rwkv_gn_w, rwkv_gn_b)):
        cload(rp[:, :, i], src.rearrange("(b p) -> p b", p=128), rp, allow_slow_non_contiguous=True)
    P.ts("dve", rp[:, :, 4], rp[:, :, 4], -1.0, 1.0, ALU.mult, ALU.add, [rp.b], [rp.b])
    cload(tmpA[0:32, :], rwkv_w_up, tmpA)
    cload(tmpA[32:64, :], rwkv_a_up, tmpA)
    P.cp("dve", lora[:], tmpA[0:64, :], [tmpA.b], [lora.b])
    P.memset("dve", kaug[:, :, 64:67], 1.0, [kaug.b])
    P.memset("dve", raw[:], 0.0, [raw.b])
    P.memset("pool", mvaug[:, :, :, :].rearrange("p a h e -> p (a h) e")[:, :, 64:65], 1.0, [mvaug.b])

    def norm_T(src_dram, n, gtile, xtile):
        P.load(xtile, xtile[0:n, :], src_dram)
        P.em.op("act", lambda e: e.activation(out=xb[0:n, :], in_=xtile[0:n, :], func=AF.Square,
                                              accum_out=small[0:n, 0:1]), [xtile.b], [xb.b, small.b])
        P.act(small[0:n, 1:2], small[0:n, 0:1], AF.Ln, [small.b], [small.b], scale=1.0 / D, bias=EPS)
        P.act(small[0:n, 2:3], small[0:n, 1:2], AF.Exp, [small.b], [small.b], scale=-0.5)
        P.act(xb[0:n, :], xtile[0:n, :], AF.Copy, [xtile.b, small.b], [xb.b], scale=small[0:n, 2:3])
        for c in range(8):
            P.tr(ps_tr[:, c * 128:c * 128 + n], xb[0:n, c * 128:(c + 1) * 128], identb[0:n, 0:n], [xb.b, identb.b], [ps_tr.b])
        P.tt("dve", xnT[:, :, 0:n], ps_tr[:, :].rearrange("p (c t) -> p c t", t=128)[:, :, 0:n],
             gtile[:, :].unsqueeze(2).to_broadcast([128, 8, n]), ALU.mult, [ps_tr.b, gtile.b], [xnT.b])

    def proj_tm(n, c0, c1, pst, W=None):
        W = W or Wb
        for c in range(8):
            P.mm(pst[0:n, 0:c1 - c0], xnT[:, c, 0:n], W[:, c, c0:c1], c == 0, c == 7, [xnT.b, W.b], [pst.b], chain=(c > 0))

    def headnorm(n, pst, nh, gain, dst, out_bf=None, out_bf_b=None):
        w = nh * 64
        v3 = lambda ap: ap.rearrange("p (h d) -> p h d", d=64)
        P.act(tmpA[0:n, 0:w], pst[0:n, 0:w], AF.Square, [pst.b], [tmpA.b])
        P.em.op("dve", lambda e: e.tensor_reduce(out=small[0:n, 8:8 + nh], in_=v3(tmpA[0:n, 0:w]),
                                                 axis=AX.X, op=ALU.add), [tmpA.b], [small.b])
        P.act(small[0:n, 16:16 + nh], small[0:n, 8:8 + nh], AF.Ln, [small.b], [small.b], scale=1.0 / 64, bias=EPS)
        P.act(small[0:n, 24:24 + nh], small[0:n, 16:16 + nh], AF.Exp, [small.b], [small.b], scale=-0.5)
        P.tt("dve", v3(tmpA[0:n, 0:w]), v3(pst[0:n, 0:w]),
             small[0:n, 24:24 + nh].unsqueeze(2).to_broadcast([n, nh, 64]), ALU.mult, [pst.b, small.b], [tmpA.b])
        P.tt("dve", v3(dst[0:n, 0:w]), v3(tmpA[0:n, 0:w]),
             gain[0:n, :].unsqueeze(1).to_broadcast([n, nh, 64]), ALU.mult, [tmpA.b, gain.b], [dst.b])
        if out_bf is not None:
            P.cp("act", out_bf, v3(dst[0:n, 0:w]), [dst.b], [out_bf_b])

    def c_update(n, j, cprev, ccur):
        pst = pg()
        P.mm(pst[0:n, 0:6], trif[0:n, 0:n], lf[0:n, :], True, cprev is None, [trif.b, lf.b], [pst.b])
        if cprev is not None:
            P.mm(pst[0:n, 0:6], lastf[:, 0:n], cprev[:, :], False, True, [lastf.b, cprev.b], [pst.b], chain=True)
        P.cp("act", ccur[0:n, :], pst[0:n, 0:6], [pst.b], [ccur.b])
        P.ts("dve", negc[0:n, j, :], ccur[0:n, :], -1.0, None, ALU.mult, None, [ccur.b], [kvb[j]])

    def q_cpieces(n, ccur):
        P.cp("dve", qaug[0:n, :, 64], ccur[0:n, :], [ccur.b], [qaug.b])
        P.tt("dve", cr[0:n, :], ccur[0:n, :], qaug[0:n, :, 64], ALU.subtract, [ccur.b, qaug.b], [cr.b])
        P.cp("dve", qaug[0:n, :, 65], cr[0:n, :], [cr.b], [qaug.b])
        P.tt("dve", cr[0:n, :], cr[0:n, :], qaug[0:n, :, 65], ALU.subtract, [cr.b, qaug.b], [cr.b])
        P.cp("dve", qaug[0:n, :, 66], cr[0:n, :], [cr.b], [qaug.b])

    def k_to_T(n, j):
        for h in range(6):
            P.tr(ps_tr[0:67, h * 128:h * 128 + n], kaug[0:n, h, :], identb[0:n, 0:n], [kaug.b, identb.b], [ps_tr.b])
        P.cp("act", kT[:, :, j * 128:j * 128 + n], ps_tr[0:67, 0:768].rearrange("p (h t) -> p h t", t=128)[:, :, 0:n],
             [ps_tr.b], [kvb[j]])

    def q_to_T(n, qoff):
        for h in range(6):
            P.tr(ps_tr[0:67, h * 128:h * 128 + n], qaug[0:n, h, :], identb[0:n, 0:n], [qaug.b, identb.b], [ps_tr.b])
        P.cp("act", qT[:, :, qoff:qoff + n], ps_tr[0:67, 0:768].rearrange("p (h t) -> p h t", t=128)[:, :, 0:n],
             [ps_tr.b], [qT.b])

    def token_tile(src, n, j, qoff, o_k, o_v, o_l, cprev, ccur):
        xtile = xt[0]
        norm_T(src, n, ng, xtile)
        p0 = pg()
        proj_tm(n, C_Q, C_Q + 384, p0)
        headnorm(n, p0, 6, gq, tmpB, out_bf=qaug[0:n, :, 0:64], out_bf_b=qaug.b)
        p1 = pg()
        proj_tm(n, C_K, C_K + 384, p1)
        headnorm(n, p1, 6, gk, tmpB, out_bf=kaug[0:n, :, 0:64], out_bf_b=kaug.b)
        P.store(o_k, tmpB, tmpB[0:n, 0:384])
        p2 = pg()
        proj_tm(n, C_V, C_V + 390, p2)
        P.cp("act", tmpC[0:n, 0:384], p2[0:n, 0:384], [p2.b], [tmpC.b])
        P.store(o_v, tmpC, tmpC[0:n, 0:384])
        P.cp("dve", Vaug[0:n, j, :, 0:64], tmpC[0:n, 0:384].rearrange("p (h d) -> p h d", d=64), [tmpC.b], [kvb[j]])
        P.tt("dve", lf[0:n, :], p2[0:n, 384:390], bfb[0:n, :], ALU.add, [p2.b, bfb.b], [lf.b])
        P.act(lf[0:n, :], lf[0:n, :], AF.Exp, [lf.b], [lf.b], scale=-1.0)
        P.act(lf[0:n, :], lf[0:n, :], AF.Ln, [lf.b], [lf.b], bias=1.0)
        P.ts("dve", lf[0:n, :], lf[0:n, :], -1.0, None, ALU.mult, None, [lf.b], [lf.b])
        P.store(o_l, lf, lf[0:n, :])
        c_update(n, j, cprev, ccur)
        q_cpieces(n, ccur)
        k_to_T(n, j)
        q_to_T(n, qoff)
        p3 = pg()
        proj_tm(n, C_MQ, C_MQ + 256, p3)
        headnorm(n, p3, 4, gmq, tmpB, out_bf=mqa[0:n, :, :], out_bf_b=mqa.b)
        for h in range(4):
            P.tr(ps_tr[0:64, h * 128:h * 128 + n], mqa[0:n, h, :], identb[0:n, 0:n], [mqa.b, identb.b], [ps_tr.b])
        P.cp("act", mqT[:, :, qoff:qoff + n], ps_tr[0:64, 0:512].rearrange("p (h t) -> p h t", t=128)[:, :, 0:n],
             [ps_tr.b], [mqT.b])

    def fm_proj(n, qoff):
        gblocks = [(C_GF + 128 * i, i) for i in range(3)] + [(C_GM + 128 * i, 6 + i) for i in range(2)]
        for g0 in (0, 4):
            pst = pg()
            lst_ = gblocks[g0:g0 + 4]
            for jj, (c0, ch) in enumerate(lst_):
                for c in range(8):
                    P.mm(pst[:, jj * 128:jj * 128 + n], Wb[:, c, c0:c0 + 128], xnT[:, c, 0:n], c == 0, c == 7, [xnT.b, Wb.b], [pst.b], chain=(c > 0))
            for jj, (c0, ch) in enumerate(lst_):
                P.act(gt[:, ch, qoff:qoff + n], pst[:, jj * 128:jj * 128 + n], AF.Silu, [pst.b], [gtb[ch]])
        blocks = [(C_RW + 128 * i, 128) for i in range(9)] + [(C_WD, 64)] + [(C_GR + 128 * i, 128) for i in range(3)]
        for g0 in range(0, 13, 4):
            pst = pg()
            nb = min(4, 13 - g0)
            for jj in range(nb):
                c0, m = blocks[g0 + jj]
                for c in range(8):
                    P.mm(pst[0:m, jj * 128:jj * 128 + n], Wb[:, c, c0:c0 + m], xnT[:, c, 0:n], c == 0, c == 7, [xnT.b, Wb.b], [pst.b], chain=(c > 0))
            P.cp("act" if (g0 // 4) % 2 == 0 else "dve", raw[:, g0:g0 + nb, 1:1 + n], pst[:, 0:nb * 128].rearrange("p (b t) -> p b t", t=128)[:, :, 0:n], [pst.b], [raw.b])

    def store_shift(dst, n):
        P.store(dst[0:1152].rearrange("(b p) -> p b", p=128), raw, raw[:, 0:9, n], allow_slow_non_contiguous=True)
        P.store(dst[1152:1216].rearrange("(b p) -> p b", p=64), raw, raw[0:64, 9:10, n], allow_slow_non_contiguous=True)
        P.store(dst[1216:1600].rearrange("(b p) -> p b", p=128), raw, raw[:, 10:13, n], allow_slow_non_contiguous=True)

    def rwkv_pre(n, qoff):
        cur = lambda blk, p0=0, p1=128: raw[p0:p1, blk, 1:1 + n]
        prv = lambda blk, p0=0, p1=128: raw[p0:p1, blk, 0:n]
        P.tt("dve", xw[:, 0:n], prv(9, 0, 64), cur(9, 0, 64), ALU.subtract, [raw.b], [xw.b])
        P.stt(xw[:, 0:n], xw[:, 0:n], mu[0:64, 9:10], cur(9, 0, 64), ALU.mult, ALU.add, [xw.b, mu.b, raw.b], [xw.b])
        P.act(twd[0:32, 0:n], xw[0:32, 0:n], AF.Tanh, [xw.b], [twd.b])
        P.cp("dve", twd[32:64, 0:n], xw[32:64, 0:n], [xw.b], [twd.b])
        for p in range(3):
            blk = 10 + p
            P.tt("dve", xs[:, 3, 0:n], prv(blk), cur(blk), ALU.subtract, [raw.b], [xs.b])
            P.stt(xs[:, 3, 0:n], xs[:, 3, 0:n], mu[:, blk:blk + 1], cur(blk), ALU.mult, ALU.add, [xs.b, mu.b, raw.b], [xs.b])
            P.act(gt[:, 3 + p, qoff:qoff + n], xs[:, 3, 0:n], AF.Silu, [xs.b], [gtb[3 + p]])
        nlev = 0
        while (1 << nlev) < n:
            nlev += 1
        nlev -= 1
        return nlev

    def rwkv_pair(n, qoff, p, nlev):
        par = rpc[0] % 2
        rpc[0] += 1
        yT_, bon_ = r_yT2[par], r_bon2[par]
        cur = lambda blk, p0=0, p1=128: raw[p0:p1, blk, 1:1 + n]
        prv = lambda blk, p0=0, p1=128: raw[p0:p1, blk, 0:n]
        if True:
            S3 = lambda tl: tl[:, 0, 0:n]
            bc = lambda i: rp[:, p, i:i + 1]
            for i, blk in enumerate((p, 3 + p, 6 + p)):
                P.tt("dve", xs[:, i, 0:n], prv(blk), cur(blk), ALU.subtract, [raw.b], [xs.b])
                P.stt(xs[:, i, 0:n], xs[:, i, 0:n], mu[:, blk:blk + 1], cur(blk), ALU.mult, ALU.add, [xs.b, mu.b, raw.b], [xs.b])
            xr, xk, xv = xs[:, 0, 0:n], xs[:, 1, 0:n], xs[:, 2, 0:n]
            pw = pg()
            P.mm(pw[:, 0:n], lora[0:32, p * 128:(p + 1) * 128], twd[0:32, 0:n], True, True, [lora.b, twd.b], [pw.b])
            P.mm(pw[:, 128:128 + n], lora[32:64, p * 128:(p + 1) * 128], twd[32:64, 0:n], True, True, [lora.b, twd.b], [pw.b])
            P.act(S3(r_lw), pw[:, 0:n], AF.Sigmoid, [pw.b, rp.b], [r_lw.b], bias=bc(0))
            P.ts("dve", S3(r_lw), S3(r_lw), -0.6065306597126334, None, ALU.mult, None, [r_lw.b], [r_lw.b])
            P.act(S3(r_a), pw[:, 128:128 + n], AF.Sigmoid, [pw.b, rp.b], [r_a.b], bias=bc(1))
            P.em.op("dve", lambda e: e.tensor_tensor_scan(out=r_g[:, 0, 0:n], data0=ones[:, 0:n], data1=r_lw[:, 0, 0:n],
                                                          initial=0.0, op0=ALU.mult, op1=ALU.add),
                    [ones.b, r_lw.b], [r_g.b])
            P.act(S3(r_eg), S3(r_g), AF.Exp, [r_g.b], [r_eg.b])
            P.act(S3(r_eng), S3(r_g), AF.Exp, [r_g.b], [r_eng.b], scale=-1.0)
            P.tt("dve", S3(r_egm), S3(r_g), S3(r_lw), ALU.subtract, [r_g.b, r_lw.b], [r_egm.b])
            P.act(S3(r_egm), S3(r_egm), AF.Exp, [r_egm.b], [r_egm.b])
            P.ts("dve", S3(r_kk), xk, bc(2), None, ALU.mult, None, [xs.b, rp.b], [r_kk.b])
            P.tt("dve", S3(r_t1), S3(r_kk), S3(r_kk), ALU.mult, [r_kk.b], [r_t1.b])
            pss = pg()
            P.mm(pss[:, 0:n], bones[:, :], r_t1[:, 0, 0:n], True, True, [bones.b, r_t1.b], [pss.b])
            P.ts("dve", S3(r_t1), pss[:, 0:n], 1e-24, None, ALU.max, None, [pss.b], [r_t1.b])
            P.act(S3(r_t1), S3(r_t1), AF.Ln, [r_t1.b], [r_t1.b], scale=float(2 ** 40))
            P.act(S3(r_t1), S3(r_t1), AF.Exp, [r_t1.b], [r_t1.b], scale=-0.5, bias=13.862943611198906)
            P.tt("dve", S3(r_kk), S3(r_kk), S3(r_t1), ALU.mult, [r_kk.b, r_t1.b], [r_kk.b])
            P.ts("pool", S3(r_t2), S3(r_a), bc(3), bc(4), ALU.mult, ALU.add, [r_a.b, rp.b], [r_t2.b])
            P.tt("pool", S3(r_t2), S3(r_t2), xk, ALU.mult, [r_t2.b, xs.b], [r_t2.b])
            P.stt(ART[:, 0, 0, 0:n], S3(r_kk), -1.0, S3(r_egm), ALU.mult, ALU.mult, [r_kk.b, r_egm.b], [ART.b])
            P.tt("pool", ART[:, 0, 1, 0:n], xr, S3(r_eg), ALU.mult, [xs.b, r_eg.b], [ART.b])
            P.tt("dve", S3(r_t1), S3(r_a), S3(r_kk), ALU.mult, [r_a.b, r_kk.b], [r_t1.b])
            P.tt("dve", S3(BTt), S3(r_t1), S3(r_eng), ALU.mult, [r_t1.b, r_eng.b], [BTt.b])
            P.tt("pool", S3(KTt), S3(r_t2), S3(r_eng), ALU.mult, [r_t2.b, r_eng.b], [KTt.b])
            P.cp("act", S3(vbt), xv, [xs.b], [vbt.b])
            P.tt("dve", S3(r_t1), xr, S3(r_t2), ALU.mult, [xs.b, r_t2.b], [r_t1.b])
            P.ts("dve", S3(r_t1), S3(r_t1), bc(5), None, ALU.mult, None, [r_t1.b, rp.b], [r_t1.b])
            psb = pg()
            P.mm(psb[:, 0:n], bones[:, :], r_t1[:, 0, 0:n], True, True, [bones.b, r_t1.b], [psb.b])
            P.tt("dve", S3(bon_), psb[:, 0:n], xv, ALU.mult, [psb.b, xs.b], [bon_.b])
            for i, (src_ap, sb_) in enumerate(((ART[:, 0, 0, 0:n], ART.b), (BTt[:, 0, 0:n], BTt.b), (KTt[:, 0, 0:n], KTt.b), (vbt[:, 0, 0:n], vbt.b))):
                P.tr(ps_tr[0:n, i * 128:(i + 1) * 128], src_ap, identb[:, :], [sb_, identb.b], [ps_tr.b])
            for i, dstt in enumerate((tokA, tokB, tokK, tokV)):
                P.cp("act" if i % 2 else "dve", dstt[0:n, :], ps_tr[0:n, i * 128:(i + 1) * 128], [ps_tr.b], [dstt.b])
            for hh in range(2):
                hb = hh * 64
                g12 = pg()
                for a_ in range(2):
                    P.mm(g12[0:n, a_ * 128:a_ * 128 + n], BTt[hb:hb + 64, 0, 0:n], ART[hb:hb + 64, 0, a_, 0:n], True, True, [BTt.b, ART.b], [g12.b])
                    P.mm(g12[0:n, 256 + a_ * 128:256 + a_ * 128 + n], KTt[hb:hb + 64, 0, 0:n], ART[hb:hb + 64, 0, a_, 0:n], True, True, [KTt.b, ART.b], [g12.b])
                P.tt("dve", gm[hh][0:n, :, 0:n], g12[0:n, :].rearrange("s (a t) -> s a t", t=128)[:, :, 0:n], mask4[0:n, :, 0:n], ALU.mult,
                     [g12.b, mask4.b], [gm[hh].b])
                g3 = pg()
                P.mm(g3[0:n, 0:n], ART[hb:hb + 64, 0, 0, 0:n], BTt[hb:hb + 64, 0, 0:n], True, True, [ART.b, BTt.b], [g3.b])
                P.tt("dve", PP[hh][0][0:n, 0, 0:n], g3[0:n, 0:n], msl[0:n, 0:n], ALU.mult, [g3.b, msl.b], [PP[hh][0].b])
                P.cp("act", PP[hh][0][0:n, 1, 0:n], gm[hh][0:n, 0, 0:n], [gm[hh].b], [PP[hh][0].b])
                P.tt("pool", XX[hh][0][0:n, 0:n], gm[hh][0:n, 0, 0:n], identb[0:n, 0:n], ALU.add, [gm[hh].b, identb.b], [XX[hh][0].b])
            def emit_sq(j):
                ci, ni = (j - 1) % 2, j % 2
                for hh in range(2):
                    psq = pg()
                    Pc = PP[hh][ci]
                    P.mm(psq[0:n, 0:n], Pc[0:n, 1, 0:n], Pc[0:n, 0, 0:n], True, True, [Pc.b], [psq.b])
                    if j < nlev:
                        P.mm(psq[0:n, 128:128 + n], Pc[0:n, 0, 0:n], Pc[0:n, 1, 0:n], True, True, [Pc.b], [psq.b])
                        P.cp("dve" if hh else "act", PP[hh][ni][0:n, :, 0:n], psq[0:n, 0:256].rearrange("s (a t) -> s a t", t=128)[:, :, 0:n], [psq.b], [PP[hh][ni].b])
                    else:
                        P.cp("dve" if hh else "act", PP[hh][ni][0:n, 0, 0:n], psq[0:n, 0:n], [psq.b], [PP[hh][ni].b])

            def emit_x(j):
                ci, ni = (j - 1) % 2, j % 2
                for hh in range(2):
                    px = pg()
                    P.mm(px[0:n, 0:n], PP[hh][ni][0:n, 0, 0:n], XX[hh][ci][0:n, 0:n], True, True, [PP[hh][ni].b, XX[hh][ci].b], [px.b])
                    P.tt("dve", XX[hh][ni][0:n, 0:n], px[0:n, 0:n], XX[hh][ci][0:n, 0:n], ALU.add, [px.b, XX[hh][ci].b], [XX[hh][ni].b])

            for j in range(1, nlev + 1):
                emit_sq(j)
                if j > 1:
                    emit_x(j - 1)
                gn_step(1 if nlev >= 4 else 2)
            emit_x(nlev)
            fi = nlev % 2
            for hh in range(2):
                hb = hh * 64
                TTm = XX[hh][fi]
                pw_ = pg()
                P.mm(pw_[0:64, 0:n], tokA[0:n, hb:hb + 64], TTm[0:n, 0:n], True, True, [tokA.b, TTm.b], [pw_.b])
                P.cp("act", WT[hb:hb + 64, 0, 0:n], pw_[0:64, 0:n], [pw_.b], [WT.b])
                P.mm(pw_[0:n, 128:192], gm[hh][0:n, 2, 0:n], tokV[0:n, hb:hb + 64], True, True, [gm[hh].b, tokV.b], [pw_.b])
                P.cp("dve", LVs[0:n, hh, :], pw_[0:n, 128:192], [pw_.b], [LVs.b])
                P.mm(pw_[0:n, 256:320], TTm[0:n, 0:n], LVs[0:n, hh, :], True, True, [TTm.b, LVs.b], [pw_.b])
                P.cp("act", U0[0:n, hh, :], pw_[0:n, 256:320], [pw_.b], [U0.b])
            flush_gn()
            pU = pg()
            for hh in range(2):
                hb = hh * 64
                P.mm(pU[0:n, hb:hb + 64], WT[hb:hb + 64, 0, 0:n], Hb[hb:hb + 64, p, :], True, True, [WT.b, Hb.b], [pU.b])
            P.tt("dve", Ub[0:n, :, :], pU[0:n, 0:128].rearrange("t (h v) -> t h v", v=64), U0[0:n, :, :], ALU.add, [pU.b, U0.b], [Ub.b])
            pY = pg()
            for hh in range(2):
                hb = hh * 64
                dst = pY[0:64, hh * 128:hh * 128 + n]
                P.mm(dst, Hb[hb:hb + 64, p, :], ART[hb:hb + 64, 0, 1, 0:n], True, False, [Hb.b, ART.b], [pY.b])
                P.mm(dst, Ub[0:n, hh, :], gm[hh][0:n, 1, 0:n], False, False, [Ub.b, gm[hh].b], [pY.b], chain=True)
                P.mm(dst, tokV[0:n, hb:hb + 64], gm[hh][0:n, 3, 0:n], False, True, [tokV.b, gm[hh].b], [pY.b], chain=True)
            for hh in range(2):
                hb = hh * 64
                P.cp("act" if hh else "dve", yT_[hb:hb + 64, 0, 0:n], pY[0:64, hh * 128:hh * 128 + n], [pY.b], [yT_.b])
            pD = pg()
            for hh in range(2):
                hb = hh * 64
                P.mm(pD[0:64, hb:hb + 64], tokB[0:n, hb:hb + 64], Ub[0:n, hh, :], True, False, [tokB.b, Ub.b], [pD.b])
                P.mm(pD[0:64, hb:hb + 64], tokK[0:n, hb:hb + 64], tokV[0:n, hb:hb + 64], False, True, [tokK.b, tokV.b], [pD.b], chain=True)
            for hh in range(2):
                hb = hh * 64
                P.cp("act" if hh else "dve", Dp[hb:hb + 64, 0, :], pD[0:64, hb:hb + 64], [pD.b], [Dp.b])
            P.tt("dve", Hf[:, p, :], Hf[:, p, :], Dp[:, 0, :], ALU.add, [Hf.b, Dp.b], [Hf.b])
            P.ts("dve", Hf[:, p, :], Hf[:, p, :], r_eg[:, 0, n - 1:n], None, ALU.mult, None, [Hf.b, r_eg.b], [Hf.b])
            P.cp("act", Hb[:, p, :], Hf[:, p, :], [Hf.b], [Hb.b])
            def gn_steps(p=p, n=n, qoff=qoff, yT_=yT_, bon_=bon_):
                y_ = yT_[:, 0, 0:n]
                t_ = tmpA[:, 0:n]

                def sA():
                    pm = pg()
                    P.mm(pm[:, 0:n], bavg[:, :], y_, True, True, [bavg.b, yT_.b], [pm.b])
                    P.tt("dve", y_, y_, pm[:, 0:n], ALU.subtract, [yT_.b, pm.b], [yT_.b])
                    P.tt("dve", t_, y_, y_, ALU.mult, [yT_.b], [tmpA.b])

                def sB():
                    pvv = pg()
                    P.mm(pvv[:, 0:n], bavg[:, :], t_, True, True, [bavg.b, tmpA.b], [pvv.b])
                    P.act(t_, pvv[:, 0:n], AF.Ln, [pvv.b], [tmpA.b], bias=GN_EPS)
                    P.act(t_, t_, AF.Exp, [tmpA.b], [tmpA.b], scale=-0.5)

                def sC():
                    P.tt("dve", y_, y_, t_, ALU.mult, [yT_.b, tmpA.b], [yT_.b])
                    P.ts("dve", y_, y_, rp[:, p, 6:7], rp[:, p, 7:8], ALU.mult, ALU.add, [yT_.b, rp.b], [yT_.b])

                def sD():
                    P.tt("dve", y_, y_, bon_[:, 0, 0:n], ALU.add, [yT_.b, bon_.b], [yT_.b])
                    P.tt("dve", og[:, 3 + p, qoff:qoff + n], y_, gt[:, 3 + p, qoff:qoff + n], ALU.mult, [yT_.b, gtb[3 + p]], [ogb[3 + p]])
                return [sA, sB, sC, sD]
            flush_gn()
            pend_gn[0] = gn_steps()

    def rwkv_chunk(n, qoff):
        nlev = rwkv_pre(n, qoff)
        for p in range(3):
            rwkv_pair(n, qoff, p, nlev)

    def store_state(dst):
        pst = pg()
        for p in range(3):
            P.tr(pst[0:64, p * 128:(p + 1) * 128], Hf[:, p, :], identf[:, :], [Hf.b, identf.b], [pst.b])
        P.cp("act", stS[:, :, :], pst[0:64, 0:384].rearrange("v (h k) -> v h k", k=64), [pst.b], [stS.b])
        P.store(dst.rearrange("h v k -> v h k"), stS, stS[:, :, :])

    pti = [0]

    oun2 = [oun, ounB]
    hcnt = [0]
    pending = [None]

    def flush_tail():
        if pending[0] is not None:
            t_ = pending[0]
            pending[0] = None
            t_()

    def attention(nq, heads, kfn, vfn, bfn, entries, krows, out_fn):
        for h in heads:
            po = ps_o2[h % 2]
            nent = len(entries)

            def pv(i, ent, ptt):
                j, nk, q0, diag = ent
                vap, vb_ = vfn(h, j, nk)
                P.mm(po[0:65, q0:nq], vap, ptt[0:nk, 0:nq - q0], i == 0, i == nent - 1, [vb_, ptt.b], [po.b])
            prev = None
            for i, ent in enumerate(entries):
                j, nk, q0, diag = ent
                pss_ = ps_s2[pti[0] % 2]
                ptt = pts[pti[0] % 3]
                pti[0] += 1
                kap, kb = kfn(h, j, nk)
                qap, qb = qfn_cur[0](h, q0, nq)
                P.mm(pss_[0:nk, 0:nq - q0], kap, qap, True, True, [kb, qb], [pss_.b])
                bias = bfn(h, j, nk)
                if bias is not None:
                    P.act(ptt[0:nk, 0:nq - q0], pss_[0:nk, 0:nq - q0], AF.Exp, [pss_.b, bias[1]], [ptt.b], bias=bias[0])
                else:
                    P.act(ptt[0:nk, 0:nq - q0], pss_[0:nk, 0:nq - q0], AF.Exp, [pss_.b], [ptt.b])
                if diag:
                    P.asel(ptt[0:nk, 0:nk], ptt[0:nk, 0:nk], [[1, nk]], ALU.is_ge, 0.0, 0, -1, [ptt.b], [ptt.b])
                if prev is not None:
                    pv(*prev)
                prev = (i, ent, ptt)
            pv(*prev)
            ou = oun2[hcnt[0] % 2]
            hcnt[0] += 1
            P.cp("act", ou[0:65, 0:nq], po[0:65, 0:nq], [po.b], [ou.b])

            def tail(h=h, ou=ou, nq=nq, out_fn=out_fn):
                P.act(ou[64:65, 0:nq], ou[64:65, 0:nq], AF.Ln, [ou.b], [ou.b])
                P.act(ou[64:65, 0:nq], ou[64:65, 0:nq], AF.Exp, [ou.b], [ou.b], scale=-1.0)
                pb = pg()
                P.mm(pb[0:64, 0:nq], ones[64:65, 0:64], ou[64:65, 0:nq], True, True, [ones.b, ou.b], [pb.b])
                out_fn(h, pb, ou)
            flush_tail()
            pending[0] = tail

    qfn_cur = [None]

    def fox_heads(nq, heads, fox_entries):
        qfn_cur[0] = lambda h, q0, nq_: (qT[0:67, h, q0:nq_], qT.b)

        def fox_out(h, pb, ou):
            hb = (h % 2) * 64
            P.tt("dve", opair[hb:hb + 64, 0:nq], ou[0:64, 0:nq], pb[0:64, 0:nq], ALU.mult, [ou.b, pb.b], [opair.b])
            if h % 2 == 1:
                c = h // 2
                P.tt("dve", og[:, c, 0:nq], opair[:, 0:nq], gt[:, c, 0:nq], ALU.mult, [opair.b, gtb[c]], [ogb[c]])
        attention(nq, heads,
                  lambda h, j, nk: (kT[0:67, h, j * 128:j * 128 + nk], kvb[j]),
                  lambda h, j, nk: (Vaug[0:nk, j, h, :], kvb[j]),
                  lambda h, j, nk: (negc[0:nk, j, h:h + 1], kvb[j]),
                  fox_entries, 67, fox_out)

    def mem_heads(nq, heads, mem_k, mem_v, mem_kb, mem_vb):
        qfn_cur[0] = lambda h, q0, nq_: (mqT[0:64, h, q0:nq_], mqT.b)

        def mem_out(h, pb, ou):
            hb = (h % 2) * 64
            P.tt("dve", opair[hb:hb + 64, 0:nq], ou[0:64, 0:nq], pb[0:64, 0:nq], ALU.mult, [ou.b, pb.b], [opair.b])
            if h % 2 == 1:
                c = 6 + h // 2
                P.tt("dve", og[:, c, 0:nq], opair[:, 0:nq], gt[:, c, 0:nq], ALU.mult, [opair.b, gtb[c]], [ogb[c]])
        attention(nq, heads,
                  lambda h, j, nk: (mem_k[0:64, h, j * 128:j * 128 + nk], mem_kb),
                  lambda h, j, nk: (mem_v[0:nk, j, h, :], mem_vb),
                  lambda h, j, nk: None,
                  [(0, 128, 0, False), (1, 128, 0, False)], 64, mem_out)

    def run_attention(nq, fox_entries, mem_k, mem_v, mem_kb, mem_vb):
        fox_heads(nq, range(6), fox_entries)
        mem_heads(nq, range(4), mem_k, mem_v, mem_kb, mem_vb)

    wsti = [0]

    def out_proj(tiles):
        accs = [[pg(), pg()] for _ in tiles]
        for c in range(8):
            w = wst[wsti[0] % 2]
            wsti[0] += 1
            P.em.dma("sp", w[:, :], wo_bf[:, c, :], reads=[wo_b], writes=[w.b], dbuf=w.b)
            for ti, (n, qoff, src, dst) in enumerate(tiles):
                for cb in range(2):
                    pst = accs[ti][cb]
                    P.mm(pst[0:n, 0:512], og[:, c, qoff:qoff + n], w[:, cb * 512:(cb + 1) * 512], c == 0, c == 7, [ogb[c], w.b], [pst.b])
        for ti, (n, qoff, src, dst) in enumerate(tiles):
            xtile = xt[1]
            P.load(xtile, xtile[0:n, :], src)
            for cb in range(2):
                pst = accs[ti][cb]
                P.tt("dve", xtile[0:n, cb * 512:(cb + 1) * 512], xtile[0:n, cb * 512:(cb + 1) * 512], pst[0:n, 0:512], ALU.add,
                     [xtile.b, pst.b], [xtile.b])
            P.store(dst, xtile, xtile[0:n, :])

    w_in_v = w_in.rearrange("(c p) n -> p c n", p=128)
    w_out_v = w_out.rearrange("(c p) n -> p c n", p=128)
    w_mem_v = w_mem_kv.rearrange("(c p) n -> p c n", p=128)
    kq = 0
    Wm = Vaug[:, :, :, :].rearrange("p a h e -> p (a h e)")[:, 0:4096].rearrange("p (c n) -> p c n", n=512)
    for c in range(8):
        s_ = xt[kq % 2]
        P.load(s_, s_[:, 0:512], w_mem_v[:, c, :])
        P.cp(P.rot(), Wm[:, c, :], s_[:, 0:512], [s_.b], kvb)
        kq += 1
    for blk in range(2):
        xtile = xt[blk % 2]
        norm_T(memp[blk * 128:(blk + 1) * 128, :], 128, mng, xtile)
        pst = pg()
        for c in range(8):
            P.mm(pst[:, 0:512], xnT[:, c, :], Wm[:, c, :], c == 0, c == 7, [xnT.b] + kvb, [pst.b], chain=(c > 0))
        headnorm(128, pst, 4, gmk, tmpB, out_bf=mqa[:, :, :], out_bf_b=mqa.b)
        P.store(o_mkp[blk * 128:(blk + 1) * 128, :], tmpB, tmpB[:, 0:256])
        P.cp("act", tmpC[:, 0:256], pst[:, 256:512], [pst.b], [tmpC.b])
        P.store(o_mvp[blk * 128:(blk + 1) * 128, :], tmpC, tmpC[:, 0:256])
        P.cp("dve", mvaug[:, blk, :, 0:64], pst[:, 256:512].rearrange("p (h d) -> p h d", d=64), [pst.b], [mvaug.b])
        for h in range(4):
            P.tr(ps_tr[0:64, h * 128:(h + 1) * 128], mqa[:, h, :], identb[:, :], [mqa.b, identb.b], [ps_tr.b])
        P.cp("act", mkT[:, :, blk * 128:(blk + 1) * 128], ps_tr[0:64, 0:512].rearrange("p (h t) -> p h t", t=128), [ps_tr.b], [mkT.b])
    P.memset("pool", Vaug[:, :, :, :].rearrange("p a h e -> p (a h) e")[:, :, 64:65], 1.0, kvb)
    for c in range(8):
        for (c0, c1) in ((0, 1024), (1024, 2048), (2048, 3072), (3072, NIN)):
            s_ = xt[kq % 2]
            P.load(s_, s_[:, 0:c1 - c0], w_in_v[:, c, c0:c1])
            P.cp(P.rot(), Wb[:, c, c0:c1], s_[:, 0:c1 - c0], [s_.b], [Wb.b])
            kq += 1
        s_ = xt[kq % 2]
        P.load(s_, s_[:, :], w_out_v[:, c, :])
        w = wst[c % 2]
        P.cp(P.rot(), w[:, :], s_[:, :], [s_.b], [w.b])
        P.em.dma("pool", wo_bf[:, c, :], w[:, :], reads=[w.b], writes=[wo_b], dbuf=w.b)
        kq += 1

    P.memset("dve", Hf[:], 0.0, [Hf.b])
    P.memset("dve", Hb[:], 0.0, [Hb.b])
    NG = NT // 2
    for g in range(NG):
        entries = [(j, 128, 0, False) for j in range(2 * g)] + [(2 * g, 128, 0, True), (2 * g + 1, 128, 128, True)]
        for tt_ in range(2):
            t = 2 * g + tt_
            sl = slice(t * 128, (t + 1) * 128)
            token_tile(xp[sl, :], 128, t, tt_ * 128, o_fkp[sl, :], o_fvp[sl, :], o_flp[sl, :],
                       None if t == 0 else cc[(t - 1) % 2], cc[t % 2])
            fm_proj(128, tt_ * 128)
            if tt_ == 0:
                rwkv_chunk(128, 0)
            else:
                nlev = rwkv_pre(128, 128)
                for p in range(3):
                    rwkv_pair(128, 128, p, nlev)
                    fox_heads(NQ, (2 * p, 2 * p + 1), entries)
            if t == NT - 1:
                store_shift(o_rhp, 128)
            else:
                P.cp("dve", raw[:, :, 0:1], raw[:, :, 128:129], [raw.b], [raw.b])
        mem_heads(NQ, range(4), mkT, mvaug, mkT.b, mvaug.b)
        flush_tail()
        flush_gn()
        out_proj([(128, tt_ * 128, xp[(2 * g + tt_) * 128:(2 * g + tt_ + 1) * 128, :], o_yp[(2 * g + tt_) * 128:(2 * g + tt_ + 1) * 128, :])
                  for tt_ in range(2)])
    store_state(o_rsp)

    for b in range(SB_):
        sl = slice(b * SS, (b + 1) * SS)
        for j in range(8):
            ks = slice(j * 128, (j + 1) * 128)
            P.load(tmpB, tmpB[:, 0:384], cfk[b, ks, :])
            P.cp("act", kaug[:, :, 0:64], tmpB[:, 0:384].rearrange("p (h d) -> p h d", d=64), [tmpB.b], [kaug.b])
            k_to_T(128, j)
            P.load(tmpC, tmpC[:, 0:384], cfv[b, ks, :])
            P.cp("dve", Vaug[:, j, :, 0:64], tmpC[:, 0:384].rearrange("p (h d) -> p h d", d=64), [tmpC.b], [kvb[j]])
            P.load(lf, lf[:, :], cfl[b, ks, :])
            c_update(128, j, None if j == 0 else cc[(j - 1) % 2], cc[j % 2])
        P.load(stS, stS[:, :, :], srw[b].rearrange("h v k -> v h k"))
        pst = pg()
        for h in range(6):
            P.tr(pst[0:64, h * 64:(h + 1) * 64], stS[:, h, :], identf[0:64, 0:64], [stS.b, identf.b], [pst.b])
        for h in range(6):
            p, hb = h // 2, (h % 2) * 64
            P.cp("act" if h % 2 else "dve", Hf[hb:hb + 64, p, :], pst[0:64, h * 64:(h + 1) * 64], [pst.b], [Hf.b])
        P.cp("act", Hb[:, :, :], Hf[:, :, :], [Hf.b], [Hb.b])
        P.em.dma("sp", raw[:, 0:9, 0], ssh[b, 0:1152].rearrange("(b p) -> p b", p=128), reads=(), writes=[raw.b], dbuf=raw.b,
                 allow_slow_non_contiguous=True)
        P.em.dma("sp", raw[0:64, 9:10, 0], ssh[b, 1152:1216].rearrange("(b p) -> p b", p=64), reads=(), writes=[raw.b], dbuf=raw.b,
                 allow_slow_non_contiguous=True)
        P.em.dma("sp", raw[:, 10:13, 0], ssh[b, 1216:1600].rearrange("(b p) -> p b", p=128), reads=(), writes=[raw.b], dbuf=raw.b,
                 allow_slow_non_contiguous=True)
        token_tile(xsm[sl, :], SS, 8, 0, o_fks[sl, :], o_fvs[sl, :], o_fls[sl, :], cc[7 % 2], cc[8 % 2])
        fm_proj(SS, 0)
        rwkv_chunk(SS, 0)
        store_shift(o_rhs[b], SS)
        store_state(o_rss[b])
        for blk in range(2):
            ks = slice(blk * 128, (blk + 1) * 128)
            P.load(tmpB, tmpB[:, 0:256], cmk[b, ks, :])
            P.cp("act", mqa[:, :, :], tmpB[:, 0:256].rearrange("p (h d) -> p h d", d=64), [tmpB.b], [mqa.b])
            for h in range(4):
                P.tr(ps_tr[0:64, h * 128:(h + 1) * 128], mqa[:, h, :], identb[:, :], [mqa.b, identb.b], [ps_tr.b])
            P.cp("act", mkT[:, :, blk * 128:(blk + 1) * 128], ps_tr[0:64, 0:512].rearrange("p (h t) -> p h t", t=128), [ps_tr.b], [mkT.b])
            P.load(tmpC, tmpC[:, 0:256], cmv[b, ks, :])
            P.cp("dve", mvaug[:, blk, :, 0:64], tmpC[:, 0:256].rearrange("p (h d) -> p h d", d=64), [tmpC.b], [mvaug.b])
        entries = [(j, 128, 0, False) for j in range(8)] + [(8, SS, 0, True)]
        run_attention(SS, entries, mkT, mvaug, mkT.b, mvaug.b)
        flush_tail()
        flush_gn()
        out_proj([(SS, 0, xsm[sl, :], o_ys[sl, :])])

    P.em.final_wait("pool")
    P.em.replay()
    return nc


_NC = None


def kernel(x_prompt, x_sample, mem_prompt, cache_fox_k, cache_fox_v, cache_fox_logf,
           cache_mem_k, cache_mem_v, state_rwkv, state_rwkv_shift,
           norm_g, w_in, fox_q_g, fox_k_g, fox_b_f, rwkv_mu, rwkv_w0, rwkv_w_up, rwkv_a0,
           rwkv_a_up, rwkv_k_k, rwkv_k_a, rwkv_r_k, rwkv_gn_w, rwkv_gn_b,
           mem_norm_g, w_mem_kv, mem_q_g, mem_k_g, w_out):
    global _NC
    f = lambda a: np.ascontiguousarray(np.asarray(a, dtype=np.float32))
    if _NC is None:
        _NC = build()
    nc = _NC
    shared = dict(norm_g=f(norm_g[0]), w_in=f(w_in[0]), fox_q_g=f(fox_q_g[0]), fox_k_g=f(fox_k_g[0]),
                  fox_b_f=f(fox_b_f[0]), rwkv_mu=f(rwkv_mu[0]), rwkv_w0=f(rwkv_w0[0]), rwkv_w_up=f(rwkv_w_up[0]),
                  rwkv_a0=f(rwkv_a0[0]), rwkv_a_up=f(rwkv_a_up[0]), rwkv_k_k=f(rwkv_k_k[0]), rwkv_k_a=f(rwkv_k_a[0]),
                  rwkv_r_k=f(rwkv_r_k[0]), rwkv_gn_w=f(rwkv_gn_w[0]), rwkv_gn_b=f(rwkv_gn_b[0]),
                  mem_norm_g=f(mem_norm_g[0]), w_mem_kv=f(w_mem_kv[0]), mem_q_g=f(mem_q_g[0]), mem_k_g=f(mem_k_g[0]),
                  w_out=f(w_out[0]))
    in_maps = []
    for c in range(8):
        bs = slice(4 * c, 4 * c + 4)
        m = dict(shared)
        m.update(xp=f(x_prompt[c]), xsm=f(x_sample[bs]).reshape(64, D), memp=f(mem_prompt[c]),
                 cfk=f(cache_fox_k[0, bs]).reshape(4, PAST, 384), cfv=f(cache_fox_v[0, bs]).reshape(4, PAST, 384),
                 cfl=f(cache_fox_logf[0, bs]), cmk=f(cache_mem_k[0, bs]).reshape(4, 256, 256),
                 cmv=f(cache_mem_v[0, bs]).reshape(4, 256, 256), srw=f(state_rwkv[0, bs]),
                 ssh=f(state_rwkv_shift[0, bs]).reshape(4, 1600))
        in_maps.append(m)
    res = run_bass_kernel_spmd(nc, in_maps, core_ids=list(range(8)))
    R = res.results
    cat = lambda k: np.stack([np.asarray(R[c][k]) for c in range(8)])
    yp = cat("o_yp")
    ys = cat("o_ys").reshape(32, 16, D)
    fkp = cat("o_fkp").reshape(1, 8, T, 6, 64)
    fvp = cat("o_fvp").reshape(1, 8, T, 6, 64)
    flp = cat("o_flp").reshape(1, 8, T, 6)
    mkp = cat("o_mkp").reshape(1, 8, 256, 4, 64)
    mvp = cat("o_mvp").reshape(1, 8, 256, 4, 64)
    rsp = cat("o_rsp").reshape(1, 8, 6, 64, 64)
    rhp = cat("o_rhp").reshape(1, 8, 1, 1600)
    fks = cat("o_fks").reshape(1, 32, 16, 6, 64)
    fvs = cat("o_fvs").reshape(1, 32, 16, 6, 64)
    fls = cat("o_fls").reshape(1, 32, 16, 6)
    rss = cat("o_rss").reshape(1, 32, 6, 64, 64)
    rhs = cat("o_rhs").reshape(1, 32, 1, 1600)
    return (yp, ys, fkp, fvp, flp, mkp, mvp, rsp, rhp, fks, fvs, fls, rss, rhs)
```

```python
import numpy as np
from contextlib import ExitStack
import concourse.bass as bass
import concourse.mybir as mybir
from concourse.bass_utils import run_bass_kernel_spmd

F32 = mybir.dt.float32
BF16 = mybir.dt.bfloat16
AF = mybir.ActivationFunctionType
ALU = mybir.AluOpType
AX = mybir.AxisListType

D = 1024
T = 4096
NT = T // 128
SB_ = 4
SS = 16
PAST = 1024
NIN = 3654
EPS = 1e-6
GN_EPS = 64e-5
C_Q, C_K, C_V, C_F, C_GF = 0, 384, 768, 1152, 1158
C_RW = 1542
C_RR, C_RK, C_RV, C_WD, C_AD, C_GR = C_RW, C_RW + 384, C_RW + 768, C_RW + 1152, C_RW + 1184, C_RW + 1216
C_MQ, C_GM = 3142, 3398


class Buf:
    __slots__ = ("w", "r", "dsem", "dcnt", "name", "excl")

    def __init__(self, name="", excl=False):
        self.w = None
        self.r = []
        self.dsem = None
        self.dcnt = 0
        self.name = name
        self.excl = excl


class Emit:
    ENG = ("pe", "act", "dve", "pool", "sp")

    def __init__(self, nc, stack):
        self.nc = nc
        self.stack = stack
        self.ops = {e: [] for e in self.ENG}
        self.cnt = {e: 0 for e in self.ENG}
        self.sems = {}
        for e in self.ENG:
            self.sems[e] = stack.enter_context(nc.semaphore("sem_" + e))
        self.known = {e: {} for e in self.ENG}
        self.nd = 0
        self.dbufs = []

    def _waits(self, eng, reads, writes):
        need = {}
        for b in reads:
            if b.w is not None:
                k, v = b.w
                if need.get(k, 0) < v:
                    need[k] = v
            if b.excl:
                for k, v in b.r:
                    if k != eng and need.get(k, 0) < v:
                        need[k] = v
        for b in writes:
            if b.w is not None:
                k, v = b.w
                if need.get(k, 0) < v:
                    need[k] = v
            for k, v in b.r:
                if need.get(k, 0) < v:
                    need[k] = v
        out = []
        kn = self.known[eng]
        for k, v in need.items():
            if kn.get(k, 0) < v:
                kn[k] = v
                out.append((self.sems[k], v))
        return out

    def _mark(self, ev, reads, writes):
        for b in reads:
            b.r = [x for x in b.r if x[0] != ev[0]]
            b.r.append(ev)
        for b in writes:
            b.w = ev
            b.r = []

    def op(self, eng, fn, reads=(), writes=(), chain=False):
        prev_known_pe = self.known["pe"].get("pe", 0) if eng == "pe" else None
        wl = self._waits(eng, reads, writes)
        if chain and eng == "pe":
            sem_pe = self.sems["pe"]
            keep = []
            for s_, v_ in wl:
                if s_ is sem_pe and v_ == self.cnt["pe"]:
                    self.known["pe"]["pe"] = prev_known_pe
                    continue
                keep.append((s_, v_))
            wl = keep
        self.cnt[eng] += 1
        ev = (eng, self.cnt[eng])
        sem = self.sems[eng]

        def run(e, fn=fn, wl=wl, sem=sem):
            for s, v in wl:
                e.wait_ge(s, v)
            fn(e).then_inc(sem, 1)
        self.ops[eng].append(run)
        self._mark(ev, reads, writes)
        return ev

    def dma(self, eng, out, in_, reads=(), writes=(), dbuf=None, **kw):
        if dbuf.dsem is None:
            dbuf.dsem = {}
            dbuf.dcnt = {}
            self.dbufs.append(dbuf)
        if eng not in dbuf.dsem:
            self.nd += 1
            key = "d%d" % self.nd
            self.sems[key] = self.stack.enter_context(self.nc.semaphore("sem_" + key))
            dbuf.dsem[eng] = key
            dbuf.dcnt[eng] = 0
        wl = self._waits(eng, reads, writes)
        dbuf.dcnt[eng] += 16
        key = dbuf.dsem[eng]
        ev = (key, dbuf.dcnt[eng])
        sem = self.sems[key]

        def run(e, wl=wl, sem=sem, out=out, in_=in_, kw=kw):
            for s, v in wl:
                e.wait_ge(s, v)
            e.dma_start(out=out, in_=in_, **kw).then_inc(sem, 16)
        self.ops[eng].append(run)
        self._mark(ev, reads, writes)
        return ev

    def final_wait(self, eng):
        wl = []
        kn = self.known[eng]
        for b in self.dbufs:
            for q, key in b.dsem.items():
                v = b.dcnt[q]
                if kn.get(key, 0) < v:
                    kn[key] = v
                    wl.append((self.sems[key], v))

        def run(e, wl=wl):
            for s, v in wl:
                e.wait_ge(s, v)
        self.ops[eng].append(run)

    def replay(self):
        nc = self.nc
        ops = self.ops
        with nc.Block() as block:
            @block.tensor
            def _(e):
                for f in ops["pe"]:
                    f(e)

            @block.scalar
            def _(e):
                for f in ops["act"]:
                    f(e)

            @block.vector
            def _(e):
                for f in ops["dve"]:
                    f(e)

            @block.gpsimd
            def _(e):
                for f in ops["pool"]:
                    f(e)

            @block.sync
            def _(e):
                for f in ops["sp"]:
                    f(e)


class TT:
    def __init__(self, ap, name=""):
        self.t = ap
        self.b = Buf(name)

    def __getitem__(self, k):
        return self.t[k]


class Prog:
    def __init__(self):
        self.nc = bass.Bass("TRN2", target_bir_lowering=False)
        self.st = ExitStack()
        self.em = Emit(self.nc, self.st)
        self.rr = 0

    def dram(self, name, shape, kind):
        return self.nc.dram_tensor(name, list(shape), F32, kind=kind).ap()

    def sb(self, name, shape, dt=F32):
        return TT(self.st.enter_context(self.nc.sbuf_tensor(name, list(shape), dt)), name)

    def ps(self, name, shape, dt=F32):
        t = TT(self.st.enter_context(self.nc.psum_tensor(name, list(shape), dt)), name)
        t.b.excl = True
        return t

    def act(self, out, in_, func, r, w, **kw):
        return self.em.op("act", lambda e: e.activation(out=out, in_=in_, func=func, **kw), r, w)

    def tt(self, eng, out, in0, in1, op, r, w):
        return self.em.op(eng, lambda e: e.tensor_tensor(out=out, in0=in0, in1=in1, op=op), r, w)

    def ts(self, eng, out, in0, s1, s2, op0, op1, r, w):
        if s2 is None:
            return self.em.op(eng, lambda e: e.tensor_scalar(out=out, in0=in0, scalar1=s1, scalar2=None, op0=op0), r, w)
        return self.em.op(eng, lambda e: e.tensor_scalar(out=out, in0=in0, scalar1=s1, scalar2=s2, op0=op0, op1=op1), r, w)

    def stt(self, out, in0, scalar, in1, op0, op1, r, w):
        return self.em.op("dve", lambda e: e.scalar_tensor_tensor(out=out, in0=in0, scalar=scalar, in1=in1, op0=op0, op1=op1), r, w)

    def cp(self, eng, out, in_, r, w):
        if eng == "act":
            return self.em.op("act", lambda e: e.activation(out=out, in_=in_, func=AF.Copy), r, w)
        return self.em.op(eng, lambda e: e.tensor_copy(out=out, in_=in_), r, w)

    def mm(self, out, lhsT, rhs, start, stop, r, w, chain=False):
        return self.em.op("pe", lambda e: e.matmul(out, lhsT=lhsT, rhs=rhs, start=start, stop=stop), r, w, chain=chain)

    def tr(self, out, in_, ident, r, w):
        return self.em.op("pe", lambda e: e.transpose(out, in_, ident), r, w)

    def memset(self, eng, ap, val, w):
        return self.em.op(eng, lambda e: e.memset(ap, val), (), w)

    def asel(self, out, in_, pattern, op, fill, base, cm, r, w):
        return self.em.op("pool", lambda e: e.affine_select(out=out, in_=in_, pattern=pattern, compare_op=op,
                                                            fill=fill, base=base, channel_multiplier=cm), r, w)

    def load(self, out_tt, out_ap, in_ap, **kw):
        return self.em.dma("sp", out_ap, in_ap, reads=(), writes=[out_tt.b], dbuf=out_tt.b, **kw)

    def store(self, out_ap, in_tt, in_ap, **kw):
        return self.em.dma("pool", out_ap, in_ap, reads=[in_tt.b], writes=(), dbuf=in_tt.b, **kw)

    def rot(self):
        self.rr += 1
        return ("act", "dve", "pool")[self.rr % 3]


def build():
    P = Prog()
    nc = P.nc
    IN, OUT = "ExternalInput", "ExternalOutput"
    NQ = 256
    xp = P.dram("xp", [T, D], IN)
    xsm = P.dram("xsm", [SB_ * SS, D], IN)
    memp = P.dram("memp", [256, D], IN)
    cfk = P.dram("cfk", [SB_, PAST, 384], IN)
    cfv = P.dram("cfv", [SB_, PAST, 384], IN)
    cfl = P.dram("cfl", [SB_, PAST, 6], IN)
    cmk = P.dram("cmk", [SB_, 256, 256], IN)
    cmv = P.dram("cmv", [SB_, 256, 256], IN)
    srw = P.dram("srw", [SB_, 6, 64, 64], IN)
    ssh = P.dram("ssh", [SB_, 1600], IN)
    norm_g = P.dram("norm_g", [D], IN)
    w_in = P.dram("w_in", [D, NIN], IN)
    fox_q_g = P.dram("fox_q_g", [64], IN)
    fox_k_g = P.dram("fox_k_g", [64], IN)
    fox_b_f = P.dram("fox_b_f", [6], IN)
    rwkv_mu = P.dram("rwkv_mu", [1600], IN)
    rwkv_w0 = P.dram("rwkv_w0", [384], IN)
    rwkv_w_up = P.dram("rwkv_w_up", [32, 384], IN)
    rwkv_a0 = P.dram("rwkv_a0", [384], IN)
    rwkv_a_up = P.dram("rwkv_a_up", [32, 384], IN)
    rwkv_k_k = P.dram("rwkv_k_k", [384], IN)
    rwkv_k_a = P.dram("rwkv_k_a", [384], IN)
    rwkv_r_k = P.dram("rwkv_r_k", [384], IN)
    rwkv_gn_w = P.dram("rwkv_gn_w", [384], IN)
    rwkv_gn_b = P.dram("rwkv_gn_b", [384], IN)
    mem_norm_g = P.dram("mem_norm_g", [D], IN)
    w_mem_kv = P.dram("w_mem_kv", [D, 512], IN)
    mem_q_g = P.dram("mem_q_g", [64], IN)
    mem_k_g = P.dram("mem_k_g", [64], IN)
    w_out = P.dram("w_out", [D, D], IN)

    o_yp = P.dram("o_yp", [T, D], OUT)
    o_ys = P.dram("o_ys", [SB_ * SS, D], OUT)
    o_fkp = P.dram("o_fkp", [T, 384], OUT)
    o_fvp = P.dram("o_fvp", [T, 384], OUT)
    o_flp = P.dram("o_flp", [T, 6], OUT)
    o_mkp = P.dram("o_mkp", [256, 256], OUT)
    o_mvp = P.dram("o_mvp", [256, 256], OUT)
    o_rsp = P.dram("o_rsp", [6, 64, 64], OUT)
    o_rhp = P.dram("o_rhp", [1600], OUT)
    o_fks = P.dram("o_fks", [SB_ * SS, 384], OUT)
    o_fvs = P.dram("o_fvs", [SB_ * SS, 384], OUT)
    o_fls = P.dram("o_fls", [SB_ * SS, 6], OUT)
    o_rss = P.dram("o_rss", [SB_, 6, 64, 64], OUT)
    o_rhs = P.dram("o_rhs", [SB_, 1600], OUT)

    Wb = P.sb("Wb", [128, 8, NIN], BF16)
    wst = [P.sb("wst%d" % i, [128, D], BF16) for i in range(2)]
    wo_bf = nc.dram_tensor("wo_bf", [128, 8, D], BF16, kind="Internal").ap()
    wo_b = Buf("wo_bf")
    kT = P.sb("kT", [67, 6, T], BF16)
    Vaug = P.sb("Vaug", [128, NT, 6, 65], BF16)
    negc = P.sb("negc", [128, NT, 6])
    kvb = [Buf("kv%d" % i) for i in range(NT)]
    xt = [P.sb("xt%d" % i, [128, D]) for i in range(2)]
    xb = P.sb("xb", [128, D], BF16)
    xnT = P.sb("xnT", [128, 8, 128], BF16)
    identb = P.sb("identb", [128, 128], BF16)
    identf = P.sb("identf", [128, 128])
    trif = P.sb("trif", [128, 128])
    lastf = P.sb("lastf", [128, 128])
    bones = P.sb("bones", [128, 128])
    bavg = P.sb("bavg", [128, 128])
    ones = P.sb("ones", [128, 128])
    mask4 = P.sb("mask4", [128, 4, 128], BF16)
    msl = P.sb("msl", [128, 128], BF16)
    ng = P.sb("ng", [128, 8])
    mng = P.sb("mng", [128, 8])
    gq = P.sb("gq", [128, 64])
    gk = P.sb("gk", [128, 64])
    gmq = P.sb("gmq", [128, 64])
    gmk = P.sb("gmk", [128, 64])
    bfb = P.sb("bfb", [128, 6])
    small = P.sb("small", [128, 64])
    tmpA = P.sb("tmpA", [128, 384])
    tmpB = P.sb("tmpB", [128, 384])
    tmpC = P.sb("tmpC", [128, 384])
    qaug = P.sb("qaug", [128, 6, 67], BF16)
    kaug = P.sb("kaug", [128, 6, 67], BF16)
    mqa = P.sb("mqa", [128, 4, 64], BF16)
    cc = [P.sb("cc%d" % i, [128, 6]) for i in range(2)]
    cr = P.sb("cr", [128, 6])
    lf = P.sb("lf", [128, 6])
    raw = P.sb("raw", [128, 13, 129])
    xs = P.sb("xs", [128, 4, 128])
    xw = P.sb("xw", [64, 128])
    mu = P.sb("mu", [128, 13])
    rp = P.sb("rp", [128, 3, 8])
    lora = P.sb("lora", [64, 384], BF16)
    qT = P.sb("qT", [67, 6, NQ], BF16)
    mqT = P.sb("mqT", [64, 4, NQ], BF16)
    mkT = P.sb("mkT", [64, 4, 256], BF16)
    mvaug = P.sb("mvaug", [128, 2, 4, 65], BF16)
    gt = P.sb("gt", [128, 8, NQ], BF16)
    og = P.sb("og", [128, 8, NQ], BF16)
    gtb = [Buf("gt%d" % i) for i in range(8)]
    ogb = [Buf("og%d" % i) for i in range(8)]
    pts = [P.sb("pt%d" % i, [128, NQ], BF16) for i in range(3)]
    oun = P.sb("oun", [65, NQ])
    ounB = P.sb("ounB", [65, NQ])
    opair = P.sb("opair", [128, NQ])
    W3 = lambda name, dt=F32: P.sb(name, [128, 1, 128], dt)
    r_lw, r_a, r_g, r_eg, r_egm, r_eng = W3("r_lw"), W3("r_a"), W3("r_g"), W3("r_eg"), W3("r_egm"), W3("r_eng")
    r_kk, r_t1, r_t2 = W3("r_kk"), W3("r_t1"), W3("r_t2")
    r_yT2 = [W3("r_yT0"), W3("r_yT1")]
    r_bon2 = [W3("r_bon0"), W3("r_bon1")]
    rpc = [0]
    pend_gn = [None]

    def gn_step(k=1):
        for _ in range(k):
            if pend_gn[0]:
                pend_gn[0].pop(0)()

    def flush_gn():
        while pend_gn[0]:
            pend_gn[0].pop(0)()
    ART = P.sb("ART", [128, 1, 2, 128], BF16)
    BTt = W3("BTt", BF16)
    KTt = W3("KTt", BF16)
    vbt = W3("vbt", BF16)
    twd = P.sb("twd", [64, 128], BF16)
    tokA = P.sb("tokA", [128, 128], BF16)
    tokB = P.sb("tokB", [128, 128], BF16)
    tokK = P.sb("tokK", [128, 128], BF16)
    tokV = P.sb("tokV", [128, 128], BF16)
    gm = [P.sb("gm%d" % h, [128, 4, 128], BF16) for h in range(2)]
    PP = [[P.sb("PP%d_%d" % (h, i), [128, 2, 128], BF16) for i in range(2)] for h in range(2)]
    XX = [[P.sb("XX%d_%d" % (h, i), [128, 128], BF16) for i in range(2)] for h in range(2)]
    WT = P.sb("WT", [128, 1, 128], BF16)
    LVs = P.sb("LVs", [128, 2, 64], BF16)
    U0 = P.sb("U0", [128, 2, 64])
    Ub = P.sb("Ub", [128, 2, 64], BF16)
    Hf = P.sb("Hf", [128, 3, 64])
    Hb = P.sb("Hb", [128, 3, 64], BF16)
    Dp = P.sb("Dp", [128, 1, 64])
    stS = P.sb("stS", [64, 6, 64])

    ps_tr = P.ps("ps_tr", [128, 1024], BF16)
    ps_sA = P.ps("ps_sA", [128, 512])
    ps_sB = P.ps("ps_sB", [128, 512])
    ps_o = P.ps("ps_o", [128, 512])
    pgs = [P.ps("pg%d" % i, [128, 512]) for i in range(4)]
    pgi = [0]

    def pg():
        pgi[0] += 1
        return pgs[pgi[0] % len(pgs)]

    class View:
        def __init__(self, ap):
            self.t = ap
            self.b = Buf(excl=True)

        def __getitem__(self, k):
            return self.t[k]
    ps_s2 = [ps_sA, ps_sB]
    ps_o2 = [View(ps_o[:, 0:256]), View(ps_o[:, 256:512])]
    ps_o2[1].b = ps_o2[0].b

    P.memset("pool", identb[:], 0.0, [identb.b])
    P.asel(identb[:], identb[:], [[-1, 128]], ALU.not_equal, 1.0, 0, 1, [identb.b], [identb.b])
    P.memset("pool", identf[:], 0.0, [identf.b])
    P.asel(identf[:], identf[:], [[-1, 128]], ALU.not_equal, 1.0, 0, 1, [identf.b], [identf.b])
    P.memset("pool", trif[:], 1.0, [trif.b])
    P.asel(trif[:], trif[:], [[1, 128]], ALU.is_ge, 0.0, 0, -1, [trif.b], [trif.b])
    P.memset("pool", lastf[:], 1.0, [lastf.b])
    P.asel(lastf[:], lastf[:], [[0, 128]], ALU.is_ge, 0.0, -127, 1, [lastf.b], [lastf.b])
    P.memset("dve", bones[:], 0.0, [bones.b])
    P.memset("dve", bones[0:64, 0:64], 1.0, [bones.b])
    P.memset("dve", bones[64:128, 64:128], 1.0, [bones.b])
    P.ts("dve", bavg[:], bones[:], 1.0 / 64, None, ALU.mult, None, [bones.b], [bavg.b])
    P.memset("dve", ones[:], 1.0, [ones.b])
    P.memset("pool", mask4[:], 1.0, [mask4.b])
    for i in range(4):
        P.asel(mask4[:, i, :], mask4[:, i, :], [[1, 128]], ALU.is_ge, 0.0, (-1 if i % 2 == 0 else 0), -1, [mask4.b], [mask4.b])
    P.memset("pool", msl[:], 1.0, [msl.b])
    P.asel(msl[:], msl[:], [[-1, 128]], ALU.is_ge, 0.0, -1, 1, [msl.b], [msl.b])

    def cload(out_ap, in_ap, tt_, **kw):
        P.em.dma("sp", out_ap, in_ap, reads=(), writes=[tt_.b], dbuf=tt_.b, **kw)

    cload(ng[:], norm_g.rearrange("(c p) -> p c", p=128), ng, allow_slow_non_contiguous=True)
    cload(mng[:], mem_norm_g.rearrange("(c p) -> p c", p=128), mng, allow_slow_non_contiguous=True)
    for tl, src in ((gq, fox_q_g), (gk, fox_k_g), (gmq, mem_q_g), (gmk, mem_k_g)):
        cload(tl[:], src.partition_broadcast(128), tl)
    cload(bfb[:], fox_b_f.partition_broadcast(128), bfb)
    P.ts("dve", gq[:], gq[:], 0.125, None, ALU.mult, None, [gq.b], [gq.b])
    P.ts("dve", gmq[:], gmq[:], 0.125, None, ALU.mult, None, [gmq.b], [gmq.b])
    cload(mu[:, 0:9], rwkv_mu[0:1152].rearrange("(b p) -> p b", p=128), mu, allow_slow_non_contiguous=True)
    cload(mu[0:64, 9:10], rwkv_mu[1152:1216].rearrange("(b p) -> p b", p=64), mu, allow_slow_non_contiguous=True)
    cload(mu[:, 10:13], rwkv_mu[1216:1600].rearrange("(b p) -> p b", p=128), mu, allow_slow_non_contiguous=True)
    for i, src in enumerate((rwkv_w0, rwkv_a0, rwkv_k_k, rwkv_k_a, rwkv_k_a, rwkv_r_k, rwkv_gn_w, rwkv_gn_b)):
        cload(rp[:, :, i], src.rearrange("(b p) -> p b", p=128), rp, allow_slow_non_contiguous=True)
    P.ts("dve", rp[:, :, 4], rp[:, :, 4], -1.0, 1.0, ALU.mult, ALU.add, [rp.b], [rp.b])
    cload(tmpA[0:32, :], rwkv_w_up, tmpA)
    cload(tmpA[32:64, :], rwkv_a_up, tmpA)
    P.cp("dve", lora[:], tmpA[0:64, :], [tmpA.b], [lora.b])
    P.memset("dve", kaug[:, :, 64:67], 1.0, [kaug.b])
    P.memset("dve", raw[:], 0.0, [raw.b])
    P.memset("pool", mvaug[:, :, :, :].rearrange("p a h e -> p (a h) e")[:, :, 64:65], 1.0, [mvaug.b])

    def norm_T(src_dram, n, gtile, xtile):
        P.load(xtile, xtile[0:n, :], src_dram)
        P.em.op("act", lambda e: e.activation(out=xb[0:n, :], in_=xtile[0:n, :], func=AF.Square,
                                              accum_out=small[0:n, 0:1]), [xtile.b], [xb.b, small.b])
        P.act(small[0:n, 1:2], small[0:n, 0:1], AF.Ln, [small.b], [small.b], scale=1.0 / D, bias=EPS)
        P.act(small[0:n, 2:3], small[0:n, 1:2], AF.Exp, [small.b], [small.b], scale=-0.5)
        P.act(xb[0:n, :], xtile[0:n, :], AF.Copy, [xtile.b, small.b], [xb.b], scale=small[0:n, 2:3])
        for c in range(8):
            P.tr(ps_tr[:, c * 128:c * 128 + n], xb[0:n, c * 128:(c + 1) * 128], identb[0:n, 0:n], [xb.b, identb.b], [ps_tr.b])
        P.tt("dve", xnT[:, :, 0:n], ps_tr[:, :].rearrange("p (c t) -> p c t", t=128)[:, :, 0:n],
             gtile[:, :].unsqueeze(2).to_broadcast([128, 8, n]), ALU.mult, [ps_tr.b, gtile.b], [xnT.b])

    def proj_tm(n, c0, c1, pst, W=None):
        W = W or Wb
        for c in range(8):
            P.mm(pst[0:n, 0:c1 - c0], xnT[:, c, 0:n], W[:, c, c0:c1], c == 0, c == 7, [xnT.b, W.b], [pst.b], chain=(c > 0))

    def headnorm(n, pst, nh, gain, dst, out_bf=None, out_bf_b=None):
        w = nh * 64
        v3 = lambda ap: ap.rearrange("p (h d) -> p h d", d=64)
        P.act(tmpA[0:n, 0:w], pst[0:n, 0:w], AF.Square, [pst.b], [tmpA.b])
        P.em.op("dve", lambda e: e.tensor_reduce(out=small[0:n, 8:8 + nh], in_=v3(tmpA[0:n, 0:w]),
                                                 axis=AX.X, op=ALU.add), [tmpA.b], [small.b])
        P.act(small[0:n, 16:16 + nh], small[0:n, 8:8 + nh], AF.Ln, [small.b], [small.b], scale=1.0 / 64, bias=EPS)
        P.act(small[0:n, 24:24 + nh], small[0:n, 16:16 + nh], AF.Exp, [small.b], [small.b], scale=-0.5)
        P.tt("dve", v3(tmpA[0:n, 0:w]), v3(pst[0:n, 0:w]),
             small[0:n, 24:24 + nh].unsqueeze(2).to_broadcast([n, nh, 64]), ALU.mult, [pst.b, small.b], [tmpA.b])
        P.tt("dve", v3(dst[0:n, 0:w]), v3(tmpA[0:n, 0:w]),
             gain[0:n, :].unsqueeze(1).to_broadcast([n, nh, 64]), ALU.mult, [tmpA.b, gain.b], [dst.b])
        if out_bf is not None:
            P.cp("act", out_bf, v3(dst[0:n, 0:w]), [dst.b], [out_bf_b])

    def c_update(n, j, cprev, ccur):
        pst = pg()
        P.mm(pst[0:n, 0:6], trif[0:n, 0:n], lf[0:n, :], True, cprev is None, [trif.b, lf.b], [pst.b])
        if cprev is not None:
            P.mm(pst[0:n, 0:6], lastf[:, 0:n], cprev[:, :], False, True, [lastf.b, cprev.b], [pst.b], chain=True)
        P.cp("act", ccur[0:n, :], pst[0:n, 0:6], [pst.b], [ccur.b])
        P.ts("dve", negc[0:n, j, :], ccur[0:n, :], -1.0, None, ALU.mult, None, [ccur.b], [kvb[j]])

    def q_cpieces(n, ccur):
        P.cp("dve", qaug[0:n, :, 64], ccur[0:n, :], [ccur.b], [qaug.b])
        P.tt("dve", cr[0:n, :], ccur[0:n, :], qaug[0:n, :, 64], ALU.subtract, [ccur.b, qaug.b], [cr.b])
        P.cp("dve", qaug[0:n, :, 65], cr[0:n, :], [cr.b], [qaug.b])
        P.tt("dve", cr[0:n, :], cr[0:n, :], qaug[0:n, :, 65], ALU.subtract, [cr.b, qaug.b], [cr.b])
        P.cp("dve", qaug[0:n, :, 66], cr[0:n, :], [cr.b], [qaug.b])

    def k_to_T(n, j):
        for h in range(6):
            P.tr(ps_tr[0:67, h * 128:h * 128 + n], kaug[0:n, h, :], identb[0:n, 0:n], [kaug.b, identb.b], [ps_tr.b])
        P.cp("act", kT[:, :, j * 128:j * 128 + n], ps_tr[0:67, 0:768].rearrange("p (h t) -> p h t", t=128)[:, :, 0:n],
             [ps_tr.b], [kvb[j]])

    def q_to_T(n, qoff):
        for h in range(6):
            P.tr(ps_tr[0:67, h * 128:h * 128 + n], qaug[0:n, h, :], identb[0:n, 0:n], [qaug.b, identb.b], [ps_tr.b])
        P.cp("act", qT[:, :, qoff:qoff + n], ps_tr[0:67, 0:768].rearrange("p (h t) -> p h t", t=128)[:, :, 0:n],
             [ps_tr.b], [qT.b])

    def token_tile(src, n, j, qoff, o_k, o_v, o_l, cprev, ccur):
        xtile = xt[0]
        norm_T(src, n, ng, xtile)
        p0 = pg()
        proj_tm(n, C_Q, C_Q + 384, p0)
        headnorm(n, p0, 6, gq, tmpB, out_bf=qaug[0:n, :, 0:64], out_bf_b=qaug.b)
        p1 = pg()
        proj_tm(n, C_K, C_K + 384, p1)
        headnorm(n, p1, 6, gk, tmpB, out_bf=kaug[0:n, :, 0:64], out_bf_b=kaug.b)
        P.store(o_k, tmpB, tmpB[0:n, 0:384])
        p2 = pg()
        proj_tm(n, C_V, C_V + 390, p2)
        P.cp("act", tmpC[0:n, 0:384], p2[0:n, 0:384], [p2.b], [tmpC.b])
        P.store(o_v, tmpC, tmpC[0:n, 0:384])
        P.cp("dve", Vaug[0:n, j, :, 0:64], tmpC[0:n, 0:384].rearrange("p (h d) -> p h d", d=64), [tmpC.b], [kvb[j]])
        P.tt("dve", lf[0:n, :], p2[0:n, 384:390], bfb[0:n, :], ALU.add, [p2.b, bfb.b], [lf.b])
        P.act(lf[0:n, :], lf[0:n, :], AF.Exp, [lf.b], [lf.b], scale=-1.0)
        P.act(lf[0:n, :], lf[0:n, :], AF.Ln, [lf.b], [lf.b], bias=1.0)
        P.ts("dve", lf[0:n, :], lf[0:n, :], -1.0, None, ALU.mult, None, [lf.b], [lf.b])
        P.store(o_l, lf, lf[0:n, :])
        c_update(n, j, cprev, ccur)
        q_cpieces(n, ccur)
        k_to_T(n, j)
        q_to_T(n, qoff)
        p3 = pg()
        proj_tm(n, C_MQ, C_MQ + 256, p3)
        headnorm(n, p3, 4, gmq, tmpB, out_bf=mqa[0:n, :, :], out_bf_b=mqa.b)
        for h in range(4):
            P.tr(ps_tr[0:64, h * 128:h * 128 + n], mqa[0:n, h, :], identb[0:n, 0:n], [mqa.b, identb.b], [ps_tr.b])
        P.cp("act", mqT[:, :, qoff:qoff + n], ps_tr[0:64, 0:512].rearrange("p (h t) -> p h t", t=128)[:, :, 0:n],
             [ps_tr.b], [mqT.b])

    def fm_proj(n, qoff):
        gblocks = [(C_GF + 128 * i, i) for i in range(3)] + [(C_GM + 128 * i, 6 + i) for i in range(2)]
        for g0 in (0, 4):
            pst = pg()
            lst_ = gblocks[g0:g0 + 4]
            for jj, (c0, ch) in enumerate(lst_):
                for c in range(8):
                    P.mm(pst[:, jj * 128:jj * 128 + n], Wb[:, c, c0:c0 + 128], xnT[:, c, 0:n], c == 0, c == 7, [xnT.b, Wb.b], [pst.b], chain=(c > 0))
            for jj, (c0, ch) in enumerate(lst_):
                P.act(gt[:, ch, qoff:qoff + n], pst[:, jj * 128:jj * 128 + n], AF.Silu, [pst.b], [gtb[ch]])
        blocks = [(C_RW + 128 * i, 128) for i in range(9)] + [(C_WD, 64)] + [(C_GR + 128 * i, 128) for i in range(3)]
        for g0 in range(0, 13, 4):
            pst = pg()
            nb = min(4, 13 - g0)
            for jj in range(nb):
                c0, m = blocks[g0 + jj]
                for c in range(8):
                    P.mm(pst[0:m, jj * 128:jj * 128 + n], Wb[:, c, c0:c0 + m], xnT[:, c, 0:n], c == 0, c == 7, [xnT.b, Wb.b], [pst.b], chain=(c > 0))
            P.cp("act" if (g0 // 4) % 2 == 0 else "dve", raw[:, g0:g0 + nb, 1:1 + n], pst[:, 0:nb * 128].rearrange("p (b t) -> p b t", t=128)[:, :, 0:n], [pst.b], [raw.b])

    def store_shift(dst, n):
        P.store(dst[0:1152].rearrange("(b p) -> p b", p=128), raw, raw[:, 0:9, n], allow_slow_non_contiguous=True)
        P.store(dst[1152:1216].rearrange("(b p) -> p b", p=64), raw, raw[0:64, 9:10, n], allow_slow_non_contiguous=True)
        P.store(dst[1216:1600].rearrange("(b p) -> p b", p=128), raw, raw[:, 10:13, n], allow_slow_non_contiguous=True)

    def rwkv_pre(n, qoff):
        cur = lambda blk, p0=0, p1=128: raw[p0:p1, blk, 1:1 + n]
        prv = lambda blk, p0=0, p1=128: raw[p0:p1, blk, 0:n]
        P.tt("dve", xw[:, 0:n], prv(9, 0, 64), cur(9, 0, 64), ALU.subtract, [raw.b], [xw.b])
        P.stt(xw[:, 0:n], xw[:, 0:n], mu[0:64, 9:10], cur(9, 0, 64), ALU.mult, ALU.add, [xw.b, mu.b, raw.b], [xw.b])
        P.act(twd[0:32, 0:n], xw[0:32, 0:n], AF.Tanh, [xw.b], [twd.b])
        P.cp("dve", twd[32:64, 0:n], xw[32:64, 0:n], [xw.b], [twd.b])
        for p in range(3):
            blk = 10 + p
            P.tt("dve", xs[:, 3, 0:n], prv(blk), cur(blk), ALU.subtract, [raw.b], [xs.b])
            P.stt(xs[:, 3, 0:n], xs[:, 3, 0:n], mu[:, blk:blk + 1], cur(blk), ALU.mult, ALU.add, [xs.b, mu.b, raw.b], [xs.b])
            P.act(gt[:, 3 + p, qoff:qoff + n], xs[:, 3, 0:n], AF.Silu, [xs.b], [gtb[3 + p]])
        nlev = 0
        while (1 << nlev) < n:
            nlev += 1
        nlev -= 1
        return nlev

    def rwkv_pair(n, qoff, p, nlev):
        par = rpc[0] % 2
        rpc[0] += 1
        yT_, bon_ = r_yT2[par], r_bon2[par]
        cur = lambda blk, p0=0, p1=128: raw[p0:p1, blk, 1:1 + n]
        prv = lambda blk, p0=0, p1=128: raw[p0:p1, blk, 0:n]
        if True:
            S3 = lambda tl: tl[:, 0, 0:n]
            bc = lambda i: rp[:, p, i:i + 1]
            for i, blk in enumerate((p, 3 + p, 6 + p)):
                P.tt("dve", xs[:, i, 0:n], prv(blk), cur(blk), ALU.subtract, [raw.b], [xs.b])
                P.stt(xs[:, i, 0:n], xs[:, i, 0:n], mu[:, blk:blk + 1], cur(blk), ALU.mult, ALU.add, [xs.b, mu.b, raw.b], [xs.b])
            xr, xk, xv = xs[:, 0, 0:n], xs[:, 1, 0:n], xs[:, 2, 0:n]
            pw = pg()
            P.mm(pw[:, 0:n], lora[0:32, p * 128:(p + 1) * 128], twd[0:32, 0:n], True, True, [lora.b, twd.b], [pw.b])
            P.mm(pw[:, 128:128 + n], lora[32:64, p * 128:(p + 1) * 128], twd[32:64, 0:n], True, True, [lora.b, twd.b], [pw.b])
            P.act(S3(r_lw), pw[:, 0:n], AF.Sigmoid, [pw.b, rp.b], [r_lw.b], bias=bc(0))
            P.ts("dve", S3(r_lw), S3(r_lw), -0.6065306597126334, None, ALU.mult, None, [r_lw.b], [r_lw.b])
            P.act(S3(r_a), pw[:, 128:128 + n], AF.Sigmoid, [pw.b, rp.b], [r_a.b], bias=bc(1))
            P.em.op("dve", lambda e: e.tensor_tensor_scan(out=r_g[:, 0, 0:n], data0=ones[:, 0:n], data1=r_lw[:, 0, 0:n],
                                                          initial=0.0, op0=ALU.mult, op1=ALU.add),
                    [ones.b, r_lw.b], [r_g.b])
            P.act(S3(r_eg), S3(r_g), AF.Exp, [r_g.b], [r_eg.b])
            P.act(S3(r_eng), S3(r_g), AF.Exp, [r_g.b], [r_eng.b], scale=-1.0)
            P.tt("dve", S3(r_egm), S3(r_g), S3(r_lw), ALU.subtract, [r_g.b, r_lw.b], [r_egm.b])
            P.act(S3(r_egm), S3(r_egm), AF.Exp, [r_egm.b], [r_egm.b])
            P.ts("dve", S3(r_kk), xk, bc(2), None, ALU.mult, None, [xs.b, rp.b], [r_kk.b])
            P.tt("dve", S3(r_t1), S3(r_kk), S3(r_kk), ALU.mult, [r_kk.b], [r_t1.b])
            pss = pg()
            P.mm(pss[:, 0:n], bones[:, :], r_t1[:, 0, 0:n], True, True, [bones.b, r_t1.b], [pss.b])
            P.ts("dve", S3(r_t1), pss[:, 0:n], 1e-24, None, ALU.max, None, [pss.b], [r_t1.b])
            P.act(S3(r_t1), S3(r_t1), AF.Ln, [r_t1.b], [r_t1.b], scale=float(2 ** 40))
            P.act(S3(r_t1), S3(r_t1), AF.Exp, [r_t1.b], [r_t1.b], scale=-0.5, bias=13.862943611198906)
            P.tt("dve", S3(r_kk), S3(r_kk), S3(r_t1), ALU.mult, [r_kk.b, r_t1.b], [r_kk.b])
            P.ts("pool", S3(r_t2), S3(r_a), bc(3), bc(4), ALU.mult, ALU.add, [r_a.b, rp.b], [r_t2.b])
            P.tt("pool", S3(r_t2), S3(r_t2), xk, ALU.mult, [r_t2.b, xs.b], [r_t2.b])
            P.stt(ART[:, 0, 0, 0:n], S3(r_kk), -1.0, S3(r_egm), ALU.mult, ALU.mult, [r_kk.b, r_egm.b], [ART.b])
            P.tt("pool", ART[:, 0, 1, 0:n], xr, S3(r_eg), ALU.mult, [xs.b, r_eg.b], [ART.b])
            P.tt("dve", S3(r_t1), S3(r_a), S3(r_kk), ALU.mult, [r_a.b, r_kk.b], [r_t1.b])
            P.tt("dve", S3(BTt), S3(r_t1), S3(r_eng), ALU.mult, [r_t1.b, r_eng.b], [BTt.b])
            P.tt("pool", S3(KTt), S3(r_t2), S3(r_eng), ALU.mult, [r_t2.b, r_eng.b], [KTt.b])
            P.cp("act", S3(vbt), xv, [xs.b], [vbt.b])
            P.tt("dve", S3(r_t1), xr, S3(r_t2), ALU.mult, [xs.b, r_t2.b], [r_t1.b])
            P.ts("dve", S3(r_t1), S3(r_t1), bc(5), None, ALU.mult, None, [r_t1.b, rp.b], [r_t1.b])
            psb = pg()
            P.mm(psb[:, 0:n], bones[:, :], r_t1[:, 0, 0:n], True, True, [bones.b, r_t1.b], [psb.b])
            P.tt("dve", S3(bon_), psb[:, 0:n], xv, ALU.mult, [psb.b, xs.b], [bon_.b])
            for i, (src_ap, sb_) in enumerate(((ART[:, 0, 0, 0:n], ART.b), (BTt[:, 0, 0:n], BTt.b), (KTt[:, 0, 0:n], KTt.b), (vbt[:, 0, 0:n], vbt.b))):
                P.tr(ps_tr[0:n, i * 128:(i + 1) * 128], src_ap, identb[:, :], [sb_, identb.b], [ps_tr.b])
            for i, dstt in enumerate((tokA, tokB, tokK, tokV)):
                P.cp("act" if i % 2 else "dve", dstt[0:n, :], ps_tr[0:n, i * 128:(i + 1) * 128], [ps_tr.b], [dstt.b])
            for hh in range(2):
                hb = hh * 64
                g12 = pg()
                for a_ in range(2):
                    P.mm(g12[0:n, a_ * 128:a_ * 128 + n], BTt[hb:hb + 64, 0, 0:n], ART[hb:hb + 64, 0, a_, 0:n], True, True, [BTt.b, ART.b], [g12.b])
                    P.mm(g12[0:n, 256 + a_ * 128:256 + a_ * 128 + n], KTt[hb:hb + 64, 0, 0:n], ART[hb:hb + 64, 0, a_, 0:n], True, True, [KTt.b, ART.b], [g12.b])
                P.tt("dve", gm[hh][0:n, :, 0:n], g12[0:n, :].rearrange("s (a t) -> s a t", t=128)[:, :, 0:n], mask4[0:n, :, 0:n], ALU.mult,
                     [g12.b, mask4.b], [gm[hh].b])
                g3 = pg()
                P.mm(g3[0:n, 0:n], ART[hb:hb + 64, 0, 0, 0:n], BTt[hb:hb + 64, 0, 0:n], True, True, [ART.b, BTt.b], [g3.b])
                P.tt("dve", PP[hh][0][0:n, 0, 0:n], g3[0:n, 0:n], msl[0:n, 0:n], ALU.mult, [g3.b, msl.b], [PP[hh][0].b])
                P.cp("act", PP[hh][0][0:n, 1, 0:n], gm[hh][0:n, 0, 0:n], [gm[hh].b], [PP[hh][0].b])
                P.tt("pool", XX[hh][0][0:n, 0:n], gm[hh][0:n, 0, 0:n], identb[0:n, 0:n], ALU.add, [gm[hh].b, identb.b], [XX[hh][0].b])
            for hh in range(2):
                hb = hh * 64
                pl_ = pg()
                P.mm(pl_[0:n, 0:64], gm[hh][0:n, 2, 0:n], tokV[0:n, hb:hb + 64], True, True, [gm[hh].b, tokV.b], [pl_.b])
                P.cp("dve", LVs[0:n, hh, :], pl_[0:n, 0:64], [pl_.b], [LVs.b])
            def emit_sq(j):
                ci, ni = (j - 1) % 2, j % 2
                for hh in range(2):
                    psq = pg()
                    Pc = PP[hh][ci]
                    P.mm(psq[0:n, 0:n], Pc[0:n, 1, 0:n], Pc[0:n, 0, 0:n], True, True, [Pc.b], [psq.b])
                    if j < nlev:
                        P.mm(psq[0:n, 128:128 + n], Pc[0:n, 0, 0:n], Pc[0:n, 1, 0:n], True, True, [Pc.b], [psq.b])
                        P.cp("dve" if hh else "act", PP[hh][ni][0:n, :, 0:n], psq[0:n, 0:256].rearrange("s (a t) -> s a t", t=128)[:, :, 0:n], [psq.b], [PP[hh][ni].b])
                    else:
                        P.cp("dve" if hh else "act", PP[hh][ni][0:n, 0, 0:n], psq[0:n, 0:n], [psq.b], [PP[hh][ni].b])

            def emit_x(j):
                ci, ni = (j - 1) % 2, j % 2
                for hh in range(2):
                    px = pg()
                    P.mm(px[0:n, 0:n], PP[hh][ni][0:n, 0, 0:n], XX[hh][ci][0:n, 0:n], True, True, [PP[hh][ni].b, XX[hh][ci].b], [px.b])
                    P.tt("dve", XX[hh][ni][0:n, 0:n], px[0:n, 0:n], XX[hh][ci][0:n, 0:n], ALU.add, [px.b, XX[hh][ci].b], [XX[hh][ni].b])

            for j in range(1, nlev + 1):
                emit_sq(j)
                if j > 1:
                    emit_x(j - 1)
                gn_step(1 if nlev >= 4 else 2)
            emit_x(nlev)
            fi = nlev % 2
            pws = []
            for hh in range(2):
                hb = hh * 64
                TTm = XX[hh][fi]
                pa_, pu_ = pg(), pg()
                pws.append((pa_, pu_))
                P.mm(pa_[0:64, 0:n], tokA[0:n, hb:hb + 64], TTm[0:n, 0:n], True, True, [tokA.b, TTm.b], [pa_.b])
                P.mm(pu_[0:n, 0:64], TTm[0:n, 0:n], LVs[0:n, hh, :], True, True, [TTm.b, LVs.b], [pu_.b])
            for hh in range(2):
                hb = hh * 64
                pa_, pu_ = pws[hh]
                P.cp("act", WT[hb:hb + 64, 0, 0:n], pa_[0:64, 0:n], [pa_.b], [WT.b])
                P.cp("dve", U0[0:n, hh, :], pu_[0:n, 0:64], [pu_.b], [U0.b])
            flush_gn()
            pU = pg()
            for hh in range(2):
                hb = hh * 64
                P.mm(pU[0:n, hb:hb + 64], WT[hb:hb + 64, 0, 0:n], Hb[hb:hb + 64, p, :], True, True, [WT.b, Hb.b], [pU.b])
            P.tt("dve", Ub[0:n, :, :], pU[0:n, 0:128].rearrange("t (h v) -> t h v", v=64), U0[0:n, :, :], ALU.add, [pU.b, U0.b], [Ub.b])
            pY = pg()
            for hh in range(2):
                hb = hh * 64
                dst = pY[0:64, hh * 128:hh * 128 + n]
                P.mm(dst, Hb[hb:hb + 64, p, :], ART[hb:hb + 64, 0, 1, 0:n], True, False, [Hb.b, ART.b], [pY.b])
                P.mm(dst, Ub[0:n, hh, :], gm[hh][0:n, 1, 0:n], False, False, [Ub.b, gm[hh].b], [pY.b], chain=True)
                P.mm(dst, tokV[0:n, hb:hb + 64], gm[hh][0:n, 3, 0:n], False, True, [tokV.b, gm[hh].b], [pY.b], chain=True)
            for hh in range(2):
                hb = hh * 64
                P.cp("act" if hh else "dve", yT_[hb:hb + 64, 0, 0:n], pY[0:64, hh * 128:hh * 128 + n], [pY.b], [yT_.b])
            pD = pg()
            for hh in range(2):
                hb = hh * 64
                P.mm(pD[0:64, hb:hb + 64], tokB[0:n, hb:hb + 64], Ub[0:n, hh, :], True, False, [tokB.b, Ub.b], [pD.b])
                P.mm(pD[0:64, hb:hb + 64], tokK[0:n, hb:hb + 64], tokV[0:n, hb:hb + 64], False, True, [tokK.b, tokV.b], [pD.b], chain=True)
            for hh in range(2):
                hb = hh * 64
                P.cp("act" if hh else "dve", Dp[hb:hb + 64, 0, :], pD[0:64, hb:hb + 64], [pD.b], [Dp.b])
            P.tt("dve", Hf[:, p, :], Hf[:, p, :], Dp[:, 0, :], ALU.add, [Hf.b, Dp.b], [Hf.b])
            P.ts("dve", Hf[:, p, :], Hf[:, p, :], r_eg[:, 0, n - 1:n], None, ALU.mult, None, [Hf.b, r_eg.b], [Hf.b])
            P.cp("act", Hb[:, p, :], Hf[:, p, :], [Hf.b], [Hb.b])
            def gn_steps(p=p, n=n, qoff=qoff, yT_=yT_, bon_=bon_):
                y_ = yT_[:, 0, 0:n]
                t_ = tmpA[:, 0:n]

                def sA():
                    pm = pg()
                    P.mm(pm[:, 0:n], bavg[:, :], y_, True, True, [bavg.b, yT_.b], [pm.b])
                    P.tt("dve", y_, y_, pm[:, 0:n], ALU.subtract, [yT_.b, pm.b], [yT_.b])
                    P.tt("dve", t_, y_, y_, ALU.mult, [yT_.b], [tmpA.b])

                def sB():
                    pvv = pg()
                    P.mm(pvv[:, 0:n], bavg[:, :], t_, True, True, [bavg.b, tmpA.b], [pvv.b])
                    P.act(t_, pvv[:, 0:n], AF.Ln, [pvv.b], [tmpA.b], bias=GN_EPS)
                    P.act(t_, t_, AF.Exp, [tmpA.b], [tmpA.b], scale=-0.5)

                def sC():
                    P.tt("dve", y_, y_, t_, ALU.mult, [yT_.b, tmpA.b], [yT_.b])
                    P.ts("dve", y_, y_, rp[:, p, 6:7], rp[:, p, 7:8], ALU.mult, ALU.add, [yT_.b, rp.b], [yT_.b])

                def sD():
                    P.tt("dve", y_, y_, bon_[:, 0, 0:n], ALU.add, [yT_.b, bon_.b], [yT_.b])
                    P.tt("dve", og[:, 3 + p, qoff:qoff + n], y_, gt[:, 3 + p, qoff:qoff + n], ALU.mult, [yT_.b, gtb[3 + p]], [ogb[3 + p]])
                return [sA, sB, sC, sD]
            flush_gn()
            pend_gn[0] = gn_steps()

    def rwkv_chunk(n, qoff):
        nlev = rwkv_pre(n, qoff)
        for p in range(3):
            rwkv_pair(n, qoff, p, nlev)

    def store_state(dst):
        pst = pg()
        for p in range(3):
            P.tr(pst[0:64, p * 128:(p + 1) * 128], Hf[:, p, :], identf[:, :], [Hf.b, identf.b], [pst.b])
        P.cp("act", stS[:, :, :], pst[0:64, 0:384].rearrange("v (h k) -> v h k", k=64), [pst.b], [stS.b])
        P.store(dst.rearrange("h v k -> v h k"), stS, stS[:, :, :])

    pti = [0]

    oun2 = [oun, ounB]
    hcnt = [0]
    pending = [None]

    def flush_tail():
        if pending[0] is not None:
            t_ = pending[0]
            pending[0] = None
            t_()

    def attention(nq, heads, kfn, vfn, bfn, entries, krows, out_fn):
        for h in heads:
            po = ps_o2[h % 2]
            nent = len(entries)

            def pv(i, ent, ptt):
                j, nk, q0, diag = ent
                vap, vb_ = vfn(h, j, nk)
                P.mm(po[0:65, q0:nq], vap, ptt[0:nk, 0:nq - q0], i == 0, i == nent - 1, [vb_, ptt.b], [po.b])
            prev = None
            for i, ent in enumerate(entries):
                j, nk, q0, diag = ent
                pss_ = ps_s2[pti[0] % 2]
                ptt = pts[pti[0] % 3]
                pti[0] += 1
                kap, kb = kfn(h, j, nk)
                qap, qb = qfn_cur[0](h, q0, nq)
                P.mm(pss_[0:nk, 0:nq - q0], kap, qap, True, True, [kb, qb], [pss_.b])
                bias = bfn(h, j, nk)
                if bias is not None:
                    P.act(ptt[0:nk, 0:nq - q0], pss_[0:nk, 0:nq - q0], AF.Exp, [pss_.b, bias[1]], [ptt.b], bias=bias[0])
                else:
                    P.act(ptt[0:nk, 0:nq - q0], pss_[0:nk, 0:nq - q0], AF.Exp, [pss_.b], [ptt.b])
                if diag:
                    P.asel(ptt[0:nk, 0:nk], ptt[0:nk, 0:nk], [[1, nk]], ALU.is_ge, 0.0, 0, -1, [ptt.b], [ptt.b])
                if prev is not None:
                    pv(*prev)
                prev = (i, ent, ptt)
            pv(*prev)
            ou = oun2[hcnt[0] % 2]
            hcnt[0] += 1
            P.cp("act", ou[0:65, 0:nq], po[0:65, 0:nq], [po.b], [ou.b])

            def tail(h=h, ou=ou, nq=nq, out_fn=out_fn):
                P.act(ou[64:65, 0:nq], ou[64:65, 0:nq], AF.Ln, [ou.b], [ou.b])
                P.act(ou[64:65, 0:nq], ou[64:65, 0:nq], AF.Exp, [ou.b], [ou.b], scale=-1.0)
                pb = pg()
                P.mm(pb[0:64, 0:nq], ones[64:65, 0:64], ou[64:65, 0:nq], True, True, [ones.b, ou.b], [pb.b])
                out_fn(h, pb, ou)
            flush_tail()
            pending[0] = tail

    qfn_cur = [None]

    def fox_heads(nq, heads, fox_entries):
        qfn_cur[0] = lambda h, q0, nq_: (qT[0:67, h, q0:nq_], qT.b)

        def fox_out(h, pb, ou):
            hb = (h % 2) * 64
            P.tt("dve", opair[hb:hb + 64, 0:nq], ou[0:64, 0:nq], pb[0:64, 0:nq], ALU.mult, [ou.b, pb.b], [opair.b])
            if h % 2 == 1:
                c = h // 2
                P.tt("dve", og[:, c, 0:nq], opair[:, 0:nq], gt[:, c, 0:nq], ALU.mult, [opair.b, gtb[c]], [ogb[c]])
        attention(nq, heads,
                  lambda h, j, nk: (kT[0:67, h, j * 128:j * 128 + nk], kvb[j]),
                  lambda h, j, nk: (Vaug[0:nk, j, h, :], kvb[j]),
                  lambda h, j, nk: (negc[0:nk, j, h:h + 1], kvb[j]),
                  fox_entries, 67, fox_out)

    def mem_heads(nq, heads, mem_k, mem_v, mem_kb, mem_vb):
        qfn_cur[0] = lambda h, q0, nq_: (mqT[0:64, h, q0:nq_], mqT.b)

        def mem_out(h, pb, ou):
            hb = (h % 2) * 64
            P.tt("dve", opair[hb:hb + 64, 0:nq], ou[0:64, 0:nq], pb[0:64, 0:nq], ALU.mult, [ou.b, pb.b], [opair.b])
            if h % 2 == 1:
                c = 6 + h // 2
                P.tt("dve", og[:, c, 0:nq], opair[:, 0:nq], gt[:, c, 0:nq], ALU.mult, [opair.b, gtb[c]], [ogb[c]])
        attention(nq, heads,
                  lambda h, j, nk: (mem_k[0:64, h, j * 128:j * 128 + nk], mem_kb),
                  lambda h, j, nk: (mem_v[0:nk, j, h, :], mem_vb),
                  lambda h, j, nk: None,
                  [(0, 128, 0, False), (1, 128, 0, False)], 64, mem_out)

    def run_attention(nq, fox_entries, mem_k, mem_v, mem_kb, mem_vb):
        fox_heads(nq, range(6), fox_entries)
        mem_heads(nq, range(4), mem_k, mem_v, mem_kb, mem_vb)

    wsti = [0]

    def out_proj(tiles):
        accs = [[pg(), pg()] for _ in tiles]
        for c in range(8):
            w = wst[wsti[0] % 2]
            wsti[0] += 1
            P.em.dma("sp", w[:, :], wo_bf[:, c, :], reads=[wo_b], writes=[w.b], dbuf=w.b)
            for ti, (n, qoff, src, dst) in enumerate(tiles):
                for cb in range(2):
                    pst = accs[ti][cb]
                    P.mm(pst[0:n, 0:512], og[:, c, qoff:qoff + n], w[:, cb * 512:(cb + 1) * 512], c == 0, c == 7, [ogb[c], w.b], [pst.b])
        for ti, (n, qoff, src, dst) in enumerate(tiles):
            xtile = xt[1]
            P.load(xtile, xtile[0:n, :], src)
            for cb in range(2):
                pst = accs[ti][cb]
                P.tt("dve", xtile[0:n, cb * 512:(cb + 1) * 512], xtile[0:n, cb * 512:(cb + 1) * 512], pst[0:n, 0:512], ALU.add,
                     [xtile.b, pst.b], [xtile.b])
            P.store(dst, xtile, xtile[0:n, :])

    w_in_v = w_in.rearrange("(c p) n -> p c n", p=128)
    w_out_v = w_out.rearrange("(c p) n -> p c n", p=128)
    w_mem_v = w_mem_kv.rearrange("(c p) n -> p c n", p=128)
    kq = 0
    Wm = Vaug[:, :, :, :].rearrange("p a h e -> p (a h e)")[:, 0:4096].rearrange("p (c n) -> p c n", n=512)
    for c in range(8):
        s_ = xt[kq % 2]
        P.load(s_, s_[:, 0:512], w_mem_v[:, c, :])
        P.cp(P.rot(), Wm[:, c, :], s_[:, 0:512], [s_.b], kvb)
        kq += 1
    for blk in range(2):
        xtile = xt[blk % 2]
        norm_T(memp[blk * 128:(blk + 1) * 128, :], 128, mng, xtile)
        pst = pg()
        for c in range(8):
            P.mm(pst[:, 0:512], xnT[:, c, :], Wm[:, c, :], c == 0, c == 7, [xnT.b] + kvb, [pst.b], chain=(c > 0))
        headnorm(128, pst, 4, gmk, tmpB, out_bf=mqa[:, :, :], out_bf_b=mqa.b)
        P.store(o_mkp[blk * 128:(blk + 1) * 128, :], tmpB, tmpB[:, 0:256])
        P.cp("act", tmpC[:, 0:256], pst[:, 256:512], [pst.b], [tmpC.b])
        P.store(o_mvp[blk * 128:(blk + 1) * 128, :], tmpC, tmpC[:, 0:256])
        P.cp("dve", mvaug[:, blk, :, 0:64], pst[:, 256:512].rearrange("p (h d) -> p h d", d=64), [pst.b], [mvaug.b])
        for h in range(4):
            P.tr(ps_tr[0:64, h * 128:(h + 1) * 128], mqa[:, h, :], identb[:, :], [mqa.b, identb.b], [ps_tr.b])
        P.cp("act", mkT[:, :, blk * 128:(blk + 1) * 128], ps_tr[0:64, 0:512].rearrange("p (h t) -> p h t", t=128), [ps_tr.b], [mkT.b])
    P.memset("pool", Vaug[:, :, :, :].rearrange("p a h e -> p (a h) e")[:, :, 64:65], 1.0, kvb)
    for c in range(8):
        for (c0, c1) in ((0, 1024), (1024, 2048), (2048, 3072), (3072, NIN)):
            s_ = xt[kq % 2]
            P.load(s_, s_[:, 0:c1 - c0], w_in_v[:, c, c0:c1])
            P.cp(P.rot(), Wb[:, c, c0:c1], s_[:, 0:c1 - c0], [s_.b], [Wb.b])
            kq += 1
        s_ = xt[kq % 2]
        P.load(s_, s_[:, :], w_out_v[:, c, :])
        w = wst[c % 2]
        P.cp(P.rot(), w[:, :], s_[:, :], [s_.b], [w.b])
        P.em.dma("pool", wo_bf[:, c, :], w[:, :], reads=[w.b], writes=[wo_b], dbuf=w.b)
        kq += 1

    P.memset("dve", Hf[:], 0.0, [Hf.b])
    P.memset("dve", Hb[:], 0.0, [Hb.b])
    NG = NT // 2
    for g in range(NG):
        entries = [(j, 128, 0, False) for j in range(2 * g)] + [(2 * g, 128, 0, True), (2 * g + 1, 128, 128, True)]
        for tt_ in range(2):
            t = 2 * g + tt_
            sl = slice(t * 128, (t + 1) * 128)
            token_tile(xp[sl, :], 128, t, tt_ * 128, o_fkp[sl, :], o_fvp[sl, :], o_flp[sl, :],
                       None if t == 0 else cc[(t - 1) % 2], cc[t % 2])
            fm_proj(128, tt_ * 128)
            if tt_ == 0:
                rwkv_chunk(128, 0)
            else:
                nlev = rwkv_pre(128, 128)
                for p in range(3):
                    rwkv_pair(128, 128, p, nlev)
                    fox_heads(NQ, (2 * p, 2 * p + 1), entries)
            if t == NT - 1:
                store_shift(o_rhp, 128)
            else:
                P.cp("dve", raw[:, :, 0:1], raw[:, :, 128:129], [raw.b], [raw.b])
        mem_heads(NQ, range(4), mkT, mvaug, mkT.b, mvaug.b)
        flush_tail()
        flush_gn()
        out_proj([(128, tt_ * 128, xp[(2 * g + tt_) * 128:(2 * g + tt_ + 1) * 128, :], o_yp[(2 * g + tt_) * 128:(2 * g + tt_ + 1) * 128, :])
                  for tt_ in range(2)])
    store_state(o_rsp)

    for b in range(SB_):
        sl = slice(b * SS, (b + 1) * SS)
        for j in range(8):
            ks = slice(j * 128, (j + 1) * 128)
            P.load(tmpB, tmpB[:, 0:384], cfk[b, ks, :])
            P.cp("act", kaug[:, :, 0:64], tmpB[:, 0:384].rearrange("p (h d) -> p h d", d=64), [tmpB.b], [kaug.b])
            k_to_T(128, j)
            P.load(tmpC, tmpC[:, 0:384], cfv[b, ks, :])
            P.cp("dve", Vaug[:, j, :, 0:64], tmpC[:, 0:384].rearrange("p (h d) -> p h d", d=64), [tmpC.b], [kvb[j]])
            P.load(lf, lf[:, :], cfl[b, ks, :])
            c_update(128, j, None if j == 0 else cc[(j - 1) % 2], cc[j % 2])
        P.load(stS, stS[:, :, :], srw[b].rearrange("h v k -> v h k"))
        pst = pg()
        for h in range(6):
            P.tr(pst[0:64, h * 64:(h + 1) * 64], stS[:, h, :], identf[0:64, 0:64], [stS.b, identf.b], [pst.b])
        for h in range(6):
            p, hb = h // 2, (h % 2) * 64
            P.cp("act" if h % 2 else "dve", Hf[hb:hb + 64, p, :], pst[0:64, h * 64:(h + 1) * 64], [pst.b], [Hf.b])
        P.cp("act", Hb[:, :, :], Hf[:, :, :], [Hf.b], [Hb.b])
        P.em.dma("sp", raw[:, 0:9, 0], ssh[b, 0:1152].rearrange("(b p) -> p b", p=128), reads=(), writes=[raw.b], dbuf=raw.b,
                 allow_slow_non_contiguous=True)
        P.em.dma("sp", raw[0:64, 9:10, 0], ssh[b, 1152:1216].rearrange("(b p) -> p b", p=64), reads=(), writes=[raw.b], dbuf=raw.b,
                 allow_slow_non_contiguous=True)
        P.em.dma("sp", raw[:, 10:13, 0], ssh[b, 1216:1600].rearrange("(b p) -> p b", p=128), reads=(), writes=[raw.b], dbuf=raw.b,
                 allow_slow_non_contiguous=True)
        token_tile(xsm[sl, :], SS, 8, 0, o_fks[sl, :], o_fvs[sl, :], o_fls[sl, :], cc[7 % 2], cc[8 % 2])
        fm_proj(SS, 0)
        rwkv_chunk(SS, 0)
        store_shift(o_rhs[b], SS)
        store_state(o_rss[b])
        for blk in range(2):
            ks = slice(blk * 128, (blk + 1) * 128)
            P.load(tmpB, tmpB[:, 0:256], cmk[b, ks, :])
            P.cp("act", mqa[:, :, :], tmpB[:, 0:256].rearrange("p (h d) -> p h d", d=64), [tmpB.b], [mqa.b])
            for h in range(4):
                P.tr(ps_tr[0:64, h * 128:(h + 1) * 128], mqa[:, h, :], identb[:, :], [mqa.b, identb.b], [ps_tr.b])
            P.cp("act", mkT[:, :, blk * 128:(blk + 1) * 128], ps_tr[0:64, 0:512].rearrange("p (h t) -> p h t", t=128), [ps_tr.b], [mkT.b])
            P.load(tmpC, tmpC[:, 0:256], cmv[b, ks, :])
            P.cp("dve", mvaug[:, blk, :, 0:64], tmpC[:, 0:256].rearrange("p (h d) -> p h d", d=64), [tmpC.b], [mvaug.b])
        entries = [(j, 128, 0, False) for j in range(8)] + [(8, SS, 0, True)]
        run_attention(SS, entries, mkT, mvaug, mkT.b, mvaug.b)
        flush_tail()
        flush_gn()
        out_proj([(SS, 0, xsm[sl, :], o_ys[sl, :])])

    P.em.final_wait("pool")
    P.em.replay()
    return nc


_NC = None


def kernel(x_prompt, x_sample, mem_prompt, cache_fox_k, cache_fox_v, cache_fox_logf,
           cache_mem_k, cache_mem_v, state_rwkv, state_rwkv_shift,
           norm_g, w_in, fox_q_g, fox_k_g, fox_b_f, rwkv_mu, rwkv_w0, rwkv_w_up, rwkv_a0,
           rwkv_a_up, rwkv_k_k, rwkv_k_a, rwkv_r_k, rwkv_gn_w, rwkv_gn_b,
           mem_norm_g, w_mem_kv, mem_q_g, mem_k_g, w_out):
    global _NC
    f = lambda a: np.ascontiguousarray(np.asarray(a, dtype=np.float32))
    if _NC is None:
        _NC = build()
    nc = _NC
    shared = dict(norm_g=f(norm_g[0]), w_in=f(w_in[0]), fox_q_g=f(fox_q_g[0]), fox_k_g=f(fox_k_g[0]),
                  fox_b_f=f(fox_b_f[0]), rwkv_mu=f(rwkv_mu[0]), rwkv_w0=f(rwkv_w0[0]), rwkv_w_up=f(rwkv_w_up[0]),
                  rwkv_a0=f(rwkv_a0[0]), rwkv_a_up=f(rwkv_a_up[0]), rwkv_k_k=f(rwkv_k_k[0]), rwkv_k_a=f(rwkv_k_a[0]),
                  rwkv_r_k=f(rwkv_r_k[0]), rwkv_gn_w=f(rwkv_gn_w[0]), rwkv_gn_b=f(rwkv_gn_b[0]),
                  mem_norm_g=f(mem_norm_g[0]), w_mem_kv=f(w_mem_kv[0]), mem_q_g=f(mem_q_g[0]), mem_k_g=f(mem_k_g[0]),
                  w_out=f(w_out[0]))
    in_maps = []
    for c in range(8):
        bs = slice(4 * c, 4 * c + 4)
        m = dict(shared)
        m.update(xp=f(x_prompt[c]), xsm=f(x_sample[bs]).reshape(64, D), memp=f(mem_prompt[c]),
                 cfk=f(cache_fox_k[0, bs]).reshape(4, PAST, 384), cfv=f(cache_fox_v[0, bs]).reshape(4, PAST, 384),
                 cfl=f(cache_fox_logf[0, bs]), cmk=f(cache_mem_k[0, bs]).reshape(4, 256, 256),
                 cmv=f(cache_mem_v[0, bs]).reshape(4, 256, 256), srw=f(state_rwkv[0, bs]),
                 ssh=f(state_rwkv_shift[0, bs]).reshape(4, 1600))
        in_maps.append(m)
    res = run_bass_kernel_spmd(nc, in_maps, core_ids=list(range(8)))
    R = res.results
    cat = lambda k: np.stack([np.asarray(R[c][k]) for c in range(8)])
    yp = cat("o_yp")
    ys = cat("o_ys").reshape(32, 16, D)
    fkp = cat("o_fkp").reshape(1, 8, T, 6, 64)
    fvp = cat("o_fvp").reshape(1, 8, T, 6, 64)
    flp = cat("o_flp").reshape(1, 8, T, 6)
    mkp = cat("o_mkp").reshape(1, 8, 256, 4, 64)
    mvp = cat("o_mvp").reshape(1, 8, 256, 4, 64)
    rsp = cat("o_rsp").reshape(1, 8, 6, 64, 64)
    rhp = cat("o_rhp").reshape(1, 8, 1, 1600)
    fks = cat("o_fks").reshape(1, 32, 16, 6, 64)
    fvs = cat("o_fvs").reshape(1, 32, 16, 6, 64)
    fls = cat("o_fls").reshape(1, 32, 16, 6)
    rss = cat("o_rss").reshape(1, 32, 6, 64, 64)
    rhs = cat("o_rhs").reshape(1, 32, 1, 1600)
    return (yp, ys, fkp, fvp, flp, mkp, mvp, rsp, rhp, fks, fvs, fls, rss, rhs)
```

```python
import numpy as np
from contextlib import ExitStack
import concourse.bass as bass
import concourse.mybir as mybir
from concourse.bass_utils import run_bass_kernel_spmd

F32 = mybir.dt.float32
BF16 = mybir.dt.bfloat16
AF = mybir.ActivationFunctionType
ALU = mybir.AluOpType
AX = mybir.AxisListType

D = 1024
T = 4096
NT = T // 128
SB_ = 4
SS = 16
PAST = 1024
NIN = 3654
EPS = 1e-6
GN_EPS = 64e-5
C_Q, C_K, C_V, C_F, C_GF = 0, 384, 768, 1152, 1158
C_RW = 1542
C_RR, C_RK, C_RV, C_WD, C_AD, C_GR = C_RW, C_RW + 384, C_RW + 768, C_RW + 1152, C_RW + 1184, C_RW + 1216
C_MQ, C_GM = 3142, 3398


class Buf:
    __slots__ = ("w", "r", "dsem", "dcnt", "name", "excl")

    def __init__(self, name="", excl=False):
        self.w = None
        self.r = []
        self.dsem = None
        self.dcnt = 0
        self.name = name
        self.excl = excl


class Emit:
    ENG = ("pe", "act", "dve", "pool", "sp")

    def __init__(self, nc, stack):
        self.nc = nc
        self.stack = stack
        self.ops = {e: [] for e in self.ENG}
        self.cnt = {e: 0 for e in self.ENG}
        self.sems = {}
        for e in self.ENG:
            self.sems[e] = stack.enter_context(nc.semaphore("sem_" + e))
        self.known = {e: {} for e in self.ENG}
        self.nd = 0
        self.dbufs = []

    def _waits(self, eng, reads, writes):
        need = {}
        for b in reads:
            if b.w is not None:
                k, v = b.w
                if need.get(k, 0) < v:
                    need[k] = v
            if b.excl:
                for k, v in b.r:
                    if k != eng and need.get(k, 0) < v:
                        need[k] = v
        for b in writes:
            if b.w is not None:
                k, v = b.w
                if need.get(k, 0) < v:
                    need[k] = v
            for k, v in b.r:
                if need.get(k, 0) < v:
                    need[k] = v
        out = []
        kn = self.known[eng]
        for k, v in need.items():
            if kn.get(k, 0) < v:
                kn[k] = v
                out.append((self.sems[k], v))
        return out

    def _mark(self, ev, reads, writes):
        for b in reads:
            b.r = [x for x in b.r if x[0] != ev[0]]
            b.r.append(ev)
        for b in writes:
            b.w = ev
            b.r = []

    def op(self, eng, fn, reads=(), writes=(), chain=False):
        prev_known_pe = self.known["pe"].get("pe", 0) if eng == "pe" else None
        wl = self._waits(eng, reads, writes)
        if chain and eng == "pe":
            sem_pe = self.sems["pe"]
            keep = []
            for s_, v_ in wl:
                if s_ is sem_pe and v_ == self.cnt["pe"]:
                    self.known["pe"]["pe"] = prev_known_pe
                    continue
                keep.append((s_, v_))
            wl = keep
        self.cnt[eng] += 1
        ev = (eng, self.cnt[eng])
        sem = self.sems[eng]

        def run(e, fn=fn, wl=wl, sem=sem):
            for s, v in wl:
                e.wait_ge(s, v)
            fn(e).then_inc(sem, 1)
        self.ops[eng].append(run)
        self._mark(ev, reads, writes)
        return ev

    def dma(self, eng, out, in_, reads=(), writes=(), dbuf=None, **kw):
        if dbuf.dsem is None:
            dbuf.dsem = {}
            dbuf.dcnt = {}
            self.dbufs.append(dbuf)
        if eng not in dbuf.dsem:
            self.nd += 1
            key = "d%d" % self.nd
            self.sems[key] = self.stack.enter_context(self.nc.semaphore("sem_" + key))
            dbuf.dsem[eng] = key
            dbuf.dcnt[eng] = 0
        wl = self._waits(eng, reads, writes)
        dbuf.dcnt[eng] += 16
        key = dbuf.dsem[eng]
        ev = (key, dbuf.dcnt[eng])
        sem = self.sems[key]

        def run(e, wl=wl, sem=sem, out=out, in_=in_, kw=kw):
            for s, v in wl:
                e.wait_ge(s, v)
            e.dma_start(out=out, in_=in_, **kw).then_inc(sem, 16)
        self.ops[eng].append(run)
        self._mark(ev, reads, writes)
        return ev

    def final_wait(self, eng):
        wl = []
        kn = self.known[eng]
        for b in self.dbufs:
            for q, key in b.dsem.items():
                v = b.dcnt[q]
                if kn.get(key, 0) < v:
                    kn[key] = v
                    wl.append((self.sems[key], v))

        def run(e, wl=wl):
            for s, v in wl:
                e.wait_ge(s, v)
        self.ops[eng].append(run)

    def replay(self):
        nc = self.nc
        ops = self.ops
        with nc.Block() as block:
            @block.tensor
            def _(e):
                for f in ops["pe"]:
                    f(e)

            @block.scalar
            def _(e):
                for f in ops["act"]:
                    f(e)

            @block.vector
            def _(e):
                for f in ops["dve"]:
                    f(e)

            @block.gpsimd
            def _(e):
                for f in ops["pool"]:
                    f(e)

            @block.sync
            def _(e):
                for f in ops["sp"]:
                    f(e)


class TT:
    def __init__(self, ap, name=""):
        self.t = ap
        self.b = Buf(name)

    def __getitem__(self, k):
        return self.t[k]


class Prog:
    def __init__(self):
        self.nc = bass.Bass("TRN2", target_bir_lowering=False)
        self.st = ExitStack()
        self.em = Emit(self.nc, self.st)
        self.rr = 0

    def dram(self, name, shape, kind):
        return self.nc.dram_tensor(name, list(shape), F32, kind=kind).ap()

    def sb(self, name, shape, dt=F32):
        return TT(self.st.enter_context(self.nc.sbuf_tensor(name, list(shape), dt)), name)

    def ps(self, name, shape, dt=F32):
        t = TT(self.st.enter_context(self.nc.psum_tensor(name, list(shape), dt)), name)
        t.b.excl = True
        return t

    def act(self, out, in_, func, r, w, **kw):
        return self.em.op("act", lambda e: e.activation(out=out, in_=in_, func=func, **kw), r, w)

    def tt(self, eng, out, in0, in1, op, r, w):
        return self.em.op(eng, lambda e: e.tensor_tensor(out=out, in0=in0, in1=in1, op=op), r, w)

    def ts(self, eng, out, in0, s1, s2, op0, op1, r, w):
        if s2 is None:
            return self.em.op(eng, lambda e: e.tensor_scalar(out=out, in0=in0, scalar1=s1, scalar2=None, op0=op0), r, w)
        return self.em.op(eng, lambda e: e.tensor_scalar(out=out, in0=in0, scalar1=s1, scalar2=s2, op0=op0, op1=op1), r, w)

    def stt(self, out, in0, scalar, in1, op0, op1, r, w):
        return self.em.op("dve", lambda e: e.scalar_tensor_tensor(out=out, in0=in0, scalar=scalar, in1=in1, op0=op0, op1=op1), r, w)

    def cp(self, eng, out, in_, r, w):
        if eng == "act":
            return self.em.op("act", lambda e: e.activation(out=out, in_=in_, func=AF.Copy), r, w)
        return self.em.op(eng, lambda e: e.tensor_copy(out=out, in_=in_), r, w)

    def mm(self, out, lhsT, rhs, start, stop, r, w, chain=False):
        return self.em.op("pe", lambda e: e.matmul(out, lhsT=lhsT, rhs=rhs, start=start, stop=stop), r, w, chain=chain)

    def tr(self, out, in_, ident, r, w):
        return self.em.op("pe", lambda e: e.transpose(out, in_, ident), r, w)

    def memset(self, eng, ap, val, w):
        return self.em.op(eng, lambda e: e.memset(ap, val), (), w)

    def asel(self, out, in_, pattern, op, fill, base, cm, r, w):
        return self.em.op("pool", lambda e: e.affine_select(out=out, in_=in_, pattern=pattern, compare_op=op,
                                                            fill=fill, base=base, channel_multiplier=cm), r, w)

    def load(self, out_tt, out_ap, in_ap, **kw):
        return self.em.dma("sp", out_ap, in_ap, reads=(), writes=[out_tt.b], dbuf=out_tt.b, **kw)

    def store(self, out_ap, in_tt, in_ap, **kw):
        return self.em.dma("pool", out_ap, in_ap, reads=[in_tt.b], writes=(), dbuf=in_tt.b, **kw)

    def rot(self):
        self.rr += 1
        return ("act", "dve", "pool")[self.rr % 3]


def build():
    P = Prog()
    nc = P.nc
    IN, OUT = "ExternalInput", "ExternalOutput"
    NQ = 256
    xp = P.dram("xp", [T, D], IN)
    xsm = P.dram("xsm", [SB_ * SS, D], IN)
    memp = P.dram("memp", [256, D], IN)
    cfk = P.dram("cfk", [SB_, PAST, 384], IN)
    cfv = P.dram("cfv", [SB_, PAST, 384], IN)
    cfl = P.dram("cfl", [SB_, PAST, 6], IN)
    cmk = P.dram("cmk", [SB_, 256, 256], IN)
    cmv = P.dram("cmv", [SB_, 256, 256], IN)
    srw = P.dram("srw", [SB_, 6, 64, 64], IN)
    ssh = P.dram("ssh", [SB_, 1600], IN)
    norm_g = P.dram("norm_g", [D], IN)
    w_in = P.dram("w_in", [D, NIN], IN)
    fox_q_g = P.dram("fox_q_g", [64], IN)
    fox_k_g = P.dram("fox_k_g", [64], IN)
    fox_b_f = P.dram("fox_b_f", [6], IN)
    rwkv_mu = P.dram("rwkv_mu", [1600], IN)
    rwkv_w0 = P.dram("rwkv_w0", [384], IN)
    rwkv_w_up = P.dram("rwkv_w_up", [32, 384], IN)
    rwkv_a0 = P.dram("rwkv_a0", [384], IN)
    rwkv_a_up = P.dram("rwkv_a_up", [32, 384], IN)
    rwkv_k_k = P.dram("rwkv_k_k", [384], IN)
    rwkv_k_a = P.dram("rwkv_k_a", [384], IN)
    rwkv_r_k = P.dram("rwkv_r_k", [384], IN)
    rwkv_gn_w = P.dram("rwkv_gn_w", [384], IN)
    rwkv_gn_b = P.dram("rwkv_gn_b", [384], IN)
    mem_norm_g = P.dram("mem_norm_g", [D], IN)
    w_mem_kv = P.dram("w_mem_kv", [D, 512], IN)
    mem_q_g = P.dram("mem_q_g", [64], IN)
    mem_k_g = P.dram("mem_k_g", [64], IN)
    w_out = P.dram("w_out", [D, D], IN)

    o_yp = P.dram("o_yp", [T, D], OUT)
    o_ys = P.dram("o_ys", [SB_ * SS, D], OUT)
    o_fkp = P.dram("o_fkp", [T, 384], OUT)
    o_fvp = P.dram("o_fvp", [T, 384], OUT)
    o_flp = P.dram("o_flp", [T, 6], OUT)
    o_mkp = P.dram("o_mkp", [256, 256], OUT)
    o_mvp = P.dram("o_mvp", [256, 256], OUT)
    o_rsp = P.dram("o_rsp", [6, 64, 64], OUT)
    o_rhp = P.dram("o_rhp", [1600], OUT)
    o_fks = P.dram("o_fks", [SB_ * SS, 384], OUT)
    o_fvs = P.dram("o_fvs", [SB_ * SS, 384], OUT)
    o_fls = P.dram("o_fls", [SB_ * SS, 6], OUT)
    o_rss = P.dram("o_rss", [SB_, 6, 64, 64], OUT)
    o_rhs = P.dram("o_rhs", [SB_, 1600], OUT)

    Wb = P.sb("Wb", [128, 8, NIN], BF16)
    wst = [P.sb("wst%d" % i, [128, D], BF16) for i in range(2)]
    wo_bf = nc.dram_tensor("wo_bf", [128, 8, D], BF16, kind="Internal").ap()
    wo_b = Buf("wo_bf")
    kT = P.sb("kT", [67, 6, T], BF16)
    Vaug = P.sb("Vaug", [128, NT, 6, 65], BF16)
    negc = P.sb("negc", [128, NT, 6])
    kvb = [Buf("kv%d" % i) for i in range(NT)]
    xt = [P.sb("xt%d" % i, [128, D]) for i in range(2)]
    xb = P.sb("xb", [128, D], BF16)
    xnT = P.sb("xnT", [128, 8, 128], BF16)
    identb = P.sb("identb", [128, 128], BF16)
    identf = P.sb("identf", [128, 128])
    trif = P.sb("trif", [128, 128])
    lastf = P.sb("lastf", [128, 128])
    bones = P.sb("bones", [128, 128])
    bavg = P.sb("bavg", [128, 128])
    ones = P.sb("ones", [128, 128])
    mask4 = P.sb("mask4", [128, 4, 128], BF16)
    msl = P.sb("msl", [128, 128], BF16)
    ng = P.sb("ng", [128, 8])
    mng = P.sb("mng", [128, 8])
    gq = P.sb("gq", [128, 64])
    gk = P.sb("gk", [128, 64])
    gmq = P.sb("gmq", [128, 64])
    gmk = P.sb("gmk", [128, 64])
    bfb = P.sb("bfb", [128, 6])
    small = P.sb("small", [128, 64])
    tmpA = P.sb("tmpA", [128, 384])
    tmpB = P.sb("tmpB", [128, 384])
    tmpC = P.sb("tmpC", [128, 384])
    qaug = P.sb("qaug", [128, 6, 67], BF16)
    kaug = P.sb("kaug", [128, 6, 67], BF16)
    mqa = P.sb("mqa", [128, 4, 64], BF16)
    cc = [P.sb("cc%d" % i, [128, 6]) for i in range(2)]
    cr = P.sb("cr", [128, 6])
    lf = P.sb("lf", [128, 6])
    raw = P.sb("raw", [128, 13, 129])
    xs = P.sb("xs", [128, 4, 128])
    xw = P.sb("xw", [64, 128])
    mu = P.sb("mu", [128, 13])
    rp = P.sb("rp", [128, 3, 8])
    lora = P.sb("lora", [64, 384], BF16)
    qT = P.sb("qT", [67, 6, NQ], BF16)
    mqT = P.sb("mqT", [64, 4, NQ], BF16)
    mkT = P.sb("mkT", [64, 4, 256], BF16)
    mvaug = P.sb("mvaug", [128, 2, 4, 65], BF16)
    gt = P.sb("gt", [128, 8, NQ], BF16)
    og = P.sb("og", [128, 8, NQ], BF16)
    gtb = [Buf("gt%d" % i) for i in range(8)]
    ogb = [Buf("og%d" % i) for i in range(8)]
    pts = [P.sb("pt%d" % i, [128, NQ], BF16) for i in range(3)]
    oun = P.sb("oun", [65, NQ])
    ounB = P.sb("ounB", [65, NQ])
    opair = P.sb("opair", [128, NQ])
    W3 = lambda name, dt=F32: P.sb(name, [128, 1, 128], dt)
    r_lw, r_a, r_g, r_eg, r_egm, r_eng = W3("r_lw"), W3("r_a"), W3("r_g"), W3("r_eg"), W3("r_egm"), W3("r_eng")
    r_kk, r_t1, r_t2 = W3("r_kk"), W3("r_t1"), W3("r_t2")
    r_yT2 = [W3("r_yT0"), W3("r_yT1")]
    r_bon2 = [W3("r_bon0"), W3("r_bon1")]
    rpc = [0]
    pend_gn = [None]

    def gn_step(k=1):
        for _ in range(k):
            if pend_gn[0]:
                pend_gn[0].pop(0)()

    def flush_gn():
        while pend_gn[0]:
            pend_gn[0].pop(0)()
    ART = P.sb("ART", [128, 1, 2, 128], BF16)
    BTt = W3("BTt", BF16)
    KTt = W3("KTt", BF16)
    vbt = W3("vbt", BF16)
    twd = P.sb("twd", [64, 128], BF16)
    tokA = P.sb("tokA", [128, 128], BF16)
    tokB = P.sb("tokB", [128, 128], BF16)
    tokK = P.sb("tokK", [128, 128], BF16)
    tokV = P.sb("tokV", [128, 128], BF16)
    gm = [P.sb("gm%d" % h, [128, 4, 128], BF16) for h in range(2)]
    PP = [[P.sb("PP%d_%d" % (h, i), [128, 2, 128], BF16) for i in range(2)] for h in range(2)]
    XX = [[P.sb("XX%d_%d" % (h, i), [128, 128], BF16) for i in range(2)] for h in range(2)]
    WT = P.sb("WT", [128, 1, 128], BF16)
    LVs = P.sb("LVs", [128, 2, 64], BF16)
    U0 = P.sb("U0", [128, 2, 64])
    Ub = P.sb("Ub", [128, 2, 64], BF16)
    Hf = P.sb("Hf", [128, 3, 64])
    Hb = P.sb("Hb", [128, 3, 64], BF16)
    Dp = P.sb("Dp", [128, 1, 64])
    stS = P.sb("stS", [64, 6, 64])

    ps_tr = P.ps("ps_tr", [128, 1024], BF16)
    ps_sA = P.ps("ps_sA", [128, 512])
    ps_sB = P.ps("ps_sB", [128, 512])
    ps_o = P.ps("ps_o", [128, 512])
    pgs = [P.ps("pg%d" % i, [128, 512]) for i in range(4)]
    pgi = [0]

    def pg():
        pgi[0] += 1
        return pgs[pgi[0] % len(pgs)]

    class View:
        def __init__(self, ap):
            self.t = ap
            self.b = Buf(excl=True)

        def __getitem__(self, k):
            return self.t[k]
    ps_s2 = [ps_sA, ps_sB]
    ps_o2 = [View(ps_o[:, 0:256]), View(ps_o[:, 256:512])]
    ps_o2[1].b = ps_o2[0].b

    P.memset("pool", identb[:], 0.0, [identb.b])
    P.asel(identb[:], identb[:], [[-1, 128]], ALU.not_equal, 1.0, 0, 1, [identb.b], [identb.b])
    P.memset("pool", identf[:], 0.0, [identf.b])
    P.asel(identf[:], identf[:], [[-1, 128]], ALU.not_equal, 1.0, 0, 1, [identf.b], [identf.b])
    P.memset("pool", trif[:], 1.0, [trif.b])
    P.asel(trif[:], trif[:], [[1, 128]], ALU.is_ge, 0.0, 0, -1, [trif.b], [trif.b])
    P.memset("pool", lastf[:], 1.0, [lastf.b])
    P.asel(lastf[:], lastf[:], [[0, 128]], ALU.is_ge, 0.0, -127, 1, [lastf.b], [lastf.b])
    P.memset("dve", bones[:], 0.0, [bones.b])
    P.memset("dve", bones[0:64, 0:64], 1.0, [bones.b])
    P.memset("dve", bones[64:128, 64:128], 1.0, [bones.b])
    P.ts("dve", bavg[:], bones[:], 1.0 / 64, None, ALU.mult, None, [bones.b], [bavg.b])
    P.memset("dve", ones[:], 1.0, [ones.b])
    P.memset("pool", mask4[:], 1.0, [mask4.b])
    for i in range(4):
        P.asel(mask4[:, i, :], mask4[:, i, :], [[1, 128]], ALU.is_ge, 0.0, (-1 if i % 2 == 0 else 0), -1, [mask4.b], [mask4.b])
    P.memset("pool", msl[:], 1.0, [msl.b])
    P.asel(msl[:], msl[:], [[-1, 128]], ALU.is_ge, 0.0, -1, 1, [msl.b], [msl.b])

    def cload(out_ap, in_ap, tt_, **kw):
        P.em.dma("sp", out_ap, in_ap, reads=(), writes=[tt_.b], dbuf=tt_.b, **kw)

    cload(ng[:], norm_g.rearrange("(c p) -> p c", p=128), ng, allow_slow_non_contiguous=True)
    cload(mng[:], mem_norm_g.rearrange("(c p) -> p c", p=128), mng, allow_slow_non_contiguous=True)
    for tl, src in ((gq, fox_q_g), (gk, fox_k_g), (gmq, mem_q_g), (gmk, mem_k_g)):
        cload(tl[:], src.partition_broadcast(128), tl)
    cload(bfb[:], fox_b_f.partition_broadcast(128), bfb)
    P.ts("dve", gq[:], gq[:], 0.125, None, ALU.mult, None, [gq.b], [gq.b])
    P.ts("dve", gmq[:], gmq[:], 0.125, None, ALU.mult, None, [gmq.b], [gmq.b])
    cload(mu[:, 0:9], rwkv_mu[0:1152].rearrange("(b p) -> p b", p=128), mu, allow_slow_non_contiguous=True)
    cload(mu[0:64, 9:10], rwkv_mu[1152:1216].rearrange("(b p) -> p b", p=64), mu, allow_slow_non_contiguous=True)
    cload(mu[:, 10:13], rwkv_mu[1216:1600].rearrange("(b p) -> p b", p=128), mu, allow_slow_non_contiguous=True)
    for i, src in enumerate((rwkv_w0, rwkv_a0, rwkv_k_k, rwkv_k_a, rwkv_k_a, rwkv_r_k, rwkv_gn_w, rwkv_gn_b)):
        cload(rp[:, :, i], src.rearrange("(b p) -> p b", p=128), rp, allow_slow_non_contiguous=True)
    P.ts("dve", rp[:, :, 4], rp[:, :, 4], -1.0, 1.0, ALU.mult, ALU.add, [rp.b], [rp.b])
    cload(tmpA[0:32, :], rwkv_w_up, tmpA)
    cload(tmpA[32:64, :], rwkv_a_up, tmpA)
    P.cp("dve", lora[:], tmpA[0:64, :], [tmpA.b], [lora.b])
    P.memset("dve", kaug[:, :, 64:67], 1.0, [kaug.b])
    P.memset("dve", raw[:], 0.0, [raw.b])
    P.memset("pool", mvaug[:, :, :, :].rearrange("p a h e -> p (a h) e")[:, :, 64:65], 1.0, [mvaug.b])

    def norm_T(src_dram, n, gtile, xtile):
        P.load(xtile, xtile[0:n, :], src_dram)
        P.em.op("act", lambda e: e.activation(out=xb[0:n, :], in_=xtile[0:n, :], func=AF.Square,
                                              accum_out=small[0:n, 0:1]), [xtile.b], [xb.b, small.b])
        P.act(small[0:n, 1:2], small[0:n, 0:1], AF.Ln, [small.b], [small.b], scale=1.0 / D, bias=EPS)
        P.act(small[0:n, 2:3], small[0:n, 1:2], AF.Exp, [small.b], [small.b], scale=-0.5)
        P.act(xb[0:n, :], xtile[0:n, :], AF.Copy, [xtile.b, small.b], [xb.b], scale=small[0:n, 2:3])
        for c in range(8):
            P.tr(ps_tr[:, c * 128:c * 128 + n], xb[0:n, c * 128:(c + 1) * 128], identb[0:n, 0:n], [xb.b, identb.b], [ps_tr.b])
        P.tt("dve", xnT[:, :, 0:n], ps_tr[:, :].rearrange("p (c t) -> p c t", t=128)[:, :, 0:n],
             gtile[:, :].unsqueeze(2).to_broadcast([128, 8, n]), ALU.mult, [ps_tr.b, gtile.b], [xnT.b])

    def proj_tm(n, c0, c1, pst, W=None):
        W = W or Wb
        for c in range(8):
            P.mm(pst[0:n, 0:c1 - c0], xnT[:, c, 0:n], W[:, c, c0:c1], c == 0, c == 7, [xnT.b, W.b], [pst.b], chain=(c > 0))

    def headnorm(n, pst, nh, gain, dst, out_bf=None, out_bf_b=None):
        w = nh * 64
        v3 = lambda ap: ap.rearrange("p (h d) -> p h d", d=64)
        P.act(tmpA[0:n, 0:w], pst[0:n, 0:w], AF.Square, [pst.b], [tmpA.b])
        P.em.op("dve", lambda e: e.tensor_reduce(out=small[0:n, 8:8 + nh], in_=v3(tmpA[0:n, 0:w]),
                                                 axis=AX.X, op=ALU.add), [tmpA.b], [small.b])
        P.act(small[0:n, 16:16 + nh], small[0:n, 8:8 + nh], AF.Ln, [small.b], [small.b], scale=1.0 / 64, bias=EPS)
        P.act(small[0:n, 24:24 + nh], small[0:n, 16:16 + nh], AF.Exp, [small.b], [small.b], scale=-0.5)
        P.tt("dve", v3(tmpA[0:n, 0:w]), v3(pst[0:n, 0:w]),
             small[0:n, 24:24 + nh].unsqueeze(2).to_broadcast([n, nh, 64]), ALU.mult, [pst.b, small.b], [tmpA.b])
        P.tt("dve", v3(dst[0:n, 0:w]), v3(tmpA[0:n, 0:w]),
             gain[0:n, :].unsqueeze(1).to_broadcast([n, nh, 64]), ALU.mult, [tmpA.b, gain.b], [dst.b])
        if out_bf is not None:
            P.cp("act", out_bf, v3(dst[0:n, 0:w]), [dst.b], [out_bf_b])

    def c_update(n, j, cprev, ccur):
        pst = pg()
        P.mm(pst[0:n, 0:6], trif[0:n, 0:n], lf[0:n, :], True, cprev is None, [trif.b, lf.b], [pst.b])
        if cprev is not None:
            P.mm(pst[0:n, 0:6], lastf[:, 0:n], cprev[:, :], False, True, [lastf.b, cprev.b], [pst.b], chain=True)
        P.cp("act", ccur[0:n, :], pst[0:n, 0:6], [pst.b], [ccur.b])
        P.ts("dve", negc[0:n, j, :], ccur[0:n, :], -1.0, None, ALU.mult, None, [ccur.b], [kvb[j]])

    def q_cpieces(n, ccur):
        P.cp("dve", qaug[0:n, :, 64], ccur[0:n, :], [ccur.b], [qaug.b])
        P.tt("dve", cr[0:n, :], ccur[0:n, :], qaug[0:n, :, 64], ALU.subtract, [ccur.b, qaug.b], [cr.b])
        P.cp("dve", qaug[0:n, :, 65], cr[0:n, :], [cr.b], [qaug.b])
        P.tt("dve", cr[0:n, :], cr[0:n, :], qaug[0:n, :, 65], ALU.subtract, [cr.b, qaug.b], [cr.b])
        P.cp("dve", qaug[0:n, :, 66], cr[0:n, :], [cr.b], [qaug.b])

    def k_to_T(n, j):
        for h in range(6):
            P.tr(ps_tr[0:67, h * 128:h * 128 + n], kaug[0:n, h, :], identb[0:n, 0:n], [kaug.b, identb.b], [ps_tr.b])
        P.cp("act", kT[:, :, j * 128:j * 128 + n], ps_tr[0:67, 0:768].rearrange("p (h t) -> p h t", t=128)[:, :, 0:n],
             [ps_tr.b], [kvb[j]])

    def q_to_T(n, qoff):
        for h in range(6):
            P.tr(ps_tr[0:67, h * 128:h * 128 + n], qaug[0:n, h, :], identb[0:n, 0:n], [qaug.b, identb.b], [ps_tr.b])
        P.cp("act", qT[:, :, qoff:qoff + n], ps_tr[0:67, 0:768].rearrange("p (h t) -> p h t", t=128)[:, :, 0:n],
             [ps_tr.b], [qT.b])

    def token_tile(src, n, j, qoff, o_k, o_v, o_l, cprev, ccur):
        xtile = xt[0]
        norm_T(src, n, ng, xtile)
        p0 = pg()
        proj_tm(n, C_Q, C_Q + 384, p0)
        headnorm(n, p0, 6, gq, tmpB, out_bf=qaug[0:n, :, 0:64], out_bf_b=qaug.b)
        p1 = pg()
        proj_tm(n, C_K, C_K + 384, p1)
        headnorm(n, p1, 6, gk, tmpB, out_bf=kaug[0:n, :, 0:64], out_bf_b=kaug.b)
        P.store(o_k, tmpB, tmpB[0:n, 0:384])
        p2 = pg()
        proj_tm(n, C_V, C_V + 390, p2)
        P.cp("act", tmpC[0:n, 0:384], p2[0:n, 0:384], [p2.b], [tmpC.b])
        P.store(o_v, tmpC, tmpC[0:n, 0:384])
        P.cp("dve", Vaug[0:n, j, :, 0:64], tmpC[0:n, 0:384].rearrange("p (h d) -> p h d", d=64), [tmpC.b], [kvb[j]])
        P.tt("dve", lf[0:n, :], p2[0:n, 384:390], bfb[0:n, :], ALU.add, [p2.b, bfb.b], [lf.b])
        P.act(lf[0:n, :], lf[0:n, :], AF.Exp, [lf.b], [lf.b], scale=-1.0)
        P.act(lf[0:n, :], lf[0:n, :], AF.Ln, [lf.b], [lf.b], bias=1.0)
        P.ts("dve", lf[0:n, :], lf[0:n, :], -1.0, None, ALU.mult, None, [lf.b], [lf.b])
        P.store(o_l, lf, lf[0:n, :])
        c_update(n, j, cprev, ccur)
        q_cpieces(n, ccur)
        k_to_T(n, j)
        q_to_T(n, qoff)
        p3 = pg()
        proj_tm(n, C_MQ, C_MQ + 256, p3)
        headnorm(n, p3, 4, gmq, tmpB, out_bf=mqa[0:n, :, :], out_bf_b=mqa.b)
        for h in range(4):
            P.tr(ps_tr[0:64, h * 128:h * 128 + n], mqa[0:n, h, :], identb[0:n, 0:n], [mqa.b, identb.b], [ps_tr.b])
        P.cp("act", mqT[:, :, qoff:qoff + n], ps_tr[0:64, 0:512].rearrange("p (h t) -> p h t", t=128)[:, :, 0:n],
             [ps_tr.b], [mqT.b])

    def fm_proj(n, qoff):
        gblocks = [(C_GF + 128 * i, i) for i in range(3)] + [(C_GM + 128 * i, 6 + i) for i in range(2)]
        for g0 in (0, 4):
            pst = pg()
            lst_ = gblocks[g0:g0 + 4]
            for jj, (c0, ch) in enumerate(lst_):
                for c in range(8):
                    P.mm(pst[:, jj * 128:jj * 128 + n], Wb[:, c, c0:c0 + 128], xnT[:, c, 0:n], c == 0, c == 7, [xnT.b, Wb.b], [pst.b], chain=(c > 0))
            for jj, (c0, ch) in enumerate(lst_):
                P.act(gt[:, ch, qoff:qoff + n], pst[:, jj * 128:jj * 128 + n], AF.Silu, [pst.b], [gtb[ch]])
        blocks = [(C_RW + 128 * i, 128) for i in range(9)] + [(C_WD, 64)] + [(C_GR + 128 * i, 128) for i in range(3)]
        for g0 in range(0, 13, 4):
            pst = pg()
            nb = min(4, 13 - g0)
            for jj in range(nb):
                c0, m = blocks[g0 + jj]
                for c in range(8):
                    P.mm(pst[0:m, jj * 128:jj * 128 + n], Wb[:, c, c0:c0 + m], xnT[:, c, 0:n], c == 0, c == 7, [xnT.b, Wb.b], [pst.b], chain=(c > 0))
            P.cp("act" if (g0 // 4) % 2 == 0 else "dve", raw[:, g0:g0 + nb, 1:1 + n], pst[:, 0:nb * 128].rearrange("p (b t) -> p b t", t=128)[:, :, 0:n], [pst.b], [raw.b])

    def store_shift(dst, n):
        P.store(dst[0:1152].rearrange("(b p) -> p b", p=128), raw, raw[:, 0:9, n], allow_slow_non_contiguous=True)
        P.store(dst[1152:1216].rearrange("(b p) -> p b", p=64), raw, raw[0:64, 9:10, n], allow_slow_non_contiguous=True)
        P.store(dst[1216:1600].rearrange("(b p) -> p b", p=128), raw, raw[:, 10:13, n], allow_slow_non_contiguous=True)

    def rwkv_pre(n, qoff):
        cur = lambda blk, p0=0, p1=128: raw[p0:p1, blk, 1:1 + n]
        prv = lambda blk, p0=0, p1=128: raw[p0:p1, blk, 0:n]
        P.tt("dve", xw[:, 0:n], prv(9, 0, 64), cur(9, 0, 64), ALU.subtract, [raw.b], [xw.b])
        P.stt(xw[:, 0:n], xw[:, 0:n], mu[0:64, 9:10], cur(9, 0, 64), ALU.mult, ALU.add, [xw.b, mu.b, raw.b], [xw.b])
        P.act(twd[0:32, 0:n], xw[0:32, 0:n], AF.Tanh, [xw.b], [twd.b])
        P.cp("dve", twd[32:64, 0:n], xw[32:64, 0:n], [xw.b], [twd.b])
        for p in range(3):
            blk = 10 + p
            P.tt("dve", xs[:, 3, 0:n], prv(blk), cur(blk), ALU.subtract, [raw.b], [xs.b])
            P.stt(xs[:, 3, 0:n], xs[:, 3, 0:n], mu[:, blk:blk + 1], cur(blk), ALU.mult, ALU.add, [xs.b, mu.b, raw.b], [xs.b])
            P.act(gt[:, 3 + p, qoff:qoff + n], xs[:, 3, 0:n], AF.Silu, [xs.b], [gtb[3 + p]])
        nlev = 0
        while (1 << nlev) < n:
            nlev += 1
        nlev -= 1
        return nlev

    def rwkv_pair(n, qoff, p, nlev):
        par = rpc[0] % 2
        rpc[0] += 1
        yT_, bon_ = r_yT2[par], r_bon2[par]
        cur = lambda blk, p0=0, p1=128: raw[p0:p1, blk, 1:1 + n]
        prv = lambda blk, p0=0, p1=128: raw[p0:p1, blk, 0:n]
        if True:
            S3 = lambda tl: tl[:, 0, 0:n]
            bc = lambda i: rp[:, p, i:i + 1]
            for i, blk in enumerate((p, 3 + p, 6 + p)):
                P.tt("dve", xs[:, i, 0:n], prv(blk), cur(blk), ALU.subtract, [raw.b], [xs.b])
                P.stt(xs[:, i, 0:n], xs[:, i, 0:n], mu[:, blk:blk + 1], cur(blk), ALU.mult, ALU.add, [xs.b, mu.b, raw.b], [xs.b])
            xr, xk, xv = xs[:, 0, 0:n], xs[:, 1, 0:n], xs[:, 2, 0:n]
            pw = pg()
            P.mm(pw[:, 0:n], lora[0:32, p * 128:(p + 1) * 128], twd[0:32, 0:n], True, True, [lora.b, twd.b], [pw.b])
            P.mm(pw[:, 128:128 + n], lora[32:64, p * 128:(p + 1) * 128], twd[32:64, 0:n], True, True, [lora.b, twd.b], [pw.b])
            P.ts("dve", S3(r_kk), xk, bc(2), None, ALU.mult, None, [xs.b, rp.b], [r_kk.b])
            P.tt("dve", S3(r_t1), S3(r_kk), S3(r_kk), ALU.mult, [r_kk.b], [r_t1.b])
            pss = pg()
            P.mm(pss[:, 0:n], bones[:, :], r_t1[:, 0, 0:n], True, True, [bones.b, r_t1.b], [pss.b])
            P.act(S3(r_lw), pw[:, 0:n], AF.Sigmoid, [pw.b, rp.b], [r_lw.b], bias=bc(0))
            P.ts("dve", S3(r_lw), S3(r_lw), -0.6065306597126334, None, ALU.mult, None, [r_lw.b], [r_lw.b])
            P.act(S3(r_a), pw[:, 128:128 + n], AF.Sigmoid, [pw.b, rp.b], [r_a.b], bias=bc(1))
            P.em.op("dve", lambda e: e.tensor_tensor_scan(out=r_g[:, 0, 0:n], data0=ones[:, 0:n], data1=r_lw[:, 0, 0:n],
                                                          initial=0.0, op0=ALU.mult, op1=ALU.add),
                    [ones.b, r_lw.b], [r_g.b])
            P.act(S3(r_eg), S3(r_g), AF.Exp, [r_g.b], [r_eg.b])
            P.act(S3(r_eng), S3(r_g), AF.Exp, [r_g.b], [r_eng.b], scale=-1.0)
            P.tt("dve", S3(r_egm), S3(r_g), S3(r_lw), ALU.subtract, [r_g.b, r_lw.b], [r_egm.b])
            P.act(S3(r_egm), S3(r_egm), AF.Exp, [r_egm.b], [r_egm.b])
            P.ts("dve", S3(r_t1), pss[:, 0:n], 1e-24, None, ALU.max, None, [pss.b], [r_t1.b])
            P.act(S3(r_t1), S3(r_t1), AF.Ln, [r_t1.b], [r_t1.b], scale=float(2 ** 40))
            P.act(S3(r_t1), S3(r_t1), AF.Exp, [r_t1.b], [r_t1.b], scale=-0.5, bias=13.862943611198906)
            P.tt("dve", S3(r_kk), S3(r_kk), S3(r_t1), ALU.mult, [r_kk.b, r_t1.b], [r_kk.b])
            P.ts("pool", S3(r_t2), S3(r_a), bc(3), bc(4), ALU.mult, ALU.add, [r_a.b, rp.b], [r_t2.b])
            P.tt("pool", S3(r_t2), S3(r_t2), xk, ALU.mult, [r_t2.b, xs.b], [r_t2.b])
            P.stt(ART[:, 0, 0, 0:n], S3(r_kk), -1.0, S3(r_egm), ALU.mult, ALU.mult, [r_kk.b, r_egm.b], [ART.b])
            P.tt("pool", ART[:, 0, 1, 0:n], xr, S3(r_eg), ALU.mult, [xs.b, r_eg.b], [ART.b])
            P.tt("dve", S3(r_t1), S3(r_a), S3(r_kk), ALU.mult, [r_a.b, r_kk.b], [r_t1.b])
            P.tt("dve", S3(BTt), S3(r_t1), S3(r_eng), ALU.mult, [r_t1.b, r_eng.b], [BTt.b])
            P.tt("pool", S3(KTt), S3(r_t2), S3(r_eng), ALU.mult, [r_t2.b, r_eng.b], [KTt.b])
            P.cp("act", S3(vbt), xv, [xs.b], [vbt.b])
            P.tt("dve", S3(r_t1), xr, S3(r_t2), ALU.mult, [xs.b, r_t2.b], [r_t1.b])
            P.ts("dve", S3(r_t1), S3(r_t1), bc(5), None, ALU.mult, None, [r_t1.b, rp.b], [r_t1.b])
            psb = pg()
            P.mm(psb[:, 0:n], bones[:, :], r_t1[:, 0, 0:n], True, True, [bones.b, r_t1.b], [psb.b])
            P.tt("dve", S3(bon_), psb[:, 0:n], xv, ALU.mult, [psb.b, xs.b], [bon_.b])
            for i, (src_ap, sb_) in enumerate(((ART[:, 0, 0, 0:n], ART.b), (BTt[:, 0, 0:n], BTt.b), (KTt[:, 0, 0:n], KTt.b), (vbt[:, 0, 0:n], vbt.b))):
                P.tr(ps_tr[0:n, i * 128:(i + 1) * 128], src_ap, identb[:, :], [sb_, identb.b], [ps_tr.b])
            for i, dstt in enumerate((tokA, tokB, tokK, tokV)):
                P.cp("act" if i % 2 else "dve", dstt[0:n, :], ps_tr[0:n, i * 128:(i + 1) * 128], [ps_tr.b], [dstt.b])
            for hh in range(2):
                hb = hh * 64
                g12 = pg()
                for a_ in range(2):
                    P.mm(g12[0:n, a_ * 128:a_ * 128 + n], BTt[hb:hb + 64, 0, 0:n], ART[hb:hb + 64, 0, a_, 0:n], True, True, [BTt.b, ART.b], [g12.b])
                    P.mm(g12[0:n, 256 + a_ * 128:256 + a_ * 128 + n], KTt[hb:hb + 64, 0, 0:n], ART[hb:hb + 64, 0, a_, 0:n], True, True, [KTt.b, ART.b], [g12.b])
                P.tt("dve", gm[hh][0:n, :, 0:n], g12[0:n, :].rearrange("s (a t) -> s a t", t=128)[:, :, 0:n], mask4[0:n, :, 0:n], ALU.mult,
                     [g12.b, mask4.b], [gm[hh].b])
                g3 = pg()
                P.mm(g3[0:n, 0:n], ART[hb:hb + 64, 0, 0, 0:n], BTt[hb:hb + 64, 0, 0:n], True, True, [ART.b, BTt.b], [g3.b])
                P.tt("dve", PP[hh][0][0:n, 0, 0:n], g3[0:n, 0:n], msl[0:n, 0:n], ALU.mult, [g3.b, msl.b], [PP[hh][0].b])
                P.cp("act", PP[hh][0][0:n, 1, 0:n], gm[hh][0:n, 0, 0:n], [gm[hh].b], [PP[hh][0].b])
                P.tt("pool", XX[hh][0][0:n, 0:n], gm[hh][0:n, 0, 0:n], identb[0:n, 0:n], ALU.add, [gm[hh].b, identb.b], [XX[hh][0].b])
            for hh in range(2):
                hb = hh * 64
                pl_ = pg()
                P.mm(pl_[0:n, 0:64], gm[hh][0:n, 2, 0:n], tokV[0:n, hb:hb + 64], True, True, [gm[hh].b, tokV.b], [pl_.b])
                P.cp("dve", LVs[0:n, hh, :], pl_[0:n, 0:64], [pl_.b], [LVs.b])
            def emit_sq(j):
                ci, ni = (j - 1) % 2, j % 2
                for hh in range(2):
                    psq = pg()
                    Pc = PP[hh][ci]
                    P.mm(psq[0:n, 0:n], Pc[0:n, 1, 0:n], Pc[0:n, 0, 0:n], True, True, [Pc.b], [psq.b])
                    if j < nlev:
                        P.mm(psq[0:n, 128:128 + n], Pc[0:n, 0, 0:n], Pc[0:n, 1, 0:n], True, True, [Pc.b], [psq.b])
                        P.cp("dve" if hh else "act", PP[hh][ni][0:n, :, 0:n], psq[0:n, 0:256].rearrange("s (a t) -> s a t", t=128)[:, :, 0:n], [psq.b], [PP[hh][ni].b])
                    else:
                        P.cp("dve" if hh else "act", PP[hh][ni][0:n, 0, 0:n], psq[0:n, 0:n], [psq.b], [PP[hh][ni].b])

            def emit_x(j):
                ci, ni = (j - 1) % 2, j % 2
                for hh in range(2):
                    px = pg()
                    P.mm(px[0:n, 0:n], PP[hh][ni][0:n, 0, 0:n], XX[hh][ci][0:n, 0:n], True, True, [PP[hh][ni].b, XX[hh][ci].b], [px.b])
                    P.tt("dve", XX[hh][ni][0:n, 0:n], px[0:n, 0:n], XX[hh][ci][0:n, 0:n], ALU.add, [px.b, XX[hh][ci].b], [XX[hh][ni].b])

            for j in range(1, nlev + 1):
                emit_sq(j)
                if j > 1:
                    emit_x(j - 1)
                gn_step(1 if nlev >= 4 else 2)
            emit_x(nlev)
            fi = nlev % 2
            pws = []
            for hh in range(2):
                hb = hh * 64
                TTm = XX[hh][fi]
                pa_, pu_ = pg(), pg()
                pws.append((pa_, pu_))
                P.mm(pa_[0:64, 0:n], tokA[0:n, hb:hb + 64], TTm[0:n, 0:n], True, True, [tokA.b, TTm.b], [pa_.b])
                P.mm(pu_[0:n, 0:64], TTm[0:n, 0:n], LVs[0:n, hh, :], True, True, [TTm.b, LVs.b], [pu_.b])
            for hh in range(2):
                hb = hh * 64
                pa_, pu_ = pws[hh]
                P.cp("act", WT[hb:hb + 64, 0, 0:n], pa_[0:64, 0:n], [pa_.b], [WT.b])
                P.cp("dve", U0[0:n, hh, :], pu_[0:n, 0:64], [pu_.b], [U0.b])
            flush_gn()
            pU = pg()
            for hh in range(2):
                hb = hh * 64
                P.mm(pU[0:n, hb:hb + 64], WT[hb:hb + 64, 0, 0:n], Hb[hb:hb + 64, p, :], True, True, [WT.b, Hb.b], [pU.b])
            P.tt("dve", Ub[0:n, :, :], pU[0:n, 0:128].rearrange("t (h v) -> t h v", v=64), U0[0:n, :, :], ALU.add, [pU.b, U0.b], [Ub.b])
            pY = pg()
            for hh in range(2):
                hb = hh * 64
                dst = pY[0:64, hh * 128:hh * 128 + n]
                P.mm(dst, Hb[hb:hb + 64, p, :], ART[hb:hb + 64, 0, 1, 0:n], True, False, [Hb.b, ART.b], [pY.b])
                P.mm(dst, Ub[0:n, hh, :], gm[hh][0:n, 1, 0:n], False, False, [Ub.b, gm[hh].b], [pY.b], chain=True)
                P.mm(dst, tokV[0:n, hb:hb + 64], gm[hh][0:n, 3, 0:n], False, True, [tokV.b, gm[hh].b], [pY.b], chain=True)
            for hh in range(2):
                hb = hh * 64
                P.cp("act" if hh else "dve", yT_[hb:hb + 64, 0, 0:n], pY[0:64, hh * 128:hh * 128 + n], [pY.b], [yT_.b])
            pD = pg()
            for hh in range(2):
                hb = hh * 64
                P.mm(pD[0:64, hb:hb + 64], tokB[0:n, hb:hb + 64], Ub[0:n, hh, :], True, False, [tokB.b, Ub.b], [pD.b])
                P.mm(pD[0:64, hb:hb + 64], tokK[0:n, hb:hb + 64], tokV[0:n, hb:hb + 64], False, True, [tokK.b, tokV.b], [pD.b], chain=True)
            for hh in range(2):
                hb = hh * 64
                P.cp("act" if hh else "dve", Dp[hb:hb + 64, 0, :], pD[0:64, hb:hb + 64], [pD.b], [Dp.b])
            P.tt("dve", Hf[:, p, :], Hf[:, p, :], Dp[:, 0, :], ALU.add, [Hf.b, Dp.b], [Hf.b])
            P.ts("dve", Hf[:, p, :], Hf[:, p, :], r_eg[:, 0, n - 1:n], None, ALU.mult, None, [Hf.b, r_eg.b], [Hf.b])
            P.cp("act", Hb[:, p, :], Hf[:, p, :], [Hf.b], [Hb.b])
            def gn_steps(p=p, n=n, qoff=qoff, yT_=yT_, bon_=bon_):
                y_ = yT_[:, 0, 0:n]
                t_ = tmpA[:, 0:n]

                def sA():
                    pm = pg()
                    P.mm(pm[:, 0:n], bavg[:, :], y_, True, True, [bavg.b, yT_.b], [pm.b])
                    P.tt("dve", y_, y_, pm[:, 0:n], ALU.subtract, [yT_.b, pm.b], [yT_.b])
                    P.tt("dve", t_, y_, y_, ALU.mult, [yT_.b], [tmpA.b])

                def sB():
                    pvv = pg()
                    P.mm(pvv[:, 0:n], bavg[:, :], t_, True, True, [bavg.b, tmpA.b], [pvv.b])
                    P.act(t_, pvv[:, 0:n], AF.Ln, [pvv.b], [tmpA.b], bias=GN_EPS)
                    P.act(t_, t_, AF.Exp, [tmpA.b], [tmpA.b], scale=-0.5)

                def sC():
                    P.tt("dve", y_, y_, t_, ALU.mult, [yT_.b, tmpA.b], [yT_.b])
                    P.ts("dve", y_, y_, rp[:, p, 6:7], rp[:, p, 7:8], ALU.mult, ALU.add, [yT_.b, rp.b], [yT_.b])

                def sD():
                    P.tt("dve", y_, y_, bon_[:, 0, 0:n], ALU.add, [yT_.b, bon_.b], [yT_.b])
                    P.tt("dve", og[:, 3 + p, qoff:qoff + n], y_, gt[:, 3 + p, qoff:qoff + n], ALU.mult, [yT_.b, gtb[3 + p]], [ogb[3 + p]])
                return [sA, sB, sC, sD]
            flush_gn()
            pend_gn[0] = gn_steps()

    def rwkv_chunk(n, qoff):
        nlev = rwkv_pre(n, qoff)
        for p in range(3):
            rwkv_pair(n, qoff, p, nlev)

    def store_state(dst):
        pst = pg()
        for p in range(3):
            P.tr(pst[0:64, p * 128:(p + 1) * 128], Hf[:, p, :], identf[:, :], [Hf.b, identf.b], [pst.b])
        P.cp("act", stS[:, :, :], pst[0:64, 0:384].rearrange("v (h k) -> v h k", k=64), [pst.b], [stS.b])
        P.store(dst.rearrange("h v k -> v h k"), stS, stS[:, :, :])

    pti = [0]

    oun2 = [oun, ounB]
    hcnt = [0]
    pending = [None]

    def flush_tail():
        if pending[0] is not None:
            t_ = pending[0]
            pending[0] = None
            t_()

    def attention(nq, heads, kfn, vfn, bfn, entries, krows, out_fn):
        for h in heads:
            po = ps_o2[h % 2]
            nent = len(entries)

            def pv(i, ent, ptt):
                j, nk, q0, diag = ent
                vap, vb_ = vfn(h, j, nk)
                P.mm(po[0:65, q0:nq], vap, ptt[0:nk, 0:nq - q0], i == 0, i == nent - 1, [vb_, ptt.b], [po.b])
            prev = None
            for i, ent in enumerate(entries):
                j, nk, q0, diag = ent
                pss_ = ps_s2[pti[0] % 2]
                ptt = pts[pti[0] % 3]
                pti[0] += 1
                kap, kb = kfn(h, j, nk)
                qap, qb = qfn_cur[0](h, q0, nq)
                P.mm(pss_[0:nk, 0:nq - q0], kap, qap, True, True, [kb, qb], [pss_.b])
                bias = bfn(h, j, nk)
                if bias is not None:
                    P.act(ptt[0:nk, 0:nq - q0], pss_[0:nk, 0:nq - q0], AF.Exp, [pss_.b, bias[1]], [ptt.b], bias=bias[0])
                else:
                    P.act(ptt[0:nk, 0:nq - q0], pss_[0:nk, 0:nq - q0], AF.Exp, [pss_.b], [ptt.b])
                if diag:
                    P.asel(ptt[0:nk, 0:nk], ptt[0:nk, 0:nk], [[1, nk]], ALU.is_ge, 0.0, 0, -1, [ptt.b], [ptt.b])
                if prev is not None:
                    pv(*prev)
                prev = (i, ent, ptt)
            pv(*prev)
            ou = oun2[hcnt[0] % 2]
            hcnt[0] += 1
            P.cp("act", ou[0:65, 0:nq], po[0:65, 0:nq], [po.b], [ou.b])

            def tail(h=h, ou=ou, nq=nq, out_fn=out_fn):
                P.act(ou[64:65, 0:nq], ou[64:65, 0:nq], AF.Ln, [ou.b], [ou.b])
                P.act(ou[64:65, 0:nq], ou[64:65, 0:nq], AF.Exp, [ou.b], [ou.b], scale=-1.0)
                pb = pg()
                P.mm(pb[0:64, 0:nq], ones[64:65, 0:64], ou[64:65, 0:nq], True, True, [ones.b, ou.b], [pb.b])
                out_fn(h, pb, ou)
            flush_tail()
            pending[0] = tail

    qfn_cur = [None]

    def fox_heads(nq, heads, fox_entries):
        qfn_cur[0] = lambda h, q0, nq_: (qT[0:67, h, q0:nq_], qT.b)

        def fox_out(h, pb, ou):
            hb = (h % 2) * 64
            P.tt("dve", opair[hb:hb + 64, 0:nq], ou[0:64, 0:nq], pb[0:64, 0:nq], ALU.mult, [ou.b, pb.b], [opair.b])
            if h % 2 == 1:
                c = h // 2
                P.tt("dve", og[:, c, 0:nq], opair[:, 0:nq], gt[:, c, 0:nq], ALU.mult, [opair.b, gtb[c]], [ogb[c]])
        attention(nq, heads,
                  lambda h, j, nk: (kT[0:67, h, j * 128:j * 128 + nk], kvb[j]),
                  lambda h, j, nk: (Vaug[0:nk, j, h, :], kvb[j]),
                  lambda h, j, nk: (negc[0:nk, j, h:h + 1], kvb[j]),
                  fox_entries, 67, fox_out)

    def mem_heads(nq, heads, mem_k, mem_v, mem_kb, mem_vb):
        qfn_cur[0] = lambda h, q0, nq_: (mqT[0:64, h, q0:nq_], mqT.b)

        def mem_out(h, pb, ou):
            hb = (h % 2) * 64
            P.tt("dve", opair[hb:hb + 64, 0:nq], ou[0:64, 0:nq], pb[0:64, 0:nq], ALU.mult, [ou.b, pb.b], [opair.b])
            if h % 2 == 1:
                c = 6 + h // 2
                P.tt("dve", og[:, c, 0:nq], opair[:, 0:nq], gt[:, c, 0:nq], ALU.mult, [opair.b, gtb[c]], [ogb[c]])
        attention(nq, heads,
                  lambda h, j, nk: (mem_k[0:64, h, j * 128:j * 128 + nk], mem_kb),
                  lambda h, j, nk: (mem_v[0:nk, j, h, :], mem_vb),
                  lambda h, j, nk: None,
                  [(0, 128, 0, False), (1, 128, 0, False)], 64, mem_out)

    def run_attention(nq, fox_entries, mem_k, mem_v, mem_kb, mem_vb):
        fox_heads(nq, range(6), fox_entries)
        mem_heads(nq, range(4), mem_k, mem_v, mem_kb, mem_vb)

    wsti = [0]

    def out_proj(tiles):
        accs = [[pg(), pg()] for _ in tiles]
        for c in range(8):
            w = wst[wsti[0] % 2]
            wsti[0] += 1
            P.em.dma("sp", w[:, :], wo_bf[:, c, :], reads=[wo_b], writes=[w.b], dbuf=w.b)
            for ti, (n, qoff, src, dst) in enumerate(tiles):
                for cb in range(2):
                    pst = accs[ti][cb]
                    P.mm(pst[0:n, 0:512], og[:, c, qoff:qoff + n], w[:, cb * 512:(cb + 1) * 512], c == 0, c == 7, [ogb[c], w.b], [pst.b])
        for ti, (n, qoff, src, dst) in enumerate(tiles):
            xtile = xt[1]
            P.load(xtile, xtile[0:n, :], src)
            for cb in range(2):
                pst = accs[ti][cb]
                P.tt("dve", xtile[0:n, cb * 512:(cb + 1) * 512], xtile[0:n, cb * 512:(cb + 1) * 512], pst[0:n, 0:512], ALU.add,
                     [xtile.b, pst.b], [xtile.b])
            P.store(dst, xtile, xtile[0:n, :])

    w_in_v = w_in.rearrange("(c p) n -> p c n", p=128)
    w_out_v = w_out.rearrange("(c p) n -> p c n", p=128)
    w_mem_v = w_mem_kv.rearrange("(c p) n -> p c n", p=128)
    kq = 0
    Wm = Vaug[:, :, :, :].rearrange("p a h e -> p (a h e)")[:, 0:4096].rearrange("p (c n) -> p c n", n=512)
    for c in range(8):
        s_ = xt[kq % 2]
        P.load(s_, s_[:, 0:512], w_mem_v[:, c, :])
        P.cp(P.rot(), Wm[:, c, :], s_[:, 0:512], [s_.b], kvb)
        kq += 1
    for blk in range(2):
        xtile = xt[blk % 2]
        norm_T(memp[blk * 128:(blk + 1) * 128, :], 128, mng, xtile)
        pst = pg()
        for c in range(8):
            P.mm(pst[:, 0:512], xnT[:, c, :], Wm[:, c, :], c == 0, c == 7, [xnT.b] + kvb, [pst.b], chain=(c > 0))
        headnorm(128, pst, 4, gmk, tmpB, out_bf=mqa[:, :, :], out_bf_b=mqa.b)
        P.store(o_mkp[blk * 128:(blk + 1) * 128, :], tmpB, tmpB[:, 0:256])
        P.cp("act", tmpC[:, 0:256], pst[:, 256:512], [pst.b], [tmpC.b])
        P.store(o_mvp[blk * 128:(blk + 1) * 128, :], tmpC, tmpC[:, 0:256])
        P.cp("dve", mvaug[:, blk, :, 0:64], pst[:, 256:512].rearrange("p (h d) -> p h d", d=64), [pst.b], [mvaug.b])
        for h in range(4):
            P.tr(ps_tr[0:64, h * 128:(h + 1) * 128], mqa[:, h, :], identb[:, :], [mqa.b, identb.b], [ps_tr.b])
        P.cp("act", mkT[:, :, blk * 128:(blk + 1) * 128], ps_tr[0:64, 0:512].rearrange("p (h t) -> p h t", t=128), [ps_tr.b], [mkT.b])
    P.memset("pool", Vaug[:, :, :, :].rearrange("p a h e -> p (a h) e")[:, :, 64:65], 1.0, kvb)
    for c in range(8):
        for (c0, c1) in ((0, 1024), (1024, 2048), (2048, 3072), (3072, NIN)):
            s_ = xt[kq % 2]
            P.load(s_, s_[:, 0:c1 - c0], w_in_v[:, c, c0:c1])
            P.cp(P.rot(), Wb[:, c, c0:c1], s_[:, 0:c1 - c0], [s_.b], [Wb.b])
            kq += 1
        s_ = xt[kq % 2]
        P.load(s_, s_[:, :], w_out_v[:, c, :])
        w = wst[c % 2]
        P.cp(P.rot(), w[:, :], s_[:, :], [s_.b], [w.b])
        P.em.dma("pool", wo_bf[:, c, :], w[:, :], reads=[w.b], writes=[wo_b], dbuf=w.b)
        kq += 1

    P.memset("dve", Hf[:], 0.0, [Hf.b])
    P.memset("dve", Hb[:], 0.0, [Hb.b])
    NG = NT // 2
    for g in range(NG):
        entries = [(j, 128, 0, False) for j in range(2 * g)] + [(2 * g, 128, 0, True), (2 * g + 1, 128, 128, True)]
        for tt_ in range(2):
            t = 2 * g + tt_
            sl = slice(t * 128, (t + 1) * 128)
            token_tile(xp[sl, :], 128, t, tt_ * 128, o_fkp[sl, :], o_fvp[sl, :], o_flp[sl, :],
                       None if t == 0 else cc[(t - 1) % 2], cc[t % 2])
            fm_proj(128, tt_ * 128)
            if tt_ == 0:
                rwkv_chunk(128, 0)
            else:
                nlev = rwkv_pre(128, 128)
                for p in range(3):
                    rwkv_pair(128, 128, p, nlev)
                    fox_heads(NQ, (2 * p, 2 * p + 1), entries)
            if t == NT - 1:
                store_shift(o_rhp, 128)
            else:
                P.cp("dve", raw[:, :, 0:1], raw[:, :, 128:129], [raw.b], [raw.b])
        mem_heads(NQ, range(4), mkT, mvaug, mkT.b, mvaug.b)
        flush_tail()
        flush_gn()
        out_proj([(128, tt_ * 128, xp[(2 * g + tt_) * 128:(2 * g + tt_ + 1) * 128, :], o_yp[(2 * g + tt_) * 128:(2 * g + tt_ + 1) * 128, :])
                  for tt_ in range(2)])
    store_state(o_rsp)

    for b in range(SB_):
        sl = slice(b * SS, (b + 1) * SS)
        for j in range(8):
            ks = slice(j * 128, (j + 1) * 128)
            P.load(tmpB, tmpB[:, 0:384], cfk[b, ks, :])
            P.cp("act", kaug[:, :, 0:64], tmpB[:, 0:384].rearrange("p (h d) -> p h d", d=64), [tmpB.b], [kaug.b])
            k_to_T(128, j)
            P.load(tmpC, tmpC[:, 0:384], cfv[b, ks, :])
            P.cp("dve", Vaug[:, j, :, 0:64], tmpC[:, 0:384].rearrange("p (h d) -> p h d", d=64), [tmpC.b], [kvb[j]])
            P.load(lf, lf[:, :], cfl[b, ks, :])
            c_update(128, j, None if j == 0 else cc[(j - 1) % 2], cc[j % 2])
        P.load(stS, stS[:, :, :], srw[b].rearrange("h v k -> v h k"))
        pst = pg()
        for h in range(6):
            P.tr(pst[0:64, h * 64:(h + 1) * 64], stS[:, h, :], identf[0:64, 0:64], [stS.b, identf.b], [pst.b])
        for h in range(6):
            p, hb = h // 2, (h % 2) * 64
            P.cp("act" if h % 2 else "dve", Hf[hb:hb + 64, p, :], pst[0:64, h * 64:(h + 1) * 64], [pst.b], [Hf.b])
        P.cp("act", Hb[:, :, :], Hf[:, :, :], [Hf.b], [Hb.b])
        P.em.dma("sp", raw[:, 0:9, 0], ssh[b, 0:1152].rearrange("(b p) -> p b", p=128), reads=(), writes=[raw.b], dbuf=raw.b,
                 allow_slow_non_contiguous=True)
        P.em.dma("sp", raw[0:64, 9:10, 0], ssh[b, 1152:1216].rearrange("(b p) -> p b", p=64), reads=(), writes=[raw.b], dbuf=raw.b,
                 allow_slow_non_contiguous=True)
        P.em.dma("sp", raw[:, 10:13, 0], ssh[b, 1216:1600].rearrange("(b p) -> p b", p=128), reads=(), writes=[raw.b], dbuf=raw.b,
                 allow_slow_non_contiguous=True)
        token_tile(xsm[sl, :], SS, 8, 0, o_fks[sl, :], o_fvs[sl, :], o_fls[sl, :], cc[7 % 2], cc[8 % 2])
        fm_proj(SS, 0)
        rwkv_chunk(SS, 0)
        store_shift(o_rhs[b], SS)
        store_state(o_rss[b])
        for blk in range(2):
            ks = slice(blk * 128, (blk + 1) * 128)
            P.load(tmpB, tmpB[:, 0:256], cmk[b, ks, :])
            P.cp("act", mqa[:, :, :], tmpB[:, 0:256].rearrange("p (h d) -> p h d", d=64), [tmpB.b], [mqa.b])
            for h in range(4):
                P.tr(ps_tr[0:64, h * 128:(h + 1) * 128], mqa[:, h, :], identb[:, :], [mqa.b, identb.b], [ps_tr.b])
            P.cp("act", mkT[:, :, blk * 128:(blk + 1) * 128], ps_tr[0:64, 0:512].rearrange("p (h t) -> p h t", t=128), [ps_tr.b], [mkT.b])
            P.load(tmpC, tmpC[:, 0:256], cmv[b, ks, :])
            P.cp("dve", mvaug[:, blk, :, 0:64], tmpC[:, 0:256].rearrange("p (h d) -> p h d", d=64), [tmpC.b], [mvaug.b])
        entries = [(j, 128, 0, False) for j in range(8)] + [(8, SS, 0, True)]
        run_attention(SS, entries, mkT, mvaug, mkT.b, mvaug.b)
        flush_tail()
        flush_gn()
        out_proj([(SS, 0, xsm[sl, :], o_ys[sl, :])])

    P.em.final_wait("pool")
    P.em.replay()
    return nc


_NC = None


def kernel(x_prompt, x_sample, mem_prompt, cache_fox_k, cache_fox_v, cache_fox_logf,
           cache_mem_k, cache_mem_v, state_rwkv, state_rwkv_shift,
           norm_g, w_in, fox_q_g, fox_k_g, fox_b_f, rwkv_mu, rwkv_w0, rwkv_w_up, rwkv_a0,
           rwkv_a_up, rwkv_k_k, rwkv_k_a, rwkv_r_k, rwkv_gn_w, rwkv_gn_b,
           mem_norm_g, w_mem_kv, mem_q_g, mem_k_g, w_out):
    global _NC
    f = lambda a: np.ascontiguousarray(np.asarray(a, dtype=np.float32))
    if _NC is None:
        _NC = build()
    nc = _NC
    shared = dict(norm_g=f(norm_g[0]), w_in=f(w_in[0]), fox_q_g=f(fox_q_g[0]), fox_k_g=f(fox_k_g[0]),
                  fox_b_f=f(fox_b_f[0]), rwkv_mu=f(rwkv_mu[0]), rwkv_w0=f(rwkv_w0[0]), rwkv_w_up=f(rwkv_w_up[0]),
                  rwkv_a0=f(rwkv_a0[0]), rwkv_a_up=f(rwkv_a_up[0]), rwkv_k_k=f(rwkv_k_k[0]), rwkv_k_a=f(rwkv_k_a[0]),
                  rwkv_r_k=f(rwkv_r_k[0]), rwkv_gn_w=f(rwkv_gn_w[0]), rwkv_gn_b=f(rwkv_gn_b[0]),
                  mem_norm_g=f(mem_norm_g[0]), w_mem_kv=f(w_mem_kv[0]), mem_q_g=f(mem_q_g[0]), mem_k_g=f(mem_k_g[0]),
                  w_out=f(w_out[0]))
    in_maps = []
    for c in range(8):
        bs = slice(4 * c, 4 * c + 4)
        m = dict(shared)
        m.update(xp=f(x_prompt[c]), xsm=f(x_sample[bs]).reshape(64, D), memp=f(mem_prompt[c]),
                 cfk=f(cache_fox_k[0, bs]).reshape(4, PAST, 384), cfv=f(cache_fox_v[0, bs]).reshape(4, PAST, 384),
                 cfl=f(cache_fox_logf[0, bs]), cmk=f(cache_mem_k[0, bs]).reshape(4, 256, 256),
                 cmv=f(cache_mem_v[0, bs]).reshape(4, 256, 256), srw=f(state_rwkv[0, bs]),
                 ssh=f(state_rwkv_shift[0, bs]).reshape(4, 1600))
        in_maps.append(m)
    res = run_bass_kernel_spmd(nc, in_maps, core_ids=list(range(8)))
    R = res.results
    cat = lambda k: np.stack([np.asarray(R[c][k]) for c in range(8)])
    yp = cat("o_yp")
    ys = cat("o_ys").reshape(32, 16, D)
    fkp = cat("o_fkp").reshape(1, 8, T, 6, 64)
    fvp = cat("o_fvp").reshape(1, 8, T, 6, 64)
    flp = cat("o_flp").reshape(1, 8, T, 6)
    mkp = cat("o_mkp").reshape(1, 8, 256, 4, 64)
    mvp = cat("o_mvp").reshape(1, 8, 256, 4, 64)
    rsp = cat("o_rsp").reshape(1, 8, 6, 64, 64)
    rhp = cat("o_rhp").reshape(1, 8, 1, 1600)
    fks = cat("o_fks").reshape(1, 32, 16, 6, 64)
    fvs = cat("o_fvs").reshape(1, 32, 16, 6, 64)
    fls = cat("o_fls").reshape(1, 32, 16, 6)
    rss = cat("o_rss").reshape(1, 32, 6, 64, 64)
    rhs = cat("o_rhs").reshape(1, 32, 1, 1600)
    return (yp, ys, fkp, fvp, flp, mkp, mvp, rsp, rhp, fks, fvs, fls, rss, rhs)
```

```python
import numpy as np
from contextlib import ExitStack
import concourse.bass as bass
import concourse.mybir as mybir
from concourse.bass_utils import run_bass_kernel_spmd

F32 = mybir.dt.float32
BF16 = mybir.dt.bfloat16
AF = mybir.ActivationFunctionType
ALU = mybir.AluOpType
AX = mybir.AxisListType

D = 1024
T = 4096
NT = T // 128
SB_ = 4
SS = 16
PAST = 1024
NIN = 3654
EPS = 1e-6
GN_EPS = 64e-5
C_Q, C_K, C_V, C_F, C_GF = 0, 384, 768, 1152, 1158
C_RW = 1542
C_RR, C_RK, C_RV, C_WD, C_AD, C_GR = C_RW, C_RW + 384, C_RW + 768, C_RW + 1152, C_RW + 1184, C_RW + 1216
C_MQ, C_GM = 3142, 3398


class Buf:
    __slots__ = ("w", "r", "dsem", "dcnt", "name", "excl")

    def __init__(self, name="", excl=False):
        self.w = None
        self.r = []
        self.dsem = None
        self.dcnt = 0
        self.name = name
        self.excl = excl


class Emit:
    ENG = ("pe", "act", "dve", "pool", "sp")

    def __init__(self, nc, stack):
        self.nc = nc
        self.stack = stack
        self.ops = {e: [] for e in self.ENG}
        self.cnt = {e: 0 for e in self.ENG}
        self.sems = {}
        for e in self.ENG:
            self.sems[e] = stack.enter_context(nc.semaphore("sem_" + e))
        self.known = {e: {} for e in self.ENG}
        self.nd = 0
        self.dbufs = []

    def _waits(self, eng, reads, writes):
        need = {}
        for b in reads:
            if b.w is not None:
                k, v = b.w
                if need.get(k, 0) < v:
                    need[k] = v
            if b.excl:
                for k, v in b.r:
                    if k != eng and need.get(k, 0) < v:
                        need[k] = v
        for b in writes:
            if b.w is not None:
                k, v = b.w
                if need.get(k, 0) < v:
                    need[k] = v
            for k, v in b.r:
                if need.get(k, 0) < v:
                    need[k] = v
        out = []
        kn = self.known[eng]
        for k, v in need.items():
            if kn.get(k, 0) < v:
                kn[k] = v
                out.append((self.sems[k], v))
        return out

    def _mark(self, ev, reads, writes):
        for b in reads:
            b.r = [x for x in b.r if x[0] != ev[0]]
            b.r.append(ev)
        for b in writes:
            b.w = ev
            b.r = []

    def op(self, eng, fn, reads=(), writes=(), chain=False):
        prev_known_pe = self.known["pe"].get("pe", 0) if eng == "pe" else None
        wl = self._waits(eng, reads, writes)
        if chain and eng == "pe":
            sem_pe = self.sems["pe"]
            keep = []
            for s_, v_ in wl:
                if s_ is sem_pe and v_ == self.cnt["pe"]:
                    self.known["pe"]["pe"] = prev_known_pe
                    continue
                keep.append((s_, v_))
            wl = keep
        self.cnt[eng] += 1
        ev = (eng, self.cnt[eng])
        sem = self.sems[eng]

        def run(e, fn=fn, wl=wl, sem=sem):
            for s, v in wl:
                e.wait_ge(s, v)
            fn(e).then_inc(sem, 1)
        self.ops[eng].append(run)
        self._mark(ev, reads, writes)
        return ev

    def dma(self, eng, out, in_, reads=(), writes=(), dbuf=None, **kw):
        if dbuf.dsem is None:
            dbuf.dsem = {}
            dbuf.dcnt = {}
            self.dbufs.append(dbuf)
        if eng not in dbuf.dsem:
            self.nd += 1
            key = "d%d" % self.nd
            self.sems[key] = self.stack.enter_context(self.nc.semaphore("sem_" + key))
            dbuf.dsem[eng] = key
            dbuf.dcnt[eng] = 0
        wl = self._waits(eng, reads, writes)
        dbuf.dcnt[eng] += 16
        key = dbuf.dsem[eng]
        ev = (key, dbuf.dcnt[eng])
        sem = self.sems[key]

        def run(e, wl=wl, sem=sem, out=out, in_=in_, kw=kw):
            for s, v in wl:
                e.wait_ge(s, v)
            e.dma_start(out=out, in_=in_, **kw).then_inc(sem, 16)
        self.ops[eng].append(run)
        self._mark(ev, reads, writes)
        return ev

    def final_wait(self, eng):
        wl = []
        kn = self.known[eng]
        for b in self.dbufs:
            for q, key in b.dsem.items():
                v = b.dcnt[q]
                if kn.get(key, 0) < v:
                    kn[key] = v
                    wl.append((self.sems[key], v))

        def run(e, wl=wl):
            for s, v in wl:
                e.wait_ge(s, v)
        self.ops[eng].append(run)

    def replay(self):
        nc = self.nc
        ops = self.ops
        with nc.Block() as block:
            @block.tensor
            def _(e):
                for f in ops["pe"]:
                    f(e)

            @block.scalar
            def _(e):
                for f in ops["act"]:
                    f(e)

            @block.vector
            def _(e):
                for f in ops["dve"]:
                    f(e)

            @block.gpsimd
            def _(e):
                for f in ops["pool"]:
                    f(e)

            @block.sync
            def _(e):
                for f in ops["sp"]:
                    f(e)


class TT:
    def __init__(self, ap, name=""):
        self.t = ap
        self.b = Buf(name)

    def __getitem__(self, k):
        return self.t[k]


class Prog:
    def __init__(self):
        self.nc = bass.Bass("TRN2", target_bir_lowering=False)
        self.st = ExitStack()
        self.em = Emit(self.nc, self.st)
        self.rr = 0

    def dram(self, name, shape, kind):
        return self.nc.dram_tensor(name, list(shape), F32, kind=kind).ap()

    def sb(self, name, shape, dt=F32):
        return TT(self.st.enter_context(self.nc.sbuf_tensor(name, list(shape), dt)), name)

    def ps(self, name, shape, dt=F32):
        t = TT(self.st.enter_context(self.nc.psum_tensor(name, list(shape), dt)), name)
        t.b.excl = True
        return t

    def act(self, out, in_, func, r, w, **kw):
        return self.em.op("act", lambda e: e.activation(out=out, in_=in_, func=func, **kw), r, w)

    def tt(self, eng, out, in0, in1, op, r, w):
        return self.em.op(eng, lambda e: e.tensor_tensor(out=out, in0=in0, in1=in1, op=op), r, w)

    def ts(self, eng, out, in0, s1, s2, op0, op1, r, w):
        if s2 is None:
            return self.em.op(eng, lambda e: e.tensor_scalar(out=out, in0=in0, scalar1=s1, scalar2=None, op0=op0), r, w)
        return self.em.op(eng, lambda e: e.tensor_scalar(out=out, in0=in0, scalar1=s1, scalar2=s2, op0=op0, op1=op1), r, w)

    def stt(self, out, in0, scalar, in1, op0, op1, r, w):
        return self.em.op("dve", lambda e: e.scalar_tensor_tensor(out=out, in0=in0, scalar=scalar, in1=in1, op0=op0, op1=op1), r, w)

    def cp(self, eng, out, in_, r, w):
        if eng == "act":
            return self.em.op("act", lambda e: e.activation(out=out, in_=in_, func=AF.Copy), r, w)
        return self.em.op(eng, lambda e: e.tensor_copy(out=out, in_=in_), r, w)

    def mm(self, out, lhsT, rhs, start, stop, r, w, chain=False):
        return self.em.op("pe", lambda e: e.matmul(out, lhsT=lhsT, rhs=rhs, start=start, stop=stop), r, w, chain=chain)

    def tr(self, out, in_, ident, r, w):
        return self.em.op("pe", lambda e: e.transpose(out, in_, ident), r, w)

    def memset(self, eng, ap, val, w):
        return self.em.op(eng, lambda e: e.memset(ap, val), (), w)

    def asel(self, out, in_, pattern, op, fill, base, cm, r, w):
        return self.em.op("pool", lambda e: e.affine_select(out=out, in_=in_, pattern=pattern, compare_op=op,
                                                            fill=fill, base=base, channel_multiplier=cm), r, w)

    def load(self, out_tt, out_ap, in_ap, **kw):
        return self.em.dma("sp", out_ap, in_ap, reads=(), writes=[out_tt.b], dbuf=out_tt.b, **kw)

    def store(self, out_ap, in_tt, in_ap, **kw):
        return self.em.dma("pool", out_ap, in_ap, reads=[in_tt.b], writes=(), dbuf=in_tt.b, **kw)

    def rot(self):
        self.rr += 1
        return ("act", "dve", "pool")[self.rr % 3]


def build():
    P = Prog()
    nc = P.nc
    IN, OUT = "ExternalInput", "ExternalOutput"
    NQ = 256
    xp = P.dram("xp", [T, D], IN)
    xsm = P.dram("xsm", [SB_ * SS, D], IN)
    memp = P.dram("memp", [256, D], IN)
    cfk = P.dram("cfk", [SB_, PAST, 384], IN)
    cfv = P.dram("cfv", [SB_, PAST, 384], IN)
    cfl = P.dram("cfl", [SB_, PAST, 6], IN)
    cmk = P.dram("cmk", [SB_, 256, 256], IN)
    cmv = P.dram("cmv", [SB_, 256, 256], IN)
    srw = P.dram("srw", [SB_, 6, 64, 64], IN)
    ssh = P.dram("ssh", [SB_, 1600], IN)
    norm_g = P.dram("norm_g", [D], IN)
    w_in = P.dram("w_in", [D, NIN], IN)
    fox_q_g = P.dram("fox_q_g", [64], IN)
    fox_k_g = P.dram("fox_k_g", [64], IN)
    fox_b_f = P.dram("fox_b_f", [6], IN)
    rwkv_mu = P.dram("rwkv_mu", [1600], IN)
    rwkv_w0 = P.dram("rwkv_w0", [384], IN)
    rwkv_w_up = P.dram("rwkv_w_up", [32, 384], IN)
    rwkv_a0 = P.dram("rwkv_a0", [384], IN)
    rwkv_a_up = P.dram("rwkv_a_up", [32, 384], IN)
    rwkv_k_k = P.dram("rwkv_k_k", [384], IN)
    rwkv_k_a = P.dram("rwkv_k_a", [384], IN)
    rwkv_r_k = P.dram("rwkv_r_k", [384], IN)
    rwkv_gn_w = P.dram("rwkv_gn_w", [384], IN)
    rwkv_gn_b = P.dram("rwkv_gn_b", [384], IN)
    mem_norm_g = P.dram("mem_norm_g", [D], IN)
    w_mem_kv = P.dram("w_mem_kv", [D, 512], IN)
    mem_q_g = P.dram("mem_q_g", [64], IN)
    mem_k_g = P.dram("mem_k_g", [64], IN)
    w_out = P.dram("w_out", [D, D], IN)

    o_yp = P.dram("o_yp", [T, D], OUT)
    o_ys = P.dram("o_ys", [SB_ * SS, D], OUT)
    o_fkp = P.dram("o_fkp", [T, 384], OUT)
    o_fvp = P.dram("o_fvp", [T, 384], OUT)
    o_flp = P.dram("o_flp", [T, 6], OUT)
    o_mkp = P.dram("o_mkp", [256, 256], OUT)
    o_mvp = P.dram("o_mvp", [256, 256], OUT)
    o_rsp = P.dram("o_rsp", [6, 64, 64], OUT)
    o_rhp = P.dram("o_rhp", [1600], OUT)
    o_fks = P.dram("o_fks", [SB_ * SS, 384], OUT)
    o_fvs = P.dram("o_fvs", [SB_ * SS, 384], OUT)
    o_fls = P.dram("o_fls", [SB_ * SS, 6], OUT)
    o_rss = P.dram("o_rss", [SB_, 6, 64, 64], OUT)
    o_rhs = P.dram("o_rhs", [SB_, 1600], OUT)

    Wb = P.sb("Wb", [128, 8, NIN], BF16)
    wst = [P.sb("wst%d" % i, [128, D], BF16) for i in range(2)]
    wo_bf = nc.dram_tensor("wo_bf", [128, 8, D], BF16, kind="Internal").ap()
    wo_b = Buf("wo_bf")
    kT = P.sb("kT", [67, 6, T], BF16)
    Vaug = P.sb("Vaug", [128, NT, 6, 65], BF16)
    negc = P.sb("negc", [128, NT, 6])
    kvb = [Buf("kv%d" % i) for i in range(NT)]
    xt = [P.sb("xt%d" % i, [128, D]) for i in range(2)]
    xb = P.sb("xb", [128, D], BF16)
    xnT = P.sb("xnT", [128, 8, 128], BF16)
    identb = P.sb("identb", [128, 128], BF16)
    identf = P.sb("identf", [128, 128])
    trif = P.sb("trif", [128, 128])
    lastf = P.sb("lastf", [128, 128])
    bones = P.sb("bones", [128, 128])
    bavg = P.sb("bavg", [128, 128])
    ones = P.sb("ones", [128, 128])
    mask4 = P.sb("mask4", [128, 4, 128], BF16)
    msl = P.sb("msl", [128, 128], BF16)
    ng = P.sb("ng", [128, 8])
    mng = P.sb("mng", [128, 8])
    gq = P.sb("gq", [128, 64])
    gk = P.sb("gk", [128, 64])
    gmq = P.sb("gmq", [128, 64])
    gmk = P.sb("gmk", [128, 64])
    bfb = P.sb("bfb", [128, 6])
    small = P.sb("small", [128, 64])
    tmpA = P.sb("tmpA", [128, 384])
    tmpB = P.sb("tmpB", [128, 384])
    tmpC = P.sb("tmpC", [128, 384])
    qaug = P.sb("qaug", [128, 6, 67], BF16)
    kaug = P.sb("kaug", [128, 6, 67], BF16)
    mqa = P.sb("mqa", [128, 4, 64], BF16)
    cc = [P.sb("cc%d" % i, [128, 6]) for i in range(2)]
    cr = P.sb("cr", [128, 6])
    lf = P.sb("lf", [128, 6])
    raw = P.sb("raw", [128, 13, 129])
    xs = P.sb("xs", [128, 4, 128])
    xw = P.sb("xw", [64, 128])
    mu = P.sb("mu", [128, 13])
    rp = P.sb("rp", [128, 3, 8])
    lora = P.sb("lora", [64, 384], BF16)
    qT = P.sb("qT", [67, 6, NQ], BF16)
    mqT = P.sb("mqT", [64, 4, NQ], BF16)
    mkT = P.sb("mkT", [64, 4, 256], BF16)
    mvaug = P.sb("mvaug", [128, 2, 4, 65], BF16)
    gt = P.sb("gt", [128, 8, NQ], BF16)
    og = P.sb("og", [128, 8, NQ], BF16)
    gtb = [Buf("gt%d" % i) for i in range(8)]
    ogb = [Buf("og%d" % i) for i in range(8)]
    pts = [P.sb("pt%d" % i, [128, NQ], BF16) for i in range(3)]
    oun = P.sb("oun", [65, NQ])
    ounB = P.sb("ounB", [65, NQ])
    opair = P.sb("opair", [128, NQ])
    W3 = lambda name, dt=F32: P.sb(name, [128, 1, 128], dt)
    r_lw, r_a, r_g, r_eg, r_egm, r_eng = W3("r_lw"), W3("r_a"), W3("r_g"), W3("r_eg"), W3("r_egm"), W3("r_eng")
    r_kk, r_t1, r_t2 = W3("r_kk"), W3("r_t1"), W3("r_t2")
    r_yT2 = [W3("r_yT0"), W3("r_yT1")]
    r_bon2 = [W3("r_bon0"), W3("r_bon1")]
    rpc = [0]
    pend_gn = [None]

    def gn_step(k=1):
        for _ in range(k):
            if pend_gn[0]:
                pend_gn[0].pop(0)()

    def flush_gn():
        while pend_gn[0]:
            pend_gn[0].pop(0)()
    ART = P.sb("ART", [128, 1, 2, 128], BF16)
    BTt = W3("BTt", BF16)
    KTt = W3("KTt", BF16)
    vbt = W3("vbt", BF16)
    twd = P.sb("twd", [64, 128], BF16)
    tokA = P.sb("tokA", [128, 128], BF16)
    tokB = P.sb("tokB", [128, 128], BF16)
    tokK = P.sb("tokK", [128, 128], BF16)
    tokV = P.sb("tokV", [128, 128], BF16)
    gm = [P.sb("gm%d" % h, [128, 4, 128], BF16) for h in range(2)]
    PP = [[P.sb("PP%d_%d" % (h, i), [128, 2, 128], BF16) for i in range(2)] for h in range(2)]
    XX = [[P.sb("XX%d_%d" % (h, i), [128, 128], BF16) for i in range(2)] for h in range(2)]
    WT = P.sb("WT", [128, 1, 128], BF16)
    LVs = P.sb("LVs", [128, 2, 64], BF16)
    U0 = P.sb("U0", [128, 2, 64])
    Ub = P.sb("Ub", [128, 2, 64], BF16)
    Hf = P.sb("Hf", [128, 3, 64])
    Hb = P.sb("Hb", [128, 3, 64], BF16)
    Dp = P.sb("Dp", [128, 1, 64])
    stS = P.sb("stS", [64, 6, 64])

    ps_tr = P.ps("ps_tr", [128, 1024], BF16)
    ps_sA = P.ps("ps_sA", [128, 512])
    ps_sB = P.ps("ps_sB", [128, 512])
    ps_o = P.ps("ps_o", [128, 512])
    pgs = [P.ps("pg%d" % i, [128, 512]) for i in range(4)]
    pgi = [0]

    def pg():
        pgi[0] += 1
        return pgs[pgi[0] % len(pgs)]

    class View:
        def __init__(self, ap):
            self.t = ap
            self.b = Buf(excl=True)

        def __getitem__(self, k):
            return self.t[k]
    ps_s2 = [ps_sA, ps_sB]
    ps_o2 = [View(ps_o[:, 0:256]), View(ps_o[:, 256:512])]
    ps_o2[1].b = ps_o2[0].b

    P.memset("pool", identb[:], 0.0, [identb.b])
    P.asel(identb[:], identb[:], [[-1, 128]], ALU.not_equal, 1.0, 0, 1, [identb.b], [identb.b])
    P.memset("pool", identf[:], 0.0, [identf.b])
    P.asel(identf[:], identf[:], [[-1, 128]], ALU.not_equal, 1.0, 0, 1, [identf.b], [identf.b])
    P.memset("pool", trif[:], 1.0, [trif.b])
    P.asel(trif[:], trif[:], [[1, 128]], ALU.is_ge, 0.0, 0, -1, [trif.b], [trif.b])
    P.memset("pool", lastf[:], 1.0, [lastf.b])
    P.asel(lastf[:], lastf[:], [[0, 128]], ALU.is_ge, 0.0, -127, 1, [lastf.b], [lastf.b])
    P.memset("dve", bones[:], 0.0, [bones.b])
    P.memset("dve", bones[0:64, 0:64], 1.0, [bones.b])
    P.memset("dve", bones[64:128, 64:128], 1.0, [bones.b])
    P.ts("dve", bavg[:], bones[:], 1.0 / 64, None, ALU.mult, None, [bones.b], [bavg.b])
    P.memset("dve", ones[:], 1.0, [ones.b])
    P.memset("pool", mask4[:], 1.0, [mask4.b])
    for i in range(4):
        P.asel(mask4[:, i, :], mask4[:, i, :], [[1, 128]], ALU.is_ge, 0.0, (-1 if i % 2 == 0 else 0), -1, [mask4.b], [mask4.b])
    P.memset("pool", msl[:], 1.0, [msl.b])
    P.asel(msl[:], msl[:], [[-1, 128]], ALU.is_ge, 0.0, -1, 1, [msl.b], [msl.b])

    def cload(out_ap, in_ap, tt_, **kw):
        P.em.dma("sp", out_ap, in_ap, reads=(), writes=[tt_.b], dbuf=tt_.b, **kw)

    cload(ng[:], norm_g.rearrange("(c p) -> p c", p=128), ng, allow_slow_non_contiguous=True)
    cload(mng[:], mem_norm_g.rearrange("(c p) -> p c", p=128), mng, allow_slow_non_contiguous=True)
    for tl, src in ((gq, fox_q_g), (gk, fox_k_g), (gmq, mem_q_g), (gmk, mem_k_g)):
        cload(tl[:], src.partition_broadcast(128), tl)
    cload(bfb[:], fox_b_f.partition_broadcast(128), bfb)
    P.ts("dve", gq[:], gq[:], 0.125, None, ALU.mult, None, [gq.b], [gq.b])
    P.ts("dve", gmq[:], gmq[:], 0.125, None, ALU.mult, None, [gmq.b], [gmq.b])
    cload(mu[:, 0:9], rwkv_mu[0:1152].rearrange("(b p) -> p b", p=128), mu, allow_slow_non_contiguous=True)
    cload(mu[0:64, 9:10], rwkv_mu[1152:1216].rearrange("(b p) -> p b", p=64), mu, allow_slow_non_contiguous=True)
    cload(mu[:, 10:13], rwkv_mu[1216:1600].rearrange("(b p) -> p b", p=128), mu, allow_slow_non_contiguous=True)
    for i, src in enumerate((rwkv_w0, rwkv_a0, rwkv_k_k, rwkv_k_a, rwkv_k_a, rwkv_r_k, rwkv_gn_w, rwkv_gn_b)):
        cload(rp[:, :, i], src.rearrange("(b p) -> p b", p=128), rp, allow_slow_non_contiguous=True)
    P.ts("dve", rp[:, :, 4], rp[:, :, 4], -1.0, 1.0, ALU.mult, ALU.add, [rp.b], [rp.b])
    cload(tmpA[0:32, :], rwkv_w_up, tmpA)
    cload(tmpA[32:64, :], rwkv_a_up, tmpA)
    P.cp("dve", lora[:], tmpA[0:64, :], [tmpA.b], [lora.b])
    P.memset("dve", kaug[:, :, 64:67], 1.0, [kaug.b])
    P.memset("dve", raw[:], 0.0, [raw.b])
    P.memset("pool", mvaug[:, :, :, :].rearrange("p a h e -> p (a h) e")[:, :, 64:65], 1.0, [mvaug.b])

    def norm_T(src_dram, n, gtile, xtile):
        P.load(xtile, xtile[0:n, :], src_dram)
        P.em.op("act", lambda e: e.activation(out=xb[0:n, :], in_=xtile[0:n, :], func=AF.Square,
                                              accum_out=small[0:n, 0:1]), [xtile.b], [xb.b, small.b])
        P.act(small[0:n, 1:2], small[0:n, 0:1], AF.Ln, [small.b], [small.b], scale=1.0 / D, bias=EPS)
        P.act(small[0:n, 2:3], small[0:n, 1:2], AF.Exp, [small.b], [small.b], scale=-0.5)
        P.act(xb[0:n, :], xtile[0:n, :], AF.Copy, [xtile.b, small.b], [xb.b], scale=small[0:n, 2:3])
        for c in range(8):
            P.tr(ps_tr[:, c * 128:c * 128 + n], xb[0:n, c * 128:(c + 1) * 128], identb[0:n, 0:n], [xb.b, identb.b], [ps_tr.b])
        P.tt("dve", xnT[:, :, 0:n], ps_tr[:, :].rearrange("p (c t) -> p c t", t=128)[:, :, 0:n],
             gtile[:, :].unsqueeze(2).to_broadcast([128, 8, n]), ALU.mult, [ps_tr.b, gtile.b], [xnT.b])

    def proj_tm(n, c0, c1, pst, W=None):
        W = W or Wb
        for c in range(8):
            P.mm(pst[0:n, 0:c1 - c0], xnT[:, c, 0:n], W[:, c, c0:c1], c == 0, c == 7, [xnT.b, W.b], [pst.b], chain=(c > 0))

    def headnorm(n, pst, nh, gain, dst, out_bf=None, out_bf_b=None):
        w = nh * 64
        v3 = lambda ap: ap.rearrange("p (h d) -> p h d", d=64)
        P.act(tmpA[0:n, 0:w], pst[0:n, 0:w], AF.Square, [pst.b], [tmpA.b])
        P.em.op("dve", lambda e: e.tensor_reduce(out=small[0:n, 8:8 + nh], in_=v3(tmpA[0:n, 0:w]),
                                                 axis=AX.X, op=ALU.add), [tmpA.b], [small.b])
        P.act(small[0:n, 16:16 + nh], small[0:n, 8:8 + nh], AF.Ln, [small.b], [small.b], scale=1.0 / 64, bias=EPS)
        P.act(small[0:n, 24:24 + nh], small[0:n, 16:16 + nh], AF.Exp, [small.b], [small.b], scale=-0.5)
        P.tt("dve", v3(tmpA[0:n, 0:w]), v3(pst[0:n, 0:w]),
             small[0:n, 24:24 + nh].unsqueeze(2).to_broadcast([n, nh, 64]), ALU.mult, [pst.b, small.b], [tmpA.b])
        P.tt("dve", v3(dst[0:n, 0:w]), v3(tmpA[0:n, 0:w]),
             gain[0:n, :].unsqueeze(1).to_broadcast([n, nh, 64]), ALU.mult, [tmpA.b, gain.b], [dst.b])
        if out_bf is not None:
            P.cp("act", out_bf, v3(dst[0:n, 0:w]), [dst.b], [out_bf_b])

    def c_update(n, j, cprev, ccur):
        pst = pg()
        P.mm(pst[0:n, 0:6], trif[0:n, 0:n], lf[0:n, :], True, cprev is None, [trif.b, lf.b], [pst.b])
        if cprev is not None:
            P.mm(pst[0:n, 0:6], lastf[:, 0:n], cprev[:, :], False, True, [lastf.b, cprev.b], [pst.b], chain=True)
        P.cp("act", ccur[0:n, :], pst[0:n, 0:6], [pst.b], [ccur.b])
        P.ts("dve", negc[0:n, j, :], ccur[0:n, :], -1.0, None, ALU.mult, None, [ccur.b], [kvb[j]])

    def q_cpieces(n, ccur):
        P.cp("dve", qaug[0:n, :, 64], ccur[0:n, :], [ccur.b], [qaug.b])
        P.tt("dve", cr[0:n, :], ccur[0:n, :], qaug[0:n, :, 64], ALU.subtract, [ccur.b, qaug.b], [cr.b])
        P.cp("dve", qaug[0:n, :, 65], cr[0:n, :], [cr.b], [qaug.b])
        P.tt("dve", cr[0:n, :], cr[0:n, :], qaug[0:n, :, 65], ALU.subtract, [cr.b, qaug.b], [cr.b])
        P.cp("dve", qaug[0:n, :, 66], cr[0:n, :], [cr.b], [qaug.b])

    def k_to_T(n, j):
        for h in range(6):
            P.tr(ps_tr[0:67, h * 128:h * 128 + n], kaug[0:n, h, :], identb[0:n, 0:n], [kaug.b, identb.b], [ps_tr.b])
        P.cp("act", kT[:, :, j * 128:j * 128 + n], ps_tr[0:67, 0:768].rearrange("p (h t) -> p h t", t=128)[:, :, 0:n],
             [ps_tr.b], [kvb[j]])

    def q_to_T(n, qoff):
        for h in range(6):
            P.tr(ps_tr[0:67, h * 128:h * 128 + n], qaug[0:n, h, :], identb[0:n, 0:n], [qaug.b, identb.b], [ps_tr.b])
        P.cp("act", qT[:, :, qoff:qoff + n], ps_tr[0:67, 0:768].rearrange("p (h t) -> p h t", t=128)[:, :, 0:n],
             [ps_tr.b], [qT.b])

    def token_tile(src, n, j, qoff, o_k, o_v, o_l, cprev, ccur):
        xtile = xt[0]
        norm_T(src, n, ng, xtile)
        p0 = pg()
        proj_tm(n, C_Q, C_Q + 384, p0)
        headnorm(n, p0, 6, gq, tmpB, out_bf=qaug[0:n, :, 0:64], out_bf_b=qaug.b)
        p1 = pg()
        proj_tm(n, C_K, C_K + 384, p1)
        headnorm(n, p1, 6, gk, tmpB, out_bf=kaug[0:n, :, 0:64], out_bf_b=kaug.b)
        P.store(o_k, tmpB, tmpB[0:n, 0:384])
        p2 = pg()
        proj_tm(n, C_V, C_V + 390, p2)
        P.cp("act", tmpC[0:n, 0:384], p2[0:n, 0:384], [p2.b], [tmpC.b])
        P.store(o_v, tmpC, tmpC[0:n, 0:384])
        P.cp("dve", Vaug[0:n, j, :, 0:64], tmpC[0:n, 0:384].rearrange("p (h d) -> p h d", d=64), [tmpC.b], [kvb[j]])
        P.tt("dve", lf[0:n, :], p2[0:n, 384:390], bfb[0:n, :], ALU.add, [p2.b, bfb.b], [lf.b])
        P.act(lf[0:n, :], lf[0:n, :], AF.Exp, [lf.b], [lf.b], scale=-1.0)
        P.act(lf[0:n, :], lf[0:n, :], AF.Ln, [lf.b], [lf.b], bias=1.0)
        P.ts("dve", lf[0:n, :], lf[0:n, :], -1.0, None, ALU.mult, None, [lf.b], [lf.b])
        P.store(o_l, lf, lf[0:n, :])
        c_update(n, j, cprev, ccur)
        q_cpieces(n, ccur)
        k_to_T(n, j)
        q_to_T(n, qoff)
        p3 = pg()
        proj_tm(n, C_MQ, C_MQ + 256, p3)
        headnorm(n, p3, 4, gmq, tmpB, out_bf=mqa[0:n, :, :], out_bf_b=mqa.b)
        for h in range(4):
            P.tr(ps_tr[0:64, h * 128:h * 128 + n], mqa[0:n, h, :], identb[0:n, 0:n], [mqa.b, identb.b], [ps_tr.b])
        P.cp("act", mqT[:, :, qoff:qoff + n], ps_tr[0:64, 0:512].rearrange("p (h t) -> p h t", t=128)[:, :, 0:n],
             [ps_tr.b], [mqT.b])

    def fm_proj(n, qoff):
        gblocks = [(C_GF + 128 * i, i) for i in range(3)] + [(C_GM + 128 * i, 6 + i) for i in range(2)]
        for g0 in (0, 4):
            pst = pg()
            lst_ = gblocks[g0:g0 + 4]
            for jj, (c0, ch) in enumerate(lst_):
                for c in range(8):
                    P.mm(pst[:, jj * 128:jj * 128 + n], Wb[:, c, c0:c0 + 128], xnT[:, c, 0:n], c == 0, c == 7, [xnT.b, Wb.b], [pst.b], chain=(c > 0))
            for jj, (c0, ch) in enumerate(lst_):
                P.act(gt[:, ch, qoff:qoff + n], pst[:, jj * 128:jj * 128 + n], AF.Silu, [pst.b], [gtb[ch]])
        blocks = [(C_RW + 128 * i, 128) for i in range(9)] + [(C_WD, 64)] + [(C_GR + 128 * i, 128) for i in range(3)]
        for g0 in range(0, 13, 4):
            pst = pg()
            nb = min(4, 13 - g0)
            for jj in range(nb):
                c0, m = blocks[g0 + jj]
                for c in range(8):
                    P.mm(pst[0:m, jj * 128:jj * 128 + n], Wb[:, c, c0:c0 + m], xnT[:, c, 0:n], c == 0, c == 7, [xnT.b, Wb.b], [pst.b], chain=(c > 0))
            P.cp("act" if (g0 // 4) % 2 == 0 else "dve", raw[:, g0:g0 + nb, 1:1 + n], pst[:, 0:nb * 128].rearrange("p (b t) -> p b t", t=128)[:, :, 0:n], [pst.b], [raw.b])

    def store_shift(dst, n):
        P.store(dst[0:1152].rearrange("(b p) -> p b", p=128), raw, raw[:, 0:9, n], allow_slow_non_contiguous=True)
        P.store(dst[1152:1216].rearrange("(b p) -> p b", p=64), raw, raw[0:64, 9:10, n], allow_slow_non_contiguous=True)
        P.store(dst[1216:1600].rearrange("(b p) -> p b", p=128), raw, raw[:, 10:13, n], allow_slow_non_contiguous=True)

    def rwkv_pre(n, qoff):
        cur = lambda blk, p0=0, p1=128: raw[p0:p1, blk, 1:1 + n]
        prv = lambda blk, p0=0, p1=128: raw[p0:p1, blk, 0:n]
        P.tt("dve", xw[:, 0:n], prv(9, 0, 64), cur(9, 0, 64), ALU.subtract, [raw.b], [xw.b])
        P.stt(xw[:, 0:n], xw[:, 0:n], mu[0:64, 9:10], cur(9, 0, 64), ALU.mult, ALU.add, [xw.b, mu.b, raw.b], [xw.b])
        P.act(twd[0:32, 0:n], xw[0:32, 0:n], AF.Tanh, [xw.b], [twd.b])
        P.cp("dve", twd[32:64, 0:n], xw[32:64, 0:n], [xw.b], [twd.b])
        for p in range(3):
            blk = 10 + p
            P.tt("dve", xs[:, 3, 0:n], prv(blk), cur(blk), ALU.subtract, [raw.b], [xs.b])
            P.stt(xs[:, 3, 0:n], xs[:, 3, 0:n], mu[:, blk:blk + 1], cur(blk), ALU.mult, ALU.add, [xs.b, mu.b, raw.b], [xs.b])
            P.act(gt[:, 3 + p, qoff:qoff + n], xs[:, 3, 0:n], AF.Silu, [xs.b], [gtb[3 + p]])
        nlev = 0
        while (1 << nlev) < n:
            nlev += 1
        nlev -= 1
        return nlev

    def rwkv_pair(n, qoff, p, nlev):
        par = rpc[0] % 2
        rpc[0] += 1
        yT_, bon_ = r_yT2[par], r_bon2[par]
        cur = lambda blk, p0=0, p1=128: raw[p0:p1, blk, 1:1 + n]
        prv = lambda blk, p0=0, p1=128: raw[p0:p1, blk, 0:n]
        if True:
            S3 = lambda tl: tl[:, 0, 0:n]
            bc = lambda i: rp[:, p, i:i + 1]
            for i, blk in enumerate((p, 3 + p, 6 + p)):
                P.tt("dve", xs[:, i, 0:n], prv(blk), cur(blk), ALU.subtract, [raw.b], [xs.b])
                P.stt(xs[:, i, 0:n], xs[:, i, 0:n], mu[:, blk:blk + 1], cur(blk), ALU.mult, ALU.add, [xs.b, mu.b, raw.b], [xs.b])
            xr, xk, xv = xs[:, 0, 0:n], xs[:, 1, 0:n], xs[:, 2, 0:n]
            pw = pg()
            P.mm(pw[:, 0:n], lora[0:32, p * 128:(p + 1) * 128], twd[0:32, 0:n], True, True, [lora.b, twd.b], [pw.b])
            P.mm(pw[:, 128:128 + n], lora[32:64, p * 128:(p + 1) * 128], twd[32:64, 0:n], True, True, [lora.b, twd.b], [pw.b])
            P.act(S3(r_lw), pw[:, 0:n], AF.Sigmoid, [pw.b, rp.b], [r_lw.b], bias=bc(0))
            P.ts("dve", S3(r_lw), S3(r_lw), -0.6065306597126334, None, ALU.mult, None, [r_lw.b], [r_lw.b])
            P.act(S3(r_a), pw[:, 128:128 + n], AF.Sigmoid, [pw.b, rp.b], [r_a.b], bias=bc(1))
            P.em.op("dve", lambda e: e.tensor_tensor_scan(out=r_g[:, 0, 0:n], data0=ones[:, 0:n], data1=r_lw[:, 0, 0:n],
                                                          initial=0.0, op0=ALU.mult, op1=ALU.add),
                    [ones.b, r_lw.b], [r_g.b])
            P.act(S3(r_eg), S3(r_g), AF.Exp, [r_g.b], [r_eg.b])
            P.act(S3(r_eng), S3(r_g), AF.Exp, [r_g.b], [r_eng.b], scale=-1.0)
            P.tt("dve", S3(r_egm), S3(r_g), S3(r_lw), ALU.subtract, [r_g.b, r_lw.b], [r_egm.b])
            P.act(S3(r_egm), S3(r_egm), AF.Exp, [r_egm.b], [r_egm.b])
            P.ts("dve", S3(r_kk), xk, bc(2), None, ALU.mult, None, [xs.b, rp.b], [r_kk.b])
            P.tt("dve", S3(r_t1), S3(r_kk), S3(r_kk), ALU.mult, [r_kk.b], [r_t1.b])
            pss = pg()
            P.mm(pss[:, 0:n], bones[:, :], r_t1[:, 0, 0:n], True, True, [bones.b, r_t1.b], [pss.b])
            P.ts("dve", S3(r_t1), pss[:, 0:n], 1e-24, None, ALU.max, None, [pss.b], [r_t1.b])
            P.act(S3(r_t1), S3(r_t1), AF.Ln, [r_t1.b], [r_t1.b], scale=float(2 ** 40))
            P.act(S3(r_t1), S3(r_t1), AF.Exp, [r_t1.b], [r_t1.b], scale=-0.5, bias=13.862943611198906)
            P.tt("dve", S3(r_kk), S3(r_kk), S3(r_t1), ALU.mult, [r_kk.b, r_t1.b], [r_kk.b])
            P.ts("pool", S3(r_t2), S3(r_a), bc(3), bc(4), ALU.mult, ALU.add, [r_a.b, rp.b], [r_t2.b])
            P.tt("pool", S3(r_t2), S3(r_t2), xk, ALU.mult, [r_t2.b, xs.b], [r_t2.b])
            P.stt(ART[:, 0, 0, 0:n], S3(r_kk), -1.0, S3(r_egm), ALU.mult, ALU.mult, [r_kk.b, r_egm.b], [ART.b])
            P.tt("pool", ART[:, 0, 1, 0:n], xr, S3(r_eg), ALU.mult, [xs.b, r_eg.b], [ART.b])
            P.tt("dve", S3(r_t1), S3(r_a), S3(r_kk), ALU.mult, [r_a.b, r_kk.b], [r_t1.b])
            P.tt("dve", S3(BTt), S3(r_t1), S3(r_eng), ALU.mult, [r_t1.b, r_eng.b], [BTt.b])
            P.tt("pool", S3(KTt), S3(r_t2), S3(r_eng), ALU.mult, [r_t2.b, r_eng.b], [KTt.b])
            P.cp("act", S3(vbt), xv, [xs.b], [vbt.b])
            P.tt("dve", S3(r_t1), xr, S3(r_t2), ALU.mult, [xs.b, r_t2.b], [r_t1.b])
            P.ts("dve", S3(r_t1), S3(r_t1), bc(5), None, ALU.mult, None, [r_t1.b, rp.b], [r_t1.b])
            psb = pg()
            P.mm(psb[:, 0:n], bones[:, :], r_t1[:, 0, 0:n], True, True, [bones.b, r_t1.b], [psb.b])
            P.tt("dve", S3(bon_), psb[:, 0:n], xv, ALU.mult, [psb.b, xs.b], [bon_.b])
            for i, (src_ap, sb_) in enumerate(((ART[:, 0, 0, 0:n], ART.b), (BTt[:, 0, 0:n], BTt.b), (KTt[:, 0, 0:n], KTt.b), (vbt[:, 0, 0:n], vbt.b))):
                P.tr(ps_tr[0:n, i * 128:(i + 1) * 128], src_ap, identb[:, :], [sb_, identb.b], [ps_tr.b])
            for i, dstt in enumerate((tokA, tokB, tokK, tokV)):
                P.cp("act" if i % 2 else "dve", dstt[0:n, :], ps_tr[0:n, i * 128:(i + 1) * 128], [ps_tr.b], [dstt.b])
            for hh in range(2):
                hb = hh * 64
                g12 = pg()
                for a_ in range(2):
                    P.mm(g12[0:n, a_ * 128:a_ * 128 + n], BTt[hb:hb + 64, 0, 0:n], ART[hb:hb + 64, 0, a_, 0:n], True, True, [BTt.b, ART.b], [g12.b])
                    P.mm(g12[0:n, 256 + a_ * 128:256 + a_ * 128 + n], KTt[hb:hb + 64, 0, 0:n], ART[hb:hb + 64, 0, a_, 0:n], True, True, [KTt.b, ART.b], [g12.b])
                P.tt("dve", gm[hh][0:n, :, 0:n], g12[0:n, :].rearrange("s (a t) -> s a t", t=128)[:, :, 0:n], mask4[0:n, :, 0:n], ALU.mult,
                     [g12.b, mask4.b], [gm[hh].b])
                g3 = pg()
                P.mm(g3[0:n, 0:n], ART[hb:hb + 64, 0, 0, 0:n], BTt[hb:hb + 64, 0, 0:n], True, True, [ART.b, BTt.b], [g3.b])
                P.tt("dve", PP[hh][0][0:n, 0, 0:n], g3[0:n, 0:n], msl[0:n, 0:n], ALU.mult, [g3.b, msl.b], [PP[hh][0].b])
                P.cp("act", PP[hh][0][0:n, 1, 0:n], gm[hh][0:n, 0, 0:n], [gm[hh].b], [PP[hh][0].b])
                P.tt("pool", XX[hh][0][0:n, 0:n], gm[hh][0:n, 0, 0:n], identb[0:n, 0:n], ALU.add, [gm[hh].b, identb.b], [XX[hh][0].b])
            for hh in range(2):
                hb = hh * 64
                pl_ = pg()
                P.mm(pl_[0:n, 0:64], gm[hh][0:n, 2, 0:n], tokV[0:n, hb:hb + 64], True, True, [gm[hh].b, tokV.b], [pl_.b])
                P.cp("dve", LVs[0:n, hh, :], pl_[0:n, 0:64], [pl_.b], [LVs.b])
            def emit_sq(j):
                ci, ni = (j - 1) % 2, j % 2
                for hh in range(2):
                    psq = pg()
                    Pc = PP[hh][ci]
                    P.mm(psq[0:n, 0:n], Pc[0:n, 1, 0:n], Pc[0:n, 0, 0:n], True, True, [Pc.b], [psq.b])
                    if j < nlev:
                        P.mm(psq[0:n, 128:128 + n], Pc[0:n, 0, 0:n], Pc[0:n, 1, 0:n], True, True, [Pc.b], [psq.b])
                        P.cp("dve" if hh else "act", PP[hh][ni][0:n, :, 0:n], psq[0:n, 0:256].rearrange("s (a t) -> s a t", t=128)[:, :, 0:n], [psq.b], [PP[hh][ni].b])
                    else:
                        P.cp("dve" if hh else "act", PP[hh][ni][0:n, 0, 0:n], psq[0:n, 0:n], [psq.b], [PP[hh][ni].b])

            def emit_x(j):
                ci, ni = (j - 1) % 2, j % 2
                for hh in range(2):
                    px = pg()
                    P.mm(px[0:n, 0:n], PP[hh][ni][0:n, 0, 0:n], XX[hh][ci][0:n, 0:n], True, True, [PP[hh][ni].b, XX[hh][ci].b], [px.b])
                    P.tt("dve", XX[hh][ni][0:n, 0:n], px[0:n, 0:n], XX[hh][ci][0:n, 0:n], ALU.add, [px.b, XX[hh][ci].b], [XX[hh][ni].b])

            for j in range(1, nlev + 1):
                emit_sq(j)
                if j > 1:
                    emit_x(j - 1)
                gn_step(1 if nlev >= 4 else 2)
            emit_x(nlev)
            fi = nlev % 2
            pws = []
            for hh in range(2):
                hb = hh * 64
                TTm = XX[hh][fi]
                pa_, pu_ = pg(), pg()
                pws.append((pa_, pu_))
                P.mm(pa_[0:64, 0:n], tokA[0:n, hb:hb + 64], TTm[0:n, 0:n], True, True, [tokA.b, TTm.b], [pa_.b])
                P.mm(pu_[0:n, 0:64], TTm[0:n, 0:n], LVs[0:n, hh, :], True, True, [TTm.b, LVs.b], [pu_.b])
            for hh in range(2):
                hb = hh * 64
                pa_, pu_ = pws[hh]
                P.cp("act", WT[hb:hb + 64, 0, 0:n], pa_[0:64, 0:n], [pa_.b], [WT.b])
                P.cp("dve", U0[0:n, hh, :], pu_[0:n, 0:64], [pu_.b], [U0.b])
            flush_gn()
            pU = pg()
            for hh in range(2):
                hb = hh * 64
                P.mm(pU[0:n, hb:hb + 64], WT[hb:hb + 64, 0, 0:n], Hb[hb:hb + 64, p, :], True, True, [WT.b, Hb.b], [pU.b])
            P.tt("dve", Ub[0:n, :, :], pU[0:n, 0:128].rearrange("t (h v) -> t h v", v=64), U0[0:n, :, :], ALU.add, [pU.b, U0.b], [Ub.b])
            pY = pg()
            for hh in range(2):
                hb = hh * 64
                dst = pY[0:64, hh * 128:hh * 128 + n]
                P.mm(dst, Hb[hb:hb + 64, p, :], ART[hb:hb + 64, 0, 1, 0:n], True, False, [Hb.b, ART.b], [pY.b])
                P.mm(dst, Ub[0:n, hh, :], gm[hh][0:n, 1, 0:n], False, False, [Ub.b, gm[hh].b], [pY.b], chain=True)
                P.mm(dst, tokV[0:n, hb:hb + 64], gm[hh][0:n, 3, 0:n], False, True, [tokV.b, gm[hh].b], [pY.b], chain=True)
            for hh in range(2):
                hb = hh * 64
                P.cp("act" if hh else "dve", yT_[hb:hb + 64, 0, 0:n], pY[0:64, hh * 128:hh * 128 + n], [pY.b], [yT_.b])
            pD = pg()
            for hh in range(2):
                hb = hh * 64
                P.mm(pD[0:64, hb:hb + 64], tokB[0:n, hb:hb + 64], Ub[0:n, hh, :], True, False, [tokB.b, Ub.b], [pD.b])
                P.mm(pD[0:64, hb:hb + 64], tokK[0:n, hb:hb + 64], tokV[0:n, hb:hb + 64], False, True, [tokK.b, tokV.b], [pD.b], chain=True)
            for hh in range(2):
                hb = hh * 64
                P.cp("act" if hh else "dve", Dp[hb:hb + 64, 0, :], pD[0:64, hb:hb + 64], [pD.b], [Dp.b])
            P.tt("dve", Hf[:, p, :], Hf[:, p, :], Dp[:, 0, :], ALU.add, [Hf.b, Dp.b], [Hf.b])
            P.ts("dve", Hf[:, p, :], Hf[:, p, :], r_eg[:, 0, n - 1:n], None, ALU.mult, None, [Hf.b, r_eg.b], [Hf.b])
            P.cp("act", Hb[:, p, :], Hf[:, p, :], [Hf.b], [Hb.b])
            def gn_steps(p=p, n=n, qoff=qoff, yT_=yT_, bon_=bon_):
                y_ = yT_[:, 0, 0:n]
                t_ = tmpA[:, 0:n]

                def sA():
                    pm = pg()
                    P.mm(pm[:, 0:n], bavg[:, :], y_, True, True, [bavg.b, yT_.b], [pm.b])
                    P.tt("dve", y_, y_, pm[:, 0:n], ALU.subtract, [yT_.b, pm.b], [yT_.b])
                    P.tt("dve", t_, y_, y_, ALU.mult, [yT_.b], [tmpA.b])

                def sB():
                    pvv = pg()
                    P.mm(pvv[:, 0:n], bavg[:, :], t_, True, True, [bavg.b, tmpA.b], [pvv.b])
                    P.act(t_, pvv[:, 0:n], AF.Ln, [pvv.b], [tmpA.b], bias=GN_EPS)
                    P.act(t_, t_, AF.Exp, [tmpA.b], [tmpA.b], scale=-0.5)

                def sC():
                    P.tt("dve", y_, y_, t_, ALU.mult, [yT_.b, tmpA.b], [yT_.b])
                    P.ts("dve", y_, y_, rp[:, p, 6:7], rp[:, p, 7:8], ALU.mult, ALU.add, [yT_.b, rp.b], [yT_.b])

                def sD():
                    P.tt("dve", y_, y_, bon_[:, 0, 0:n], ALU.add, [yT_.b, bon_.b], [yT_.b])
                    P.tt("dve", og[:, 3 + p, qoff:qoff + n], y_, gt[:, 3 + p, qoff:qoff + n], ALU.mult, [yT_.b, gtb[3 + p]], [ogb[3 + p]])
                return [sA, sB, sC, sD]
            flush_gn()
            pend_gn[0] = gn_steps()

    def rwkv_chunk(n, qoff):
        nlev = rwkv_pre(n, qoff)
        for p in range(3):
            rwkv_pair(n, qoff, p, nlev)

    def store_state(dst):
        pst = pg()
        for p in range(3):
            P.tr(pst[0:64, p * 128:(p + 1) * 128], Hf[:, p, :], identf[:, :], [Hf.b, identf.b], [pst.b])
        P.cp("act", stS[:, :, :], pst[0:64, 0:384].rearrange("v (h k) -> v h k", k=64), [pst.b], [stS.b])
        P.store(dst.rearrange("h v k -> v h k"), stS, stS[:, :, :])

    pti = [0]

    oun2 = [oun, ounB]
    hcnt = [0]
    pending = [None]

    def flush_tail():
        if pending[0] is not None:
            t_ = pending[0]
            pending[0] = None
            t_()

    def attention(nq, heads, kfn, vfn, bfn, entries, krows, out_fn):
        for h in heads:
            po = ps_o2[h % 2]
            nent = len(entries)

            def pv(i, ent, ptt):
                j, nk, q0, diag = ent
                vap, vb_ = vfn(h, j, nk)
                P.mm(po[0:65, q0:nq], vap, ptt[0:nk, 0:nq - q0], i == 0, i == nent - 1, [vb_, ptt.b], [po.b])
            prev = None
            for i, ent in enumerate(entries):
                j, nk, q0, diag = ent
                pss_ = ps_s2[pti[0] % 2]
                ptt = pts[pti[0] % 3]
                pti[0] += 1
                kap, kb = kfn(h, j, nk)
                qap, qb = qfn_cur[0](h, q0, nq)
                P.mm(pss_[0:nk, 0:nq - q0], kap, qap, True, True, [kb, qb], [pss_.b])
                bias = bfn(h, j, nk)
                if bias is not None:
                    P.act(ptt[0:nk, 0:nq - q0], pss_[0:nk, 0:nq - q0], AF.Exp, [pss_.b, bias[1]], [ptt.b], bias=bias[0])
                else:
                    P.act(ptt[0:nk, 0:nq - q0], pss_[0:nk, 0:nq - q0], AF.Exp, [pss_.b], [ptt.b])
                if diag:
                    P.asel(ptt[0:nk, 0:nk], ptt[0:nk, 0:nk], [[1, nk]], ALU.is_ge, 0.0, 0, -1, [ptt.b], [ptt.b])
                if prev is not None:
                    pv(*prev)
                prev = (i, ent, ptt)
            pv(*prev)
            ou = oun2[hcnt[0] % 2]
            hcnt[0] += 1
            P.cp("dve", ou[0:65, 0:nq], po[0:65, 0:nq], [po.b], [ou.b])

            def tail(h=h, ou=ou, nq=nq, out_fn=out_fn):
                P.act(ou[64:65, 0:nq], ou[64:65, 0:nq], AF.Ln, [ou.b], [ou.b])
                P.act(ou[64:65, 0:nq], ou[64:65, 0:nq], AF.Exp, [ou.b], [ou.b], scale=-1.0)
                pb = pg()
                P.mm(pb[0:64, 0:nq], ones[64:65, 0:64], ou[64:65, 0:nq], True, True, [ones.b, ou.b], [pb.b])
                out_fn(h, pb, ou)
            flush_tail()
            pending[0] = tail

    qfn_cur = [None]

    def fox_heads(nq, heads, fox_entries):
        qfn_cur[0] = lambda h, q0, nq_: (qT[0:67, h, q0:nq_], qT.b)

        def fox_out(h, pb, ou):
            hb = (h % 2) * 64
            P.tt("dve", opair[hb:hb + 64, 0:nq], ou[0:64, 0:nq], pb[0:64, 0:nq], ALU.mult, [ou.b, pb.b], [opair.b])
            if h % 2 == 1:
                c = h // 2
                P.tt("dve", og[:, c, 0:nq], opair[:, 0:nq], gt[:, c, 0:nq], ALU.mult, [opair.b, gtb[c]], [ogb[c]])
        attention(nq, heads,
                  lambda h, j, nk: (kT[0:67, h, j * 128:j * 128 + nk], kvb[j]),
                  lambda h, j, nk: (Vaug[0:nk, j, h, :], kvb[j]),
                  lambda h, j, nk: (negc[0:nk, j, h:h + 1], kvb[j]),
                  fox_entries, 67, fox_out)

    def mem_heads(nq, heads, mem_k, mem_v, mem_kb, mem_vb):
        qfn_cur[0] = lambda h, q0, nq_: (mqT[0:64, h, q0:nq_], mqT.b)

        def mem_out(h, pb, ou):
            hb = (h % 2) * 64
            P.tt("dve", opair[hb:hb + 64, 0:nq], ou[0:64, 0:nq], pb[0:64, 0:nq], ALU.mult, [ou.b, pb.b], [opair.b])
            if h % 2 == 1:
                c = 6 + h // 2
                P.tt("dve", og[:, c, 0:nq], opair[:, 0:nq], gt[:, c, 0:nq], ALU.mult, [opair.b, gtb[c]], [ogb[c]])
        attention(nq, heads,
                  lambda h, j, nk: (mem_k[0:64, h, j * 128:j * 128 + nk], mem_kb),
                  lambda h, j, nk: (mem_v[0:nk, j, h, :], mem_vb),
                  lambda h, j, nk: None,
                  [(0, 128, 0, False), (1, 128, 0, False)], 64, mem_out)

    def run_attention(nq, fox_entries, mem_k, mem_v, mem_kb, mem_vb):
        fox_heads(nq, range(6), fox_entries)
        mem_heads(nq, range(4), mem_k, mem_v, mem_kb, mem_vb)

    wsti = [0]

    def out_proj(tiles):
        accs = [[pg(), pg()] for _ in tiles]
        for c in range(8):
            w = wst[wsti[0] % 2]
            wsti[0] += 1
            P.em.dma("sp", w[:, :], wo_bf[:, c, :], reads=[wo_b], writes=[w.b], dbuf=w.b)
            for ti, (n, qoff, src, dst) in enumerate(tiles):
                for cb in range(2):
                    pst = accs[ti][cb]
                    P.mm(pst[0:n, 0:512], og[:, c, qoff:qoff + n], w[:, cb * 512:(cb + 1) * 512], c == 0, c == 7, [ogb[c], w.b], [pst.b])
        for ti, (n, qoff, src, dst) in enumerate(tiles):
            xtile = xt[1]
            P.load(xtile, xtile[0:n, :], src)
            for cb in range(2):
                pst = accs[ti][cb]
                P.tt("dve", xtile[0:n, cb * 512:(cb + 1) * 512], xtile[0:n, cb * 512:(cb + 1) * 512], pst[0:n, 0:512], ALU.add,
                     [xtile.b, pst.b], [xtile.b])
            P.store(dst, xtile, xtile[0:n, :])

    w_in_v = w_in.rearrange("(c p) n -> p c n", p=128)
    w_out_v = w_out.rearrange("(c p) n -> p c n", p=128)
    w_mem_v = w_mem_kv.rearrange("(c p) n -> p c n", p=128)
    kq = 0
    Wm = Vaug[:, :, :, :].rearrange("p a h e -> p (a h e)")[:, 0:4096].rearrange("p (c n) -> p c n", n=512)
    for c in range(8):
        s_ = xt[kq % 2]
        P.load(s_, s_[:, 0:512], w_mem_v[:, c, :])
        P.cp(P.rot(), Wm[:, c, :], s_[:, 0:512], [s_.b], kvb)
        kq += 1
    for blk in range(2):
        xtile = xt[blk % 2]
        norm_T(memp[blk * 128:(blk + 1) * 128, :], 128, mng, xtile)
        pst = pg()
        for c in range(8):
            P.mm(pst[:, 0:512], xnT[:, c, :], Wm[:, c, :], c == 0, c == 7, [xnT.b] + kvb, [pst.b], chain=(c > 0))
        headnorm(128, pst, 4, gmk, tmpB, out_bf=mqa[:, :, :], out_bf_b=mqa.b)
        P.store(o_mkp[blk * 128:(blk + 1) * 128, :], tmpB, tmpB[:, 0:256])
        P.cp("act", tmpC[:, 0:256], pst[:, 256:512], [pst.b], [tmpC.b])
        P.store(o_mvp[blk * 128:(blk + 1) * 128, :], tmpC, tmpC[:, 0:256])
        P.cp("dve", mvaug[:, blk, :, 0:64], pst[:, 256:512].rearrange("p (h d) -> p h d", d=64), [pst.b], [mvaug.b])
        for h in range(4):
            P.tr(ps_tr[0:64, h * 128:(h + 1) * 128], mqa[:, h, :], identb[:, :], [mqa.b, identb.b], [ps_tr.b])
        P.cp("act", mkT[:, :, blk * 128:(blk + 1) * 128], ps_tr[0:64, 0:512].rearrange("p (h t) -> p h t", t=128), [ps_tr.b], [mkT.b])
    P.memset("pool", Vaug[:, :, :, :].rearrange("p a h e -> p (a h) e")[:, :, 64:65], 1.0, kvb)
    for c in range(8):
        for (c0, c1) in ((0, 1024), (1024, 2048), (2048, 3072), (3072, NIN)):
            s_ = xt[kq % 2]
            P.load(s_, s_[:, 0:c1 - c0], w_in_v[:, c, c0:c1])
            P.cp(P.rot(), Wb[:, c, c0:c1], s_[:, 0:c1 - c0], [s_.b], [Wb.b])
            kq += 1
        s_ = xt[kq % 2]
        P.load(s_, s_[:, :], w_out_v[:, c, :])
        w = wst[c % 2]
        P.cp(P.rot(), w[:, :], s_[:, :], [s_.b], [w.b])
        P.em.dma("pool", wo_bf[:, c, :], w[:, :], reads=[w.b], writes=[wo_b], dbuf=w.b)
        kq += 1

    P.memset("dve", Hf[:], 0.0, [Hf.b])
    P.memset("dve", Hb[:], 0.0, [Hb.b])
    NG = NT // 2
    for g in range(NG):
        entries = [(j, 128, 0, False) for j in range(2 * g)] + [(2 * g, 128, 0, True), (2 * g + 1, 128, 128, True)]
        for tt_ in range(2):
            t = 2 * g + tt_
            sl = slice(t * 128, (t + 1) * 128)
            token_tile(xp[sl, :], 128, t, tt_ * 128, o_fkp[sl, :], o_fvp[sl, :], o_flp[sl, :],
                       None if t == 0 else cc[(t - 1) % 2], cc[t % 2])
            fm_proj(128, tt_ * 128)
            if tt_ == 0:
                rwkv_chunk(128, 0)
            else:
                nlev = rwkv_pre(128, 128)
                for p in range(3):
                    rwkv_pair(128, 128, p, nlev)
                    fox_heads(NQ, (2 * p, 2 * p + 1), entries)
            if t == NT - 1:
                store_shift(o_rhp, 128)
            else:
                P.cp("dve", raw[:, :, 0:1], raw[:, :, 128:129], [raw.b], [raw.b])
        mem_heads(NQ, range(4), mkT, mvaug, mkT.b, mvaug.b)
        flush_tail()
        flush_gn()
        out_proj([(128, tt_ * 128, xp[(2 * g + tt_) * 128:(2 * g + tt_ + 1) * 128, :], o_yp[(2 * g + tt_) * 128:(2 * g + tt_ + 1) * 128, :])
                  for tt_ in range(2)])
    store_state(o_rsp)

    for b in range(SB_):
        sl = slice(b * SS, (b + 1) * SS)
        for j in range(8):
            ks = slice(j * 128, (j + 1) * 128)
            P.load(tmpB, tmpB[:, 0:384], cfk[b, ks, :])
            P.cp("act", kaug[:, :, 0:64], tmpB[:, 0:384].rearrange("p (h d) -> p h d", d=64), [tmpB.b], [kaug.b])
            k_to_T(128, j)
            P.load(tmpC, tmpC[:, 0:384], cfv[b, ks, :])
            P.cp("dve", Vaug[:, j, :, 0:64], tmpC[:, 0:384].rearrange("p (h d) -> p h d", d=64), [tmpC.b], [kvb[j]])
            P.load(lf, lf[:, :], cfl[b, ks, :])
            c_update(128, j, None if j == 0 else cc[(j - 1) % 2], cc[j % 2])
        P.load(stS, stS[:, :, :], srw[b].rearrange("h v k -> v h k"))
        pst = pg()
        for h in range(6):
            P.tr(pst[0:64, h * 64:(h + 1) * 64], stS[:, h, :], identf[0:64, 0:64], [stS.b, identf.b], [pst.b])
        for h in range(6):
            p, hb = h // 2, (h % 2) * 64
            P.cp("act" if h % 2 else "dve", Hf[hb:hb + 64, p, :], pst[0:64, h * 64:(h + 1) * 64], [pst.b], [Hf.b])
        P.cp("act", Hb[:, :, :], Hf[:, :, :], [Hf.b], [Hb.b])
        P.em.dma("sp", raw[:, 0:9, 0], ssh[b, 0:1152].rearrange("(b p) -> p b", p=128), reads=(), writes=[raw.b], dbuf=raw.b,
                 allow_slow_non_contiguous=True)
        P.em.dma("sp", raw[0:64, 9:10, 0], ssh[b, 1152:1216].rearrange("(b p) -> p b", p=64), reads=(), writes=[raw.b], dbuf=raw.b,
                 allow_slow_non_contiguous=True)
        P.em.dma("sp", raw[:, 10:13, 0], ssh[b, 1216:1600].rearrange("(b p) -> p b", p=128), reads=(), writes=[raw.b], dbuf=raw.b,
                 allow_slow_non_contiguous=True)
        token_tile(xsm[sl, :], SS, 8, 0, o_fks[sl, :], o_fvs[sl, :], o_fls[sl, :], cc[7 % 2], cc[8 % 2])
        fm_proj(SS, 0)
        rwkv_chunk(SS, 0)
        store_shift(o_rhs[b], SS)
        store_state(o_rss[b])
        for blk in range(2):
            ks = slice(blk * 128, (blk + 1) * 128)
            P.load(tmpB, tmpB[:, 0:256], cmk[b, ks, :])
            P.cp("act", mqa[:, :, :], tmpB[:, 0:256].rearrange("p (h d) -> p h d", d=64), [tmpB.b], [mqa.b])
            for h in range(4):
                P.tr(ps_tr[0:64, h * 128:(h + 1) * 128], mqa[:, h, :], identb[:, :], [mqa.b, identb.b], [ps_tr.b])
            P.cp("act", mkT[:, :, blk * 128:(blk + 1) * 128], ps_tr[0:64, 0:512].rearrange("p (h t) -> p h t", t=128), [ps_tr.b], [mkT.b])
            P.load(tmpC, tmpC[:, 0:256], cmv[b, ks, :])
            P.cp("dve", mvaug[:, blk, :, 0:64], tmpC[:, 0:256].rearrange("p (h d) -> p h d", d=64), [tmpC.b], [mvaug.b])
        entries = [(j, 128, 0, False) for j in range(8)] + [(8, SS, 0, True)]
        run_attention(SS, entries, mkT, mvaug, mkT.b, mvaug.b)
        flush_tail()
        flush_gn()
        out_proj([(SS, 0, xsm[sl, :], o_ys[sl, :])])

    P.em.final_wait("pool")
    P.em.replay()
    return nc


_NC = None


def kernel(x_prompt, x_sample, mem_prompt, cache_fox_k, cache_fox_v, cache_fox_logf,
           cache_mem_k, cache_mem_v, state_rwkv, state_rwkv_shift,
           norm_g, w_in, fox_q_g, fox_k_g, fox_b_f, rwkv_mu, rwkv_w0, rwkv_w_up, rwkv_a0,
           rwkv_a_up, rwkv_k_k, rwkv_k_a, rwkv_r_k, rwkv_gn_w, rwkv_gn_b,
           mem_norm_g, w_mem_kv, mem_q_g, mem_k_g, w_out):
    global _NC
    f = lambda a: np.ascontiguousarray(np.asarray(a, dtype=np.float32))
    if _NC is None:
        _NC = build()
    nc = _NC
    shared = dict(norm_g=f(norm_g[0]), w_in=f(w_in[0]), fox_q_g=f(fox_q_g[0]), fox_k_g=f(fox_k_g[0]),
                  fox_b_f=f(fox_b_f[0]), rwkv_mu=f(rwkv_mu[0]), rwkv_w0=f(rwkv_w0[0]), rwkv_w_up=f(rwkv_w_up[0]),
                  rwkv_a0=f(rwkv_a0[0]), rwkv_a_up=f(rwkv_a_up[0]), rwkv_k_k=f(rwkv_k_k[0]), rwkv_k_a=f(rwkv_k_a[0]),
                  rwkv_r_k=f(rwkv_r_k[0]), rwkv_gn_w=f(rwkv_gn_w[0]), rwkv_gn_b=f(rwkv_gn_b[0]),
                  mem_norm_g=f(mem_norm_g[0]), w_mem_kv=f(w_mem_kv[0]), mem_q_g=f(mem_q_g[0]), mem_k_g=f(mem_k_g[0]),
                  w_out=f(w_out[0]))
    in_maps = []
    for c in range(8):
        bs = slice(4 * c, 4 * c + 4)
        m = dict(shared)
        m.update(xp=f(x_prompt[c]), xsm=f(x_sample[bs]).reshape(64, D), memp=f(mem_prompt[c]),
                 cfk=f(cache_fox_k[0, bs]).reshape(4, PAST, 384), cfv=f(cache_fox_v[0, bs]).reshape(4, PAST, 384),
                 cfl=f(cache_fox_logf[0, bs]), cmk=f(cache_mem_k[0, bs]).reshape(4, 256, 256),
                 cmv=f(cache_mem_v[0, bs]).reshape(4, 256, 256), srw=f(state_rwkv[0, bs]),
                 ssh=f(state_rwkv_shift[0, bs]).reshape(4, 1600))
        in_maps.append(m)
    res = run_bass_kernel_spmd(nc, in_maps, core_ids=list(range(8)))
    R = res.results
    cat = lambda k: np.stack([np.asarray(R[c][k]) for c in range(8)])
    yp = cat("o_yp")
    ys = cat("o_ys").reshape(32, 16, D)
    fkp = cat("o_fkp").reshape(1, 8, T, 6, 64)
    fvp = cat("o_fvp").reshape(1, 8, T, 6, 64)
    flp = cat("o_flp").reshape(1, 8, T, 6)
    mkp = cat("o_mkp").reshape(1, 8, 256, 4, 64)
    mvp = cat("o_mvp").reshape(1, 8, 256, 4, 64)
    rsp = cat("o_rsp").reshape(1, 8, 6, 64, 64)
    rhp = cat("o_rhp").reshape(1, 8, 1, 1600)
    fks = cat("o_fks").reshape(1, 32, 16, 6, 64)
    fvs = cat("o_fvs").reshape(1, 32, 16, 6, 64)
    fls = cat("o_fls").reshape(1, 32, 16, 6)
    rss = cat("o_rss").reshape(1, 32, 6, 64, 64)
    rhs = cat("o_rhs").reshape(1, 32, 1, 1600)
    return (yp, ys, fkp, fvp, flp, mkp, mvp, rsp, rhp, fks, fvs, fls, rss, rhs)
```

```python
import numpy as np
from contextlib import ExitStack
import concourse.bass as bass
import concourse.mybir as mybir
from concourse.bass_utils import run_bass_kernel_spmd

F32 = mybir.dt.float32
BF16 = mybir.dt.bfloat16
AF = mybir.ActivationFunctionType
ALU = mybir.AluOpType
AX = mybir.AxisListType

D = 1024
T = 4096
NT = T // 128
SB_ = 4
SS = 16
PAST = 1024
NIN = 3654
EPS = 1e-6
GN_EPS = 64e-5
C_Q, C_K, C_V, C_F, C_GF = 0, 384, 768, 1152, 1158
C_RW = 1542
C_RR, C_RK, C_RV, C_WD, C_AD, C_GR = C_RW, C_RW + 384, C_RW + 768, C_RW + 1152, C_RW + 1184, C_RW + 1216
C_MQ, C_GM = 3142, 3398


class Buf:
    __slots__ = ("w", "r", "dsem", "dcnt", "name", "excl")

    def __init__(self, name="", excl=False):
        self.w = None
        self.r = []
        self.dsem = None
        self.dcnt = 0
        self.name = name
        self.excl = excl


class Emit:
    ENG = ("pe", "act", "dve", "pool", "sp")

    def __init__(self, nc, stack):
        self.nc = nc
        self.stack = stack
        self.ops = {e: [] for e in self.ENG}
        self.cnt = {e: 0 for e in self.ENG}
        self.sems = {}
        for e in self.ENG:
            self.sems[e] = stack.enter_context(nc.semaphore("sem_" + e))
        self.known = {e: {} for e in self.ENG}
        self.nd = 0
        self.dbufs = []

    def _waits(self, eng, reads, writes):
        need = {}
        for b in reads:
            if b.w is not None:
                k, v = b.w
                if need.get(k, 0) < v:
                    need[k] = v
            if b.excl:
                for k, v in b.r:
                    if k != eng and need.get(k, 0) < v:
                        need[k] = v
        for b in writes:
            if b.w is not None:
                k, v = b.w
                if need.get(k, 0) < v:
                    need[k] = v
            for k, v in b.r:
                if need.get(k, 0) < v:
                    need[k] = v
        out = []
        kn = self.known[eng]
        for k, v in need.items():
            if kn.get(k, 0) < v:
                kn[k] = v
                out.append((self.sems[k], v))
        return out

    def _mark(self, ev, reads, writes):
        for b in reads:
            b.r = [x for x in b.r if x[0] != ev[0]]
            b.r.append(ev)
        for b in writes:
            b.w = ev
            b.r = []

    def op(self, eng, fn, reads=(), writes=(), chain=False):
        prev_known_pe = self.known["pe"].get("pe", 0) if eng == "pe" else None
        wl = self._waits(eng, reads, writes)
        if chain and eng == "pe":
            sem_pe = self.sems["pe"]
            keep = []
            for s_, v_ in wl:
                if s_ is sem_pe and v_ == self.cnt["pe"]:
                    self.known["pe"]["pe"] = prev_known_pe
                    continue
                keep.append((s_, v_))
            wl = keep
        self.cnt[eng] += 1
        ev = (eng, self.cnt[eng])
        sem = self.sems[eng]

        def run(e, fn=fn, wl=wl, sem=sem):
            for s, v in wl:
                e.wait_ge(s, v)
            fn(e).then_inc(sem, 1)
        self.ops[eng].append(run)
        self._mark(ev, reads, writes)
        return ev

    def dma(self, eng, out, in_, reads=(), writes=(), dbuf=None, **kw):
        if dbuf.dsem is None:
            dbuf.dsem = {}
            dbuf.dcnt = {}
            self.dbufs.append(dbuf)
        if eng not in dbuf.dsem:
            self.nd += 1
            key = "d%d" % self.nd
            self.sems[key] = self.stack.enter_context(self.nc.semaphore("sem_" + key))
            dbuf.dsem[eng] = key
            dbuf.dcnt[eng] = 0
        wl = self._waits(eng, reads, writes)
        dbuf.dcnt[eng] += 16
        key = dbuf.dsem[eng]
        ev = (key, dbuf.dcnt[eng])
        sem = self.sems[key]

        def run(e, wl=wl, sem=sem, out=out, in_=in_, kw=kw):
            for s, v in wl:
                e.wait_ge(s, v)
            e.dma_start(out=out, in_=in_, **kw).then_inc(sem, 16)
        self.ops[eng].append(run)
        self._mark(ev, reads, writes)
        return ev

    def final_wait(self, eng):
        wl = []
        kn = self.known[eng]
        for b in self.dbufs:
            for q, key in b.dsem.items():
                v = b.dcnt[q]
                if kn.get(key, 0) < v:
                    kn[key] = v
                    wl.append((self.sems[key], v))

        def run(e, wl=wl):
            for s, v in wl:
                e.wait_ge(s, v)
        self.ops[eng].append(run)

    def replay(self):
        nc = self.nc
        ops = self.ops
        with nc.Block() as block:
            @block.tensor
            def _(e):
                for f in ops["pe"]:
                    f(e)

            @block.scalar
            def _(e):
                for f in ops["act"]:
                    f(e)

            @block.vector
            def _(e):
                for f in ops["dve"]:
                    f(e)

            @block.gpsimd
            def _(e):
                for f in ops["pool"]:
                    f(e)

            @block.sync
            def _(e):
                for f in ops["sp"]:
                    f(e)


class TT:
    def __init__(self, ap, name=""):
        self.t = ap
        self.b = Buf(name)

    def __getitem__(self, k):
        return self.t[k]


class Prog:
    def __init__(self):
        self.nc = bass.Bass("TRN2", target_bir_lowering=False)
        self.st = ExitStack()
        self.em = Emit(self.nc, self.st)
        self.rr = 0

    def dram(self, name, shape, kind):
        return self.nc.dram_tensor(name, list(shape), F32, kind=kind).ap()

    def sb(self, name, shape, dt=F32):
        return TT(self.st.enter_context(self.nc.sbuf_tensor(name, list(shape), dt)), name)

    def ps(self, name, shape, dt=F32):
        t = TT(self.st.enter_context(self.nc.psum_tensor(name, list(shape), dt)), name)
        t.b.excl = True
        return t

    def act(self, out, in_, func, r, w, **kw):
        return self.em.op("act", lambda e: e.activation(out=out, in_=in_, func=func, **kw), r, w)

    def tt(self, eng, out, in0, in1, op, r, w):
        return self.em.op(eng, lambda e: e.tensor_tensor(out=out, in0=in0, in1=in1, op=op), r, w)

    def ts(self, eng, out, in0, s1, s2, op0, op1, r, w):
        if s2 is None:
            return self.em.op(eng, lambda e: e.tensor_scalar(out=out, in0=in0, scalar1=s1, scalar2=None, op0=op0), r, w)
        return self.em.op(eng, lambda e: e.tensor_scalar(out=out, in0=in0, scalar1=s1, scalar2=s2, op0=op0, op1=op1), r, w)

    def stt(self, out, in0, scalar, in1, op0, op1, r, w):
        return self.em.op("dve", lambda e: e.scalar_tensor_tensor(out=out, in0=in0, scalar=scalar, in1=in1, op0=op0, op1=op1), r, w)

    def cp(self, eng, out, in_, r, w):
        if eng == "act":
            return self.em.op("act", lambda e: e.activation(out=out, in_=in_, func=AF.Copy), r, w)
        return self.em.op(eng, lambda e: e.tensor_copy(out=out, in_=in_), r, w)

    def mm(self, out, lhsT, rhs, start, stop, r, w, chain=False):
        return self.em.op("pe", lambda e: e.matmul(out, lhsT=lhsT, rhs=rhs, start=start, stop=stop), r, w, chain=chain)

    def tr(self, out, in_, ident, r, w):
        return self.em.op("pe", lambda e: e.transpose(out, in_, ident), r, w)

    def memset(self, eng, ap, val, w):
        return self.em.op(eng, lambda e: e.memset(ap, val), (), w)

    def asel(self, out, in_, pattern, op, fill, base, cm, r, w):
        return self.em.op("pool", lambda e: e.affine_select(out=out, in_=in_, pattern=pattern, compare_op=op,
                                                            fill=fill, base=base, channel_multiplier=cm), r, w)

    def load(self, out_tt, out_ap, in_ap, **kw):
        return self.em.dma("sp", out_ap, in_ap, reads=(), writes=[out_tt.b], dbuf=out_tt.b, **kw)

    def store(self, out_ap, in_tt, in_ap, **kw):
        return self.em.dma("pool", out_ap, in_ap, reads=[in_tt.b], writes=(), dbuf=in_tt.b, **kw)

    def rot(self):
        self.rr += 1
        return ("act", "dve", "pool")[self.rr % 3]


def build():
    P = Prog()
    nc = P.nc
    IN, OUT = "ExternalInput", "ExternalOutput"
    NQ = 256
    xp = P.dram("xp", [T, D], IN)
    xsm = P.dram("xsm", [SB_ * SS, D], IN)
    memp = P.dram("memp", [256, D], IN)
    cfk = P.dram("cfk", [SB_, PAST, 384], IN)
    cfv = P.dram("cfv", [SB_, PAST, 384], IN)
    cfl = P.dram("cfl", [SB_, PAST, 6], IN)
    cmk = P.dram("cmk", [SB_, 256, 256], IN)
    cmv = P.dram("cmv", [SB_, 256, 256], IN)
    srw = P.dram("srw", [SB_, 6, 64, 64], IN)
    ssh = P.dram("ssh", [SB_, 1600], IN)
    norm_g = P.dram("norm_g", [D], IN)
    w_in = P.dram("w_in", [D, NIN], IN)
    fox_q_g = P.dram("fox_q_g", [64], IN)
    fox_k_g = P.dram("fox_k_g", [64], IN)
    fox_b_f = P.dram("fox_b_f", [6], IN)
    rwkv_mu = P.dram("rwkv_mu", [1600], IN)
    rwkv_w0 = P.dram("rwkv_w0", [384], IN)
    rwkv_w_up = P.dram("rwkv_w_up", [32, 384], IN)
    rwkv_a0 = P.dram("rwkv_a0", [384], IN)
    rwkv_a_up = P.dram("rwkv_a_up", [32, 384], IN)
    rwkv_k_k = P.dram("rwkv_k_k", [384], IN)
    rwkv_k_a = P.dram("rwkv_k_a", [384], IN)
    rwkv_r_k = P.dram("rwkv_r_k", [384], IN)
    rwkv_gn_w = P.dram("rwkv_gn_w", [384], IN)
    rwkv_gn_b = P.dram("rwkv_gn_b", [384], IN)
    mem_norm_g = P.dram("mem_norm_g", [D], IN)
    w_mem_kv = P.dram("w_mem_kv", [D, 512], IN)
    mem_q_g = P.dram("mem_q_g", [64], IN)
    mem_k_g = P.dram("mem_k_g", [64], IN)
    w_out = P.dram("w_out", [D, D], IN)

    o_yp = P.dram("o_yp", [T, D], OUT)
    o_ys = P.dram("o_ys", [SB_ * SS, D], OUT)
    o_fkp = P.dram("o_fkp", [T, 384], OUT)
    o_fvp = P.dram("o_fvp", [T, 384], OUT)
    o_flp = P.dram("o_flp", [T, 6], OUT)
    o_mkp = P.dram("o_mkp", [256, 256], OUT)
    o_mvp = P.dram("o_mvp", [256, 256], OUT)
    o_rsp = P.dram("o_rsp", [6, 64, 64], OUT)
    o_rhp = P.dram("o_rhp", [1600], OUT)
    o_fks = P.dram("o_fks", [SB_ * SS, 384], OUT)
    o_fvs = P.dram("o_fvs", [SB_ * SS, 384], OUT)
    o_fls = P.dram("o_fls", [SB_ * SS, 6], OUT)
    o_rss = P.dram("o_rss", [SB_, 6, 64, 64], OUT)
    o_rhs = P.dram("o_rhs", [SB_, 1600], OUT)

    Wb = P.sb("Wb", [128, 8, NIN], BF16)
    wst = [P.sb("wst%d" % i, [128, D], BF16) for i in range(2)]
    wo_bf = nc.dram_tensor("wo_bf", [128, 8, D], BF16, kind="Internal").ap()
    wo_b = Buf("wo_bf")
    kT = P.sb("kT", [67, 6, T], BF16)
    Vaug = P.sb("Vaug", [128, NT, 6, 65], BF16)
    negc = P.sb("negc", [128, NT, 6])
    kvb = [Buf("kv%d" % i) for i in range(NT)]
    xt = [P.sb("xt%d" % i, [128, D]) for i in range(2)]
    xb = P.sb("xb", [128, D], BF16)
    xnT = P.sb("xnT", [128, 8, 128], BF16)
    identb = P.sb("identb", [128, 128], BF16)
    identf = P.sb("identf", [128, 128])
    trif = P.sb("trif", [128, 128])
    lastf = P.sb("lastf", [128, 128])
    bones = P.sb("bones", [128, 128])
    bavg = P.sb("bavg", [128, 128])
    ones = P.sb("ones", [128, 128])
    mask4 = P.sb("mask4", [128, 4, 128], BF16)
    msl = P.sb("msl", [128, 128], BF16)
    ng = P.sb("ng", [128, 8])
    mng = P.sb("mng", [128, 8])
    gq = P.sb("gq", [128, 64])
    gk = P.sb("gk", [128, 64])
    gmq = P.sb("gmq", [128, 64])
    gmk = P.sb("gmk", [128, 64])
    bfb = P.sb("bfb", [128, 6])
    small = P.sb("small", [128, 64])
    tmpA = P.sb("tmpA", [128, 384])
    tmpB = P.sb("tmpB", [128, 384])
    tmpC = P.sb("tmpC", [128, 384])
    qaug = P.sb("qaug", [128, 6, 67], BF16)
    kaug = P.sb("kaug", [128, 6, 67], BF16)
    mqa = P.sb("mqa", [128, 4, 64], BF16)
    cc = [P.sb("cc%d" % i, [128, 6]) for i in range(2)]
    cr = P.sb("cr", [128, 6])
    lf = P.sb("lf", [128, 6])
    raw = P.sb("raw", [128, 13, 129])
    xs = P.sb("xs", [128, 4, 128])
    xw = P.sb("xw", [64, 128])
    mu = P.sb("mu", [128, 13])
    rp = P.sb("rp", [128, 3, 8])
    lora = P.sb("lora", [64, 384], BF16)
    qT = P.sb("qT", [67, 6, NQ], BF16)
    mqT = P.sb("mqT", [64, 4, NQ], BF16)
    mkT = P.sb("mkT", [64, 4, 256], BF16)
    mvaug = P.sb("mvaug", [128, 2, 4, 65], BF16)
    gt = P.sb("gt", [128, 8, NQ], BF16)
    og = P.sb("og", [128, 8, NQ], BF16)
    gtb = [Buf("gt%d" % i) for i in range(8)]
    ogb = [Buf("og%d" % i) for i in range(8)]
    pts = [P.sb("pt%d" % i, [128, NQ], BF16) for i in range(3)]
    oun = P.sb("oun", [65, NQ])
    ounB = P.sb("ounB", [65, NQ])
    opair = P.sb("opair", [128, NQ])
    W3 = lambda name, dt=F32: P.sb(name, [128, 1, 128], dt)
    r_lw, r_a, r_g, r_eg, r_egm, r_eng = W3("r_lw"), W3("r_a"), W3("r_g"), W3("r_eg"), W3("r_egm"), W3("r_eng")
    r_kk, r_t1, r_t2 = W3("r_kk"), W3("r_t1"), W3("r_t2")
    r_yT2 = [W3("r_yT0"), W3("r_yT1")]
    r_bon2 = [W3("r_bon0"), W3("r_bon1")]
    rpc = [0]
    pend_gn = [None]

    def gn_step(k=1):
        for _ in range(k):
            if pend_gn[0]:
                pend_gn[0].pop(0)()

    def flush_gn():
        while pend_gn[0]:
            pend_gn[0].pop(0)()
    ART = P.sb("ART", [128, 1, 2, 128], BF16)
    BTt = W3("BTt", BF16)
    KTt = W3("KTt", BF16)
    vbt = W3("vbt", BF16)
    twd = P.sb("twd", [64, 128], BF16)
    tokA = P.sb("tokA", [128, 128], BF16)
    tokB = P.sb("tokB", [128, 128], BF16)
    tokK = P.sb("tokK", [128, 128], BF16)
    tokV = P.sb("tokV", [128, 128], BF16)
    gm = [P.sb("gm%d" % h, [128, 4, 128], BF16) for h in range(2)]
    PP = [[P.sb("PP%d_%d" % (h, i), [128, 2, 128], BF16) for i in range(2)] for h in range(2)]
    XX = [[P.sb("XX%d_%d" % (h, i), [128, 128], BF16) for i in range(2)] for h in range(2)]
    WT = P.sb("WT", [128, 1, 128], BF16)
    LVs = P.sb("LVs", [128, 2, 64], BF16)
    U0 = P.sb("U0", [128, 2, 64])
    Ub = P.sb("Ub", [128, 2, 64], BF16)
    Hf = P.sb("Hf", [128, 3, 64])
    Hb = P.sb("Hb", [128, 3, 64], BF16)
    Dp = P.sb("Dp", [128, 1, 64])
    stS = P.sb("stS", [64, 6, 64])

    ps_tr = P.ps("ps_tr", [128, 1024], BF16)
    ps_sA = P.ps("ps_sA", [128, 512])
    ps_sB = P.ps("ps_sB", [128, 512])
    ps_o = P.ps("ps_o", [128, 512])
    pgs = [P.ps("pg%d" % i, [128, 512]) for i in range(4)]
    pgi = [0]

    def pg():
        pgi[0] += 1
        return pgs[pgi[0] % len(pgs)]

    class View:
        def __init__(self, ap):
            self.t = ap
            self.b = Buf(excl=True)

        def __getitem__(self, k):
            return self.t[k]
    ps_s2 = [ps_sA, ps_sB]
    ps_o2 = [View(ps_o[:, 0:256]), View(ps_o[:, 256:512])]
    ps_o2[1].b = ps_o2[0].b

    P.memset("pool", identb[:], 0.0, [identb.b])
    P.asel(identb[:], identb[:], [[-1, 128]], ALU.not_equal, 1.0, 0, 1, [identb.b], [identb.b])
    P.memset("pool", identf[:], 0.0, [identf.b])
    P.asel(identf[:], identf[:], [[-1, 128]], ALU.not_equal, 1.0, 0, 1, [identf.b], [identf.b])
    P.memset("pool", trif[:], 1.0, [trif.b])
    P.asel(trif[:], trif[:], [[1, 128]], ALU.is_ge, 0.0, 0, -1, [trif.b], [trif.b])
    P.memset("pool", lastf[:], 1.0, [lastf.b])
    P.asel(lastf[:], lastf[:], [[0, 128]], ALU.is_ge, 0.0, -127, 1, [lastf.b], [lastf.b])
    P.memset("dve", bones[:], 0.0, [bones.b])
    P.memset("dve", bones[0:64, 0:64], 1.0, [bones.b])
    P.memset("dve", bones[64:128, 64:128], 1.0, [bones.b])
    P.ts("dve", bavg[:], bones[:], 1.0 / 64, None, ALU.mult, None, [bones.b], [bavg.b])
    P.memset("dve", ones[:], 1.0, [ones.b])
    P.memset("pool", mask4[:], 1.0, [mask4.b])
    for i in range(4):
        P.asel(mask4[:, i, :], mask4[:, i, :], [[1, 128]], ALU.is_ge, 0.0, (-1 if i % 2 == 0 else 0), -1, [mask4.b], [mask4.b])
    P.memset("pool", msl[:], 1.0, [msl.b])
    P.asel(msl[:], msl[:], [[-1, 128]], ALU.is_ge, 0.0, -1, 1, [msl.b], [msl.b])

    def cload(out_ap, in_ap, tt_, **kw):
        P.em.dma("sp", out_ap, in_ap, reads=(), writes=[tt_.b], dbuf=tt_.b, **kw)

    cload(ng[:], norm_g.rearrange("(c p) -> p c", p=128), ng, allow_slow_non_contiguous=True)
    cload(mng[:], mem_norm_g.rearrange("(c p) -> p c", p=128), mng, allow_slow_non_contiguous=True)
    for tl, src in ((gq, fox_q_g), (gk, fox_k_g), (gmq, mem_q_g), (gmk, mem_k_g)):
        cload(tl[:], src.partition_broadcast(128), tl)
    cload(bfb[:], fox_b_f.partition_broadcast(128), bfb)
    P.ts("dve", gq[:], gq[:], 0.125, None, ALU.mult, None, [gq.b], [gq.b])
    P.ts("dve", gmq[:], gmq[:], 0.125, None, ALU.mult, None, [gmq.b], [gmq.b])
    cload(mu[:, 0:9], rwkv_mu[0:1152].rearrange("(b p) -> p b", p=128), mu, allow_slow_non_contiguous=True)
    cload(mu[0:64, 9:10], rwkv_mu[1152:1216].rearrange("(b p) -> p b", p=64), mu, allow_slow_non_contiguous=True)
    cload(mu[:, 10:13], rwkv_mu[1216:1600].rearrange("(b p) -> p b", p=128), mu, allow_slow_non_contiguous=True)
    for i, src in enumerate((rwkv_w0, rwkv_a0, rwkv_k_k, rwkv_k_a, rwkv_k_a, rwkv_r_k, rwkv_gn_w, rwkv_gn_b)):
        cload(rp[:, :, i], src.rearrange("(b p) -> p b", p=128), rp, allow_slow_non_contiguous=True)
    P.ts("dve", rp[:, :, 4], rp[:, :, 4], -1.0, 1.0, ALU.mult, ALU.add, [rp.b], [rp.b])
    cload(tmpA[0:32, :], rwkv_w_up, tmpA)
    cload(tmpA[32:64, :], rwkv_a_up, tmpA)
    P.cp("dve", lora[:], tmpA[0:64, :], [tmpA.b], [lora.b])
    P.memset("dve", kaug[:, :, 64:67], 1.0, [kaug.b])
    P.memset("dve", raw[:], 0.0, [raw.b])
    P.memset("pool", mvaug[:, :, :, :].rearrange("p a h e -> p (a h) e")[:, :, 64:65], 1.0, [mvaug.b])

    def norm_T(src_dram, n, gtile, xtile):
        P.load(xtile, xtile[0:n, :], src_dram)
        P.em.op("act", lambda e: e.activation(out=xb[0:n, :], in_=xtile[0:n, :], func=AF.Square,
                                              accum_out=small[0:n, 0:1]), [xtile.b], [xb.b, small.b])
        P.act(small[0:n, 1:2], small[0:n, 0:1], AF.Ln, [small.b], [small.b], scale=1.0 / D, bias=EPS)
        P.act(small[0:n, 2:3], small[0:n, 1:2], AF.Exp, [small.b], [small.b], scale=-0.5)
        P.act(xb[0:n, :], xtile[0:n, :], AF.Copy, [xtile.b, small.b], [xb.b], scale=small[0:n, 2:3])
        for c in range(8):
            P.tr(ps_tr[:, c * 128:c * 128 + n], xb[0:n, c * 128:(c + 1) * 128], identb[0:n, 0:n], [xb.b, identb.b], [ps_tr.b])
        P.tt("dve", xnT[:, :, 0:n], ps_tr[:, :].rearrange("p (c t) -> p c t", t=128)[:, :, 0:n],
             gtile[:, :].unsqueeze(2).to_broadcast([128, 8, n]), ALU.mult, [ps_tr.b, gtile.b], [xnT.b])

    def proj_tm(n, c0, c1, pst, W=None):
        W = W or Wb
        for c in range(8):
            P.mm(pst[0:n, 0:c1 - c0], xnT[:, c, 0:n], W[:, c, c0:c1], c == 0, c == 7, [xnT.b, W.b], [pst.b], chain=(c > 0))

    def headnorm(n, pst, nh, gain, dst, out_bf=None, out_bf_b=None):
        w = nh * 64
        v3 = lambda ap: ap.rearrange("p (h d) -> p h d", d=64)
        P.act(tmpA[0:n, 0:w], pst[0:n, 0:w], AF.Square, [pst.b], [tmpA.b])
        P.em.op("dve", lambda e: e.tensor_reduce(out=small[0:n, 8:8 + nh], in_=v3(tmpA[0:n, 0:w]),
                                                 axis=AX.X, op=ALU.add), [tmpA.b], [small.b])
        P.act(small[0:n, 16:16 + nh], small[0:n, 8:8 + nh], AF.Ln, [small.b], [small.b], scale=1.0 / 64, bias=EPS)
        P.act(small[0:n, 24:24 + nh], small[0:n, 16:16 + nh], AF.Exp, [small.b], [small.b], scale=-0.5)
        P.tt("dve", v3(tmpA[0:n, 0:w]), v3(pst[0:n, 0:w]),
             small[0:n, 24:24 + nh].unsqueeze(2).to_broadcast([n, nh, 64]), ALU.mult, [pst.b, small.b], [tmpA.b])
        P.tt("dve", v3(dst[0:n, 0:w]), v3(tmpA[0:n, 0:w]),
             gain[0:n, :].unsqueeze(1).to_broadcast([n, nh, 64]), ALU.mult, [tmpA.b, gain.b], [dst.b])
        if out_bf is not None:
            P.cp("act", out_bf, v3(dst[0:n, 0:w]), [dst.b], [out_bf_b])

    def c_update(n, j, cprev, ccur):
        pst = pg()
        P.mm(pst[0:n, 0:6], trif[0:n, 0:n], lf[0:n, :], True, cprev is None, [trif.b, lf.b], [pst.b])
        if cprev is not None:
            P.mm(pst[0:n, 0:6], lastf[:, 0:n], cprev[:, :], False, True, [lastf.b, cprev.b], [pst.b], chain=True)
        P.cp("act", ccur[0:n, :], pst[0:n, 0:6], [pst.b], [ccur.b])
        P.ts("dve", negc[0:n, j, :], ccur[0:n, :], -1.0, None, ALU.mult, None, [ccur.b], [kvb[j]])

    def q_cpieces(n, ccur):
        P.cp("dve", qaug[0:n, :, 64], ccur[0:n, :], [ccur.b], [qaug.b])
        P.tt("dve", cr[0:n, :], ccur[0:n, :], qaug[0:n, :, 64], ALU.subtract, [ccur.b, qaug.b], [cr.b])
        P.cp("dve", qaug[0:n, :, 65], cr[0:n, :], [cr.b], [qaug.b])
        P.tt("dve", cr[0:n, :], cr[0:n, :], qaug[0:n, :, 65], ALU.subtract, [cr.b, qaug.b], [cr.b])
        P.cp("dve", qaug[0:n, :, 66], cr[0:n, :], [cr.b], [qaug.b])

    def k_to_T(n, j):
        for h in range(6):
            P.tr(ps_tr[0:67, h * 128:h * 128 + n], kaug[0:n, h, :], identb[0:n, 0:n], [kaug.b, identb.b], [ps_tr.b])
        P.cp("act", kT[:, :, j * 128:j * 128 + n], ps_tr[0:67, 0:768].rearrange("p (h t) -> p h t", t=128)[:, :, 0:n],
             [ps_tr.b], [kvb[j]])

    def q_to_T(n, qoff):
        for h in range(6):
            P.tr(ps_tr[0:67, h * 128:h * 128 + n], qaug[0:n, h, :], identb[0:n, 0:n], [qaug.b, identb.b], [ps_tr.b])
        P.cp("act", qT[:, :, qoff:qoff + n], ps_tr[0:67, 0:768].rearrange("p (h t) -> p h t", t=128)[:, :, 0:n],
             [ps_tr.b], [qT.b])

    def token_tile(src, n, j, qoff, o_k, o_v, o_l, cprev, ccur):
        xtile = xt[0]
        norm_T(src, n, ng, xtile)
        p0 = pg()
        proj_tm(n, C_Q, C_Q + 384, p0)
        headnorm(n, p0, 6, gq, tmpB, out_bf=qaug[0:n, :, 0:64], out_bf_b=qaug.b)
        p1 = pg()
        proj_tm(n, C_K, C_K + 384, p1)
        headnorm(n, p1, 6, gk, tmpB, out_bf=kaug[0:n, :, 0:64], out_bf_b=kaug.b)
        P.store(o_k, tmpB, tmpB[0:n, 0:384])
        p2 = pg()
        proj_tm(n, C_V, C_V + 390, p2)
        P.cp("act", tmpC[0:n, 0:384], p2[0:n, 0:384], [p2.b], [tmpC.b])
        P.store(o_v, tmpC, tmpC[0:n, 0:384])
        P.cp("dve", Vaug[0:n, j, :, 0:64], tmpC[0:n, 0:384].rearrange("p (h d) -> p h d", d=64), [tmpC.b], [kvb[j]])
        P.tt("dve", lf[0:n, :], p2[0:n, 384:390], bfb[0:n, :], ALU.add, [p2.b, bfb.b], [lf.b])
        P.act(lf[0:n, :], lf[0:n, :], AF.Exp, [lf.b], [lf.b], scale=-1.0)
        P.act(lf[0:n, :], lf[0:n, :], AF.Ln, [lf.b], [lf.b], bias=1.0)
        P.ts("dve", lf[0:n, :], lf[0:n, :], -1.0, None, ALU.mult, None, [lf.b], [lf.b])
        P.store(o_l, lf, lf[0:n, :])
        c_update(n, j, cprev, ccur)
        q_cpieces(n, ccur)
        k_to_T(n, j)
        q_to_T(n, qoff)
        p3 = pg()
        proj_tm(n, C_MQ, C_MQ + 256, p3)
        headnorm(n, p3, 4, gmq, tmpB, out_bf=mqa[0:n, :, :], out_bf_b=mqa.b)
        for h in range(4):
            P.tr(ps_tr[0:64, h * 128:h * 128 + n], mqa[0:n, h, :], identb[0:n, 0:n], [mqa.b, identb.b], [ps_tr.b])
        P.cp("act", mqT[:, :, qoff:qoff + n], ps_tr[0:64, 0:512].rearrange("p (h t) -> p h t", t=128)[:, :, 0:n],
             [ps_tr.b], [mqT.b])

    def fm_proj(n, qoff):
        gblocks = [(C_GF + 128 * i, i) for i in range(3)] + [(C_GM + 128 * i, 6 + i) for i in range(2)]
        for g0 in (0, 4):
            pst = pg()
            lst_ = gblocks[g0:g0 + 4]
            for jj, (c0, ch) in enumerate(lst_):
                for c in range(8):
                    P.mm(pst[:, jj * 128:jj * 128 + n], Wb[:, c, c0:c0 + 128], xnT[:, c, 0:n], c == 0, c == 7, [xnT.b, Wb.b], [pst.b], chain=(c > 0))
            for jj, (c0, ch) in enumerate(lst_):
                P.act(gt[:, ch, qoff:qoff + n], pst[:, jj * 128:jj * 128 + n], AF.Silu, [pst.b], [gtb[ch]])
        blocks = [(C_RW + 128 * i, 128) for i in range(9)] + [(C_WD, 64)] + [(C_GR + 128 * i, 128) for i in range(3)]
        for g0 in range(0, 13, 4):
            pst = pg()
            nb = min(4, 13 - g0)
            for jj in range(nb):
                c0, m = blocks[g0 + jj]
                for c in range(8):
                    P.mm(pst[0:m, jj * 128:jj * 128 + n], Wb[:, c, c0:c0 + m], xnT[:, c, 0:n], c == 0, c == 7, [xnT.b, Wb.b], [pst.b], chain=(c > 0))
            P.cp("act" if (g0 // 4) % 2 == 0 else "dve", raw[:, g0:g0 + nb, 1:1 + n], pst[:, 0:nb * 128].rearrange("p (b t) -> p b t", t=128)[:, :, 0:n], [pst.b], [raw.b])

    def store_shift(dst, n):
        P.store(dst[0:1152].rearrange("(b p) -> p b", p=128), raw, raw[:, 0:9, n], allow_slow_non_contiguous=True)
        P.store(dst[1152:1216].rearrange("(b p) -> p b", p=64), raw, raw[0:64, 9:10, n], allow_slow_non_contiguous=True)
        P.store(dst[1216:1600].rearrange("(b p) -> p b", p=128), raw, raw[:, 10:13, n], allow_slow_non_contiguous=True)

    def rwkv_pre(n, qoff):
        cur = lambda blk, p0=0, p1=128: raw[p0:p1, blk, 1:1 + n]
        prv = lambda blk, p0=0, p1=128: raw[p0:p1, blk, 0:n]
        P.tt("dve", xw[:, 0:n], prv(9, 0, 64), cur(9, 0, 64), ALU.subtract, [raw.b], [xw.b])
        P.stt(xw[:, 0:n], xw[:, 0:n], mu[0:64, 9:10], cur(9, 0, 64), ALU.mult, ALU.add, [xw.b, mu.b, raw.b], [xw.b])
        P.act(twd[0:32, 0:n], xw[0:32, 0:n], AF.Tanh, [xw.b], [twd.b])
        P.cp("dve", twd[32:64, 0:n], xw[32:64, 0:n], [xw.b], [twd.b])
        for p in range(3):
            blk = 10 + p
            P.tt("dve", xs[:, 3, 0:n], prv(blk), cur(blk), ALU.subtract, [raw.b], [xs.b])
            P.stt(xs[:, 3, 0:n], xs[:, 3, 0:n], mu[:, blk:blk + 1], cur(blk), ALU.mult, ALU.add, [xs.b, mu.b, raw.b], [xs.b])
            P.act(gt[:, 3 + p, qoff:qoff + n], xs[:, 3, 0:n], AF.Silu, [xs.b], [gtb[3 + p]])
        nlev = 0
        while (1 << nlev) < n:
            nlev += 1
        nlev -= 1
        return nlev

    def rwkv_pair(n, qoff, p, nlev):
        par = rpc[0] % 2
        rpc[0] += 1
        yT_, bon_ = r_yT2[par], r_bon2[par]
        cur = lambda blk, p0=0, p1=128: raw[p0:p1, blk, 1:1 + n]
        prv = lambda blk, p0=0, p1=128: raw[p0:p1, blk, 0:n]
        if True:
            S3 = lambda tl: tl[:, 0, 0:n]
            bc = lambda i: rp[:, p, i:i + 1]
            for i, blk in enumerate((p, 3 + p, 6 + p)):
                P.tt("dve", xs[:, i, 0:n], prv(blk), cur(blk), ALU.subtract, [raw.b], [xs.b])
                P.stt(xs[:, i, 0:n], xs[:, i, 0:n], mu[:, blk:blk + 1], cur(blk), ALU.mult, ALU.add, [xs.b, mu.b, raw.b], [xs.b])
            xr, xk, xv = xs[:, 0, 0:n], xs[:, 1, 0:n], xs[:, 2, 0:n]
            pw = pg()
            P.mm(pw[:, 0:n], lora[0:32, p * 128:(p + 1) * 128], twd[0:32, 0:n], True, True, [lora.b, twd.b], [pw.b])
            P.mm(pw[:, 128:128 + n], lora[32:64, p * 128:(p + 1) * 128], twd[32:64, 0:n], True, True, [lora.b, twd.b], [pw.b])
            P.act(S3(r_lw), pw[:, 0:n], AF.Sigmoid, [pw.b, rp.b], [r_lw.b], bias=bc(0))
            P.ts("dve", S3(r_lw), S3(r_lw), -0.6065306597126334, None, ALU.mult, None, [r_lw.b], [r_lw.b])
            P.act(S3(r_a), pw[:, 128:128 + n], AF.Sigmoid, [pw.b, rp.b], [r_a.b], bias=bc(1))
            P.em.op("dve", lambda e: e.tensor_tensor_scan(out=r_g[:, 0, 0:n], data0=ones[:, 0:n], data1=r_lw[:, 0, 0:n],
                                                          initial=0.0, op0=ALU.mult, op1=ALU.add),
                    [ones.b, r_lw.b], [r_g.b])
            P.act(S3(r_eg), S3(r_g), AF.Exp, [r_g.b], [r_eg.b])
            P.act(S3(r_eng), S3(r_g), AF.Exp, [r_g.b], [r_eng.b], scale=-1.0)
            P.tt("dve", S3(r_egm), S3(r_g), S3(r_lw), ALU.subtract, [r_g.b, r_lw.b], [r_egm.b])
            P.act(S3(r_egm), S3(r_egm), AF.Exp, [r_egm.b], [r_egm.b])
            P.ts("dve", S3(r_kk), xk, bc(2), None, ALU.mult, None, [xs.b, rp.b], [r_kk.b])
            P.tt("dve", S3(r_t1), S3(r_kk), S3(r_kk), ALU.mult, [r_kk.b], [r_t1.b])
            pss = pg()
            P.mm(pss[:, 0:n], bones[:, :], r_t1[:, 0, 0:n], True, True, [bones.b, r_t1.b], [pss.b])
            P.ts("dve", S3(r_t1), pss[:, 0:n], 1e-24, None, ALU.max, None, [pss.b], [r_t1.b])
            P.act(S3(r_t1), S3(r_t1), AF.Ln, [r_t1.b], [r_t1.b], scale=float(2 ** 40))
            P.act(S3(r_t1), S3(r_t1), AF.Exp, [r_t1.b], [r_t1.b], scale=-0.5, bias=13.862943611198906)
            P.tt("dve", S3(r_kk), S3(r_kk), S3(r_t1), ALU.mult, [r_kk.b, r_t1.b], [r_kk.b])
            P.ts("pool", S3(r_t2), S3(r_a), bc(3), bc(4), ALU.mult, ALU.add, [r_a.b, rp.b], [r_t2.b])
            P.tt("pool", S3(r_t2), S3(r_t2), xk, ALU.mult, [r_t2.b, xs.b], [r_t2.b])
            P.stt(ART[:, 0, 0, 0:n], S3(r_kk), -1.0, S3(r_egm), ALU.mult, ALU.mult, [r_kk.b, r_egm.b], [ART.b])
            P.tt("pool", ART[:, 0, 1, 0:n], xr, S3(r_eg), ALU.mult, [xs.b, r_eg.b], [ART.b])
            P.tt("dve", S3(r_t1), S3(r_a), S3(r_kk), ALU.mult, [r_a.b, r_kk.b], [r_t1.b])
            P.tt("dve", S3(BTt), S3(r_t1), S3(r_eng), ALU.mult, [r_t1.b, r_eng.b], [BTt.b])
            P.tt("pool", S3(KTt), S3(r_t2), S3(r_eng), ALU.mult, [r_t2.b, r_eng.b], [KTt.b])
            P.cp("pool", S3(vbt), xv, [xs.b], [vbt.b])
            P.tt("dve", S3(r_t1), xr, S3(r_t2), ALU.mult, [xs.b, r_t2.b], [r_t1.b])
            P.ts("dve", S3(r_t1), S3(r_t1), bc(5), None, ALU.mult, None, [r_t1.b, rp.b], [r_t1.b])
            psb = pg()
            P.mm(psb[:, 0:n], bones[:, :], r_t1[:, 0, 0:n], True, True, [bones.b, r_t1.b], [psb.b])
            P.tt("dve", S3(bon_), psb[:, 0:n], xv, ALU.mult, [psb.b, xs.b], [bon_.b])
            for i, (src_ap, sb_) in enumerate(((ART[:, 0, 0, 0:n], ART.b), (BTt[:, 0, 0:n], BTt.b), (KTt[:, 0, 0:n], KTt.b), (vbt[:, 0, 0:n], vbt.b))):
                P.tr(ps_tr[0:n, i * 128:(i + 1) * 128], src_ap, identb[:, :], [sb_, identb.b], [ps_tr.b])
            for i, dstt in enumerate((tokA, tokB, tokK, tokV)):
                P.cp("act" if i % 2 else "dve", dstt[0:n, :], ps_tr[0:n, i * 128:(i + 1) * 128], [ps_tr.b], [dstt.b])
            for hh in range(2):
                hb = hh * 64
                g12 = pg()
                for a_ in range(2):
                    P.mm(g12[0:n, a_ * 128:a_ * 128 + n], BTt[hb:hb + 64, 0, 0:n], ART[hb:hb + 64, 0, a_, 0:n], True, True, [BTt.b, ART.b], [g12.b])
                    P.mm(g12[0:n, 256 + a_ * 128:256 + a_ * 128 + n], KTt[hb:hb + 64, 0, 0:n], ART[hb:hb + 64, 0, a_, 0:n], True, True, [KTt.b, ART.b], [g12.b])
                P.tt("dve", gm[hh][0:n, :, 0:n], g12[0:n, :].rearrange("s (a t) -> s a t", t=128)[:, :, 0:n], mask4[0:n, :, 0:n], ALU.mult,
                     [g12.b, mask4.b], [gm[hh].b])
                g3 = pg()
                P.mm(g3[0:n, 0:n], ART[hb:hb + 64, 0, 0, 0:n], BTt[hb:hb + 64, 0, 0:n], True, True, [ART.b, BTt.b], [g3.b])
                P.tt("dve", PP[hh][0][0:n, 0, 0:n], g3[0:n, 0:n], msl[0:n, 0:n], ALU.mult, [g3.b, msl.b], [PP[hh][0].b])
                P.cp("pool", PP[hh][0][0:n, 1, 0:n], gm[hh][0:n, 0, 0:n], [gm[hh].b], [PP[hh][0].b])
                P.tt("pool", XX[hh][0][0:n, 0:n], gm[hh][0:n, 0, 0:n], identb[0:n, 0:n], ALU.add, [gm[hh].b, identb.b], [XX[hh][0].b])
            for hh in range(2):
                hb = hh * 64
                pl_ = pg()
                P.mm(pl_[0:n, 0:64], gm[hh][0:n, 2, 0:n], tokV[0:n, hb:hb + 64], True, True, [gm[hh].b, tokV.b], [pl_.b])
                P.cp("dve", LVs[0:n, hh, :], pl_[0:n, 0:64], [pl_.b], [LVs.b])
            def emit_sq(j):
                ci, ni = (j - 1) % 2, j % 2
                for hh in range(2):
                    psq = pg()
                    Pc = PP[hh][ci]
                    P.mm(psq[0:n, 0:n], Pc[0:n, 1, 0:n], Pc[0:n, 0, 0:n], True, True, [Pc.b], [psq.b])
                    if j < nlev:
                        P.mm(psq[0:n, 128:128 + n], Pc[0:n, 0, 0:n], Pc[0:n, 1, 0:n], True, True, [Pc.b], [psq.b])
                        P.cp("dve" if hh else "act", PP[hh][ni][0:n, :, 0:n], psq[0:n, 0:256].rearrange("s (a t) -> s a t", t=128)[:, :, 0:n], [psq.b], [PP[hh][ni].b])
                    else:
                        P.cp("dve" if hh else "act", PP[hh][ni][0:n, 0, 0:n], psq[0:n, 0:n], [psq.b], [PP[hh][ni].b])

            def emit_x(j):
                ci, ni = (j - 1) % 2, j % 2
                for hh in range(2):
                    px = pg()
                    P.mm(px[0:n, 0:n], PP[hh][ni][0:n, 0, 0:n], XX[hh][ci][0:n, 0:n], True, True, [PP[hh][ni].b, XX[hh][ci].b], [px.b])
                    P.tt("dve", XX[hh][ni][0:n, 0:n], px[0:n, 0:n], XX[hh][ci][0:n, 0:n], ALU.add, [px.b, XX[hh][ci].b], [XX[hh][ni].b])

            for j in range(1, nlev + 1):
                emit_sq(j)
                if j > 1:
                    emit_x(j - 1)
                gn_step(1 if nlev >= 4 else 2)
            emit_x(nlev)
            fi = nlev % 2
            pws = []
            for hh in range(2):
                hb = hh * 64
                TTm = XX[hh][fi]
                pa_, pu_ = pg(), pg()
                pws.append((pa_, pu_))
                P.mm(pa_[0:64, 0:n], tokA[0:n, hb:hb + 64], TTm[0:n, 0:n], True, True, [tokA.b, TTm.b], [pa_.b])
                P.mm(pu_[0:n, 0:64], TTm[0:n, 0:n], LVs[0:n, hh, :], True, True, [TTm.b, LVs.b], [pu_.b])
            for hh in range(2):
                hb = hh * 64
                pa_, pu_ = pws[hh]
                P.cp("act", WT[hb:hb + 64, 0, 0:n], pa_[0:64, 0:n], [pa_.b], [WT.b])
                P.cp("dve", U0[0:n, hh, :], pu_[0:n, 0:64], [pu_.b], [U0.b])
            flush_gn()
            pU = pg()
            for hh in range(2):
                hb = hh * 64
                P.mm(pU[0:n, hb:hb + 64], WT[hb:hb + 64, 0, 0:n], Hb[hb:hb + 64, p, :], True, True, [WT.b, Hb.b], [pU.b])
            P.tt("dve", Ub[0:n, :, :], pU[0:n, 0:128].rearrange("t (h v) -> t h v", v=64), U0[0:n, :, :], ALU.add, [pU.b, U0.b], [Ub.b])
            pY = pg()
            for hh in range(2):
                hb = hh * 64
                dst = pY[0:64, hh * 128:hh * 128 + n]
                P.mm(dst, Hb[hb:hb + 64, p, :], ART[hb:hb + 64, 0, 1, 0:n], True, False, [Hb.b, ART.b], [pY.b])
                P.mm(dst, Ub[0:n, hh, :], gm[hh][0:n, 1, 0:n], False, False, [Ub.b, gm[hh].b], [pY.b], chain=True)
                P.mm(dst, tokV[0:n, hb:hb + 64], gm[hh][0:n, 3, 0:n], False, True, [tokV.b, gm[hh].b], [pY.b], chain=True)
            for hh in range(2):
                hb = hh * 64
                P.cp("act" if hh else "dve", yT_[hb:hb + 64, 0, 0:n], pY[0:64, hh * 128:hh * 128 + n], [pY.b], [yT_.b])
            pD = pg()
            for hh in range(2):
                hb = hh * 64
                P.mm(pD[0:64, hb:hb + 64], tokB[0:n, hb:hb + 64], Ub[0:n, hh, :], True, False, [tokB.b, Ub.b], [pD.b])
                P.mm(pD[0:64, hb:hb + 64], tokK[0:n, hb:hb + 64], tokV[0:n, hb:hb + 64], False, True, [tokK.b, tokV.b], [pD.b], chain=True)
            for hh in range(2):
                hb = hh * 64
                P.cp("act" if hh else "dve", Dp[hb:hb + 64, 0, :], pD[0:64, hb:hb + 64], [pD.b], [Dp.b])
            P.tt("dve", Hf[:, p, :], Hf[:, p, :], Dp[:, 0, :], ALU.add, [Hf.b, Dp.b], [Hf.b])
            P.ts("dve", Hf[:, p, :], Hf[:, p, :], r_eg[:, 0, n - 1:n], None, ALU.mult, None, [Hf.b, r_eg.b], [Hf.b])
            P.cp("act", Hb[:, p, :], Hf[:, p, :], [Hf.b], [Hb.b])
            def gn_steps(p=p, n=n, qoff=qoff, yT_=yT_, bon_=bon_):
                y_ = yT_[:, 0, 0:n]
                t_ = tmpA[:, 0:n]

                def sA():
                    pm = pg()
                    P.mm(pm[:, 0:n], bavg[:, :], y_, True, True, [bavg.b, yT_.b], [pm.b])
                    P.tt("dve", y_, y_, pm[:, 0:n], ALU.subtract, [yT_.b, pm.b], [yT_.b])
                    P.tt("dve", t_, y_, y_, ALU.mult, [yT_.b], [tmpA.b])

                def sB():
                    pvv = pg()
                    P.mm(pvv[:, 0:n], bavg[:, :], t_, True, True, [bavg.b, tmpA.b], [pvv.b])
                    P.act(t_, pvv[:, 0:n], AF.Ln, [pvv.b], [tmpA.b], bias=GN_EPS)
                    P.act(t_, t_, AF.Exp, [tmpA.b], [tmpA.b], scale=-0.5)

                def sC():
                    P.tt("dve", y_, y_, t_, ALU.mult, [yT_.b, tmpA.b], [yT_.b])
                    P.ts("dve", y_, y_, rp[:, p, 6:7], rp[:, p, 7:8], ALU.mult, ALU.add, [yT_.b, rp.b], [yT_.b])

                def sD():
                    P.tt("dve", y_, y_, bon_[:, 0, 0:n], ALU.add, [yT_.b, bon_.b], [yT_.b])
                    P.tt("dve", og[:, 3 + p, qoff:qoff + n], y_, gt[:, 3 + p, qoff:qoff + n], ALU.mult, [yT_.b, gtb[3 + p]], [ogb[3 + p]])
                return [sA, sB, sC, sD]
            flush_gn()
            pend_gn[0] = gn_steps()

    def rwkv_chunk(n, qoff):
        nlev = rwkv_pre(n, qoff)
        for p in range(3):
            rwkv_pair(n, qoff, p, nlev)

    def store_state(dst):
        pst = pg()
        for p in range(3):
            P.tr(pst[0:64, p * 128:(p + 1) * 128], Hf[:, p, :], identf[:, :], [Hf.b, identf.b], [pst.b])
        P.cp("act", stS[:, :, :], pst[0:64, 0:384].rearrange("v (h k) -> v h k", k=64), [pst.b], [stS.b])
        P.store(dst.rearrange("h v k -> v h k"), stS, stS[:, :, :])

    pti = [0]

    oun2 = [oun, ounB]
    hcnt = [0]
    pending = [None]

    def flush_tail():
        if pending[0] is not None:
            t_ = pending[0]
            pending[0] = None
            t_()

    def attention(nq, heads, kfn, vfn, bfn, entries, krows, out_fn):
        for h in heads:
            po = ps_o2[h % 2]
            nent = len(entries)

            def pv(i, ent, ptt):
                j, nk, q0, diag = ent
                vap, vb_ = vfn(h, j, nk)
                P.mm(po[0:65, q0:nq], vap, ptt[0:nk, 0:nq - q0], i == 0, i == nent - 1, [vb_, ptt.b], [po.b])
            prev = None
            for i, ent in enumerate(entries):
                j, nk, q0, diag = ent
                pss_ = ps_s2[pti[0] % 2]
                ptt = pts[pti[0] % 3]
                pti[0] += 1
                kap, kb = kfn(h, j, nk)
                qap, qb = qfn_cur[0](h, q0, nq)
                P.mm(pss_[0:nk, 0:nq - q0], kap, qap, True, True, [kb, qb], [pss_.b])
                bias = bfn(h, j, nk)
                if bias is not None:
                    P.act(ptt[0:nk, 0:nq - q0], pss_[0:nk, 0:nq - q0], AF.Exp, [pss_.b, bias[1]], [ptt.b], bias=bias[0])
                else:
                    P.act(ptt[0:nk, 0:nq - q0], pss_[0:nk, 0:nq - q0], AF.Exp, [pss_.b], [ptt.b])
                if diag:
                    P.asel(ptt[0:nk, 0:nk], ptt[0:nk, 0:nk], [[1, nk]], ALU.is_ge, 0.0, 0, -1, [ptt.b], [ptt.b])
                if prev is not None:
                    pv(*prev)
                prev = (i, ent, ptt)
            pv(*prev)
            ou = oun2[hcnt[0] % 2]
            hcnt[0] += 1
            P.cp("dve", ou[0:65, 0:nq], po[0:65, 0:nq], [po.b], [ou.b])

            def tail(h=h, ou=ou, nq=nq, out_fn=out_fn):
                P.act(ou[64:65, 0:nq], ou[64:65, 0:nq], AF.Ln, [ou.b], [ou.b])
                P.act(ou[64:65, 0:nq], ou[64:65, 0:nq], AF.Exp, [ou.b], [ou.b], scale=-1.0)
                pb = pg()
                P.mm(pb[0:64, 0:nq], ones[64:65, 0:64], ou[64:65, 0:nq], True, True, [ones.b, ou.b], [pb.b])
                out_fn(h, pb, ou)
            flush_tail()
            pending[0] = tail

    qfn_cur = [None]

    def fox_heads(nq, heads, fox_entries):
        qfn_cur[0] = lambda h, q0, nq_: (qT[0:67, h, q0:nq_], qT.b)

        def fox_out(h, pb, ou):
            hb = (h % 2) * 64
            P.tt("dve", opair[hb:hb + 64, 0:nq], ou[0:64, 0:nq], pb[0:64, 0:nq], ALU.mult, [ou.b, pb.b], [opair.b])
            if h % 2 == 1:
                c = h // 2
                P.tt("dve", og[:, c, 0:nq], opair[:, 0:nq], gt[:, c, 0:nq], ALU.mult, [opair.b, gtb[c]], [ogb[c]])
        attention(nq, heads,
                  lambda h, j, nk: (kT[0:67, h, j * 128:j * 128 + nk], kvb[j]),
                  lambda h, j, nk: (Vaug[0:nk, j, h, :], kvb[j]),
                  lambda h, j, nk: (negc[0:nk, j, h:h + 1], kvb[j]),
                  fox_entries, 67, fox_out)

    def mem_heads(nq, heads, mem_k, mem_v, mem_kb, mem_vb):
        qfn_cur[0] = lambda h, q0, nq_: (mqT[0:64, h, q0:nq_], mqT.b)

        def mem_out(h, pb, ou):
            hb = (h % 2) * 64
            P.tt("dve", opair[hb:hb + 64, 0:nq], ou[0:64, 0:nq], pb[0:64, 0:nq], ALU.mult, [ou.b, pb.b], [opair.b])
            if h % 2 == 1:
                c = 6 + h // 2
                P.tt("dve", og[:, c, 0:nq], opair[:, 0:nq], gt[:, c, 0:nq], ALU.mult, [opair.b, gtb[c]], [ogb[c]])
        attention(nq, heads,
                  lambda h, j, nk: (mem_k[0:64, h, j * 128:j * 128 + nk], mem_kb),
                  lambda h, j, nk: (mem_v[0:nk, j, h, :], mem_vb),
                  lambda h, j, nk: None,
                  [(0, 128, 0, False), (1, 128, 0, False)], 64, mem_out)

    def run_attention(nq, fox_entries, mem_k, mem_v, mem_kb, mem_vb):
        fox_heads(nq, range(6), fox_entries)
        mem_heads(nq, range(4), mem_k, mem_v, mem_kb, mem_vb)

    wsti = [0]

    def out_proj(tiles):
        accs = [[pg(), pg()] for _ in tiles]
        for c in range(8):
            w = wst[wsti[0] % 2]
            wsti[0] += 1
            P.em.dma("sp", w[:, :], wo_bf[:, c, :], reads=[wo_b], writes=[w.b], dbuf=w.b)
            for ti, (n, qoff, src, dst) in enumerate(tiles):
                for cb in range(2):
                    pst = accs[ti][cb]
                    P.mm(pst[0:n, 0:512], og[:, c, qoff:qoff + n], w[:, cb * 512:(cb + 1) * 512], c == 0, c == 7, [ogb[c], w.b], [pst.b])
        for ti, (n, qoff, src, dst) in enumerate(tiles):
            xtile = xt[1]
            P.load(xtile, xtile[0:n, :], src)
            for cb in range(2):
                pst = accs[ti][cb]
                P.tt("dve", xtile[0:n, cb * 512:(cb + 1) * 512], xtile[0:n, cb * 512:(cb + 1) * 512], pst[0:n, 0:512], ALU.add,
                     [xtile.b, pst.b], [xtile.b])
            P.store(dst, xtile, xtile[0:n, :])

    w_in_v = w_in.rearrange("(c p) n -> p c n", p=128)
    w_out_v = w_out.rearrange("(c p) n -> p c n", p=128)
    w_mem_v = w_mem_kv.rearrange("(c p) n -> p c n", p=128)
    kq = 0
    Wm = Vaug[:, :, :, :].rearrange("p a h e -> p (a h e)")[:, 0:4096].rearrange("p (c n) -> p c n", n=512)
    for c in range(8):
        s_ = xt[kq % 2]
        P.load(s_, s_[:, 0:512], w_mem_v[:, c, :])
        P.cp(P.rot(), Wm[:, c, :], s_[:, 0:512], [s_.b], kvb)
        kq += 1
    for blk in range(2):
        xtile = xt[blk % 2]
        norm_T(memp[blk * 128:(blk + 1) * 128, :], 128, mng, xtile)
        pst = pg()
        for c in range(8):
            P.mm(pst[:, 0:512], xnT[:, c, :], Wm[:, c, :], c == 0, c == 7, [xnT.b] + kvb, [pst.b], chain=(c > 0))
        headnorm(128, pst, 4, gmk, tmpB, out_bf=mqa[:, :, :], out_bf_b=mqa.b)
        P.store(o_mkp[blk * 128:(blk + 1) * 128, :], tmpB, tmpB[:, 0:256])
        P.cp("act", tmpC[:, 0:256], pst[:, 256:512], [pst.b], [tmpC.b])
        P.store(o_mvp[blk * 128:(blk + 1) * 128, :], tmpC, tmpC[:, 0:256])
        P.cp("dve", mvaug[:, blk, :, 0:64], pst[:, 256:512].rearrange("p (h d) -> p h d", d=64), [pst.b], [mvaug.b])
        for h in range(4):
            P.tr(ps_tr[0:64, h * 128:(h + 1) * 128], mqa[:, h, :], identb[:, :], [mqa.b, identb.b], [ps_tr.b])
        P.cp("act", mkT[:, :, blk * 128:(blk + 1) * 128], ps_tr[0:64, 0:512].rearrange("p (h t) -> p h t", t=128), [ps_tr.b], [mkT.b])
    P.memset("pool", Vaug[:, :, :, :].rearrange("p a h e -> p (a h) e")[:, :, 64:65], 1.0, kvb)
    for c in range(8):
        for (c0, c1) in ((0, 1024), (1024, 2048), (2048, 3072), (3072, NIN)):
            s_ = xt[kq % 2]
            P.load(s_, s_[:, 0:c1 - c0], w_in_v[:, c, c0:c1])
            P.cp(P.rot(), Wb[:, c, c0:c1], s_[:, 0:c1 - c0], [s_.b], [Wb.b])
            kq += 1
        s_ = xt[kq % 2]
        P.load(s_, s_[:, :], w_out_v[:, c, :])
        w = wst[c % 2]
        P.cp(P.rot(), w[:, :], s_[:, :], [s_.b], [w.b])
        P.em.dma("pool", wo_bf[:, c, :], w[:, :], reads=[w.b], writes=[wo_b], dbuf=w.b)
        kq += 1

    P.memset("dve", Hf[:], 0.0, [Hf.b])
    P.memset("dve", Hb[:], 0.0, [Hb.b])
    NG = NT // 2
    for g in range(NG):
        entries = [(j, 128, 0, False) for j in range(2 * g)] + [(2 * g, 128, 0, True), (2 * g + 1, 128, 128, True)]
        for tt_ in range(2):
            t = 2 * g + tt_
            sl = slice(t * 128, (t + 1) * 128)
            token_tile(xp[sl, :], 128, t, tt_ * 128, o_fkp[sl, :], o_fvp[sl, :], o_flp[sl, :],
                       None if t == 0 else cc[(t - 1) % 2], cc[t % 2])
            fm_proj(128, tt_ * 128)
            if tt_ == 0:
                rwkv_chunk(128, 0)
            else:
                nlev = rwkv_pre(128, 128)
                for p in range(3):
                    rwkv_pair(128, 128, p, nlev)
                    fox_heads(NQ, (2 * p, 2 * p + 1), entries)
            if t == NT - 1:
                store_shift(o_rhp, 128)
            else:
                P.cp("dve", raw[:, :, 0:1], raw[:, :, 128:129], [raw.b], [raw.b])
        mem_heads(NQ, range(4), mkT, mvaug, mkT.b, mvaug.b)
        flush_tail()
        flush_gn()
        out_proj([(128, tt_ * 128, xp[(2 * g + tt_) * 128:(2 * g + tt_ + 1) * 128, :], o_yp[(2 * g + tt_) * 128:(2 * g + tt_ + 1) * 128, :])
                  for tt_ in range(2)])
    store_state(o_rsp)

    for b in range(SB_):
        sl = slice(b * SS, (b + 1) * SS)
        for j in range(8):
            ks = slice(j * 128, (j + 1) * 128)
            P.load(tmpB, tmpB[:, 0:384], cfk[b, ks, :])
            P.cp("act", kaug[:, :, 0:64], tmpB[:, 0:384].rearrange("p (h d) -> p h d", d=64), [tmpB.b], [kaug.b])
            k_to_T(128, j)
            P.load(tmpC, tmpC[:, 0:384], cfv[b, ks, :])
            P.cp("dve", Vaug[:, j, :, 0:64], tmpC[:, 0:384].rearrange("p (h d) -> p h d", d=64), [tmpC.b], [kvb[j]])
            P.load(lf, lf[:, :], cfl[b, ks, :])
            c_update(128, j, None if j == 0 else cc[(j - 1) % 2], cc[j % 2])
        P.load(stS, stS[:, :, :], srw[b].rearrange("h v k -> v h k"))
        pst = pg()
        for h in range(6):
            P.tr(pst[0:64, h * 64:(h + 1) * 64], stS[:, h, :], identf[0:64, 0:64], [stS.b, identf.b], [pst.b])
        for h in range(6):
            p, hb = h // 2, (h % 2) * 64
            P.cp("act" if h % 2 else "dve", Hf[hb:hb + 64, p, :], pst[0:64, h * 64:(h + 1) * 64], [pst.b], [Hf.b])
        P.cp("act", Hb[:, :, :], Hf[:, :, :], [Hf.b], [Hb.b])
        P.em.dma("sp", raw[:, 0:9, 0], ssh[b, 0:1152].rearrange("(b p) -> p b", p=128), reads=(), writes=[raw.b], dbuf=raw.b,
                 allow_slow_non_contiguous=True)
        P.em.dma("sp", raw[0:64, 9:10, 0], ssh[b, 1152:1216].rearrange("(b p) -> p b", p=64), reads=(), writes=[raw.b], dbuf=raw.b,
                 allow_slow_non_contiguous=True)
        P.em.dma("sp", raw[:, 10:13, 0], ssh[b, 1216:1600].rearrange("(b p) -> p b", p=128), reads=(), writes=[raw.b], dbuf=raw.b,
                 allow_slow_non_contiguous=True)
        token_tile(xsm[sl, :], SS, 8, 0, o_fks[sl, :], o_fvs[sl, :], o_fls[sl, :], cc[7 % 2], cc[8 % 2])
        fm_proj(SS, 0)
        rwkv_chunk(SS, 0)
        store_shift(o_rhs[b], SS)
        store_state(o_rss[b])
        for blk in range(2):
            ks = slice(blk * 128, (blk + 1) * 128)
            P.load(tmpB, tmpB[:, 0:256], cmk[b, ks, :])
            P.cp("act", mqa[:, :, :], tmpB[:, 0:256].rearrange("p (h d) -> p h d", d=64), [tmpB.b], [mqa.b])
            for h in range(4):
                P.tr(ps_tr[0:64, h * 128:(h + 1) * 128], mqa[:, h, :], identb[:, :], [mqa.b, identb.b], [ps_tr.b])
            P.cp("act", mkT[:, :, blk * 128:(blk + 1) * 128], ps_tr[0:64, 0:512].rearrange("p (h t) -> p h t", t=128), [ps_tr.b], [mkT.b])
            P.load(tmpC, tmpC[:, 0:256], cmv[b, ks, :])
            P.cp("dve", mvaug[:, blk, :, 0:64], tmpC[:, 0:256].rearrange("p (h d) -> p h d", d=64), [tmpC.b], [mvaug.b])
        entries = [(j, 128, 0, False) for j in range(8)] + [(8, SS, 0, True)]
        run_attention(SS, entries, mkT, mvaug, mkT.b, mvaug.b)
        flush_tail()
        flush_gn()
        out_proj([(SS, 0, xsm[sl, :], o_ys[sl, :])])

    P.em.final_wait("pool")
    P.em.replay()
    return nc


_NC = None


def kernel(x_prompt, x_sample, mem_prompt, cache_fox_k, cache_fox_v, cache_fox_logf,
           cache_mem_k, cache_mem_v, state_rwkv, state_rwkv_shift,
           norm_g, w_in, fox_q_g, fox_k_g, fox_b_f, rwkv_mu, rwkv_w0, rwkv_w_up, rwkv_a0,
           rwkv_a_up, rwkv_k_k, rwkv_k_a, rwkv_r_k, rwkv_gn_w, rwkv_gn_b,
           mem_norm_g, w_mem_kv, mem_q_g, mem_k_g, w_out):
    global _NC
    f = lambda a: np.ascontiguousarray(np.asarray(a, dtype=np.float32))
    if _NC is None:
        _NC = build()
    nc = _NC
    shared = dict(norm_g=f(norm_g[0]), w_in=f(w_in[0]), fox_q_g=f(fox_q_g[0]), fox_k_g=f(fox_k_g[0]),
                  fox_b_f=f(fox_b_f[0]), rwkv_mu=f(rwkv_mu[0]), rwkv_w0=f(rwkv_w0[0]), rwkv_w_up=f(rwkv_w_up[0]),
                  rwkv_a0=f(rwkv_a0[0]), rwkv_a_up=f(rwkv_a_up[0]), rwkv_k_k=f(rwkv_k_k[0]), rwkv_k_a=f(rwkv_k_a[0]),
                  rwkv_r_k=f(rwkv_r_k[0]), rwkv_gn_w=f(rwkv_gn_w[0]), rwkv_gn_b=f(rwkv_gn_b[0]),
                  mem_norm_g=f(mem_norm_g[0]), w_mem_kv=f(w_mem_kv[0]), mem_q_g=f(mem_q_g[0]), mem_k_g=f(mem_k_g[0]),
                  w_out=f(w_out[0]))
    in_maps = []
    for c in range(8):
        bs = slice(4 * c, 4 * c + 4)
        m = dict(shared)
        m.update(xp=f(x_prompt[c]), xsm=f(x_sample[bs]).reshape(64, D), memp=f(mem_prompt[c]),
                 cfk=f(cache_fox_k[0, bs]).reshape(4, PAST, 384), cfv=f(cache_fox_v[0, bs]).reshape(4, PAST, 384),
                 cfl=f(cache_fox_logf[0, bs]), cmk=f(cache_mem_k[0, bs]).reshape(4, 256, 256),
                 cmv=f(cache_mem_v[0, bs]).reshape(4, 256, 256), srw=f(state_rwkv[0, bs]),
                 ssh=f(state_rwkv_shift[0, bs]).reshape(4, 1600))
        in_maps.append(m)
    res = run_bass_kernel_spmd(nc, in_maps, core_ids=list(range(8)))
    R = res.results
    cat = lambda k: np.stack([np.asarray(R[c][k]) for c in range(8)])
    yp = cat("o_yp")
    ys = cat("o_ys").reshape(32, 16, D)
    fkp = cat("o_fkp").reshape(1, 8, T, 6, 64)
    fvp = cat("o_fvp").reshape(1, 8, T, 6, 64)
    flp = cat("o_flp").reshape(1, 8, T, 6)
    mkp = cat("o_mkp").reshape(1, 8, 256, 4, 64)
    mvp = cat("o_mvp").reshape(1, 8, 256, 4, 64)
    rsp = cat("o_rsp").reshape(1, 8, 6, 64, 64)
    rhp = cat("o_rhp").reshape(1, 8, 1, 1600)
    fks = cat("o_fks").reshape(1, 32, 16, 6, 64)
    fvs = cat("o_fvs").reshape(1, 32, 16, 6, 64)
    fls = cat("o_fls").reshape(1, 32, 16, 6)
    rss = cat("o_rss").reshape(1, 32, 6, 64, 64)
    rhs = cat("o_rhs").reshape(1, 32, 1, 1600)
    return (yp, ys, fkp, fvp, flp, mkp, mvp, rsp, rhp, fks, fvs, fls, rss, rhs)
```
